# Optimizing a Trainium2 kernel written in Bass

```python
import math
import jax, jax.numpy as jnp
from jax import lax
import numpy as np

D_MODEL = 1024
BATCH = 32
SEQ = 2048
DEPTH = 2

MEM_LEN = 256
CONV_CH = 512
CONV_K = 31
SG_CH = 512
SG_GROUPS = 4
SG_CHUNK = 128
HEAD_DIM = 64
HEADS_PER_GROUP = 4
DIL_GROUPS = ((128, 1), (512, 4), (2048, 16))
ATT_HEADS = HEADS_PER_GROUP * len(DIL_GROUPS)
ATT_BLOCK = 128
REL_BUCKETS = 32
REL_MAX_DIST = 2048
N_BRANCH = 3
X_HEADS = 4
X_HEAD_DIM = D_MODEL // X_HEADS
D_FF = 2816
FFN_CONV_K = 3
NORM_EPS = 1e-6
LN_EPS = 1e-5

COL_A = 2 * CONV_CH
COL_B = 2 * SG_CH
COL_C = 3 * ATT_HEADS * HEAD_DIM
COL_G = N_BRANCH * D_MODEL
OFF_B = COL_A
OFF_C = COL_A + COL_B
OFF_G = COL_A + COL_B + COL_C
IN_COLS = COL_A + COL_B + COL_C + COL_G
ATT_OUT = HEADS_PER_GROUP * HEAD_DIM

kernel_name = "hybrid_conv_sgmlp_dilated_attn_block"


def rms_norm(x, g):
    xf = x.astype(jnp.float32)
    y = xf * lax.rsqrt(jnp.mean(xf * xf, axis=-1, keepdims=True) + NORM_EPS)
    return (y * g.astype(jnp.float32)).astype(x.dtype)


def layer_norm(x, g, b):
    xf = x.astype(jnp.float32)
    mu = jnp.mean(xf, axis=-1, keepdims=True)
    var = jnp.mean(jnp.square(xf - mu), axis=-1, keepdims=True)
    y = (xf - mu) * lax.rsqrt(var + LN_EPS)
    return (y * g.astype(jnp.float32) + b.astype(jnp.float32)).astype(x.dtype)


def causal_dwconv(x, w, b):
    k, c = w.shape
    y = lax.conv_general_dilated(x, w[:, None, :], window_strides=(1,), padding=[(k - 1, 0)],
                                 dimension_numbers=('NWC', 'WIO', 'NWC'), feature_group_count=c)
    return y + b


def t5_bucket(dist):
    n = jnp.maximum(dist, 0)
    max_exact = REL_BUCKETS // 2
    nf = jnp.maximum(n, 1).astype(jnp.float32)
    large = max_exact + (jnp.log(nf / max_exact) / math.log(REL_MAX_DIST / max_exact)
                         * (REL_BUCKETS - max_exact)).astype(jnp.int32)
    large = jnp.minimum(large, REL_BUCKETS - 1)
    return jnp.where(n < max_exact, n, large)


def dilated_window_attention(q, k, v, rel_table, dilation, window):
    B, S, H, E = q.shape
    span = window // dilation
    unit = dilation * ATT_BLOCK
    s_pad = -(-S // unit) * unit
    L = s_pad // dilation
    nb = L // ATT_BLOCK

    def to_blocks(t):
        t = jnp.pad(t, ((0, 0), (0, s_pad - S), (0, 0), (0, 0)))
        t = t.reshape(B, L, dilation, H, E).swapaxes(1, 2)
        return t.reshape(B, dilation, nb, ATT_BLOCK, H, E)

    def with_prev(t):
        prev = jnp.pad(t, ((0, 0), (0, 0), (1, 0), (0, 0), (0, 0), (0, 0)))[:, :, :-1]
        return jnp.concatenate([prev, t], axis=3)

    qb = to_blocks(q)
    kk = with_prev(to_blocks(k))
    vv = with_prev(to_blocks(v))

    qi = jnp.arange(ATT_BLOCK)[:, None]
    ki = jnp.arange(2 * ATT_BLOCK)[None, :]
    rel = qi + ATT_BLOCK - ki
    bias = rel_table[t5_bucket(rel * dilation)].transpose(2, 0, 1).astype(jnp.float32)
    blk_idx = jnp.arange(nb)[:, None, None]
    valid = (rel >= 0) & (rel <= span) & ((blk_idx > 0) | (ki >= ATT_BLOCK))

    scores = jnp.einsum('bcnqhe,bcnkhe->bcnhqk', qb, kk).astype(jnp.float32) * (E ** -0.5) + bias
    scores = jnp.where(valid[:, None], scores, -jnp.inf)
    m = jnp.max(scores, axis=-1, keepdims=True)
    e = jnp.exp(scores - m)
    s = jnp.sum(e, axis=-1, keepdims=True)
    o = jnp.einsum('bcnhqk,bcnkhe->bcnqhe', e, vv.astype(jnp.float32)) / s.swapaxes(3, 4)
    lse = (m + jnp.log(s))[..., 0].swapaxes(3, 4)

    def from_blocks(t):
        tail = t.shape[4:]
        t = t.reshape(B, dilation, L, *tail).swapaxes(1, 2).reshape(B, s_pad, *tail)
        return t[:, :S]

    return from_blocks(o), from_blocks(lse)


def parallel_mixer(h, w_in, b_gate, conv_a_w, conv_a_b, ln_a_g, ln_a_b, w_a_out,
                   ln_b_g, ln_b_b, w_s, b_s, w_b_out, rel_bias, w_c_out, w_mix_out):
    B, S, _ = h.shape
    za = h @ w_in[:, :COL_A]
    a_val, a_gate = jnp.split(za, 2, axis=-1)
    a = a_val * jax.nn.sigmoid(a_gate)
    a = causal_dwconv(a, conv_a_w, conv_a_b)
    a = jax.nn.silu(layer_norm(a, ln_a_g, ln_a_b))
    y_a = a @ w_a_out
    zb = jax.nn.gelu(h @ w_in[:, OFF_B:OFF_C], approximate=True)
    u, v = jnp.split(zb, 2, axis=-1)
    v = layer_norm(v, ln_b_g, ln_b_b)
    v = v.reshape(B, S // SG_CHUNK, SG_CHUNK, SG_GROUPS, SG_CH // SG_GROUPS)
    causal = jnp.tril(jnp.ones((SG_CHUNK, SG_CHUNK), dtype=bool))
    ws = jnp.where(causal, w_s, 0.0)
    v = jnp.einsum('gts,bnsgc->bntgc', ws, v) + b_s.T[:, :, None]
    y_b = (u * v.reshape(B, S, SG_CH)) @ w_b_out
    zc = h @ w_in[:, OFF_C:OFF_G]
    q, k, vc = [t.reshape(B, S, ATT_HEADS, HEAD_DIM) for t in jnp.split(zc, 3, axis=-1)]
    outs, lses = [], []
    for gi, (window, dil) in enumerate(DIL_GROUPS):
        sl = slice(gi * HEADS_PER_GROUP, (gi + 1) * HEADS_PER_GROUP)
        o, l = dilated_window_attention(q[:, :, sl], k[:, :, sl], vc[:, :, sl], rel_bias[:, sl], dil, window)
        outs.append(o)
        lses.append(l)
    wts = jax.nn.softmax(jnp.stack(lses, axis=0), axis=0)
    oc = jnp.sum(wts[..., None] * jnp.stack(outs, axis=0), axis=0)
    y_c = oc.reshape(B, S, ATT_OUT).astype(h.dtype) @ w_c_out
    zg = (h @ w_in[:, OFF_G:]).reshape(B, S, N_BRANCH, D_MODEL) + b_gate
    g = jax.nn.sigmoid(zg)
    merged = g[:, :, 0] * y_a + g[:, :, 1] * y_b + g[:, :, 2] * y_c
    return merged @ w_mix_out


def memory_cross_attention(h, mem_n, w_xq, w_xkv, w_xo):
    B, S, _ = h.shape
    M = mem_n.shape[1]
    q = (h @ w_xq).reshape(B, S, X_HEADS, X_HEAD_DIM)
    k, v = [t.reshape(B, M, X_HEADS, X_HEAD_DIM) for t in jnp.split(mem_n @ w_xkv, 2, axis=-1)]
    s = jnp.einsum('bshe,bmhe->bhsm', q, k).astype(jnp.float32) * (X_HEAD_DIM ** -0.5)
    p = jax.nn.softmax(s, axis=-1)
    o = jnp.einsum('bhsm,bmhe->bshe', p, v.astype(jnp.float32)).reshape(B, S, D_MODEL)
    return o.astype(h.dtype) @ w_xo


def conv_ffn(h, w_up, conv_f_w, conv_f_b, w_down):
    gate = causal_dwconv(h @ w_up[:, :D_FF], conv_f_w, conv_f_b)
    val = h @ w_up[:, D_FF:]
    return (jax.nn.gelu(gate, approximate=True) * val) @ w_down


def setup_inputs(seed: int = 0) -> dict:
    key = jax.random.key(seed)
    ks = iter(jax.random.split(key, 40))

    def nrm(shape, scale):
        return jax.random.normal(next(ks), shape, jnp.float32) * scale

    def gain(shape):
        return 1.0 + nrm(shape, 0.05)

    L = DEPTH
    return {
        "x": nrm((BATCH, SEQ, D_MODEL), 1.0),
        "mem": nrm((BATCH, MEM_LEN, D_MODEL), 1.0),
        "rel_bias": nrm((REL_BUCKETS, ATT_HEADS), 0.5),
        "mix_pre_g": gain((L, D_MODEL)),
        "mix_post_g": gain((L, D_MODEL)),
        "w_in": nrm((L, D_MODEL, IN_COLS), D_MODEL ** -0.5),
        "b_gate": nrm((L, N_BRANCH, D_MODEL), 0.02),
        "conv_a_w": nrm((L, CONV_K, CONV_CH), CONV_K ** -0.5),
        "conv_a_b": nrm((L, CONV_CH), 0.02),
        "ln_a_g": gain((L, CONV_CH)),
        "ln_a_b": nrm((L, CONV_CH), 0.02),
        "w_a_out": nrm((L, CONV_CH, D_MODEL), CONV_CH ** -0.5),
        "ln_b_g": gain((L, SG_CH)),
        "ln_b_b": nrm((L, SG_CH), 0.02),
        "w_s": nrm((L, SG_GROUPS, SG_CHUNK, SG_CHUNK), SG_CHUNK ** -0.5),
        "b_s": 1.0 + nrm((L, SG_GROUPS, SG_CHUNK), 0.05),
        "w_b_out": nrm((L, SG_CH, D_MODEL), SG_CH ** -0.5),
        "w_c_out": nrm((L, ATT_OUT, D_MODEL), ATT_OUT ** -0.5),
        "w_mix_out": nrm((L, D_MODEL, D_MODEL), D_MODEL ** -0.5),
        "x_pre_g": gain((L, D_MODEL)),
        "x_post_g": gain((L, D_MODEL)),
        "mem_g": gain((L, D_MODEL)),
        "w_xq": nrm((L, D_MODEL, D_MODEL), D_MODEL ** -0.5),
        "w_xkv": nrm((L, D_MODEL, 2 * D_MODEL), D_MODEL ** -0.5),
        "w_xo": nrm((L, D_MODEL, D_MODEL), D_MODEL ** -0.5),
        "ffn_pre_g": gain((L, D_MODEL)),
        "ffn_post_g": gain((L, D_MODEL)),
        "w_up": nrm((L, D_MODEL, 2 * D_FF), D_MODEL ** -0.5),
        "conv_f_w": nrm((L, FFN_CONV_K, D_FF), FFN_CONV_K ** -0.5),
        "conv_f_b": nrm((L, D_FF), 0.02),
        "w_down": nrm((L, D_FF, D_MODEL), D_FF ** -0.5),
    }


def reference(x, mem, rel_bias, mix_pre_g, mix_post_g, w_in, b_gate, conv_a_w, conv_a_b,
              ln_a_g, ln_a_b, w_a_out, ln_b_g, ln_b_b, w_s, b_s, w_b_out, w_c_out, w_mix_out,
              x_pre_g, x_post_g, mem_g, w_xq, w_xkv, w_xo,
              ffn_pre_g, ffn_post_g, w_up, conv_f_w, conv_f_b, w_down):
    for l in range(DEPTH):
        h = rms_norm(x, mix_pre_g[l])
        y = parallel_mixer(h, w_in[l], b_gate[l], conv_a_w[l], conv_a_b[l], ln_a_g[l], ln_a_b[l],
                           w_a_out[l], ln_b_g[l], ln_b_b[l], w_s[l], b_s[l], w_b_out[l],
                           rel_bias, w_c_out[l], w_mix_out[l])
        x = x + rms_norm(y, mix_post_g[l])

        h = rms_norm(x, x_pre_g[l])
        y = memory_cross_attention(h, rms_norm(mem, mem_g[l]), w_xq[l], w_xkv[l], w_xo[l])
        x = x + rms_norm(y, x_post_g[l])

        h = rms_norm(x, ffn_pre_g[l])
        y = conv_ffn(h, w_up[l], conv_f_w[l], conv_f_b[l], w_down[l])
        x = x + rms_norm(y, ffn_post_g[l])
    return x
```

```python
import numpy as np
import concourse.bass as bass
import concourse.mybir as mybir
from concourse.bass_utils import run_bass_kernel_spmd

F32 = mybir.dt.float32
BF16 = mybir.dt.bfloat16
AF = mybir.ActivationFunctionType
ALU = mybir.AluOpType
AX = mybir.AxisListType

EPOCH = 12000
SAME_SYNC = True

N_CORES = 8
SEQ_PER_CORE = 4
DEPTH = 2
D = 1024
S = 2048
NSLOT = 10
XATTN_PIPE = False
DEFER_STATS = True
PPL = 304
DILS = (1, 4, 16)


class Buf:
    __slots__ = ("name", "w", "r", "al", "dsem", "dcnt")

    def __init__(self, name):
        self.name = name
        self.w = None
        self.r = {}
        self.al = []
        self.dsem = None
        self.dcnt = 0


class Dummy:
    shape = ()

    def __getitem__(self, i):
        return self

    def rearrange(self, *a, **k):
        return self


DUMMY = Dummy()


class Eng:
    def __init__(self, K, name, obj):
        self.name = name
        self.obj = obj
        self.sem = None if K.dry else K.nc.alloc_semaphore(f"s_{name}_0")
        self.nsem = 1
        self.cnt = 0
        self.own = set() if K.dry else {id(self.sem)}
        self.waited = {}
        self.nwaits = 0
        self.nins = 0


class K:
    def __init__(self, nc, dry=False):
        self.nc = nc
        self.dry = dry
        if dry:
            self.engs = {n: Eng(self, n, None) for n in ("pe", "act", "dve", "pool", "sp")}
        else:
            self.engs = {
                "pe": Eng(self, "pe", nc.tensor),
                "act": Eng(self, "act", nc.scalar),
                "dve": Eng(self, "dve", nc.vector),
                "pool": Eng(self, "pool", nc.gpsimd),
                "sp": Eng(self, "sp", nc.sync),
            }
        self.semobj = {}
        if not dry:
            for e in self.engs.values():
                self.semobj[id(e.sem)] = e.sem
        self.ndsem = 0

    def _deps(self, reads, writes):
        deps = {}

        def add(d):
            if d is None:
                return
            s, v = d
            if deps.get(s, 0) < v:
                deps[s] = v

        for b in reads:
            add(b.w)
            for a in b.al:
                add(a.w)
        for b in writes:
            add(b.w)
            for d in b.r.values():
                add(d)
            for a in b.al:
                add(a.w)
                for d in a.r.values():
                    add(d)
        return deps

    def _wait(self, E, deps):
        for s, v in deps.items():
            if s in E.own:
                if E.name in ("pe", "sp") or not SAME_SYNC:
                    continue
                if s == id(E.sem) and v > E.cnt:
                    continue
                if E.name in ("act", "dve") and (s != id(E.sem) or v < E.cnt):
                    continue
            if E.waited.get(s, 0) >= v:
                continue
            E.obj.wait_ge(self.semobj[s], v)
            E.waited[s] = v
            E.nwaits += 1

    def _tag(self, E, ins, inc):
        E.nins += 1
        if inc:
            ins.then_inc(E.sem, 1)
            E.cnt += 1
            tag = (id(E.sem), E.cnt)
            if E.cnt >= EPOCH:
                E.sem = self.nc.alloc_semaphore(f"s_{E.name}_{E.nsem}")
                E.nsem += 1
                E.cnt = 0
                E.own.add(id(E.sem))
                self.semobj[id(E.sem)] = E.sem
        else:
            tag = (id(E.sem), E.cnt + 1)
        return tag

    def op(self, eng, fn, reads=(), writes=(), inc=True):
        if self.dry:
            return None
        E = self.engs[eng]
        self._wait(E, self._deps(reads, writes))
        ins = fn(E.obj)
        tag = self._tag(E, ins, inc)
        for b in writes:
            b.w = tag
            b.r = {}
        for b in reads:
            b.r[eng] = tag
        return ins

    def dma(self, eng, pairs, reads=(), writes=(), sembuf=None):
        if self.dry:
            return None
        E = self.engs[eng]
        sb = sembuf or (writes[0] if writes else reads[0])
        if sb.dsem is None:
            sb.dsem = self.nc.alloc_semaphore(f"d_{self.ndsem}")
            self.ndsem += 1
            self.semobj[id(sb.dsem)] = sb.dsem
        deps = self._deps(reads, writes)
        if sb.dcnt:
            s = id(sb.dsem)
            if deps.get(s, 0) < sb.dcnt:
                deps[s] = sb.dcnt
        self._wait(E, deps)
        for (o, i) in pairs:
            E.obj.dma_start(out=o, in_=i).then_inc(sb.dsem, 16)
            E.nins += 1
            sb.dcnt += 16
        tag = (id(sb.dsem), sb.dcnt)
        for b in writes:
            b.w = tag
            b.r = {}
        for b in reads:
            b.r["dma_" + sb.name] = tag
        return tag

    def wait_all(self, eng, bufs):
        if self.dry:
            return
        E = self.engs[eng]
        self._wait(E, self._deps(bufs, bufs))

    def stats(self):
        return {e.name: (e.nins, e.nwaits, e.nsem) for e in self.engs.values()}


class _Stop(Exception):
    pass


class Tile:
    __slots__ = ("i", "ap", "buf")

    def __init__(self, i, ap, buf):
        self.i = i
        self.ap = ap
        self.buf = buf


class WStream:
    def __init__(self, k, ring, rbufs, seq, ap_of):
        self.k = k
        self.ring = ring
        self.rbufs = rbufs
        self.seq = seq
        self.ap_of = ap_of
        self.rec = []
        self.idx = 0
        self.loaded = 0
        self.done_ = set()

    def _issue(self):
        while self.loaded < len(self.seq):
            i = self.loaded
            if i - NSLOT >= 0 and (i - NSLOT) not in self.done_:
                break
            slot = i % NSLOT
            src, kcn = self.ap_of(self.seq[i])
            self.k.dma("pool", [(self.ring[:, slot, 0:kcn, :], src)], writes=[self.rbufs[slot]])
            self.loaded += 1

    def get(self, desc):
        i = self.idx
        self.idx += 1
        if self.k.dry:
            self.rec.append(desc)
            return Tile(i, DUMMY, None)
        assert self.seq[i] == desc, (i, self.seq[i], desc)
        self._issue()
        assert self.loaded > i, f"weight ring too small at tile {i} {desc}"
        slot = i % NSLOT
        return Tile(i, self.ring[:, slot], self.rbufs[slot])

    def done(self, t):
        if self.k.dry:
            return
        self.done_.add(t.i)
        self._issue()


class Arena:
    def __init__(self, k, nc, base, size):
        self.k, self.nc, self.base, self.size = k, nc, base, size
        self.off = 0
        self.regs = []
        self.n = 0

    def reset(self, off=0):
        self.off = off

    def alloc(self, name, shape, dtype, nbufs=1):
        esz = 2 if dtype == BF16 else 4
        per = esz
        for s_ in shape[1:]:
            per *= s_
        per = (per + 63) // 64 * 64
        lo = self.off
        hi = lo + per
        assert hi <= self.size, f"arena overflow {name}: {hi} > {self.size}"
        self.off = hi
        bufs = [Buf(f"{name}{i}") for i in range(nbufs)]
        pbuf = per // nbufs
        new = []
        for bi, b in enumerate(bufs):
            blo, bhi = lo + bi * pbuf, (lo + (bi + 1) * pbuf if bi < nbufs - 1 else hi)
            for (l2, h2, b2) in self.regs:
                if l2 < bhi and blo < h2:
                    if b2 not in b.al:
                        b.al.append(b2)
                    if b not in b2.al:
                        b2.al.append(b)
            new.append((blo, bhi, b))
        self.regs.extend(new)
        if self.k.dry:
            t = DUMMY
        else:
            self.n += 1
            t = self.nc.alloc_sbuf_tensor_at(f"ar{self.n}_{name}", list(shape), dtype, offset=self.base + lo)
        return (t, bufs[0]) if nbufs == 1 else (t, bufs)


def ts(tt, n=512):
    return slice(tt * n, (tt + 1) * n)


def build_program(nc, k, wseq, nseq=SEQ_PER_CORE, nlayer=DEPTH, stop_after=None, dbg=None):
    dry = k.dry
    T = {}
    if not dry:
        def dram(name, shape, kind="ExternalInput"):
            T[name] = nc.dram_tensor(name, list(shape), F32, kind=kind).ap()
        dram("xT", [SEQ_PER_CORE, 8, 128, S])
        dram("memT", [SEQ_PER_CORE, 8, 128, 256])
        dram("pp", [128, DEPTH * PPL])
        dram("abias", [128, 3072])
        dram("amask", [128, 3072])
        dram("pbc", [DEPTH, 128, 1536])
        dram("wsT", [DEPTH, 128, 4, 128])
        dram("trilm", [128, 4, 128])
        dram("ident", [128, 128])
        dram("w_in", [DEPTH, 58, 128, 8, 128])
        dram("w_a", [DEPTH, 8, 128, 4, 128])
        dram("w_b", [DEPTH, 8, 128, 4, 128])
        dram("w_c", [DEPTH, 8, 128, 2, 128])
        dram("w_mix", [DEPTH, 8, 128, 8, 128])
        dram("w_xq", [DEPTH, 8, 128, 8, 128])
        dram("w_xkv", [DEPTH, 16, 128, 8, 128])
        dram("w_xo", [DEPTH, 8, 128, 8, 128])
        dram("w_up", [DEPTH, 44, 128, 8, 128])
        dram("w_down", [DEPTH, 8, 128, 22, 128])
        dram("outT", [SEQ_PER_CORE, 8, 128, S], kind="ExternalOutput")

    def ap_of(desc):
        name, l, j, k0, kcn = desc
        return T[name][l, j, :, k0:k0 + kcn, :], kcn

    BASE = 16512
    XRES_O = BASE
    H_O = XRES_O + 65536
    RING_O = H_O + 32768
    C_O = RING_O + NSLOT * 2048
    coff = [C_O]

    def calloc(name, shape, dtype):
        esz = 2 if dtype == BF16 else 4
        per = esz
        for s_ in shape[1:]:
            per *= s_
        per = (per + 63) // 64 * 64
        o = coff[0]
        coff[0] += per
        if dry:
            return DUMMY
        return nc.alloc_sbuf_tensor_at(name, list(shape), dtype, offset=o)

    if dry:
        XRES = H = Y = RING = DUMMY
    else:
        XRES = nc.alloc_sbuf_tensor_at("XRES", [128, 8, S], F32, offset=XRES_O)
        H = nc.alloc_sbuf_tensor_at("H", [128, 8, S], BF16, offset=H_O)
        Y = nc.alloc_sbuf_tensor_at("Y", [128, 8, 1024], F32, offset=H_O)
        RING = nc.alloc_sbuf_tensor_at("RING", [128, NSLOT, 8, 128], BF16, offset=RING_O)
    PP = calloc("PP", [128, DEPTH * PPL], F32)
    EXPBM = calloc("EXPBM", [128, 3, 4, 2, 128], BF16)
    PBC = calloc("PBC", [128, 1536], F32)
    WSRAW = calloc("WSRAW", [128, 4, 128], BF16)
    TRILM = calloc("TRILM", [128, 4, 128], BF16)
    WSM = calloc("WSM", [128, 4, 128], BF16)
    ONESB = calloc("ONESB", [128, 128], BF16)
    IDENT = calloc("IDENT", [128, 128], BF16)
    ONESF = calloc("ONESF", [128, 128], F32)
    RSTD = calloc("RSTD", [128, 2, 512], F32)
    SQ = calloc("SQ", [128, 2, 512], BF16)
    HALO = calloc("HALO", [128, 22, 2], F32)
    EPS = calloc("EPS", [128, 2], F32)
    SMALL = calloc("SMALL", [128, 2, 8], F32)
    ST6 = calloc("ST6", [128, 2, 6], F32)
    ARENA_O = (coff[0] + 63) // 64 * 64
    ARENA_SZ = 229344 - ARENA_O
    A = Arena(k, nc, ARENA_O, ARENA_SZ)

    BX = [Buf(f"X{t}") for t in range(4)]
    BH = [Buf(f"H{t}") for t in range(4)]
    BY = [Buf("Y0"), Buf("Y1")]
    for yi, hts in ((0, (0, 1)), (1, (2, 3))):
        for t in hts:
            BY[yi].al.append(BH[t])
            BH[t].al.append(BY[yi])
    RB = [Buf(f"ring{i}") for i in range(NSLOT)]
    BPP, BEXPBM, BPBC, BWSRAW, BTRILM, BWSM, BONES, BHALO, BEPS = [Buf(n) for n in (
        "PP", "EXPBM", "PBC", "WSRAW", "TRILM", "WSM", "ONES", "HALO", "EPS")]
    BRSTD = [Buf("RSTD0"), Buf("RSTD1")]
    BIDENT = Buf("IDENT")
    BSQ = [Buf("SQ0"), Buf("SQ1")]
    BSMALL = [Buf("SM0"), Buf("SM1")]
    if dry:
        PS = [DUMMY] * 8
    else:
        PS = [nc.alloc_psum_tensor(f"ps{i}", [128, 512], F32) for i in range(8)]
    BPS = [Buf(f"ps{i}") for i in range(8)]

    ws = WStream(k, RING, RB, wseq, ap_of)

    class Banks:
        def __init__(self):
            self.rot = list(range(8))
            self.i = 0

        def set(self, lst):
            self.rot = list(lst)
            self.i = 0

        def next(self):
            b = self.rot[self.i % len(self.rot)]
            self.i += 1
            return b

    banks = Banks()

    def mm(out, lhsT, rhs, first, last, reads, psb, inc=None):
        k.op("pe", lambda e: e.matmul(out, lhsT=lhsT, rhs=rhs, start=first, stop=last), reads=reads, writes=[psb],
             inc=last if inc is None else inc)

    def act(out, in_, func, reads, writes, bias=None, scale=None, accum=None):
        kw = {}
        if bias is not None:
            kw["bias"] = bias
        if scale is not None:
            kw["scale"] = scale
        if accum is not None:
            kw["accum_out"] = accum
        k.op("act", lambda e: e.activation(out=out, in_=in_, func=func, **kw), reads=reads, writes=writes)

    def tt_(out, in0, in1, op, reads, writes, eng="dve"):
        k.op(eng, lambda e: e.tensor_tensor(out=out, in0=in0, in1=in1, op=op), reads=reads, writes=writes)

    def tsc(out, in0, s1, s2, op0, op1, reads, writes, eng="dve"):
        if op1 is None:
            k.op(eng, lambda e: e.tensor_scalar(out=out, in0=in0, scalar1=s1, scalar2=None, op0=op0), reads=reads,
                 writes=writes)
        else:
            k.op(eng, lambda e: e.tensor_scalar(out=out, in0=in0, scalar1=s1, scalar2=s2, op0=op0, op1=op1),
                 reads=reads, writes=writes)

    def stt(out, in0, scalar, in1, op0, op1, reads, writes):
        k.op("dve", lambda e: e.scalar_tensor_tensor(out=out, in0=in0, scalar=scalar, in1=in1, op0=op0, op1=op1),
             reads=reads, writes=writes)

    def cpy(out, in_, reads, writes, eng="dve"):
        k.op(eng, lambda e: e.tensor_copy(out=out, in_=in_), reads=reads, writes=writes)

    def recip(out, in_, reads, writes):
        k.op("dve", lambda e: e.reciprocal(out=out, in_=in_), reads=reads, writes=writes)

    def memset(ap, val, writes, eng="dve"):
        k.op(eng, lambda e: e.memset(ap, val), writes=writes)

    dbg_out = []

    def chk(name):
        if stop_after == name:
            raise _Stop()

    def dump(name, ap, shape, dtype, bufs):
        if dry or dbg is None or name not in dbg:
            return
        t = nc.dram_tensor("dbg_" + name, list(shape), dtype, kind="ExternalOutput").ap()
        b = Buf("dbg_" + name)
        k.dma("sp", [(t, ap)], reads=bufs, writes=[b], sembuf=b)
        dbg_out.append(b)

    def proj_fm(tile, kcn, rhs_of, rbufs_of, evac, tiles=range(4)):
        for tt in tiles:
            b = banks.next()
            for kc in range(kcn):
                mm(PS[b][:, :], tile.ap[:, kc, :], rhs_of(kc, tt), kc == 0, kc == kcn - 1,
                   [tile.buf] + rbufs_of(tt), BPS[b])
            evac(tt, PS[b], BPS[b])

    def h_rhs(kc, tt):
        return H[:, kc, ts(tt)]

    def h_bufs(tt):
        return [BH[tt]]

    def rstd_from(psb_idx, ri, scale, epscol):
        act(RSTD[:, ri, :], PS[psb_idx][:, :], AF.Sqrt, [BPS[psb_idx], BEPS], [BRSTD[ri]], bias=EPS[:, epscol:epscol + 1],
            scale=scale)
        recip(RSTD[:, ri, :], RSTD[:, ri, :], [BRSTD[ri]], [BRSTD[ri]])

    nrm_ctr = [0]

    def norm_to_H(gcol, tiles):
        for tt in tiles:
            b = banks.next()
            for kc in range(8):
                si = kc % 2
                act(SQ[:, si, :], XRES[:, kc, ts(tt)], AF.Square, [BX[tt]], [BSQ[si]])
                mm(PS[b][:, :], ONESB[:, :], SQ[:, si, :], kc == 0, kc == 7, [BONES, BSQ[si]], BPS[b], inc=True)
            ri = nrm_ctr[0] % 2
            nrm_ctr[0] += 1
            rstd_from(b, ri, 1.0 / D, 0)
            for kc in range(8):
                stt(H[:, kc, ts(tt)], XRES[:, kc, ts(tt)], PP[:, gcol + kc:gcol + kc + 1], RSTD[:, ri, :], ALU.mult,
                    ALU.mult, [BX[tt], BPP, BRSTD[ri]], [BH[tt]])

    pend = []

    def y_chunk(ps_ap, psbuf, dc, ysel, sbank, first, last):
        ysl = slice(ysel * 512, ysel * 512 + 512)
        act(Y[:, dc, ysl], ps_ap, AF.Copy, [psbuf], [BY[ysel]])
        si = dc % 2
        act(SQ[:, si, :], ps_ap, AF.Square, [psbuf], [BSQ[si]])
        pend.append((sbank, si, first, last))
        if not DEFER_STATS:
            flush_stats()

    def flush_stats(keep=0):
        while len(pend) > keep:
            sbank, si, first, last = pend.pop(0)
            mm(PS[sbank][:, :], ONESB[:, :], SQ[:, si, :], first, last, [BONES, BSQ[si]], BPS[sbank], inc=True)

    def post_update(gcol, tt, ysel, sbank):
        flush_stats()
        ysl = slice(ysel * 512, ysel * 512 + 512)
        ri = nrm_ctr[0] % 2
        nrm_ctr[0] += 1
        rstd_from(sbank, ri, 1.0 / D, 0)
        for dc in range(8):
            tt_(Y[:, dc, ysl], Y[:, dc, ysl], RSTD[:, ri, :], ALU.mult, [BY[ysel], BRSTD[ri]], [BY[ysel]])
            stt(XRES[:, dc, ts(tt)], Y[:, dc, ysl], PP[:, gcol + dc:gcol + dc + 1], XRES[:, dc, ts(tt)], ALU.mult,
                ALU.add, [BY[ysel], BPP, BX[tt]], [BX[tt]])

    if not dry:
        k.dma("sp", [(PP[:, :], T["pp"][:, :])], writes=[BPP])
    memset(ONESB[:, :], 1.0, [BONES])
    memset(ONESF[:, :], 1.0, [BONES])
    memset(EPS[:, 0:1], 1e-6, [BEPS])
    memset(EPS[:, 1:2], 1e-5, [BEPS])
    if not dry:
        k.dma("pool", [(TRILM[:, :, :], T["trilm"][:, :, :])], writes=[BTRILM])
        k.dma("pool", [(IDENT[:, :], T["ident"][:, :])], writes=[BIDENT])
    A.reset()
    AB, BAB = A.alloc("abias", [128, 3072], F32)
    AM, BAM = A.alloc("amask", [128, 3072], F32)
    if not dry:
        k.dma("sp", [(AB[:, :], T["abias"][:, :])], writes=[BAB])
        k.dma("sp", [(AM[:, :], T["amask"][:, :])], writes=[BAM])
    if not dry:
        ebm_flat = EXPBM[:, :, :, :, :].rearrange("p g h c q -> p (g h c q)")
    else:
        ebm_flat = DUMMY
    stt(ebm_flat, AB[:, :], 8.0, AM[:, :], ALU.mult, ALU.add, [BAB, BAM], [BEXPBM])

    def sublayer_mixer(l):
        pb = l * PPL
        banks.set(range(8))
        if not dry:
            k.dma("sp", [(PBC[:, :], T["pbc"][l, :, :])], writes=[BPBC])
            k.dma("pool", [(WSRAW[:, :, :], T["wsT"][l, :, :, :])], writes=[BWSRAW])
        tt_(WSM[:, :, :], WSRAW[:, :, :], TRILM[:, :, :], ALU.mult, [BWSRAW, BTRILM], [BWSM])
        norm_to_H(pb + 0, range(4))
        dump(f"h1_{l}", H[:, :, :], [128, 8, S], BF16, BH)
        if stop_after == "norm1":
            return True

        A.reset()
        OC, BOC = A.alloc("OC", [128, 2, S], BF16)
        keep_c = A.off
        Ut, BU = A.alloc("U", [128, 2, S], F32)
        qk_o = A.off
        Qb, BQ = A.alloc("Q", [128, 2, S], BF16, nbufs=2)
        Kb, BK = A.alloc("K", [128, 2, S], BF16, nbufs=2)
        end_qk = A.off
        A.reset(qk_o)
        R_, BR = A.alloc("R", [128, S], F32)
        A.reset(end_qk)
        VA, BVA = A.alloc("VA", [128, 2, 16, 2, 128], BF16, nbufs=2)
        NPT = 4
        PT, BPT = A.alloc("PT", [128, NPT, 2, 2, 128], BF16, nbufs=NPT)
        memset(VA[:, :, :, 0, 64:128], 1.0, BVA)
        memset(VA[:, :, :, 1, 0:64], 1.0, BVA)
        gctr = 0
        for hp in range(2):
            for g in range(3):
                d = DILS[g]
                nb = 16 // d
                pq = gctr % 2
                gctr += 1
                tq = ws.get(("w_in", l, 16 + g * 2 + hp, 0, 8))
                tk = ws.get(("w_in", l, 22 + g * 2 + hp, 0, 8))
                tv = ws.get(("w_in", l, 28 + g * 2 + hp, 0, 8))
                for (tile, dst, bdst) in ((tq, Qb, BQ[pq]), (tk, Kb, BK[pq])):
                    def ev(tt, ps, psb, dst=dst, bdst=bdst, pq=pq):
                        act(dst[:, pq, ts(tt)], ps[:, :], AF.Copy, [psb], [bdst])
                    proj_fm(tile, 8, h_rhs, h_bufs, ev)
                    ws.done(tile)
                    chk("Cq")
                chk("Cqk")

                def tokslice(j, d=d, nb=nb):
                    c, n = divmod(j, nb)
                    st = c + d * 128 * n
                    return slice(st, st + d * 127 + 1, d), n

                for j0 in range(0, 16, 4):
                    b = banks.next()
                    for jj in range(4):
                        tok, n = tokslice(j0 + jj)
                        for kc in range(8):
                            mm(PS[b][:, jj * 128:(jj + 1) * 128], H[:, kc, tok], tv.ap[:, kc, :], kc == 0, kc == 7,
                               [tv.buf] + BH, BPS[b], inc=(kc == 7 and jj == 3))
                    psv = PS[b][:, :].rearrange("p (j c) -> p j c", c=128)
                    act(VA[:, pq, j0:j0 + 4, 0, 0:64], psv[:, :, 0:64], AF.Copy, [BPS[b]], [BVA[pq]])
                    act(VA[:, pq, j0:j0 + 4, 1, 64:128], psv[:, :, 64:128], AF.Copy, [BPS[b]], [BVA[pq]])
                ws.done(tv)
                chk("Cv")

                def stage1(j):
                    tok, n = tokslice(j)
                    pcs = (0, 1) if n > 0 else (1,)
                    eb = j % NPT
                    for h in range(2):
                        b = banks.next()
                        pr = slice(64 * h, 64 * h + 64)
                        for pc in pcs:
                            ktok, _ = tokslice(j - 1 if pc == 0 else j)
                            mm(PS[b][:, pc * 128:(pc + 1) * 128], Kb[pr, pq, ktok], Qb[pr, pq, tok], True, False,
                               [BQ[pq], BK[pq]], BPS[b], inc=False)
                            mm(PS[b][:, pc * 128:(pc + 1) * 128], IDENT[:, :], EXPBM[:, g, hp * 2 + h, pc, :], False, True,
                               [BIDENT, BEXPBM], BPS[b], inc=(pc == 1))
                        psv = PS[b][:, 0:256].rearrange("p (c q) -> p c q", c=2)
                        if n > 0:
                            p_ap, s_ap = PT[:, eb, h], psv
                        else:
                            p_ap, s_ap = PT[:, eb, h, 1, :], psv[:, 1, :]
                        act(p_ap, s_ap, AF.Exp, [BPS[b]], [BPT[eb]], scale=0.125)

                def stage2(j):
                    tok, n = tokslice(j)
                    pcs = (0, 1) if n > 0 else (1,)
                    eb = j % NPT
                    b2 = banks.next()
                    for h in range(2):
                        for i, pc in enumerate(pcs):
                            vj = j - 1 if pc == 0 else j
                            mm(PS[b2][:, h * 128:(h + 1) * 128], VA[:, pq, vj, h, :], PT[:, eb, h, pc, :], i == 0,
                               i == len(pcs) - 1, [BVA[pq], BPT[eb]], BPS[b2], inc=(h == 1 and i == len(pcs) - 1))
                    psu = PS[b2][:, 0:256].rearrange("p (h q) -> p h q", h=2)
                    if g == 0:
                        act(Ut[:, :, tok], psu, AF.Copy, [BPS[b2]], [BU])
                    else:
                        tt_(Ut[:, :, tok], psu, Ut[:, :, tok], ALU.add, [BPS[b2], BU], [BU])

                DEPTH_C = 2
                for j in range(16):
                    stage1(j)
                    if j >= DEPTH_C:
                        stage2(j - DEPTH_C)
                for j in range(16 - DEPTH_C, 16):
                    stage2(j)
            recip(R_[0:64, :], Ut[64:128, 0, :], [BU], [BR])
            tt_(OC[0:64, hp, :], Ut[0:64, 0, :], R_[0:64, :], ALU.mult, [BU, BR], [BOC])
            recip(R_[64:128, :], Ut[0:64, 1, :], [BU], [BR])
            tt_(OC[64:128, hp, :], Ut[64:128, 1, :], R_[64:128, :], ALU.mult, [BU, BR], [BOC])
        dump(f"oc{l}", OC[:, :, :], [128, 2, S], BF16, [BOC])
        if stop_after == "C":
            return True

        A.reset(keep_c)
        AACT, BAACT = A.alloc("AACT", [128, 4, S], BF16)
        keep_a = A.off
        APAD, BAPAD = A.alloc("APAD", [128, 4, 32 + S], BF16)
        tmp_o = A.off
        SG, BSG = A.alloc("SG", [128, 2, 512], F32, nbufs=2)
        A.reset(tmp_o)
        CT, BCT = A.alloc("CT", [128, 4, 512], F32, nbufs=4)
        NDG = 16
        DG, BDG = A.alloc("DG", [128, NDG, 128], BF16, nbufs=NDG)
        SQF, BSQF = A.alloc("SQF", [128, 2, 512], F32, nbufs=2)
        MEAN, BMEAN = A.alloc("MEAN", [128, 512], F32)
        VAR, BVAR = A.alloc("VAR", [128, 512], F32)
        memset(APAD[:, :, 0:32], 0.0, [BAPAD])
        for c in range(4):
            tval = ws.get(("w_in", l, c, 0, 8))
            tgate = ws.get(("w_in", l, 4 + c, 0, 8))
            for tt in range(4):
                bv = banks.next()
                bg = banks.next()
                for kc in range(8):
                    mm(PS[bv][:, :], tval.ap[:, kc, :], H[:, kc, ts(tt)], kc == 0, kc == 7, [tval.buf, BH[tt]], BPS[bv])
                for kc in range(8):
                    mm(PS[bg][:, :], tgate.ap[:, kc, :], H[:, kc, ts(tt)], kc == 0, kc == 7, [tgate.buf, BH[tt]],
                       BPS[bg])
                si = tt % 2
                act(SG[:, si, :], PS[bg][:, :], AF.Sigmoid, [BPS[bg]], [BSG[si]])
                tt_(APAD[:, c, 32 + tt * 512:32 + (tt + 1) * 512], PS[bv][:, :], SG[:, si, :], ALU.mult,
                    [BPS[bv], BSG[si]], [BAPAD])
            ws.done(tval)
            ws.done(tgate)
        dump(f"glu{l}", APAD[:, :, :], [128, 4, 32 + S], BF16, [BAPAD])
        cw = pb + 80
        dctr = 0
        for tt in range(4):
            base = 2 + tt * 512
            for c in range(4):
                b = banks.next()
                for kk in range(31):
                    sl = dctr % NDG
                    dctr += 1
                    wcol = PP[:, cw + kk * 4 + c:cw + kk * 4 + c + 1]
                    if kk % 2 == 0:
                        act(DG[:, sl, :], IDENT[:, :], AF.Identity, [BIDENT, BPP], [BDG[sl]], scale=wcol)
                    else:
                        tsc(DG[:, sl, :], IDENT[:, :], wcol, None, ALU.mult, None, [BIDENT, BPP], [BDG[sl]])
                    mm(PS[b][:, :], DG[:, sl, :], APAD[:, c, base + kk:base + kk + 512], kk == 0, kk == 30,
                       [BDG[sl], BAPAD], BPS[b], inc=True)
                act(CT[:, c, :], PS[b][:, :], AF.Identity, [BPS[b], BPP], [BCT[c]], bias=PP[:, pb + 204 + c:pb + 205 + c])
            bsum = banks.next()
            bsq = banks.next()
            for c in range(4):
                o = CT[:, c, :]
                si = c % 2
                act(SQF[:, si, :], o, AF.Square, [BCT[c]], [BSQF[si]])
                mm(PS[bsum][:, :], ONESF[:, :], o, c == 0, c == 3, [BONES, BCT[c]], BPS[bsum], inc=True)
                mm(PS[bsq][:, :], ONESF[:, :], SQF[:, si, :], c == 0, c == 3, [BONES, BSQF[si]], BPS[bsq], inc=True)
            act(MEAN[:, :], PS[bsum][:, :], AF.Identity, [BPS[bsum]], [BMEAN], scale=1.0 / 512)
            tt_(VAR[:, :], MEAN[:, :], MEAN[:, :], ALU.mult, [BMEAN], [BVAR])
            stt(VAR[:, :], PS[bsq][:, :], 1.0 / 512, VAR[:, :], ALU.mult, ALU.subtract, [BPS[bsq], BVAR], [BVAR])
            act(VAR[:, :], VAR[:, :], AF.Sqrt, [BVAR, BEPS], [BVAR], bias=EPS[:, 1:2])
            recip(VAR[:, :], VAR[:, :], [BVAR], [BVAR])
            for c in range(4):
                tt_(CT[:, c, :], CT[:, c, :], MEAN[:, :], ALU.subtract, [BCT[c], BMEAN], [BCT[c]])
            for c in range(4):
                tt_(CT[:, c, :], CT[:, c, :], VAR[:, :], ALU.mult, [BCT[c], BVAR], [BCT[c]])
            for c in range(4):
                act(AACT[:, c, ts(tt)], CT[:, c, :], AF.Silu, [BCT[c], BPP], [BAACT], bias=PP[:, pb + 212 + c:pb + 213 + c],
                    scale=PP[:, pb + 208 + c:pb + 209 + c])
        dump(f"aact{l}", AACT[:, :, :], [128, 4, S], BF16, [BAACT])
        if stop_after == "A":
            return True

        A.reset(keep_a)
        UB, BUB = A.alloc("UB", [128, 4, S], BF16)
        keep_b = A.off
        UU, BUU = A.alloc("UU", [128, 4, S], BF16)
        VG, BVG = A.alloc("VG", [128, 2, 512], F32, nbufs=2)
        VL, BVL = A.alloc("VL", [128, 2, 512], BF16, nbufs=2)
        TB, BTB = A.alloc("TB", [128, 512], F32)
        for c in range(4):
            tu = ws.get(("w_in", l, 8 + c, 0, 8))

            def ev(tt, ps, psb, c=c):
                act(UU[:, c, ts(tt)], ps[:, :], AF.Gelu_apprx_tanh, [psb], [BUU])
            proj_fm(tu, 8, h_rhs, h_bufs, ev)
            ws.done(tu)
        tvs = [ws.get(("w_in", l, 12 + cc, 0, 8)) for cc in range(4)]
        for n in range(16):
            b = banks.next()
            tt = n // 4
            for cc in range(4):
                for kc in range(8):
                    mm(PS[b][:, cc * 128:(cc + 1) * 128], H[:, kc, n * 128:(n + 1) * 128], tvs[cc].ap[:, kc, :], kc == 0,
                       kc == 7, [tvs[cc].buf, BH[tt]], BPS[b], inc=(kc == 7 and cc == 3))
            vi = n % 2
            sm = SMALL[:, vi, :]
            act(VG[:, vi, :], PS[b][:, :], AF.Gelu_apprx_tanh, [BPS[b]], [BVG[vi]])
            k.op("dve", lambda e, o=ST6[:, vi, :], i=VG[:, vi, :]: e.bn_stats(out=o, in_=i), reads=[BVG[vi]],
                 writes=[BSMALL[vi]])
            k.op("dve", lambda e, o=SMALL[:, vi, 2:4], i=ST6[:, vi, :]: e.bn_aggr(out=o, in_=i), reads=[BSMALL[vi]],
                 writes=[BSMALL[vi]])
            act(SMALL[:, vi, 5:6], SMALL[:, vi, 3:4], AF.Sqrt, [BSMALL[vi], BEPS], [BSMALL[vi]], bias=EPS[:, 1:2])
            recip(SMALL[:, vi, 6:7], SMALL[:, vi, 5:6], [BSMALL[vi]], [BSMALL[vi]])
            tsc(VG[:, vi, :], VG[:, vi, :], SMALL[:, vi, 2:3], SMALL[:, vi, 6:7], ALU.subtract, ALU.mult,
                [BVG[vi], BSMALL[vi]], [BVG[vi]])
            tt_(VG[:, vi, :], VG[:, vi, :], PBC[:, 0:512], ALU.mult, [BVG[vi], BPBC], [BVG[vi]])
            tt_(VL[:, vi, :], VG[:, vi, :], PBC[:, 512:1024], ALU.add, [BVG[vi], BPBC], [BVL[vi]])
            b2 = banks.next()
            for g in range(4):
                mm(PS[b2][:, g * 128:(g + 1) * 128], VL[:, vi, g * 128:(g + 1) * 128], WSM[:, g, :], True, True,
                   [BVL[vi], BWSM], BPS[b2], inc=(g == 3))
            tt_(TB[:, :], PS[b2][:, :], PBC[:, 1024:1536], ALU.add, [BPS[b2], BPBC], [BTB])
            tbv = TB[:, :].rearrange("p (g t) -> p g t", g=4)
            tt_(UB[:, :, n * 128:(n + 1) * 128], tbv, UU[:, :, n * 128:(n + 1) * 128], ALU.mult, [BTB, BUU], [BUB])
        for t_ in tvs:
            ws.done(t_)
        dump(f"ub{l}", UB[:, :, :], [128, 4, S], BF16, [BUB])
        if stop_after == "B":
            return True

        A.reset(keep_b)
        MG, BMG = A.alloc("MG", [128, 8, 1024], BF16)
        SGM, BSGM = A.alloc("SGM", [128, 3, 512], F32)
        M0, BM0 = A.alloc("M0", [128, 512], F32)
        M1, BM1 = A.alloc("M1", [128, 512], F32)
        for hf in range(2):
            banks.set(range(8))
            for dc in range(8):
                wa = ws.get(("w_a", l, dc, 0, 4))
                wb = ws.get(("w_b", l, dc, 0, 4))
                wc = ws.get(("w_c", l, dc, 0, 2))
                wg = [ws.get(("w_in", l, 34 + i * 8 + dc, 0, 8)) for i in range(3)]
                for t2 in range(2):
                    tt = hf * 2 + t2
                    pa, pb_, pc_ = banks.next(), banks.next(), banks.next()
                    pg = [banks.next() for _ in range(3)]
                    for kc in range(4):
                        mm(PS[pa][:, :], wa.ap[:, kc, :], AACT[:, kc, ts(tt)], kc == 0, kc == 3, [wa.buf, BAACT], BPS[pa])
                    for kc in range(4):
                        mm(PS[pb_][:, :], wb.ap[:, kc, :], UB[:, kc, ts(tt)], kc == 0, kc == 3, [wb.buf, BUB], BPS[pb_])
                    for kc in range(2):
                        mm(PS[pc_][:, :], wc.ap[:, kc, :], OC[:, kc, ts(tt)], kc == 0, kc == 1, [wc.buf, BOC], BPS[pc_])
                    for i in range(3):
                        for kc in range(8):
                            mm(PS[pg[i]][:, :], wg[i].ap[:, kc, :], H[:, kc, ts(tt)], kc == 0, kc == 7,
                               [wg[i].buf, BH[tt]], BPS[pg[i]])
                    for i in range(3):
                        col = pb + 56 + i * 8 + dc
                        act(SGM[:, i, :], PS[pg[i]][:, :], AF.Sigmoid, [BPS[pg[i]], BPP], [BSGM], bias=PP[:, col:col + 1])
                    tt_(M0[:, :], PS[pa][:, :], SGM[:, 0, :], ALU.mult, [BPS[pa], BSGM], [BM0])
                    tt_(M1[:, :], PS[pb_][:, :], SGM[:, 1, :], ALU.mult, [BPS[pb_], BSGM], [BM1])
                    tt_(M0[:, :], M0[:, :], M1[:, :], ALU.add, [BM0, BM1], [BM0])
                    tt_(M1[:, :], PS[pc_][:, :], SGM[:, 2, :], ALU.mult, [BPS[pc_], BSGM], [BM1])
                    tt_(MG[:, dc, ts(t2)], M0[:, :], M1[:, :], ALU.add, [BM0, BM1], [BMG])
                for t_ in [wa, wb, wc] + wg:
                    ws.done(t_)
            if dbg is not None:
                dump(f"merged{l}_{hf}", MG[:, :, :], [128, 8, 1024], BF16, [BMG])
            banks.set(range(6))
            wm = [ws.get(("w_mix", l, dc, 0, 8)) for dc in range(8)]
            for t2 in range(2):
                tt = hf * 2 + t2
                sb_ = 6 + t2
                for dc in range(8):
                    b = banks.next()
                    for kc in range(8):
                        mm(PS[b][:, :], wm[dc].ap[:, kc, :], MG[:, kc, ts(t2)], kc == 0, kc == 7, [wm[dc].buf, BMG], BPS[b])
                    flush_stats()
                    y_chunk(PS[b][:, :], BPS[b], dc, hf, sb_, dc == 0, dc == 7)
                post_update(pb + 8, tt, hf, sb_)
            for t_ in wm:
                ws.done(t_)
        return False

    def sublayer_xattn(l, s):
        pb = l * PPL
        banks.set(range(8))
        A.reset()
        MEMS, BMEMS = A.alloc("MEMS", [128, 8, 256], F32)
        KT, BKT = A.alloc("KT", [128, 8, 256], BF16)
        VX, BVX = A.alloc("VX", [128, 2, 1024], BF16)
        QX, BQX = A.alloc("QX", [128, 8, S], BF16)
        PTX, BPTX = A.alloc("PTX", [128, 2, 2, 512], BF16, nbufs=2)
        RD, BRD = A.alloc("RD", [128, 2, 512], F32, nbufs=2)
        memn_o = A.off
        MEMN, BMEMN = A.alloc("MEMN", [128, 8, 256], BF16)
        A.reset(0)
        OX0, BOX0 = A.alloc("OX0", [128, 8, 512], BF16)
        A.reset(memn_o)
        OX1, BOX1 = A.alloc("OX1", [128, 8, 512], BF16)
        OXs, BOXs = [OX0, OX1], [BOX0, BOX1]
        if not dry:
            k.dma("sp", [(MEMS[:, kc, :], T["memT"][s, kc, :, :]) for kc in range(8)], writes=[BMEMS])
        b = banks.next()
        for kc in range(8):
            si = kc % 2
            act(SQ[:, si, 0:256], MEMS[:, kc, :], AF.Square, [BMEMS], [BSQ[si]])
            mm(PS[b][:, 0:256], ONESB[:, :], SQ[:, si, 0:256], kc == 0, kc == 7, [BONES, BSQ[si]], BPS[b], inc=True)
        ri = nrm_ctr[0] % 2
        nrm_ctr[0] += 1
        act(RSTD[:, ri, 0:256], PS[b][:, 0:256], AF.Sqrt, [BPS[b], BEPS], [BRSTD[ri]], bias=EPS[:, 0:1], scale=1.0 / D)
        recip(RSTD[:, ri, 0:256], RSTD[:, ri, 0:256], [BRSTD[ri]], [BRSTD[ri]])
        for kc in range(8):
            stt(MEMN[:, kc, :], MEMS[:, kc, :], PP[:, pb + 32 + kc:pb + 33 + kc], RSTD[:, ri, 0:256], ALU.mult, ALU.mult,
                [BMEMS, BPP, BRSTD[ri]], [BMEMN])
        for ec in range(8):
            w = ws.get(("w_xkv", l, ec, 0, 8))
            b = banks.next()
            for kc in range(8):
                mm(PS[b][:, 0:256], w.ap[:, kc, :], MEMN[:, kc, :], kc == 0, kc == 7, [w.buf, BMEMN], BPS[b])
            act(KT[:, ec, :], PS[b][:, 0:256], AF.Copy, [BPS[b]], [BKT])
            ws.done(w)
        for ec in range(8):
            w = ws.get(("w_xkv", l, 8 + ec, 0, 8))
            b = banks.next()
            for mb in range(2):
                for kc in range(8):
                    mm(PS[b][:, mb * 128:(mb + 1) * 128], MEMN[:, kc, mb * 128:(mb + 1) * 128], w.ap[:, kc, :], kc == 0,
                       kc == 7, [w.buf, BMEMN], BPS[b], inc=(kc == 7 and mb == 1))
            psv = PS[b][:, 0:256].rearrange("p (m c) -> p m c", m=2)
            act(VX[:, :, ec * 128:(ec + 1) * 128], psv, AF.Copy, [BPS[b]], [BVX])
            ws.done(w)
        norm_to_H(pb + 16, range(4))
        for dc in range(8):
            w = ws.get(("w_xq", l, dc, 0, 8))

            def ev(tt, ps, psb, dc=dc):
                act(QX[:, dc, ts(tt)], ps[:, :], AF.Copy, [psb], [BQX])
            proj_fm(w, 8, h_rhs, h_bufs, ev)
            ws.done(w)
        wo = [ws.get(("w_xo", l, dc, 0, 8)) for dc in range(8)]
        banks.set(range(6))

        def attn_tile(tt):
            OX, BOX = OXs[tt % 2], BOXs[tt % 2]

            def xs1(hd):
                pi = hd % 2
                bs_ = [banks.next(), banks.next()]
                for mb in range(2):
                    for ec in range(2):
                        mm(PS[bs_[mb]][:, :], KT[:, hd * 2 + ec, mb * 128:(mb + 1) * 128], QX[:, hd * 2 + ec, ts(tt)],
                           ec == 0, ec == 1, [BKT, BQX], BPS[bs_[mb]])
                    act(PTX[:, pi, mb, :], PS[bs_[mb]][:, :], AF.Exp, [BPS[bs_[mb]]], [BPTX[pi]], scale=1.0 / 16)

            def xs2(hd):
                pi = hd % 2
                bd = banks.next()
                for mb in range(2):
                    mm(PS[bd][:, :], ONESB[:, :], PTX[:, pi, mb, :], mb == 0, mb == 1, [BONES, BPTX[pi]], BPS[bd])
                recip(RD[:, pi, :], PS[bd][:, :], [BPS[bd]], [BRD[pi]])
                for ec in range(2):
                    bo = banks.next()
                    for mb in range(2):
                        mm(PS[bo][:, :], VX[:, mb, (hd * 2 + ec) * 128:(hd * 2 + ec + 1) * 128], PTX[:, pi, mb, :], mb == 0,
                           mb == 1, [BVX, BPTX[pi]], BPS[bo])
                    tt_(OX[:, hd * 2 + ec, :], PS[bo][:, :], RD[:, pi, :], ALU.mult, [BPS[bo], BRD[pi]], [BOX])

            for hd in range(4):
                xs1(hd)
                if hd > 0:
                    xs2(hd - 1)
            xs2(3)

        def xo_tile(tt):
            OX, BOX = OXs[tt % 2], BOXs[tt % 2]
            ysel = tt % 2
            sb_ = 6 + (tt % 2)
            for dc in range(8):
                b = banks.next()
                for kc in range(8):
                    mm(PS[b][:, :], wo[dc].ap[:, kc, :], OX[:, kc, :], kc == 0, kc == 7, [wo[dc].buf, BOX], BPS[b])
                flush_stats()
                y_chunk(PS[b][:, :], BPS[b], dc, ysel, sb_, dc == 0, dc == 7)
            post_update(pb + 24, tt, ysel, sb_)

        if XATTN_PIPE:
            for tt in range(4):
                attn_tile(tt)
                if tt > 0:
                    xo_tile(tt - 1)
            xo_tile(3)
        else:
            for tt in range(4):
                attn_tile(tt)
                xo_tile(tt)
        for t_ in wo:
            ws.done(t_)

    def sublayer_ffn(l, tile_done=None):
        pb = l * PPL
        A.reset()
        ACTB, BACTB = A.alloc("ACTB", [128, 22, 1024], BF16)
        GT, BGT = A.alloc("GT", [128, 2, 516], F32, nbufs=2)
        CV, BCV = A.alloc("CV", [128, 512], F32)
        GL, BGL = A.alloc("GL", [128, 2, 512], F32, nbufs=2)
        memset(HALO[:, :, :], 0.0, [BHALO])
        fw = pb + 216
        gi = 0
        for hf in range(2):
            banks.set(range(8))
            norm_to_H(pb + 40, (2 * hf, 2 * hf + 1))
            for j in range(22):
                wg = ws.get(("w_up", l, j, 0, 8))
                wv = ws.get(("w_up", l, 22 + j, 0, 8))
                for t2 in range(2):
                    tt = hf * 2 + t2
                    bg, bv = banks.next(), banks.next()
                    for kc in range(8):
                        mm(PS[bg][:, :], wg.ap[:, kc, :], H[:, kc, ts(tt)], kc == 0, kc == 7, [wg.buf, BH[tt]], BPS[bg])
                    for kc in range(8):
                        mm(PS[bv][:, :], wv.ap[:, kc, :], H[:, kc, ts(tt)], kc == 0, kc == 7, [wv.buf, BH[tt]], BPS[bv])
                    g2 = gi % 2
                    gi += 1
                    act(GT[:, g2, 2:514], PS[bg][:, :], AF.Copy, [BPS[bg]], [BGT[g2]])
                    cpy(GT[:, g2, 0:2], HALO[:, j, :], [BHALO], [BGT[g2]])
                    cpy(HALO[:, j, :], GT[:, g2, 512:514], [BGT[g2]], [BHALO])
                    act(CV[:, :], PS[bg][:, :], AF.Identity, [BPS[bg], BPP], [BCV], bias=PP[:, pb + 282 + j:pb + 283 + j],
                        scale=PP[:, fw + 44 + j:fw + 45 + j])
                    stt(CV[:, :], GT[:, g2, 0:512], PP[:, fw + j:fw + j + 1], CV[:, :], ALU.mult, ALU.add,
                        [BGT[g2], BPP, BCV], [BCV])
                    stt(CV[:, :], GT[:, g2, 1:513], PP[:, fw + 22 + j:fw + 23 + j], CV[:, :], ALU.mult, ALU.add,
                        [BGT[g2], BPP, BCV], [BCV])
                    act(GL[:, g2, :], CV[:, :], AF.Gelu_apprx_tanh, [BCV], [BGL[g2]])
                    tt_(ACTB[:, j, ts(t2)], PS[bv][:, :], GL[:, g2, :], ALU.mult, [BPS[bv], BGL[g2]], [BACTB])
                ws.done(wg)
                ws.done(wv)
            banks.set(range(6))
            ysels = (1, 0) if hf == 0 else (0, 1)
            for dc in range(8):
                wd = [ws.get(("w_down", l, dc, 0, 8)), ws.get(("w_down", l, dc, 8, 8)), ws.get(("w_down", l, dc, 16, 6))]
                for t2 in range(2):
                    b = banks.next()
                    for kc in range(22):
                        w = wd[kc // 8]
                        mm(PS[b][:, :], w.ap[:, kc % 8, :], ACTB[:, kc, ts(t2)], kc == 0, kc == 21, [w.buf, BACTB], BPS[b])
                    flush_stats()
                    y_chunk(PS[b][:, :], BPS[b], dc, ysels[t2], 6 + t2, dc == 0, dc == 7)
                for w in wd:
                    ws.done(w)
            for t2 in range(2):
                tt = hf * 2 + t2
                post_update(pb + 48, tt, ysels[t2], 6 + t2)
                if tile_done is not None:
                    tile_done(tt)

    stopped = False

    def load_x(s_, tt):
        if not dry:
            k.dma("sp", [(XRES[:, kc, ts(tt)], T["xT"][s_, kc, :, ts(tt)]) for kc in range(8)], writes=[BX[tt]])

    def store_x(s_, tt):
        if not dry:
            k.dma("sp", [(T["outT"][s_, kc, :, ts(tt)], XRES[:, kc, ts(tt)]) for kc in range(8)], reads=[BX[tt]],
                  sembuf=BX[tt])

    try:
        for tt in range(4):
            load_x(0, tt)
        for s in range(nseq):
            def tile_done(tt, s=s):
                store_x(s, tt)
                if s + 1 < nseq:
                    load_x(s + 1, tt)
            for l in range(nlayer):
                stopped = sublayer_mixer(l)
                if stopped:
                    break
                dump(f"x1_{l}", XRES[:, :, :], [128, 8, S], F32, BX)
                if stop_after == "mixer":
                    stopped = True
                    break
                sublayer_xattn(l, s)
                dump(f"x2_{l}", XRES[:, :, :], [128, 8, S], F32, BX)
                if stop_after == "xattn":
                    stopped = True
                    break
                sublayer_ffn(l, tile_done if l == nlayer - 1 else None)
            if stopped:
                for tt in range(4):
                    store_x(s, tt)
                break
    except _Stop:
        for tt in range(4):
            store_x(0, tt)
    k.wait_all("sp", BX + dbg_out)
    return ws


def make_program(nseq=SEQ_PER_CORE, nlayer=DEPTH, stop_after=None, dbg=None):
    kd = K(None, dry=True)
    wsd = build_program(None, kd, None, nseq, nlayer, stop_after, dbg)
    seq = wsd.rec
    nc = bass.Bass("TRN2", target_bir_lowering=False)
    k = K(nc)
    ws = build_program(nc, k, seq, nseq, nlayer, stop_after, dbg)
    assert ws.idx == len(seq) and ws.loaded == len(seq)
    return nc, k


def _t5_bucket(n):
    n = np.maximum(n, 0)
    nf = np.maximum(n, 1).astype(np.float32)
    large = 16 + (np.log(nf / np.float32(16)) / np.float32(np.log(2048 / 16)) * np.float32(16)).astype(np.int32)
    large = np.minimum(large, 31)
    return np.where(n < 16, n, large)


def _wtiles(W):
    L, Kd, N = W.shape
    return np.ascontiguousarray(W.reshape(L, Kd // 128, 128, N // 128, 128).transpose(0, 3, 2, 1, 4))


def _cols(v):
    v = np.asarray(v)
    C = v.shape[-1]
    lead = v.shape[:-1]
    a = v.reshape(*lead, C // 128, 128)
    a = np.moveaxis(a, -1, 0)
    return a.reshape(128, -1)


def prep_shared(inp):
    L = DEPTH
    pp = np.zeros((128, L * PPL), np.float32)
    for l in range(L):
        o = l * PPL
        pp[:, o + 0:o + 8] = _cols(inp["mix_pre_g"][l])
        pp[:, o + 8:o + 16] = _cols(inp["mix_post_g"][l])
        pp[:, o + 16:o + 24] = _cols(inp["x_pre_g"][l])
        pp[:, o + 24:o + 32] = _cols(inp["x_post_g"][l])
        pp[:, o + 32:o + 40] = _cols(inp["mem_g"][l])
        pp[:, o + 40:o + 48] = _cols(inp["ffn_pre_g"][l])
        pp[:, o + 48:o + 56] = _cols(inp["ffn_post_g"][l])
        pp[:, o + 56:o + 80] = _cols(inp["b_gate"][l])
        pp[:, o + 80:o + 204] = _cols(inp["conv_a_w"][l])
        pp[:, o + 204:o + 208] = _cols(inp["conv_a_b"][l])
        pp[:, o + 208:o + 212] = _cols(inp["ln_a_g"][l])
        pp[:, o + 212:o + 216] = _cols(inp["ln_a_b"][l])
        pp[:, o + 216:o + 282] = _cols(inp["conv_f_w"][l])
        pp[:, o + 282:o + 304] = _cols(inp["conv_f_b"][l])
    rb = np.asarray(inp["rel_bias"])
    kk = np.arange(128)[:, None]
    qq = np.arange(128)[None, :]
    abias = np.zeros((128, 3, 4, 2, 128), np.float32)
    amask = np.zeros((128, 3, 4, 2, 128), np.float32)
    for g, dil in enumerate(DILS):
        for pc in range(2):
            rel = qq + 128 - kk if pc == 0 else qq - kk
            valid = (rel >= 0) & (rel <= 128)
            bucket = _t5_bucket(rel * dil)
            for h in range(4):
                abias[:, g, h, pc, :] = rb[bucket, g * 4 + h]
                amask[:, g, h, pc, :] = np.where(valid, 0.0, -30000.0)
    pbc = np.zeros((L, 128, 1536), np.float32)
    for l in range(L):
        pbc[l, :, 0:512] = np.broadcast_to(inp["ln_b_g"][l][None, :], (128, 512))
        pbc[l, :, 512:1024] = np.broadcast_to(inp["ln_b_b"][l][None, :], (128, 512))
        pbc[l, :, 1024:1536] = np.broadcast_to(np.asarray(inp["b_s"][l]).reshape(1, 512), (128, 512))
    wsT = np.ascontiguousarray(np.asarray(inp["w_s"]).transpose(0, 3, 1, 2))
    tril = (np.arange(128)[None, :] >= np.arange(128)[:, None]).astype(np.float32)
    trilm = np.ascontiguousarray(np.broadcast_to(tril[:, None, :], (128, 4, 128)))
    w_c = np.asarray(inp["w_c_out"])
    shared = {
        "pp": pp,
        "abias": abias.reshape(128, 3072),
        "amask": amask.reshape(128, 3072),
        "pbc": pbc,
        "wsT": wsT,
        "trilm": trilm,
        "ident": np.eye(128, dtype=np.float32),
        "w_in": _wtiles(np.asarray(inp["w_in"])),
        "w_a": _wtiles(np.asarray(inp["w_a_out"])),
        "w_b": _wtiles(np.asarray(inp["w_b_out"])),
        "w_c": _wtiles(w_c),
        "w_mix": _wtiles(np.asarray(inp["w_mix_out"])),
        "w_xq": _wtiles(np.asarray(inp["w_xq"])),
        "w_xkv": _wtiles(np.asarray(inp["w_xkv"])),
        "w_xo": _wtiles(np.asarray(inp["w_xo"])),
        "w_up": _wtiles(np.asarray(inp["w_up"])),
        "w_down": _wtiles(np.asarray(inp["w_down"])),
    }
    return shared


def prep_core(inp, c, nseq=SEQ_PER_CORE):
    xs = np.asarray(inp["x"][c * SEQ_PER_CORE:c * SEQ_PER_CORE + nseq])
    ms = np.asarray(inp["mem"][c * SEQ_PER_CORE:c * SEQ_PER_CORE + nseq])
    xT = np.zeros((SEQ_PER_CORE, 8, 128, S), np.float32)
    mT = np.zeros((SEQ_PER_CORE, 8, 128, 256), np.float32)
    xT[:nseq] = xs.transpose(0, 2, 1).reshape(nseq, 8, 128, S)
    mT[:nseq] = ms.transpose(0, 2, 1).reshape(nseq, 8, 128, 256)
    return {"xT": xT, "memT": mT}


_PROG = {}


def kernel(**inputs):
    if "p" not in _PROG:
        _PROG["p"] = make_program()
    nc, _ = _PROG["p"]
    shared = prep_shared(inputs)
    in_maps = []
    for c in range(N_CORES):
        m = dict(shared)
        m.update(prep_core(inputs, c))
        in_maps.append(m)
    res = run_bass_kernel_spmd(nc, in_maps, core_ids=list(range(N_CORES)))
    out = np.empty((N_CORES * SEQ_PER_CORE, S, D), np.float32)
    for c in range(N_CORES):
        oT = res.results[c]["outT"]
        out[c * SEQ_PER_CORE:(c + 1) * SEQ_PER_CORE] = oT.reshape(SEQ_PER_CORE, D, S).transpose(0, 2, 1)
    return out
```

```python
import numpy as np
import concourse.bass as bass
import concourse.mybir as mybir
from concourse.bass_utils import run_bass_kernel_spmd

F32 = mybir.dt.float32
BF16 = mybir.dt.bfloat16
AF = mybir.ActivationFunctionType
ALU = mybir.AluOpType
AX = mybir.AxisListType

EPOCH = 12000
SAME_SYNC = True

N_CORES = 8
SEQ_PER_CORE = 4
DEPTH = 2
D = 1024
S = 2048
NSLOT = 10
XATTN_PIPE = False
DEFER_STATS = True
PPL = 304
DILS = (1, 4, 16)


class Buf:
    __slots__ = ("name", "w", "r", "al", "dsem", "dcnt")

    def __init__(self, name):
        self.name = name
        self.w = None
        self.r = {}
        self.al = []
        self.dsem = None
        self.dcnt = 0


class Dummy:
    shape = ()

    def __getitem__(self, i):
        return self

    def rearrange(self, *a, **k):
        return self


DUMMY = Dummy()


class Eng:
    def __init__(self, K, name, obj):
        self.name = name
        self.obj = obj
        self.sem = None if K.dry else K.nc.alloc_semaphore(f"s_{name}_0")
        self.nsem = 1
        self.cnt = 0
        self.own = set() if K.dry else {id(self.sem)}
        self.waited = {}
        self.nwaits = 0
        self.nins = 0


class K:
    def __init__(self, nc, dry=False):
        self.nc = nc
        self.dry = dry
        if dry:
            self.engs = {n: Eng(self, n, None) for n in ("pe", "act", "dve", "pool", "sp")}
        else:
            self.engs = {
                "pe": Eng(self, "pe", nc.tensor),
                "act": Eng(self, "act", nc.scalar),
                "dve": Eng(self, "dve", nc.vector),
                "pool": Eng(self, "pool", nc.gpsimd),
                "sp": Eng(self, "sp", nc.sync),
            }
        self.semobj = {}
        if not dry:
            for e in self.engs.values():
                self.semobj[id(e.sem)] = e.sem
        self.ndsem = 0

    def _deps(self, reads, writes):
        deps = {}

        def add(d):
            if d is None:
                return
            s, v = d
            if deps.get(s, 0) < v:
                deps[s] = v

        for b in reads:
            add(b.w)
            for a in b.al:
                add(a.w)
        for b in writes:
            add(b.w)
            for d in b.r.values():
                add(d)
            for a in b.al:
                add(a.w)
                for d in a.r.values():
                    add(d)
        return deps

    def _wait(self, E, deps):
        for s, v in deps.items():
            if s in E.own:
                if E.name in ("pe", "sp") or not SAME_SYNC:
                    continue
                if s == id(E.sem) and v > E.cnt:
                    continue
                if E.name in ("act", "dve") and (s != id(E.sem) or v < E.cnt):
                    continue
            if E.waited.get(s, 0) >= v:
                continue
            E.obj.wait_ge(self.semobj[s], v)
            E.waited[s] = v
            E.nwaits += 1

    def _tag(self, E, ins, inc):
        E.nins += 1
        if inc:
            ins.then_inc(E.sem, 1)
            E.cnt += 1
            tag = (id(E.sem), E.cnt)
            if E.cnt >= EPOCH:
                E.sem = self.nc.alloc_semaphore(f"s_{E.name}_{E.nsem}")
                E.nsem += 1
                E.cnt = 0
                E.own.add(id(E.sem))
                self.semobj[id(E.sem)] = E.sem
        else:
            tag = (id(E.sem), E.cnt + 1)
        return tag

    def op(self, eng, fn, reads=(), writes=(), inc=True):
        if self.dry:
            return None
        E = self.engs[eng]
        self._wait(E, self._deps(reads, writes))
        ins = fn(E.obj)
        tag = self._tag(E, ins, inc)
        for b in writes:
            b.w = tag
            b.r = {}
        for b in reads:
            b.r[eng] = tag
        return ins

    def dma(self, eng, pairs, reads=(), writes=(), sembuf=None):
        if self.dry:
            return None
        E = self.engs[eng]
        sb = sembuf or (writes[0] if writes else reads[0])
        if sb.dsem is None:
            sb.dsem = self.nc.alloc_semaphore(f"d_{self.ndsem}")
            self.ndsem += 1
            self.semobj[id(sb.dsem)] = sb.dsem
        deps = self._deps(reads, writes)
        if sb.dcnt:
            s = id(sb.dsem)
            if deps.get(s, 0) < sb.dcnt:
                deps[s] = sb.dcnt
        self._wait(E, deps)
        for (o, i) in pairs:
            E.obj.dma_start(out=o, in_=i).then_inc(sb.dsem, 16)
            E.nins += 1
            sb.dcnt += 16
        tag = (id(sb.dsem), sb.dcnt)
        for b in writes:
            b.w = tag
            b.r = {}
        for b in reads:
            b.r["dma_" + sb.name] = tag
        return tag

    def wait_all(self, eng, bufs):
        if self.dry:
            return
        E = self.engs[eng]
        self._wait(E, self._deps(bufs, bufs))

    def stats(self):
        return {e.name: (e.nins, e.nwaits, e.nsem) for e in self.engs.values()}


class _Stop(Exception):
    pass


class Tile:
    __slots__ = ("i", "ap", "buf")

    def __init__(self, i, ap, buf):
        self.i = i
        self.ap = ap
        self.buf = buf


class WStream:
    def __init__(self, k, ring, rbufs, seq, ap_of):
        self.k = k
        self.ring = ring
        self.rbufs = rbufs
        self.seq = seq
        self.ap_of = ap_of
        self.rec = []
        self.idx = 0
        self.loaded = 0
        self.done_ = set()

    def _issue(self):
        while self.loaded < len(self.seq):
            i = self.loaded
            if i - NSLOT >= 0 and (i - NSLOT) not in self.done_:
                break
            slot = i % NSLOT
            src, kcn = self.ap_of(self.seq[i])
            self.k.dma("pool", [(self.ring[:, slot, 0:kcn, :], src)], writes=[self.rbufs[slot]])
            self.loaded += 1

    def get(self, desc):
        i = self.idx
        self.idx += 1
        if self.k.dry:
            self.rec.append(desc)
            return Tile(i, DUMMY, None)
        assert self.seq[i] == desc, (i, self.seq[i], desc)
        self._issue()
        assert self.loaded > i, f"weight ring too small at tile {i} {desc}"
        slot = i % NSLOT
        return Tile(i, self.ring[:, slot], self.rbufs[slot])

    def done(self, t):
        if self.k.dry:
            return
        self.done_.add(t.i)
        self._issue()


class Arena:
    def __init__(self, k, nc, base, size):
        self.k, self.nc, self.base, self.size = k, nc, base, size
        self.off = 0
        self.regs = []
        self.n = 0

    def reset(self, off=0):
        self.off = off

    def alloc(self, name, shape, dtype, nbufs=1):
        esz = 2 if dtype == BF16 else 4
        per = esz
        for s_ in shape[1:]:
            per *= s_
        per = (per + 63) // 64 * 64
        lo = self.off
        hi = lo + per
        assert hi <= self.size, f"arena overflow {name}: {hi} > {self.size}"
        self.off = hi
        bufs = [Buf(f"{name}{i}") for i in range(nbufs)]
        pbuf = per // nbufs
        new = []
        for bi, b in enumerate(bufs):
            blo, bhi = lo + bi * pbuf, (lo + (bi + 1) * pbuf if bi < nbufs - 1 else hi)
            for (l2, h2, b2) in self.regs:
                if l2 < bhi and blo < h2:
                    if b2 not in b.al:
                        b.al.append(b2)
                    if b not in b2.al:
                        b2.al.append(b)
            new.append((blo, bhi, b))
        self.regs.extend(new)
        if self.k.dry:
            t = DUMMY
        else:
            self.n += 1
            t = self.nc.alloc_sbuf_tensor_at(f"ar{self.n}_{name}", list(shape), dtype, offset=self.base + lo)
        return (t, bufs[0]) if nbufs == 1 else (t, bufs)


def ts(tt, n=512):
    return slice(tt * n, (tt + 1) * n)


def build_program(nc, k, wseq, nseq=SEQ_PER_CORE, nlayer=DEPTH, stop_after=None, dbg=None):
    dry = k.dry
    T = {}
    if not dry:
        def dram(name, shape, kind="ExternalInput"):
            T[name] = nc.dram_tensor(name, list(shape), F32, kind=kind).ap()
        dram("xT", [SEQ_PER_CORE, 8, 128, S])
        dram("memT", [SEQ_PER_CORE, 8, 128, 256])
        dram("pp", [128, DEPTH * PPL])
        dram("abias", [128, 3072])
        dram("amask", [128, 3072])
        dram("pbc", [DEPTH, 128, 1536])
        dram("wsT", [DEPTH, 128, 4, 128])
        dram("trilm", [128, 4, 128])
        dram("ident", [128, 128])
        dram("w_in", [DEPTH, 58, 128, 8, 128])
        dram("w_a", [DEPTH, 8, 128, 4, 128])
        dram("w_b", [DEPTH, 8, 128, 4, 128])
        dram("w_c", [DEPTH, 8, 128, 2, 128])
        dram("w_mix", [DEPTH, 8, 128, 8, 128])
        dram("w_xq", [DEPTH, 8, 128, 8, 128])
        dram("w_xkv", [DEPTH, 16, 128, 8, 128])
        dram("w_xo", [DEPTH, 8, 128, 8, 128])
        dram("w_up", [DEPTH, 44, 128, 8, 128])
        dram("w_down", [DEPTH, 8, 128, 22, 128])
        dram("outT", [SEQ_PER_CORE, 8, 128, S], kind="ExternalOutput")

    def ap_of(desc):
        name, l, j, k0, kcn = desc
        return T[name][l, j, :, k0:k0 + kcn, :], kcn

    BASE = 16512
    XRES_O = BASE
    H_O = XRES_O + 65536
    RING_O = H_O + 32768
    C_O = RING_O + NSLOT * 2048
    coff = [C_O]

    def calloc(name, shape, dtype):
        esz = 2 if dtype == BF16 else 4
        per = esz
        for s_ in shape[1:]:
            per *= s_
        per = (per + 63) // 64 * 64
        o = coff[0]
        coff[0] += per
        if dry:
            return DUMMY
        return nc.alloc_sbuf_tensor_at(name, list(shape), dtype, offset=o)

    if dry:
        XRES = H = Y = RING = DUMMY
    else:
        XRES = nc.alloc_sbuf_tensor_at("XRES", [128, 8, S], F32, offset=XRES_O)
        H = nc.alloc_sbuf_tensor_at("H", [128, 8, S], BF16, offset=H_O)
        Y = nc.alloc_sbuf_tensor_at("Y", [128, 8, 1024], F32, offset=H_O)
        RING = nc.alloc_sbuf_tensor_at("RING", [128, NSLOT, 8, 128], BF16, offset=RING_O)
    PP = calloc("PP", [128, DEPTH * PPL], F32)
    EXPBM = calloc("EXPBM", [128, 3, 4, 2, 128], BF16)
    PBC = calloc("PBC", [128, 1536], F32)
    WSRAW = calloc("WSRAW", [128, 4, 128], BF16)
    TRILM = calloc("TRILM", [128, 4, 128], BF16)
    WSM = calloc("WSM", [128, 4, 128], BF16)
    ONESB = calloc("ONESB", [128, 128], BF16)
    IDENT = calloc("IDENT", [128, 128], BF16)
    ONESF = calloc("ONESF", [128, 128], F32)
    RSTD = calloc("RSTD", [128, 2, 512], F32)
    SQ = calloc("SQ", [128, 2, 512], BF16)
    HALO = calloc("HALO", [128, 22, 2], F32)
    EPS = calloc("EPS", [128, 2], F32)
    SMALL = calloc("SMALL", [128, 2, 8], F32)
    ST6 = calloc("ST6", [128, 2, 6], F32)
    ARENA_O = (coff[0] + 63) // 64 * 64
    ARENA_SZ = 229344 - ARENA_O
    A = Arena(k, nc, ARENA_O, ARENA_SZ)

    BX = [Buf(f"X{t}") for t in range(4)]
    BH = [Buf(f"H{t}") for t in range(4)]
    BY = [Buf("Y0"), Buf("Y1")]
    for yi, hts in ((0, (0, 1)), (1, (2, 3))):
        for t in hts:
            BY[yi].al.append(BH[t])
            BH[t].al.append(BY[yi])
    RB = [Buf(f"ring{i}") for i in range(NSLOT)]
    BPP, BEXPBM, BPBC, BWSRAW, BTRILM, BWSM, BONES, BHALO, BEPS = [Buf(n) for n in (
        "PP", "EXPBM", "PBC", "WSRAW", "TRILM", "WSM", "ONES", "HALO", "EPS")]
    BRSTD = [Buf("RSTD0"), Buf("RSTD1")]
    BIDENT = Buf("IDENT")
    BSQ = [Buf("SQ0"), Buf("SQ1")]
    BSMALL = [Buf("SM0"), Buf("SM1")]
    if dry:
        PS = [DUMMY] * 8
    else:
        PS = [nc.alloc_psum_tensor(f"ps{i}", [128, 512], F32) for i in range(8)]
    BPS = [Buf(f"ps{i}") for i in range(8)]

    ws = WStream(k, RING, RB, wseq, ap_of)

    class Banks:
        def __init__(self):
            self.rot = list(range(8))
            self.i = 0

        def set(self, lst):
            self.rot = list(lst)
            self.i = 0

        def next(self):
            b = self.rot[self.i % len(self.rot)]
            self.i += 1
            return b

    banks = Banks()

    def mm(out, lhsT, rhs, first, last, reads, psb, inc=None):
        k.op("pe", lambda e: e.matmul(out, lhsT=lhsT, rhs=rhs, start=first, stop=last), reads=reads, writes=[psb],
             inc=last if inc is None else inc)

    def act(out, in_, func, reads, writes, bias=None, scale=None, accum=None):
        kw = {}
        if bias is not None:
            kw["bias"] = bias
        if scale is not None:
            kw["scale"] = scale
        if accum is not None:
            kw["accum_out"] = accum
        k.op("act", lambda e: e.activation(out=out, in_=in_, func=func, **kw), reads=reads, writes=writes)

    def tt_(out, in0, in1, op, reads, writes, eng="dve"):
        k.op(eng, lambda e: e.tensor_tensor(out=out, in0=in0, in1=in1, op=op), reads=reads, writes=writes)

    def tsc(out, in0, s1, s2, op0, op1, reads, writes, eng="dve"):
        if op1 is None:
            k.op(eng, lambda e: e.tensor_scalar(out=out, in0=in0, scalar1=s1, scalar2=None, op0=op0), reads=reads,
                 writes=writes)
        else:
            k.op(eng, lambda e: e.tensor_scalar(out=out, in0=in0, scalar1=s1, scalar2=s2, op0=op0, op1=op1),
                 reads=reads, writes=writes)

    def stt(out, in0, scalar, in1, op0, op1, reads, writes):
        k.op("dve", lambda e: e.scalar_tensor_tensor(out=out, in0=in0, scalar=scalar, in1=in1, op0=op0, op1=op1),
             reads=reads, writes=writes)

    def cpy(out, in_, reads, writes, eng="dve"):
        k.op(eng, lambda e: e.tensor_copy(out=out, in_=in_), reads=reads, writes=writes)

    def recip(out, in_, reads, writes):
        k.op("dve", lambda e: e.reciprocal(out=out, in_=in_), reads=reads, writes=writes)

    def memset(ap, val, writes, eng="dve"):
        k.op(eng, lambda e: e.memset(ap, val), writes=writes)

    dbg_out = []

    def chk(name):
        if stop_after == name:
            raise _Stop()

    def dump(name, ap, shape, dtype, bufs):
        if dry or dbg is None or name not in dbg:
            return
        t = nc.dram_tensor("dbg_" + name, list(shape), dtype, kind="ExternalOutput").ap()
        b = Buf("dbg_" + name)
        k.dma("sp", [(t, ap)], reads=bufs, writes=[b], sembuf=b)
        dbg_out.append(b)

    def proj_fm(tile, kcn, rhs_of, rbufs_of, evac, tiles=range(4)):
        for tt in tiles:
            b = banks.next()
            for kc in range(kcn):
                mm(PS[b][:, :], tile.ap[:, kc, :], rhs_of(kc, tt), kc == 0, kc == kcn - 1,
                   [tile.buf] + rbufs_of(tt), BPS[b])
            evac(tt, PS[b], BPS[b])

    def h_rhs(kc, tt):
        return H[:, kc, ts(tt)]

    def h_bufs(tt):
        return [BH[tt]]

    def rstd_from(psb_idx, ri, scale, epscol):
        act(RSTD[:, ri, :], PS[psb_idx][:, :], AF.Sqrt, [BPS[psb_idx], BEPS], [BRSTD[ri]], bias=EPS[:, epscol:epscol + 1],
            scale=scale)
        recip(RSTD[:, ri, :], RSTD[:, ri, :], [BRSTD[ri]], [BRSTD[ri]])

    nrm_ctr = [0]

    def norm_to_H(gcol, tiles):
        for tt in tiles:
            b = banks.next()
            for kc in range(8):
                si = kc % 2
                act(SQ[:, si, :], XRES[:, kc, ts(tt)], AF.Square, [BX[tt]], [BSQ[si]])
                mm(PS[b][:, :], ONESB[:, :], SQ[:, si, :], kc == 0, kc == 7, [BONES, BSQ[si]], BPS[b], inc=True)
            ri = nrm_ctr[0] % 2
            nrm_ctr[0] += 1
            rstd_from(b, ri, 1.0 / D, 0)
            for kc in range(8):
                stt(H[:, kc, ts(tt)], XRES[:, kc, ts(tt)], PP[:, gcol + kc:gcol + kc + 1], RSTD[:, ri, :], ALU.mult,
                    ALU.mult, [BX[tt], BPP, BRSTD[ri]], [BH[tt]])

    pend = []

    def y_chunk(ps_ap, psbuf, dc, ysel, sbank, first, last):
        ysl = slice(ysel * 512, ysel * 512 + 512)
        act(Y[:, dc, ysl], ps_ap, AF.Copy, [psbuf], [BY[ysel]])
        si = dc % 2
        act(SQ[:, si, :], ps_ap, AF.Square, [psbuf], [BSQ[si]])
        pend.append((sbank, si, first, last))
        if not DEFER_STATS:
            flush_stats()

    def flush_stats(keep=0):
        while len(pend) > keep:
            sbank, si, first, last = pend.pop(0)
            mm(PS[sbank][:, :], ONESB[:, :], SQ[:, si, :], first, last, [BONES, BSQ[si]], BPS[sbank], inc=True)

    def post_update(gcol, tt, ysel, sbank):
        flush_stats()
        ysl = slice(ysel * 512, ysel * 512 + 512)
        ri = nrm_ctr[0] % 2
        nrm_ctr[0] += 1
        rstd_from(sbank, ri, 1.0 / D, 0)
        for dc in range(8):
            tt_(Y[:, dc, ysl], Y[:, dc, ysl], RSTD[:, ri, :], ALU.mult, [BY[ysel], BRSTD[ri]], [BY[ysel]])
            stt(XRES[:, dc, ts(tt)], Y[:, dc, ysl], PP[:, gcol + dc:gcol + dc + 1], XRES[:, dc, ts(tt)], ALU.mult,
                ALU.add, [BY[ysel], BPP, BX[tt]], [BX[tt]])

    if not dry:
        k.dma("sp", [(PP[:, :], T["pp"][:, :])], writes=[BPP])
    memset(ONESB[:, :], 1.0, [BONES])
    memset(ONESF[:, :], 1.0, [BONES])
    memset(EPS[:, 0:1], 1e-6, [BEPS])
    memset(EPS[:, 1:2], 1e-5, [BEPS])
    if not dry:
        k.dma("pool", [(TRILM[:, :, :], T["trilm"][:, :, :])], writes=[BTRILM])
        k.dma("pool", [(IDENT[:, :], T["ident"][:, :])], writes=[BIDENT])
    A.reset()
    AB, BAB = A.alloc("abias", [128, 3072], F32)
    AM, BAM = A.alloc("amask", [128, 3072], F32)
    if not dry:
        k.dma("sp", [(AB[:, :], T["abias"][:, :])], writes=[BAB])
        k.dma("sp", [(AM[:, :], T["amask"][:, :])], writes=[BAM])
    if not dry:
        ebm_flat = EXPBM[:, :, :, :, :].rearrange("p g h c q -> p (g h c q)")
    else:
        ebm_flat = DUMMY
    act(AB[:, :], AB[:, :], AF.Exp, [BAB], [BAB])
    tt_(ebm_flat, AB[:, :], AM[:, :], ALU.mult, [BAB, BAM], [BEXPBM])

    def sublayer_mixer(l):
        pb = l * PPL
        banks.set(range(8))
        if not dry:
            k.dma("sp", [(PBC[:, :], T["pbc"][l, :, :])], writes=[BPBC])
            k.dma("pool", [(WSRAW[:, :, :], T["wsT"][l, :, :, :])], writes=[BWSRAW])
        tt_(WSM[:, :, :], WSRAW[:, :, :], TRILM[:, :, :], ALU.mult, [BWSRAW, BTRILM], [BWSM])
        norm_to_H(pb + 0, range(4))
        dump(f"h1_{l}", H[:, :, :], [128, 8, S], BF16, BH)
        if stop_after == "norm1":
            return True

        A.reset()
        OC, BOC = A.alloc("OC", [128, 2, S], BF16)
        keep_c = A.off
        Ut, BU = A.alloc("U", [128, 2, S], F32)
        qk_o = A.off
        Qb, BQ = A.alloc("Q", [128, 2, S], BF16, nbufs=2)
        Kb, BK = A.alloc("K", [128, 2, S], BF16, nbufs=2)
        end_qk = A.off
        A.reset(qk_o)
        R_, BR = A.alloc("R", [128, S], F32)
        A.reset(end_qk)
        VA, BVA = A.alloc("VA", [128, 2, 16, 2, 128], BF16, nbufs=2)
        NPT = 4
        PT, BPT = A.alloc("PT", [128, NPT, 2, 2, 128], BF16, nbufs=NPT)
        memset(VA[:, :, :, 0, 64:128], 1.0, BVA)
        memset(VA[:, :, :, 1, 0:64], 1.0, BVA)
        gctr = 0
        for hp in range(2):
            for g in range(3):
                d = DILS[g]
                nb = 16 // d
                pq = gctr % 2
                gctr += 1
                tq = ws.get(("w_in", l, 16 + g * 2 + hp, 0, 8))
                tk = ws.get(("w_in", l, 22 + g * 2 + hp, 0, 8))
                tv = ws.get(("w_in", l, 28 + g * 2 + hp, 0, 8))
                for (tile, dst, bdst) in ((tq, Qb, BQ[pq]), (tk, Kb, BK[pq])):
                    def ev(tt, ps, psb, dst=dst, bdst=bdst, pq=pq):
                        act(dst[:, pq, ts(tt)], ps[:, :], AF.Copy, [psb], [bdst])
                    proj_fm(tile, 8, h_rhs, h_bufs, ev)
                    ws.done(tile)
                    chk("Cq")
                chk("Cqk")

                def tokslice(j, d=d, nb=nb):
                    c, n = divmod(j, nb)
                    st = c + d * 128 * n
                    return slice(st, st + d * 127 + 1, d), n

                for j0 in range(0, 16, 4):
                    b = banks.next()
                    for jj in range(4):
                        tok, n = tokslice(j0 + jj)
                        for kc in range(8):
                            mm(PS[b][:, jj * 128:(jj + 1) * 128], H[:, kc, tok], tv.ap[:, kc, :], kc == 0, kc == 7,
                               [tv.buf] + BH, BPS[b], inc=(kc == 7 and jj == 3))
                    psv = PS[b][:, :].rearrange("p (j c) -> p j c", c=128)
                    act(VA[:, pq, j0:j0 + 4, 0, 0:64], psv[:, :, 0:64], AF.Copy, [BPS[b]], [BVA[pq]])
                    act(VA[:, pq, j0:j0 + 4, 1, 64:128], psv[:, :, 64:128], AF.Copy, [BPS[b]], [BVA[pq]])
                ws.done(tv)
                chk("Cv")

                def stage1(j):
                    tok, n = tokslice(j)
                    pcs = (0, 1) if n > 0 else (1,)
                    eb = j % NPT
                    for h in range(2):
                        b = banks.next()
                        pr = slice(64 * h, 64 * h + 64)
                        for pc in pcs:
                            ktok, _ = tokslice(j - 1 if pc == 0 else j)
                            mm(PS[b][:, pc * 128:(pc + 1) * 128], Kb[pr, pq, ktok], Qb[pr, pq, tok], True, True,
                               [BQ[pq], BK[pq]], BPS[b], inc=(pc == 1))
                        psv = PS[b][:, 0:256].rearrange("p (c q) -> p c q", c=2)
                        if n > 0:
                            p_ap, s_ap = PT[:, eb, h], psv
                        else:
                            p_ap, s_ap = PT[:, eb, h, 1, :], psv[:, 1, :]
                        act(p_ap, s_ap, AF.Exp, [BPS[b]], [BPT[eb]], scale=0.125)
                    if n > 0:
                        p_ap, m_ap = PT[:, eb], EXPBM[:, g, hp * 2:hp * 2 + 2]
                    else:
                        p_ap, m_ap = PT[:, eb, :, 1, :], EXPBM[:, g, hp * 2:hp * 2 + 2, 1, :]
                    tt_(p_ap, p_ap, m_ap, ALU.mult, [BPT[eb], BEXPBM], [BPT[eb]])

                def stage2(j):
                    tok, n = tokslice(j)
                    pcs = (0, 1) if n > 0 else (1,)
                    eb = j % NPT
                    b2 = banks.next()
                    for h in range(2):
                        for i, pc in enumerate(pcs):
                            vj = j - 1 if pc == 0 else j
                            mm(PS[b2][:, h * 128:(h + 1) * 128], VA[:, pq, vj, h, :], PT[:, eb, h, pc, :], i == 0,
                               i == len(pcs) - 1, [BVA[pq], BPT[eb]], BPS[b2], inc=(h == 1 and i == len(pcs) - 1))
                    psu = PS[b2][:, 0:256].rearrange("p (h q) -> p h q", h=2)
                    if g == 0:
                        act(Ut[:, :, tok], psu, AF.Copy, [BPS[b2]], [BU])
                    else:
                        tt_(Ut[:, :, tok], psu, Ut[:, :, tok], ALU.add, [BPS[b2], BU], [BU])

                DEPTH_C = 2
                for j in range(16):
                    stage1(j)
                    if j >= DEPTH_C:
                        stage2(j - DEPTH_C)
                for j in range(16 - DEPTH_C, 16):
                    stage2(j)
            recip(R_[0:64, :], Ut[64:128, 0, :], [BU], [BR])
            tt_(OC[0:64, hp, :], Ut[0:64, 0, :], R_[0:64, :], ALU.mult, [BU, BR], [BOC])
            recip(R_[64:128, :], Ut[0:64, 1, :], [BU], [BR])
            tt_(OC[64:128, hp, :], Ut[64:128, 1, :], R_[64:128, :], ALU.mult, [BU, BR], [BOC])
        dump(f"oc{l}", OC[:, :, :], [128, 2, S], BF16, [BOC])
        if stop_after == "C":
            return True

        A.reset(keep_c)
        AACT, BAACT = A.alloc("AACT", [128, 4, S], BF16)
        keep_a = A.off
        APAD, BAPAD = A.alloc("APAD", [128, 4, 32 + S], BF16)
        tmp_o = A.off
        SG, BSG = A.alloc("SG", [128, 2, 512], F32, nbufs=2)
        A.reset(tmp_o)
        CT, BCT = A.alloc("CT", [128, 4, 512], F32, nbufs=4)
        NDG = 16
        DG, BDG = A.alloc("DG", [128, NDG, 128], BF16, nbufs=NDG)
        SQF, BSQF = A.alloc("SQF", [128, 2, 512], F32, nbufs=2)
        MEAN, BMEAN = A.alloc("MEAN", [128, 512], F32)
        VAR, BVAR = A.alloc("VAR", [128, 512], F32)
        memset(APAD[:, :, 0:32], 0.0, [BAPAD])
        for c in range(4):
            tval = ws.get(("w_in", l, c, 0, 8))
            tgate = ws.get(("w_in", l, 4 + c, 0, 8))
            for tt in range(4):
                bv = banks.next()
                bg = banks.next()
                for kc in range(8):
                    mm(PS[bv][:, :], tval.ap[:, kc, :], H[:, kc, ts(tt)], kc == 0, kc == 7, [tval.buf, BH[tt]], BPS[bv])
                for kc in range(8):
                    mm(PS[bg][:, :], tgate.ap[:, kc, :], H[:, kc, ts(tt)], kc == 0, kc == 7, [tgate.buf, BH[tt]],
                       BPS[bg])
                si = tt % 2
                act(SG[:, si, :], PS[bg][:, :], AF.Sigmoid, [BPS[bg]], [BSG[si]])
                tt_(APAD[:, c, 32 + tt * 512:32 + (tt + 1) * 512], PS[bv][:, :], SG[:, si, :], ALU.mult,
                    [BPS[bv], BSG[si]], [BAPAD])
            ws.done(tval)
            ws.done(tgate)
        dump(f"glu{l}", APAD[:, :, :], [128, 4, 32 + S], BF16, [BAPAD])
        cw = pb + 80
        dctr = 0
        for tt in range(4):
            base = 2 + tt * 512
            for c in range(4):
                b = banks.next()
                for kk in range(31):
                    sl = dctr % NDG
                    dctr += 1
                    wcol = PP[:, cw + kk * 4 + c:cw + kk * 4 + c + 1]
                    if kk % 2 == 0:
                        act(DG[:, sl, :], IDENT[:, :], AF.Identity, [BIDENT, BPP], [BDG[sl]], scale=wcol)
                    else:
                        tsc(DG[:, sl, :], IDENT[:, :], wcol, None, ALU.mult, None, [BIDENT, BPP], [BDG[sl]])
                    mm(PS[b][:, :], DG[:, sl, :], APAD[:, c, base + kk:base + kk + 512], kk == 0, kk == 30,
                       [BDG[sl], BAPAD], BPS[b], inc=True)
                act(CT[:, c, :], PS[b][:, :], AF.Identity, [BPS[b], BPP], [BCT[c]], bias=PP[:, pb + 204 + c:pb + 205 + c])
            bsum = banks.next()
            bsq = banks.next()
            for c in range(4):
                o = CT[:, c, :]
                si = c % 2
                act(SQF[:, si, :], o, AF.Square, [BCT[c]], [BSQF[si]])
                mm(PS[bsum][:, :], ONESF[:, :], o, c == 0, c == 3, [BONES, BCT[c]], BPS[bsum], inc=True)
                mm(PS[bsq][:, :], ONESF[:, :], SQF[:, si, :], c == 0, c == 3, [BONES, BSQF[si]], BPS[bsq], inc=True)
            act(MEAN[:, :], PS[bsum][:, :], AF.Identity, [BPS[bsum]], [BMEAN], scale=1.0 / 512)
            tt_(VAR[:, :], MEAN[:, :], MEAN[:, :], ALU.mult, [BMEAN], [BVAR])
            stt(VAR[:, :], PS[bsq][:, :], 1.0 / 512, VAR[:, :], ALU.mult, ALU.subtract, [BPS[bsq], BVAR], [BVAR])
            act(VAR[:, :], VAR[:, :], AF.Sqrt, [BVAR, BEPS], [BVAR], bias=EPS[:, 1:2])
            recip(VAR[:, :], VAR[:, :], [BVAR], [BVAR])
            for c in range(4):
                tt_(CT[:, c, :], CT[:, c, :], MEAN[:, :], ALU.subtract, [BCT[c], BMEAN], [BCT[c]])
            for c in range(4):
                tt_(CT[:, c, :], CT[:, c, :], VAR[:, :], ALU.mult, [BCT[c], BVAR], [BCT[c]])
            for c in range(4):
                act(AACT[:, c, ts(tt)], CT[:, c, :], AF.Silu, [BCT[c], BPP], [BAACT], bias=PP[:, pb + 212 + c:pb + 213 + c],
                    scale=PP[:, pb + 208 + c:pb + 209 + c])
        dump(f"aact{l}", AACT[:, :, :], [128, 4, S], BF16, [BAACT])
        if stop_after == "A":
            return True

        A.reset(keep_a)
        UB, BUB = A.alloc("UB", [128, 4, S], BF16)
        keep_b = A.off
        UU, BUU = A.alloc("UU", [128, 4, S], BF16)
        VG, BVG = A.alloc("VG", [128, 2, 512], F32, nbufs=2)
        VL, BVL = A.alloc("VL", [128, 2, 512], BF16, nbufs=2)
        TB, BTB = A.alloc("TB", [128, 512], F32)
        for c in range(4):
            tu = ws.get(("w_in", l, 8 + c, 0, 8))

            def ev(tt, ps, psb, c=c):
                act(UU[:, c, ts(tt)], ps[:, :], AF.Gelu_apprx_tanh, [psb], [BUU])
            proj_fm(tu, 8, h_rhs, h_bufs, ev)
            ws.done(tu)
        tvs = [ws.get(("w_in", l, 12 + cc, 0, 8)) for cc in range(4)]
        for n in range(16):
            b = banks.next()
            tt = n // 4
            for cc in range(4):
                for kc in range(8):
                    mm(PS[b][:, cc * 128:(cc + 1) * 128], H[:, kc, n * 128:(n + 1) * 128], tvs[cc].ap[:, kc, :], kc == 0,
                       kc == 7, [tvs[cc].buf, BH[tt]], BPS[b], inc=(kc == 7 and cc == 3))
            vi = n % 2
            sm = SMALL[:, vi, :]
            act(VG[:, vi, :], PS[b][:, :], AF.Gelu_apprx_tanh, [BPS[b]], [BVG[vi]])
            k.op("dve", lambda e, o=ST6[:, vi, :], i=VG[:, vi, :]: e.bn_stats(out=o, in_=i), reads=[BVG[vi]],
                 writes=[BSMALL[vi]])
            k.op("dve", lambda e, o=SMALL[:, vi, 2:4], i=ST6[:, vi, :]: e.bn_aggr(out=o, in_=i), reads=[BSMALL[vi]],
                 writes=[BSMALL[vi]])
            act(SMALL[:, vi, 5:6], SMALL[:, vi, 3:4], AF.Sqrt, [BSMALL[vi], BEPS], [BSMALL[vi]], bias=EPS[:, 1:2])
            recip(SMALL[:, vi, 6:7], SMALL[:, vi, 5:6], [BSMALL[vi]], [BSMALL[vi]])
            tsc(VG[:, vi, :], VG[:, vi, :], SMALL[:, vi, 2:3], SMALL[:, vi, 6:7], ALU.subtract, ALU.mult,
                [BVG[vi], BSMALL[vi]], [BVG[vi]])
            tt_(VG[:, vi, :], VG[:, vi, :], PBC[:, 0:512], ALU.mult, [BVG[vi], BPBC], [BVG[vi]], eng="pool")
            tt_(VL[:, vi, :], VG[:, vi, :], PBC[:, 512:1024], ALU.add, [BVG[vi], BPBC], [BVL[vi]], eng="pool")
            b2 = banks.next()
            for g in range(4):
                mm(PS[b2][:, g * 128:(g + 1) * 128], VL[:, vi, g * 128:(g + 1) * 128], WSM[:, g, :], True, True,
                   [BVL[vi], BWSM], BPS[b2], inc=(g == 3))
            tt_(TB[:, :], PS[b2][:, :], PBC[:, 1024:1536], ALU.add, [BPS[b2], BPBC], [BTB])
            tbv = TB[:, :].rearrange("p (g t) -> p g t", g=4)
            tt_(UB[:, :, n * 128:(n + 1) * 128], tbv, UU[:, :, n * 128:(n + 1) * 128], ALU.mult, [BTB, BUU], [BUB])
        for t_ in tvs:
            ws.done(t_)
        dump(f"ub{l}", UB[:, :, :], [128, 4, S], BF16, [BUB])
        if stop_after == "B":
            return True

        A.reset(keep_b)
        MG, BMG = A.alloc("MG", [128, 8, 1024], BF16)
        SGM, BSGM = A.alloc("SGM", [128, 3, 512], F32)
        M0, BM0 = A.alloc("M0", [128, 512], F32)
        M1, BM1 = A.alloc("M1", [128, 512], F32)
        for hf in range(2):
            banks.set(range(8))
            for dc in range(8):
                wa = ws.get(("w_a", l, dc, 0, 4))
                wb = ws.get(("w_b", l, dc, 0, 4))
                wc = ws.get(("w_c", l, dc, 0, 2))
                wg = [ws.get(("w_in", l, 34 + i * 8 + dc, 0, 8)) for i in range(3)]
                for t2 in range(2):
                    tt = hf * 2 + t2
                    pa, pb_, pc_ = banks.next(), banks.next(), banks.next()
                    pg = [banks.next() for _ in range(3)]
                    for kc in range(4):
                        mm(PS[pa][:, :], wa.ap[:, kc, :], AACT[:, kc, ts(tt)], kc == 0, kc == 3, [wa.buf, BAACT], BPS[pa])
                    for kc in range(4):
                        mm(PS[pb_][:, :], wb.ap[:, kc, :], UB[:, kc, ts(tt)], kc == 0, kc == 3, [wb.buf, BUB], BPS[pb_])
                    for kc in range(2):
                        mm(PS[pc_][:, :], wc.ap[:, kc, :], OC[:, kc, ts(tt)], kc == 0, kc == 1, [wc.buf, BOC], BPS[pc_])
                    for i in range(3):
                        for kc in range(8):
                            mm(PS[pg[i]][:, :], wg[i].ap[:, kc, :], H[:, kc, ts(tt)], kc == 0, kc == 7,
                               [wg[i].buf, BH[tt]], BPS[pg[i]])
                    for i in range(3):
                        col = pb + 56 + i * 8 + dc
                        act(SGM[:, i, :], PS[pg[i]][:, :], AF.Sigmoid, [BPS[pg[i]], BPP], [BSGM], bias=PP[:, col:col + 1])
                    tt_(M0[:, :], PS[pa][:, :], SGM[:, 0, :], ALU.mult, [BPS[pa], BSGM], [BM0])
                    tt_(M1[:, :], PS[pb_][:, :], SGM[:, 1, :], ALU.mult, [BPS[pb_], BSGM], [BM1])
                    tt_(M0[:, :], M0[:, :], M1[:, :], ALU.add, [BM0, BM1], [BM0])
                    tt_(M1[:, :], PS[pc_][:, :], SGM[:, 2, :], ALU.mult, [BPS[pc_], BSGM], [BM1])
                    tt_(MG[:, dc, ts(t2)], M0[:, :], M1[:, :], ALU.add, [BM0, BM1], [BMG])
                for t_ in [wa, wb, wc] + wg:
                    ws.done(t_)
            if dbg is not None:
                dump(f"merged{l}_{hf}", MG[:, :, :], [128, 8, 1024], BF16, [BMG])
            banks.set(range(6))
            wm = [ws.get(("w_mix", l, dc, 0, 8)) for dc in range(8)]
            for t2 in range(2):
                tt = hf * 2 + t2
                sb_ = 6 + t2
                for dc in range(8):
                    b = banks.next()
                    for kc in range(8):
                        mm(PS[b][:, :], wm[dc].ap[:, kc, :], MG[:, kc, ts(t2)], kc == 0, kc == 7, [wm[dc].buf, BMG], BPS[b])
                    flush_stats()
                    y_chunk(PS[b][:, :], BPS[b], dc, hf, sb_, dc == 0, dc == 7)
                post_update(pb + 8, tt, hf, sb_)
            for t_ in wm:
                ws.done(t_)
        return False

    def sublayer_xattn(l, s):
        pb = l * PPL
        banks.set(range(8))
        A.reset()
        MEMS, BMEMS = A.alloc("MEMS", [128, 8, 256], F32)
        KT, BKT = A.alloc("KT", [128, 8, 256], BF16)
        VX, BVX = A.alloc("VX", [128, 2, 1024], BF16)
        QX, BQX = A.alloc("QX", [128, 8, S], BF16)
        PTX, BPTX = A.alloc("PTX", [128, 2, 2, 512], BF16, nbufs=2)
        RD, BRD = A.alloc("RD", [128, 2, 512], F32, nbufs=2)
        memn_o = A.off
        MEMN, BMEMN = A.alloc("MEMN", [128, 8, 256], BF16)
        A.reset(0)
        OX0, BOX0 = A.alloc("OX0", [128, 8, 512], BF16)
        A.reset(memn_o)
        OX1, BOX1 = A.alloc("OX1", [128, 8, 512], BF16)
        OXs, BOXs = [OX0, OX1], [BOX0, BOX1]
        if not dry:
            k.dma("sp", [(MEMS[:, kc, :], T["memT"][s, kc, :, :]) for kc in range(8)], writes=[BMEMS])
        b = banks.next()
        for kc in range(8):
            si = kc % 2
            act(SQ[:, si, 0:256], MEMS[:, kc, :], AF.Square, [BMEMS], [BSQ[si]])
            mm(PS[b][:, 0:256], ONESB[:, :], SQ[:, si, 0:256], kc == 0, kc == 7, [BONES, BSQ[si]], BPS[b], inc=True)
        ri = nrm_ctr[0] % 2
        nrm_ctr[0] += 1
        act(RSTD[:, ri, 0:256], PS[b][:, 0:256], AF.Sqrt, [BPS[b], BEPS], [BRSTD[ri]], bias=EPS[:, 0:1], scale=1.0 / D)
        recip(RSTD[:, ri, 0:256], RSTD[:, ri, 0:256], [BRSTD[ri]], [BRSTD[ri]])
        for kc in range(8):
            stt(MEMN[:, kc, :], MEMS[:, kc, :], PP[:, pb + 32 + kc:pb + 33 + kc], RSTD[:, ri, 0:256], ALU.mult, ALU.mult,
                [BMEMS, BPP, BRSTD[ri]], [BMEMN])
        for ec in range(8):
            w = ws.get(("w_xkv", l, ec, 0, 8))
            b = banks.next()
            for kc in range(8):
                mm(PS[b][:, 0:256], w.ap[:, kc, :], MEMN[:, kc, :], kc == 0, kc == 7, [w.buf, BMEMN], BPS[b])
            act(KT[:, ec, :], PS[b][:, 0:256], AF.Copy, [BPS[b]], [BKT])
            ws.done(w)
        for ec in range(8):
            w = ws.get(("w_xkv", l, 8 + ec, 0, 8))
            b = banks.next()
            for mb in range(2):
                for kc in range(8):
                    mm(PS[b][:, mb * 128:(mb + 1) * 128], MEMN[:, kc, mb * 128:(mb + 1) * 128], w.ap[:, kc, :], kc == 0,
                       kc == 7, [w.buf, BMEMN], BPS[b], inc=(kc == 7 and mb == 1))
            psv = PS[b][:, 0:256].rearrange("p (m c) -> p m c", m=2)
            act(VX[:, :, ec * 128:(ec + 1) * 128], psv, AF.Copy, [BPS[b]], [BVX])
            ws.done(w)
        norm_to_H(pb + 16, range(4))
        for dc in range(8):
            w = ws.get(("w_xq", l, dc, 0, 8))

            def ev(tt, ps, psb, dc=dc):
                act(QX[:, dc, ts(tt)], ps[:, :], AF.Copy, [psb], [BQX])
            proj_fm(w, 8, h_rhs, h_bufs, ev)
            ws.done(w)
        wo = [ws.get(("w_xo", l, dc, 0, 8)) for dc in range(8)]
        banks.set(range(6))

        def attn_tile(tt):
            OX, BOX = OXs[tt % 2], BOXs[tt % 2]

            def xs1(hd):
                pi = hd % 2
                bs_ = [banks.next(), banks.next()]
                for mb in range(2):
                    for ec in range(2):
                        mm(PS[bs_[mb]][:, :], KT[:, hd * 2 + ec, mb * 128:(mb + 1) * 128], QX[:, hd * 2 + ec, ts(tt)],
                           ec == 0, ec == 1, [BKT, BQX], BPS[bs_[mb]])
                    act(PTX[:, pi, mb, :], PS[bs_[mb]][:, :], AF.Exp, [BPS[bs_[mb]]], [BPTX[pi]], scale=1.0 / 16)

            def xs2(hd):
                pi = hd % 2
                bd = banks.next()
                for mb in range(2):
                    mm(PS[bd][:, :], ONESB[:, :], PTX[:, pi, mb, :], mb == 0, mb == 1, [BONES, BPTX[pi]], BPS[bd])
                recip(RD[:, pi, :], PS[bd][:, :], [BPS[bd]], [BRD[pi]])
                for ec in range(2):
                    bo = banks.next()
                    for mb in range(2):
                        mm(PS[bo][:, :], VX[:, mb, (hd * 2 + ec) * 128:(hd * 2 + ec + 1) * 128], PTX[:, pi, mb, :], mb == 0,
                           mb == 1, [BVX, BPTX[pi]], BPS[bo])
                    tt_(OX[:, hd * 2 + ec, :], PS[bo][:, :], RD[:, pi, :], ALU.mult, [BPS[bo], BRD[pi]], [BOX])

            for hd in range(4):
                xs1(hd)
                if hd > 0:
                    xs2(hd - 1)
            xs2(3)

        def xo_tile(tt):
            OX, BOX = OXs[tt % 2], BOXs[tt % 2]
            ysel = tt % 2
            sb_ = 6 + (tt % 2)
            for dc in range(8):
                b = banks.next()
                for kc in range(8):
                    mm(PS[b][:, :], wo[dc].ap[:, kc, :], OX[:, kc, :], kc == 0, kc == 7, [wo[dc].buf, BOX], BPS[b])
                flush_stats()
                y_chunk(PS[b][:, :], BPS[b], dc, ysel, sb_, dc == 0, dc == 7)
            post_update(pb + 24, tt, ysel, sb_)

        if XATTN_PIPE:
            for tt in range(4):
                attn_tile(tt)
                if tt > 0:
                    xo_tile(tt - 1)
            xo_tile(3)
        else:
            for tt in range(4):
                attn_tile(tt)
                xo_tile(tt)
        for t_ in wo:
            ws.done(t_)

    def sublayer_ffn(l, tile_done=None):
        pb = l * PPL
        A.reset()
        ACTB, BACTB = A.alloc("ACTB", [128, 22, 1024], BF16)
        GT, BGT = A.alloc("GT", [128, 2, 516], F32, nbufs=2)
        CV, BCV = A.alloc("CV", [128, 512], F32)
        GL, BGL = A.alloc("GL", [128, 2, 512], F32, nbufs=2)
        memset(HALO[:, :, :], 0.0, [BHALO])
        fw = pb + 216
        gi = 0
        for hf in range(2):
            banks.set(range(8))
            norm_to_H(pb + 40, (2 * hf, 2 * hf + 1))
            for j in range(22):
                wg = ws.get(("w_up", l, j, 0, 8))
                wv = ws.get(("w_up", l, 22 + j, 0, 8))
                for t2 in range(2):
                    tt = hf * 2 + t2
                    bg, bv = banks.next(), banks.next()
                    for kc in range(8):
                        mm(PS[bg][:, :], wg.ap[:, kc, :], H[:, kc, ts(tt)], kc == 0, kc == 7, [wg.buf, BH[tt]], BPS[bg])
                    for kc in range(8):
                        mm(PS[bv][:, :], wv.ap[:, kc, :], H[:, kc, ts(tt)], kc == 0, kc == 7, [wv.buf, BH[tt]], BPS[bv])
                    g2 = gi % 2
                    gi += 1
                    act(GT[:, g2, 2:514], PS[bg][:, :], AF.Copy, [BPS[bg]], [BGT[g2]])
                    cpy(GT[:, g2, 0:2], HALO[:, j, :], [BHALO], [BGT[g2]])
                    cpy(HALO[:, j, :], GT[:, g2, 512:514], [BGT[g2]], [BHALO])
                    act(CV[:, :], PS[bg][:, :], AF.Identity, [BPS[bg], BPP], [BCV], bias=PP[:, pb + 282 + j:pb + 283 + j],
                        scale=PP[:, fw + 44 + j:fw + 45 + j])
                    stt(CV[:, :], GT[:, g2, 0:512], PP[:, fw + j:fw + j + 1], CV[:, :], ALU.mult, ALU.add,
                        [BGT[g2], BPP, BCV], [BCV])
                    stt(CV[:, :], GT[:, g2, 1:513], PP[:, fw + 22 + j:fw + 23 + j], CV[:, :], ALU.mult, ALU.add,
                        [BGT[g2], BPP, BCV], [BCV])
                    act(GL[:, g2, :], CV[:, :], AF.Gelu_apprx_tanh, [BCV], [BGL[g2]])
                    tt_(ACTB[:, j, ts(t2)], PS[bv][:, :], GL[:, g2, :], ALU.mult, [BPS[bv], BGL[g2]], [BACTB])
                ws.done(wg)
                ws.done(wv)
            banks.set(range(6))
            ysels = (1, 0) if hf == 0 else (0, 1)
            for dc in range(8):
                wd = [ws.get(("w_down", l, dc, 0, 8)), ws.get(("w_down", l, dc, 8, 8)), ws.get(("w_down", l, dc, 16, 6))]
                for t2 in range(2):
                    b = banks.next()
                    for kc in range(22):
                        w = wd[kc // 8]
                        mm(PS[b][:, :], w.ap[:, kc % 8, :], ACTB[:, kc, ts(t2)], kc == 0, kc == 21, [w.buf, BACTB], BPS[b])
                    flush_stats()
                    y_chunk(PS[b][:, :], BPS[b], dc, ysels[t2], 6 + t2, dc == 0, dc == 7)
                for w in wd:
                    ws.done(w)
            for t2 in range(2):
                tt = hf * 2 + t2
                post_update(pb + 48, tt, ysels[t2], 6 + t2)
                if tile_done is not None:
                    tile_done(tt)

    stopped = False

    def load_x(s_, tt):
        if not dry:
            k.dma("sp", [(XRES[:, kc, ts(tt)], T["xT"][s_, kc, :, ts(tt)]) for kc in range(8)], writes=[BX[tt]])

    def store_x(s_, tt):
        if not dry:
            k.dma("sp", [(T["outT"][s_, kc, :, ts(tt)], XRES[:, kc, ts(tt)]) for kc in range(8)], reads=[BX[tt]],
                  sembuf=BX[tt])

    try:
        for tt in range(4):
            load_x(0, tt)
        for s in range(nseq):
            def tile_done(tt, s=s):
                store_x(s, tt)
                if s + 1 < nseq:
                    load_x(s + 1, tt)
            for l in range(nlayer):
                stopped = sublayer_mixer(l)
                if stopped:
                    break
                dump(f"x1_{l}", XRES[:, :, :], [128, 8, S], F32, BX)
                if stop_after == "mixer":
                    stopped = True
                    break
                sublayer_xattn(l, s)
                dump(f"x2_{l}", XRES[:, :, :], [128, 8, S], F32, BX)
                if stop_after == "xattn":
                    stopped = True
                    break
                sublayer_ffn(l, tile_done if l == nlayer - 1 else None)
            if stopped:
                for tt in range(4):
                    store_x(s, tt)
                break
    except _Stop:
        for tt in range(4):
            store_x(0, tt)
    k.wait_all("sp", BX + dbg_out)
    return ws


def make_program(nseq=SEQ_PER_CORE, nlayer=DEPTH, stop_after=None, dbg=None):
    kd = K(None, dry=True)
    wsd = build_program(None, kd, None, nseq, nlayer, stop_after, dbg)
    seq = wsd.rec
    nc = bass.Bass("TRN2", target_bir_lowering=False)
    k = K(nc)
    ws = build_program(nc, k, seq, nseq, nlayer, stop_after, dbg)
    assert ws.idx == len(seq) and ws.loaded == len(seq)
    return nc, k


def _t5_bucket(n):
    n = np.maximum(n, 0)
    nf = np.maximum(n, 1).astype(np.float32)
    large = 16 + (np.log(nf / np.float32(16)) / np.float32(np.log(2048 / 16)) * np.float32(16)).astype(np.int32)
    large = np.minimum(large, 31)
    return np.where(n < 16, n, large)


def _wtiles(W):
    L, Kd, N = W.shape
    return np.ascontiguousarray(W.reshape(L, Kd // 128, 128, N // 128, 128).transpose(0, 3, 2, 1, 4))


def _cols(v):
    v = np.asarray(v)
    C = v.shape[-1]
    lead = v.shape[:-1]
    a = v.reshape(*lead, C // 128, 128)
    a = np.moveaxis(a, -1, 0)
    return a.reshape(128, -1)


def prep_shared(inp):
    L = DEPTH
    pp = np.zeros((128, L * PPL), np.float32)
    for l in range(L):
        o = l * PPL
        pp[:, o + 0:o + 8] = _cols(inp["mix_pre_g"][l])
        pp[:, o + 8:o + 16] = _cols(inp["mix_post_g"][l])
        pp[:, o + 16:o + 24] = _cols(inp["x_pre_g"][l])
        pp[:, o + 24:o + 32] = _cols(inp["x_post_g"][l])
        pp[:, o + 32:o + 40] = _cols(inp["mem_g"][l])
        pp[:, o + 40:o + 48] = _cols(inp["ffn_pre_g"][l])
        pp[:, o + 48:o + 56] = _cols(inp["ffn_post_g"][l])
        pp[:, o + 56:o + 80] = _cols(inp["b_gate"][l])
        pp[:, o + 80:o + 204] = _cols(inp["conv_a_w"][l])
        pp[:, o + 204:o + 208] = _cols(inp["conv_a_b"][l])
        pp[:, o + 208:o + 212] = _cols(inp["ln_a_g"][l])
        pp[:, o + 212:o + 216] = _cols(inp["ln_a_b"][l])
        pp[:, o + 216:o + 282] = _cols(inp["conv_f_w"][l])
        pp[:, o + 282:o + 304] = _cols(inp["conv_f_b"][l])
    rb = np.asarray(inp["rel_bias"])
    kk = np.arange(128)[:, None]
    qq = np.arange(128)[None, :]
    abias = np.zeros((128, 3, 4, 2, 128), np.float32)
    amask = np.zeros((128, 3, 4, 2, 128), np.float32)
    for g, dil in enumerate(DILS):
        for pc in range(2):
            rel = qq + 128 - kk if pc == 0 else qq - kk
            valid = (rel >= 0) & (rel <= 128)
            bucket = _t5_bucket(rel * dil)
            for h in range(4):
                abias[:, g, h, pc, :] = rb[bucket, g * 4 + h]
                amask[:, g, h, pc, :] = valid
    pbc = np.zeros((L, 128, 1536), np.float32)
    for l in range(L):
        pbc[l, :, 0:512] = np.broadcast_to(inp["ln_b_g"][l][None, :], (128, 512))
        pbc[l, :, 512:1024] = np.broadcast_to(inp["ln_b_b"][l][None, :], (128, 512))
        pbc[l, :, 1024:1536] = np.broadcast_to(np.asarray(inp["b_s"][l]).reshape(1, 512), (128, 512))
    wsT = np.ascontiguousarray(np.asarray(inp["w_s"]).transpose(0, 3, 1, 2))
    tril = (np.arange(128)[None, :] >= np.arange(128)[:, None]).astype(np.float32)
    trilm = np.ascontiguousarray(np.broadcast_to(tril[:, None, :], (128, 4, 128)))
    w_c = np.asarray(inp["w_c_out"])
    shared = {
        "pp": pp,
        "abias": abias.reshape(128, 3072),
        "amask": amask.reshape(128, 3072),
        "pbc": pbc,
        "wsT": wsT,
        "trilm": trilm,
        "ident": np.eye(128, dtype=np.float32),
        "w_in": _wtiles(np.asarray(inp["w_in"])),
        "w_a": _wtiles(np.asarray(inp["w_a_out"])),
        "w_b": _wtiles(np.asarray(inp["w_b_out"])),
        "w_c": _wtiles(w_c),
        "w_mix": _wtiles(np.asarray(inp["w_mix_out"])),
        "w_xq": _wtiles(np.asarray(inp["w_xq"])),
        "w_xkv": _wtiles(np.asarray(inp["w_xkv"])),
        "w_xo": _wtiles(np.asarray(inp["w_xo"])),
        "w_up": _wtiles(np.asarray(inp["w_up"])),
        "w_down": _wtiles(np.asarray(inp["w_down"])),
    }
    return shared


def prep_core(inp, c, nseq=SEQ_PER_CORE):
    xs = np.asarray(inp["x"][c * SEQ_PER_CORE:c * SEQ_PER_CORE + nseq])
    ms = np.asarray(inp["mem"][c * SEQ_PER_CORE:c * SEQ_PER_CORE + nseq])
    xT = np.zeros((SEQ_PER_CORE, 8, 128, S), np.float32)
    mT = np.zeros((SEQ_PER_CORE, 8, 128, 256), np.float32)
    xT[:nseq] = xs.transpose(0, 2, 1).reshape(nseq, 8, 128, S)
    mT[:nseq] = ms.transpose(0, 2, 1).reshape(nseq, 8, 128, 256)
    return {"xT": xT, "memT": mT}


_PROG = {}


def kernel(**inputs):
    if "p" not in _PROG:
        _PROG["p"] = make_program()
    nc, _ = _PROG["p"]
    shared = prep_shared(inputs)
    in_maps = []
    for c in range(N_CORES):
        m = dict(shared)
        m.update(prep_core(inputs, c))
        in_maps.append(m)
    res = run_bass_kernel_spmd(nc, in_maps, core_ids=list(range(N_CORES)))
    out = np.empty((N_CORES * SEQ_PER_CORE, S, D), np.float32)
    for c in range(N_CORES):
        oT = res.results[c]["outT"]
        out[c * SEQ_PER_CORE:(c + 1) * SEQ_PER_CORE] = oT.reshape(SEQ_PER_CORE, D, S).transpose(0, 2, 1)
    return out
```

```python
import numpy as np
import concourse.bass as bass
import concourse.mybir as mybir
from concourse.bass_utils import run_bass_kernel_spmd

F32 = mybir.dt.float32
BF16 = mybir.dt.bfloat16
AF = mybir.ActivationFunctionType
ALU = mybir.AluOpType
AX = mybir.AxisListType

EPOCH = 12000
SAME_SYNC = True

N_CORES = 8
SEQ_PER_CORE = 4
DEPTH = 2
D = 1024
S = 2048
NSLOT = 10
XATTN_PIPE = False
DEFER_STATS = True
PPL = 312
DILS = (1, 4, 16)


class Buf:
    __slots__ = ("name", "w", "r", "al", "dsem", "dcnt")

    def __init__(self, name):
        self.name = name
        self.w = None
        self.r = {}
        self.al = []
        self.dsem = None
        self.dcnt = 0


class Dummy:
    shape = ()

    def __getitem__(self, i):
        return self

    def rearrange(self, *a, **k):
        return self


DUMMY = Dummy()


class Eng:
    def __init__(self, K, name, obj):
        self.name = name
        self.obj = obj
        self.sem = None if K.dry else K.nc.alloc_semaphore(f"s_{name}_0")
        self.nsem = 1
        self.cnt = 0
        self.own = set() if K.dry else {id(self.sem)}
        self.waited = {}
        self.nwaits = 0
        self.nins = 0


class K:
    def __init__(self, nc, dry=False):
        self.nc = nc
        self.dry = dry
        if dry:
            self.engs = {n: Eng(self, n, None) for n in ("pe", "act", "dve", "pool", "sp")}
        else:
            self.engs = {
                "pe": Eng(self, "pe", nc.tensor),
                "act": Eng(self, "act", nc.scalar),
                "dve": Eng(self, "dve", nc.vector),
                "pool": Eng(self, "pool", nc.gpsimd),
                "sp": Eng(self, "sp", nc.sync),
            }
        self.semobj = {}
        if not dry:
            for e in self.engs.values():
                self.semobj[id(e.sem)] = e.sem
        self.ndsem = 0

    def _deps(self, reads, writes):
        deps = {}

        def add(d):
            if d is None:
                return
            s, v = d
            if deps.get(s, 0) < v:
                deps[s] = v

        for b in reads:
            add(b.w)
            for a in b.al:
                add(a.w)
        for b in writes:
            add(b.w)
            for d in b.r.values():
                add(d)
            for a in b.al:
                add(a.w)
                for d in a.r.values():
                    add(d)
        return deps

    def _wait(self, E, deps):
        for s, v in deps.items():
            if s in E.own:
                if E.name in ("pe", "sp") or not SAME_SYNC:
                    continue
                if s == id(E.sem) and v > E.cnt:
                    continue
                if E.name in ("act", "dve") and (s != id(E.sem) or v < E.cnt):
                    continue
            if E.waited.get(s, 0) >= v:
                continue
            E.obj.wait_ge(self.semobj[s], v)
            E.waited[s] = v
            E.nwaits += 1

    def _tag(self, E, ins, inc):
        E.nins += 1
        if inc:
            ins.then_inc(E.sem, 1)
            E.cnt += 1
            tag = (id(E.sem), E.cnt)
            if E.cnt >= EPOCH:
                E.sem = self.nc.alloc_semaphore(f"s_{E.name}_{E.nsem}")
                E.nsem += 1
                E.cnt = 0
                E.own.add(id(E.sem))
                self.semobj[id(E.sem)] = E.sem
        else:
            tag = (id(E.sem), E.cnt + 1)
        return tag

    def op(self, eng, fn, reads=(), writes=(), inc=True):
        if self.dry:
            return None
        E = self.engs[eng]
        self._wait(E, self._deps(reads, writes))
        ins = fn(E.obj)
        tag = self._tag(E, ins, inc)
        for b in writes:
            b.w = tag
            b.r = {}
        for b in reads:
            b.r[eng] = tag
        return ins

    def dma(self, eng, pairs, reads=(), writes=(), sembuf=None):
        if self.dry:
            return None
        E = self.engs[eng]
        sb = sembuf or (writes[0] if writes else reads[0])
        if sb.dsem is None:
            sb.dsem = self.nc.alloc_semaphore(f"d_{self.ndsem}")
            self.ndsem += 1
            self.semobj[id(sb.dsem)] = sb.dsem
        deps = self._deps(reads, writes)
        if sb.dcnt:
            s = id(sb.dsem)
            if deps.get(s, 0) < sb.dcnt:
                deps[s] = sb.dcnt
        self._wait(E, deps)
        for (o, i) in pairs:
            E.obj.dma_start(out=o, in_=i).then_inc(sb.dsem, 16)
            E.nins += 1
            sb.dcnt += 16
        tag = (id(sb.dsem), sb.dcnt)
        for b in writes:
            b.w = tag
            b.r = {}
        for b in reads:
            b.r["dma_" + sb.name] = tag
        return tag

    def wait_all(self, eng, bufs):
        if self.dry:
            return
        E = self.engs[eng]
        self._wait(E, self._deps(bufs, bufs))

    def stats(self):
        return {e.name: (e.nins, e.nwaits, e.nsem) for e in self.engs.values()}


class _Stop(Exception):
    pass


class Tile:
    __slots__ = ("i", "ap", "buf")

    def __init__(self, i, ap, buf):
        self.i = i
        self.ap = ap
        self.buf = buf


class WStream:
    def __init__(self, k, ring, rbufs, seq, ap_of):
        self.k = k
        self.ring = ring
        self.rbufs = rbufs
        self.seq = seq
        self.ap_of = ap_of
        self.rec = []
        self.idx = 0
        self.loaded = 0
        self.done_ = set()

    def _issue(self):
        while self.loaded < len(self.seq):
            i = self.loaded
            if i - NSLOT >= 0 and (i - NSLOT) not in self.done_:
                break
            slot = i % NSLOT
            src, kcn = self.ap_of(self.seq[i])
            self.k.dma("pool", [(self.ring[:, slot, 0:kcn, :], src)], writes=[self.rbufs[slot]])
            self.loaded += 1

    def get(self, desc):
        i = self.idx
        self.idx += 1
        if self.k.dry:
            self.rec.append(desc)
            return Tile(i, DUMMY, None)
        assert self.seq[i] == desc, (i, self.seq[i], desc)
        self._issue()
        assert self.loaded > i, f"weight ring too small at tile {i} {desc}"
        slot = i % NSLOT
        return Tile(i, self.ring[:, slot], self.rbufs[slot])

    def done(self, t):
        if self.k.dry:
            return
        self.done_.add(t.i)
        self._issue()


class Arena:
    def __init__(self, k, nc, base, size):
        self.k, self.nc, self.base, self.size = k, nc, base, size
        self.off = 0
        self.regs = []
        self.n = 0

    def reset(self, off=0):
        self.off = off

    def alloc(self, name, shape, dtype, nbufs=1):
        esz = 2 if dtype == BF16 else 4
        per = esz
        for s_ in shape[1:]:
            per *= s_
        per = (per + 63) // 64 * 64
        lo = self.off
        hi = lo + per
        assert hi <= self.size, f"arena overflow {name}: {hi} > {self.size}"
        self.off = hi
        bufs = [Buf(f"{name}{i}") for i in range(nbufs)]
        pbuf = per // nbufs
        new = []
        for bi, b in enumerate(bufs):
            blo, bhi = lo + bi * pbuf, (lo + (bi + 1) * pbuf if bi < nbufs - 1 else hi)
            for (l2, h2, b2) in self.regs:
                if l2 < bhi and blo < h2:
                    if b2 not in b.al:
                        b.al.append(b2)
                    if b not in b2.al:
                        b2.al.append(b)
            new.append((blo, bhi, b))
        self.regs.extend(new)
        if self.k.dry:
            t = DUMMY
        else:
            self.n += 1
            t = self.nc.alloc_sbuf_tensor_at(f"ar{self.n}_{name}", list(shape), dtype, offset=self.base + lo)
        return (t, bufs[0]) if nbufs == 1 else (t, bufs)


def ts(tt, n=512):
    return slice(tt * n, (tt + 1) * n)


def build_program(nc, k, wseq, nseq=SEQ_PER_CORE, nlayer=DEPTH, stop_after=None, dbg=None):
    dry = k.dry
    T = {}
    if not dry:
        def dram(name, shape, kind="ExternalInput"):
            T[name] = nc.dram_tensor(name, list(shape), F32, kind=kind).ap()
        dram("xT", [SEQ_PER_CORE, 8, 128, S])
        dram("memT", [SEQ_PER_CORE, 8, 128, 256])
        dram("pp", [128, DEPTH * PPL])
        dram("abias", [128, 3072])
        dram("amask", [128, 3072])
        dram("pbc", [DEPTH, 128, 1536])
        dram("wsT", [DEPTH, 128, 4, 128])
        dram("trilm", [128, 4, 128])
        dram("ident", [128, 128])
        dram("w_in", [DEPTH, 58, 128, 8, 128])
        dram("w_a", [DEPTH, 8, 128, 4, 128])
        dram("w_b", [DEPTH, 8, 128, 4, 128])
        dram("w_c", [DEPTH, 8, 128, 2, 128])
        dram("w_mix", [DEPTH, 8, 128, 8, 128])
        dram("w_xq", [DEPTH, 8, 128, 8, 128])
        dram("w_xkv", [DEPTH, 16, 128, 8, 128])
        dram("w_xo", [DEPTH, 8, 128, 8, 128])
        dram("w_up", [DEPTH, 44, 128, 8, 128])
        dram("w_down", [DEPTH, 8, 128, 22, 128])
        dram("outT", [SEQ_PER_CORE, 8, 128, S], kind="ExternalOutput")

    def ap_of(desc):
        name, l, j, k0, kcn = desc
        return T[name][l, j, :, k0:k0 + kcn, :], kcn

    BASE = 16512
    XRES_O = BASE
    H_O = XRES_O + 65536
    RING_O = H_O + 32768
    C_O = RING_O + NSLOT * 2048
    coff = [C_O]

    def calloc(name, shape, dtype):
        esz = 2 if dtype == BF16 else 4
        per = esz
        for s_ in shape[1:]:
            per *= s_
        per = (per + 63) // 64 * 64
        o = coff[0]
        coff[0] += per
        if dry:
            return DUMMY
        return nc.alloc_sbuf_tensor_at(name, list(shape), dtype, offset=o)

    if dry:
        XRES = H = Y = RING = DUMMY
    else:
        XRES = nc.alloc_sbuf_tensor_at("XRES", [128, 8, S], F32, offset=XRES_O)
        H = nc.alloc_sbuf_tensor_at("H", [128, 8, S], BF16, offset=H_O)
        Y = nc.alloc_sbuf_tensor_at("Y", [128, 8, 1024], F32, offset=H_O)
        RING = nc.alloc_sbuf_tensor_at("RING", [128, NSLOT, 8, 128], BF16, offset=RING_O)
    PP = calloc("PP", [128, DEPTH * PPL], F32)
    EXPBM = calloc("EXPBM", [128, 3, 4, 2, 128], BF16)
    PBC = calloc("PBC", [128, 1536], F32)
    WSRAW = calloc("WSRAW", [128, 4, 128], BF16)
    TRILM = calloc("TRILM", [128, 4, 128], BF16)
    WSM = calloc("WSM", [128, 4, 128], BF16)
    ONESB = calloc("ONESB", [128, 128], BF16)
    IDENT = calloc("IDENT", [128, 128], BF16)
    ONESF = calloc("ONESF", [128, 128], F32)
    RSTD = calloc("RSTD", [128, 2, 512], F32)
    SQ = calloc("SQ", [128, 2, 512], BF16)
    HALO = calloc("HALO", [128, 22, 2], F32)
    EPS = calloc("EPS", [128, 2], F32)
    SMALL = calloc("SMALL", [128, 2, 8], F32)
    ST6 = calloc("ST6", [128, 2, 6], F32)
    ARENA_O = (coff[0] + 63) // 64 * 64
    ARENA_SZ = 229344 - ARENA_O
    A = Arena(k, nc, ARENA_O, ARENA_SZ)

    BX = [Buf(f"X{t}") for t in range(4)]
    BH = [Buf(f"H{t}") for t in range(4)]
    BY = [Buf("Y0"), Buf("Y1")]
    for yi, hts in ((0, (0, 1)), (1, (2, 3))):
        for t in hts:
            BY[yi].al.append(BH[t])
            BH[t].al.append(BY[yi])
    RB = [Buf(f"ring{i}") for i in range(NSLOT)]
    BPP, BEXPBM, BPBC, BWSRAW, BTRILM, BWSM, BONES, BHALO, BEPS = [Buf(n) for n in (
        "PP", "EXPBM", "PBC", "WSRAW", "TRILM", "WSM", "ONES", "HALO", "EPS")]
    BRSTD = [Buf("RSTD0"), Buf("RSTD1")]
    BIDENT = Buf("IDENT")
    BSQ = [Buf("SQ0"), Buf("SQ1")]
    BSMALL = [Buf("SM0"), Buf("SM1")]
    if dry:
        PS = [DUMMY] * 8
    else:
        PS = [nc.alloc_psum_tensor(f"ps{i}", [128, 512], F32) for i in range(8)]
    BPS = [Buf(f"ps{i}") for i in range(8)]

    ws = WStream(k, RING, RB, wseq, ap_of)

    class Banks:
        def __init__(self):
            self.rot = list(range(8))
            self.i = 0

        def set(self, lst):
            self.rot = list(lst)
            self.i = 0

        def next(self):
            b = self.rot[self.i % len(self.rot)]
            self.i += 1
            return b

    banks = Banks()

    def mm(out, lhsT, rhs, first, last, reads, psb, inc=None):
        k.op("pe", lambda e: e.matmul(out, lhsT=lhsT, rhs=rhs, start=first, stop=last), reads=reads, writes=[psb],
             inc=last if inc is None else inc)

    def act(out, in_, func, reads, writes, bias=None, scale=None, accum=None):
        kw = {}
        if bias is not None:
            kw["bias"] = bias
        if scale is not None:
            kw["scale"] = scale
        if accum is not None:
            kw["accum_out"] = accum
        k.op("act", lambda e: e.activation(out=out, in_=in_, func=func, **kw), reads=reads, writes=writes)

    def tt_(out, in0, in1, op, reads, writes, eng="dve"):
        k.op(eng, lambda e: e.tensor_tensor(out=out, in0=in0, in1=in1, op=op), reads=reads, writes=writes)

    def tsc(out, in0, s1, s2, op0, op1, reads, writes, eng="dve"):
        if op1 is None:
            k.op(eng, lambda e: e.tensor_scalar(out=out, in0=in0, scalar1=s1, scalar2=None, op0=op0), reads=reads,
                 writes=writes)
        else:
            k.op(eng, lambda e: e.tensor_scalar(out=out, in0=in0, scalar1=s1, scalar2=s2, op0=op0, op1=op1),
                 reads=reads, writes=writes)

    def stt(out, in0, scalar, in1, op0, op1, reads, writes):
        k.op("dve", lambda e: e.scalar_tensor_tensor(out=out, in0=in0, scalar=scalar, in1=in1, op0=op0, op1=op1),
             reads=reads, writes=writes)

    def cpy(out, in_, reads, writes, eng="dve"):
        k.op(eng, lambda e: e.tensor_copy(out=out, in_=in_), reads=reads, writes=writes)

    def recip(out, in_, reads, writes):
        k.op("dve", lambda e: e.reciprocal(out=out, in_=in_), reads=reads, writes=writes)

    def memset(ap, val, writes, eng="dve"):
        k.op(eng, lambda e: e.memset(ap, val), writes=writes)

    dbg_out = []

    def chk(name):
        if stop_after == name:
            raise _Stop()

    def dump(name, ap, shape, dtype, bufs):
        if dry or dbg is None or name not in dbg:
            return
        t = nc.dram_tensor("dbg_" + name, list(shape), dtype, kind="ExternalOutput").ap()
        b = Buf("dbg_" + name)
        k.dma("sp", [(t, ap)], reads=bufs, writes=[b], sembuf=b)
        dbg_out.append(b)

    def proj_fm(tile, kcn, rhs_of, rbufs_of, evac, tiles=range(4)):
        for tt in tiles:
            b = banks.next()
            for kc in range(kcn):
                mm(PS[b][:, :], tile.ap[:, kc, :], rhs_of(kc, tt), kc == 0, kc == kcn - 1,
                   [tile.buf] + rbufs_of(tt), BPS[b])
            evac(tt, PS[b], BPS[b])

    def h_rhs(kc, tt):
        return H[:, kc, ts(tt)]

    def h_bufs(tt):
        return [BH[tt]]

    def rstd_from(psb_idx, ri, scale, epscol):
        act(RSTD[:, ri, :], PS[psb_idx][:, :], AF.Sqrt, [BPS[psb_idx], BEPS], [BRSTD[ri]], bias=EPS[:, epscol:epscol + 1],
            scale=scale)
        recip(RSTD[:, ri, :], RSTD[:, ri, :], [BRSTD[ri]], [BRSTD[ri]])

    nrm_ctr = [0]

    def norm_to_H(gcol, tiles):
        for tt in tiles:
            b = banks.next()
            for kc in range(8):
                si = kc % 2
                act(SQ[:, si, :], XRES[:, kc, ts(tt)], AF.Square, [BX[tt]], [BSQ[si]])
                mm(PS[b][:, :], ONESB[:, :], SQ[:, si, :], kc == 0, kc == 7, [BONES, BSQ[si]], BPS[b], inc=True)
            ri = nrm_ctr[0] % 2
            nrm_ctr[0] += 1
            rstd_from(b, ri, 1.0 / D, 0)
            for kc in range(8):
                stt(H[:, kc, ts(tt)], XRES[:, kc, ts(tt)], PP[:, gcol + kc:gcol + kc + 1], RSTD[:, ri, :], ALU.mult,
                    ALU.mult, [BX[tt], BPP, BRSTD[ri]], [BH[tt]])

    pend = []

    def y_chunk(ps_ap, psbuf, dc, ysel, sbank, first, last):
        ysl = slice(ysel * 512, ysel * 512 + 512)
        act(Y[:, dc, ysl], ps_ap, AF.Copy, [psbuf], [BY[ysel]])
        si = dc % 2
        act(SQ[:, si, :], ps_ap, AF.Square, [psbuf], [BSQ[si]])
        pend.append((sbank, si, first, last))
        if not DEFER_STATS:
            flush_stats()

    def flush_stats(keep=0):
        while len(pend) > keep:
            sbank, si, first, last = pend.pop(0)
            mm(PS[sbank][:, :], ONESB[:, :], SQ[:, si, :], first, last, [BONES, BSQ[si]], BPS[sbank], inc=True)

    def post_update(gcol, tt, ysel, sbank):
        flush_stats()
        ysl = slice(ysel * 512, ysel * 512 + 512)
        ri = nrm_ctr[0] % 2
        nrm_ctr[0] += 1
        rstd_from(sbank, ri, 1.0 / D, 0)
        for dc in range(8):
            tt_(Y[:, dc, ysl], Y[:, dc, ysl], RSTD[:, ri, :], ALU.mult, [BY[ysel], BRSTD[ri]], [BY[ysel]])
            stt(XRES[:, dc, ts(tt)], Y[:, dc, ysl], PP[:, gcol + dc:gcol + dc + 1], XRES[:, dc, ts(tt)], ALU.mult,
                ALU.add, [BY[ysel], BPP, BX[tt]], [BX[tt]])

    if not dry:
        k.dma("sp", [(PP[:, :], T["pp"][:, :])], writes=[BPP])
    memset(ONESB[:, :], 1.0, [BONES])
    memset(ONESF[:, :], 1.0, [BONES])
    memset(EPS[:, 0:1], 1e-6, [BEPS])
    memset(EPS[:, 1:2], 1e-5, [BEPS])
    if not dry:
        k.dma("pool", [(TRILM[:, :, :], T["trilm"][:, :, :])], writes=[BTRILM])
        k.dma("pool", [(IDENT[:, :], T["ident"][:, :])], writes=[BIDENT])
    A.reset()
    AB, BAB = A.alloc("abias", [128, 3072], F32)
    AM, BAM = A.alloc("amask", [128, 3072], F32)
    if not dry:
        k.dma("sp", [(AB[:, :], T["abias"][:, :])], writes=[BAB])
        k.dma("sp", [(AM[:, :], T["amask"][:, :])], writes=[BAM])
    if not dry:
        ebm_flat = EXPBM[:, :, :, :, :].rearrange("p g h c q -> p (g h c q)")
    else:
        ebm_flat = DUMMY
    act(AB[:, :], AB[:, :], AF.Exp, [BAB], [BAB])
    tt_(ebm_flat, AB[:, :], AM[:, :], ALU.mult, [BAB, BAM], [BEXPBM])

    def sublayer_mixer(l):
        pb = l * PPL
        banks.set(range(8))
        if not dry:
            k.dma("sp", [(PBC[:, :], T["pbc"][l, :, :])], writes=[BPBC])
            k.dma("pool", [(WSRAW[:, :, :], T["wsT"][l, :, :, :])], writes=[BWSRAW])
        tt_(WSM[:, :, :], WSRAW[:, :, :], TRILM[:, :, :], ALU.mult, [BWSRAW, BTRILM], [BWSM])
        norm_to_H(pb + 0, range(4))
        dump(f"h1_{l}", H[:, :, :], [128, 8, S], BF16, BH)
        if stop_after == "norm1":
            return True

        A.reset()
        OC, BOC = A.alloc("OC", [128, 2, S], BF16)
        keep_c = A.off
        Ut, BU = A.alloc("U", [128, 2, S], F32)
        qk_o = A.off
        Qb, BQ = A.alloc("Q", [128, 2, S], BF16, nbufs=2)
        Kb, BK = A.alloc("K", [128, 2, S], BF16, nbufs=2)
        end_qk = A.off
        A.reset(qk_o)
        R_, BR = A.alloc("R", [128, S], F32)
        A.reset(end_qk)
        VA, BVA = A.alloc("VA", [128, 2, 16, 2, 128], BF16, nbufs=2)
        NPT = 4
        PT, BPT = A.alloc("PT", [128, NPT, 2, 2, 128], BF16, nbufs=NPT)
        memset(VA[:, :, :, 0, 64:128], 1.0, BVA)
        memset(VA[:, :, :, 1, 0:64], 1.0, BVA)
        gctr = 0
        for hp in range(2):
            for g in range(3):
                d = DILS[g]
                nb = 16 // d
                pq = gctr % 2
                gctr += 1
                tq = ws.get(("w_in", l, 16 + g * 2 + hp, 0, 8))
                tk = ws.get(("w_in", l, 22 + g * 2 + hp, 0, 8))
                tv = ws.get(("w_in", l, 28 + g * 2 + hp, 0, 8))
                for (tile, dst, bdst) in ((tq, Qb, BQ[pq]), (tk, Kb, BK[pq])):
                    def ev(tt, ps, psb, dst=dst, bdst=bdst, pq=pq):
                        act(dst[:, pq, ts(tt)], ps[:, :], AF.Copy, [psb], [bdst])
                    proj_fm(tile, 8, h_rhs, h_bufs, ev)
                    ws.done(tile)
                    chk("Cq")
                chk("Cqk")

                def tokslice(j, d=d, nb=nb):
                    c, n = divmod(j, nb)
                    st = c + d * 128 * n
                    return slice(st, st + d * 127 + 1, d), n

                for j0 in range(0, 16, 4):
                    b = banks.next()
                    for jj in range(4):
                        tok, n = tokslice(j0 + jj)
                        for kc in range(8):
                            mm(PS[b][:, jj * 128:(jj + 1) * 128], H[:, kc, tok], tv.ap[:, kc, :], kc == 0, kc == 7,
                               [tv.buf] + BH, BPS[b], inc=(kc == 7 and jj == 3))
                    psv = PS[b][:, :].rearrange("p (j c) -> p j c", c=128)
                    act(VA[:, pq, j0:j0 + 4, 0, 0:64], psv[:, :, 0:64], AF.Copy, [BPS[b]], [BVA[pq]])
                    act(VA[:, pq, j0:j0 + 4, 1, 64:128], psv[:, :, 64:128], AF.Copy, [BPS[b]], [BVA[pq]])
                ws.done(tv)
                chk("Cv")

                def stage1(j):
                    tok, n = tokslice(j)
                    pcs = (0, 1) if n > 0 else (1,)
                    eb = j % NPT
                    for h in range(2):
                        b = banks.next()
                        pr = slice(64 * h, 64 * h + 64)
                        for pc in pcs:
                            ktok, _ = tokslice(j - 1 if pc == 0 else j)
                            mm(PS[b][:, pc * 128:(pc + 1) * 128], Kb[pr, pq, ktok], Qb[pr, pq, tok], True, True,
                               [BQ[pq], BK[pq]], BPS[b], inc=(pc == 1))
                        psv = PS[b][:, 0:256].rearrange("p (c q) -> p c q", c=2)
                        if n > 0:
                            p_ap, s_ap = PT[:, eb, h], psv
                        else:
                            p_ap, s_ap = PT[:, eb, h, 1, :], psv[:, 1, :]
                        act(p_ap, s_ap, AF.Exp, [BPS[b]], [BPT[eb]], scale=0.125)
                    if n > 0:
                        p_ap, m_ap = PT[:, eb], EXPBM[:, g, hp * 2:hp * 2 + 2]
                    else:
                        p_ap, m_ap = PT[:, eb, :, 1, :], EXPBM[:, g, hp * 2:hp * 2 + 2, 1, :]
                    tt_(p_ap, p_ap, m_ap, ALU.mult, [BPT[eb], BEXPBM], [BPT[eb]])

                def stage2(j):
                    tok, n = tokslice(j)
                    pcs = (0, 1) if n > 0 else (1,)
                    eb = j % NPT
                    b2 = banks.next()
                    for h in range(2):
                        for i, pc in enumerate(pcs):
                            vj = j - 1 if pc == 0 else j
                            mm(PS[b2][:, h * 128:(h + 1) * 128], VA[:, pq, vj, h, :], PT[:, eb, h, pc, :], i == 0,
                               i == len(pcs) - 1, [BVA[pq], BPT[eb]], BPS[b2], inc=(h == 1 and i == len(pcs) - 1))
                    psu = PS[b2][:, 0:256].rearrange("p (h q) -> p h q", h=2)
                    if g == 0:
                        act(Ut[:, :, tok], psu, AF.Copy, [BPS[b2]], [BU])
                    else:
                        tt_(Ut[:, :, tok], psu, Ut[:, :, tok], ALU.add, [BPS[b2], BU], [BU])

                DEPTH_C = 2
                for j in range(16):
                    stage1(j)
                    if j >= DEPTH_C:
                        stage2(j - DEPTH_C)
                for j in range(16 - DEPTH_C, 16):
                    stage2(j)
            recip(R_[0:64, :], Ut[64:128, 0, :], [BU], [BR])
            tt_(OC[0:64, hp, :], Ut[0:64, 0, :], R_[0:64, :], ALU.mult, [BU, BR], [BOC])
            recip(R_[64:128, :], Ut[0:64, 1, :], [BU], [BR])
            tt_(OC[64:128, hp, :], Ut[64:128, 1, :], R_[64:128, :], ALU.mult, [BU, BR], [BOC])
        dump(f"oc{l}", OC[:, :, :], [128, 2, S], BF16, [BOC])
        if stop_after == "C":
            return True

        A.reset(keep_c)
        AACT, BAACT = A.alloc("AACT", [128, 4, S], BF16)
        keep_a = A.off
        APAD, BAPAD = A.alloc("APAD", [128, 4, 32 + S], BF16)
        tmp_o = A.off
        SG, BSG = A.alloc("SG", [128, 2, 512], F32, nbufs=2)
        A.reset(tmp_o)
        CT, BCT = A.alloc("CT", [128, 4, 512], F32, nbufs=4)
        NDG = 16
        DG, BDG = A.alloc("DG", [128, NDG, 128], BF16, nbufs=NDG)
        SQF, BSQF = A.alloc("SQF", [128, 2, 512], F32, nbufs=2)
        MEAN, BMEAN = A.alloc("MEAN", [128, 512], F32)
        VAR, BVAR = A.alloc("VAR", [128, 512], F32)
        memset(APAD[:, :, 0:32], 0.0, [BAPAD])
        for c in range(4):
            tval = ws.get(("w_in", l, c, 0, 8))
            tgate = ws.get(("w_in", l, 4 + c, 0, 8))
            for tt in range(4):
                bv = banks.next()
                bg = banks.next()
                for kc in range(8):
                    mm(PS[bv][:, :], tval.ap[:, kc, :], H[:, kc, ts(tt)], kc == 0, kc == 7, [tval.buf, BH[tt]], BPS[bv])
                for kc in range(8):
                    mm(PS[bg][:, :], tgate.ap[:, kc, :], H[:, kc, ts(tt)], kc == 0, kc == 7, [tgate.buf, BH[tt]],
                       BPS[bg])
                si = tt % 2
                act(SG[:, si, :], PS[bg][:, :], AF.Sigmoid, [BPS[bg]], [BSG[si]])
                tt_(APAD[:, c, 32 + tt * 512:32 + (tt + 1) * 512], PS[bv][:, :], SG[:, si, :], ALU.mult,
                    [BPS[bv], BSG[si]], [BAPAD])
            ws.done(tval)
            ws.done(tgate)
        dump(f"glu{l}", APAD[:, :, :], [128, 4, 32 + S], BF16, [BAPAD])
        cw = pb + 80
        dctr = 0
        for tt in range(4):
            base = 2 + tt * 512
            for c in range(4):
                b = banks.next()
                for kk in range(31):
                    sl = dctr % NDG
                    dctr += 1
                    wcol = PP[:, cw + kk * 4 + c:cw + kk * 4 + c + 1]
                    if kk % 2 == 0:
                        act(DG[:, sl, :], IDENT[:, :], AF.Identity, [BIDENT, BPP], [BDG[sl]], scale=wcol)
                    else:
                        tsc(DG[:, sl, :], IDENT[:, :], wcol, None, ALU.mult, None, [BIDENT, BPP], [BDG[sl]])
                    mm(PS[b][:, :], DG[:, sl, :], APAD[:, c, base + kk:base + kk + 512], kk == 0, kk == 30,
                       [BDG[sl], BAPAD], BPS[b], inc=True)
                act(CT[:, c, :], PS[b][:, :], AF.Identity, [BPS[b], BPP], [BCT[c]], bias=PP[:, pb + 204 + c:pb + 205 + c])
            bsum = banks.next()
            bsq = banks.next()
            for c in range(4):
                o = CT[:, c, :]
                si = c % 2
                act(SQF[:, si, :], o, AF.Square, [BCT[c]], [BSQF[si]])
                mm(PS[bsum][:, :], ONESF[:, :], o, c == 0, c == 3, [BONES, BCT[c]], BPS[bsum], inc=True)
                mm(PS[bsq][:, :], ONESF[:, :], SQF[:, si, :], c == 0, c == 3, [BONES, BSQF[si]], BPS[bsq], inc=True)
            act(MEAN[:, :], PS[bsum][:, :], AF.Identity, [BPS[bsum]], [BMEAN], scale=1.0 / 512)
            tt_(VAR[:, :], MEAN[:, :], MEAN[:, :], ALU.mult, [BMEAN], [BVAR])
            stt(VAR[:, :], PS[bsq][:, :], 1.0 / 512, VAR[:, :], ALU.mult, ALU.subtract, [BPS[bsq], BVAR], [BVAR])
            act(VAR[:, :], VAR[:, :], AF.Sqrt, [BVAR, BEPS], [BVAR], bias=EPS[:, 1:2])
            recip(VAR[:, :], VAR[:, :], [BVAR], [BVAR])
            for c in range(4):
                tt_(CT[:, c, :], CT[:, c, :], MEAN[:, :], ALU.subtract, [BCT[c], BMEAN], [BCT[c]])
            for c in range(4):
                tt_(CT[:, c, :], CT[:, c, :], VAR[:, :], ALU.mult, [BCT[c], BVAR], [BCT[c]])
            for c in range(4):
                act(AACT[:, c, ts(tt)], CT[:, c, :], AF.Silu, [BCT[c], BPP], [BAACT], bias=PP[:, pb + 212 + c:pb + 213 + c],
                    scale=PP[:, pb + 208 + c:pb + 209 + c])
        dump(f"aact{l}", AACT[:, :, :], [128, 4, S], BF16, [BAACT])
        if stop_after == "A":
            return True

        A.reset(keep_a)
        UB, BUB = A.alloc("UB", [128, 4, S], BF16)
        keep_b = A.off
        UU, BUU = A.alloc("UU", [128, 4, S], BF16)
        VG, BVG = A.alloc("VG", [128, 2, 512], F32, nbufs=2)
        VL, BVL = A.alloc("VL", [128, 2, 512], BF16, nbufs=2)
        TB, BTB = A.alloc("TB", [128, 512], F32)
        CB, BCB = A.alloc("CB", [128, 512], F32)
        brw = banks.next()
        for g in range(4):
            mm(PS[brw][:, g * 128:(g + 1) * 128], ONESB[:, :], WSM[:, g, :], True, True, [BONES, BWSM], BPS[brw],
               inc=(g == 3))
        for g in range(4):
            stt(CB[:, g * 128:(g + 1) * 128], PS[brw][:, g * 128:(g + 1) * 128], PP[:, pb + 308 + g:pb + 309 + g],
                PBC[:, 1024 + g * 128:1024 + (g + 1) * 128], ALU.mult, ALU.add, [BPS[brw], BPP, BPBC], [BCB])
        for c in range(4):
            tu = ws.get(("w_in", l, 8 + c, 0, 8))

            def ev(tt, ps, psb, c=c):
                act(UU[:, c, ts(tt)], ps[:, :], AF.Gelu_apprx_tanh, [psb], [BUU])
            proj_fm(tu, 8, h_rhs, h_bufs, ev)
            ws.done(tu)
        tvs = [ws.get(("w_in", l, 12 + cc, 0, 8)) for cc in range(4)]
        for n in range(16):
            b = banks.next()
            tt = n // 4
            for cc in range(4):
                for kc in range(8):
                    mm(PS[b][:, cc * 128:(cc + 1) * 128], H[:, kc, n * 128:(n + 1) * 128], tvs[cc].ap[:, kc, :], kc == 0,
                       kc == 7, [tvs[cc].buf, BH[tt]], BPS[b], inc=(kc == 7 and cc == 3))
            vi = n % 2
            sm = SMALL[:, vi, :]
            act(VG[:, vi, :], PS[b][:, :], AF.Gelu_apprx_tanh, [BPS[b]], [BVG[vi]])
            k.op("dve", lambda e, o=ST6[:, vi, :], i=VG[:, vi, :]: e.bn_stats(out=o, in_=i), reads=[BVG[vi]],
                 writes=[BSMALL[vi]])
            k.op("dve", lambda e, o=SMALL[:, vi, 2:4], i=ST6[:, vi, :]: e.bn_aggr(out=o, in_=i), reads=[BSMALL[vi]],
                 writes=[BSMALL[vi]])
            act(SMALL[:, vi, 5:6], SMALL[:, vi, 3:4], AF.Sqrt, [BSMALL[vi], BEPS], [BSMALL[vi]], bias=EPS[:, 1:2])
            recip(SMALL[:, vi, 6:7], SMALL[:, vi, 5:6], [BSMALL[vi]], [BSMALL[vi]])
            tsc(VL[:, vi, :], VG[:, vi, :], SMALL[:, vi, 2:3], SMALL[:, vi, 6:7], ALU.subtract, ALU.mult,
                [BVG[vi], BSMALL[vi]], [BVL[vi]])
            b2 = banks.next()
            for g in range(4):
                mm(PS[b2][:, g * 128:(g + 1) * 128], VL[:, vi, g * 128:(g + 1) * 128], WSM[:, g, :], True, True,
                   [BVL[vi], BWSM], BPS[b2], inc=(g == 3))
            for g in range(4):
                stt(TB[:, g * 128:(g + 1) * 128], PS[b2][:, g * 128:(g + 1) * 128], PP[:, pb + 304 + g:pb + 305 + g],
                    CB[:, g * 128:(g + 1) * 128], ALU.mult, ALU.add, [BPS[b2], BPP, BCB], [BTB])
            tbv = TB[:, :].rearrange("p (g t) -> p g t", g=4)
            tt_(UB[:, :, n * 128:(n + 1) * 128], tbv, UU[:, :, n * 128:(n + 1) * 128], ALU.mult, [BTB, BUU], [BUB])
        for t_ in tvs:
            ws.done(t_)
        dump(f"ub{l}", UB[:, :, :], [128, 4, S], BF16, [BUB])
        if stop_after == "B":
            return True

        A.reset(keep_b)
        MG, BMG = A.alloc("MG", [128, 8, 1024], BF16)
        SGM, BSGM = A.alloc("SGM", [128, 3, 512], F32)
        M0, BM0 = A.alloc("M0", [128, 512], F32)
        M1, BM1 = A.alloc("M1", [128, 512], F32)
        for hf in range(2):
            banks.set(range(8))
            for dc in range(8):
                wa = ws.get(("w_a", l, dc, 0, 4))
                wb = ws.get(("w_b", l, dc, 0, 4))
                wc = ws.get(("w_c", l, dc, 0, 2))
                wg = [ws.get(("w_in", l, 34 + i * 8 + dc, 0, 8)) for i in range(3)]
                for t2 in range(2):
                    tt = hf * 2 + t2
                    pa, pb_, pc_ = banks.next(), banks.next(), banks.next()
                    pg = [banks.next() for _ in range(3)]
                    for kc in range(4):
                        mm(PS[pa][:, :], wa.ap[:, kc, :], AACT[:, kc, ts(tt)], kc == 0, kc == 3, [wa.buf, BAACT], BPS[pa])
                    for kc in range(4):
                        mm(PS[pb_][:, :], wb.ap[:, kc, :], UB[:, kc, ts(tt)], kc == 0, kc == 3, [wb.buf, BUB], BPS[pb_])
                    for kc in range(2):
                        mm(PS[pc_][:, :], wc.ap[:, kc, :], OC[:, kc, ts(tt)], kc == 0, kc == 1, [wc.buf, BOC], BPS[pc_])
                    for i in range(3):
                        for kc in range(8):
                            mm(PS[pg[i]][:, :], wg[i].ap[:, kc, :], H[:, kc, ts(tt)], kc == 0, kc == 7,
                               [wg[i].buf, BH[tt]], BPS[pg[i]])
                    for i in range(3):
                        col = pb + 56 + i * 8 + dc
                        act(SGM[:, i, :], PS[pg[i]][:, :], AF.Sigmoid, [BPS[pg[i]], BPP], [BSGM], bias=PP[:, col:col + 1])
                    tt_(M0[:, :], PS[pa][:, :], SGM[:, 0, :], ALU.mult, [BPS[pa], BSGM], [BM0])
                    tt_(M1[:, :], PS[pb_][:, :], SGM[:, 1, :], ALU.mult, [BPS[pb_], BSGM], [BM1])
                    tt_(M0[:, :], M0[:, :], M1[:, :], ALU.add, [BM0, BM1], [BM0])
                    tt_(M1[:, :], PS[pc_][:, :], SGM[:, 2, :], ALU.mult, [BPS[pc_], BSGM], [BM1])
                    tt_(MG[:, dc, ts(t2)], M0[:, :], M1[:, :], ALU.add, [BM0, BM1], [BMG])
                for t_ in [wa, wb, wc] + wg:
                    ws.done(t_)
            if dbg is not None:
                dump(f"merged{l}_{hf}", MG[:, :, :], [128, 8, 1024], BF16, [BMG])
            banks.set(range(6))
            wm = [ws.get(("w_mix", l, dc, 0, 8)) for dc in range(8)]
            for t2 in range(2):
                tt = hf * 2 + t2
                sb_ = 6 + t2
                for dc in range(8):
                    b = banks.next()
                    for kc in range(8):
                        mm(PS[b][:, :], wm[dc].ap[:, kc, :], MG[:, kc, ts(t2)], kc == 0, kc == 7, [wm[dc].buf, BMG], BPS[b])
                    flush_stats()
                    y_chunk(PS[b][:, :], BPS[b], dc, hf, sb_, dc == 0, dc == 7)
                post_update(pb + 8, tt, hf, sb_)
            for t_ in wm:
                ws.done(t_)
        return False

    def sublayer_xattn(l, s):
        pb = l * PPL
        banks.set(range(8))
        A.reset()
        MEMS, BMEMS = A.alloc("MEMS", [128, 8, 256], F32)
        KT, BKT = A.alloc("KT", [128, 8, 256], BF16)
        VX, BVX = A.alloc("VX", [128, 2, 1024], BF16)
        QX, BQX = A.alloc("QX", [128, 8, S], BF16)
        PTX, BPTX = A.alloc("PTX", [128, 2, 2, 512], BF16, nbufs=2)
        RD, BRD = A.alloc("RD", [128, 2, 512], F32, nbufs=2)
        memn_o = A.off
        MEMN, BMEMN = A.alloc("MEMN", [128, 8, 256], BF16)
        A.reset(0)
        OX0, BOX0 = A.alloc("OX0", [128, 8, 512], BF16)
        A.reset(memn_o)
        OX1, BOX1 = A.alloc("OX1", [128, 8, 512], BF16)
        OXs, BOXs = [OX0, OX1], [BOX0, BOX1]
        if not dry:
            k.dma("sp", [(MEMS[:, kc, :], T["memT"][s, kc, :, :]) for kc in range(8)], writes=[BMEMS])
        b = banks.next()
        for kc in range(8):
            si = kc % 2
            act(SQ[:, si, 0:256], MEMS[:, kc, :], AF.Square, [BMEMS], [BSQ[si]])
            mm(PS[b][:, 0:256], ONESB[:, :], SQ[:, si, 0:256], kc == 0, kc == 7, [BONES, BSQ[si]], BPS[b], inc=True)
        ri = nrm_ctr[0] % 2
        nrm_ctr[0] += 1
        act(RSTD[:, ri, 0:256], PS[b][:, 0:256], AF.Sqrt, [BPS[b], BEPS], [BRSTD[ri]], bias=EPS[:, 0:1], scale=1.0 / D)
        recip(RSTD[:, ri, 0:256], RSTD[:, ri, 0:256], [BRSTD[ri]], [BRSTD[ri]])
        for kc in range(8):
            stt(MEMN[:, kc, :], MEMS[:, kc, :], PP[:, pb + 32 + kc:pb + 33 + kc], RSTD[:, ri, 0:256], ALU.mult, ALU.mult,
                [BMEMS, BPP, BRSTD[ri]], [BMEMN])
        for ec in range(8):
            w = ws.get(("w_xkv", l, ec, 0, 8))
            b = banks.next()
            for kc in range(8):
                mm(PS[b][:, 0:256], w.ap[:, kc, :], MEMN[:, kc, :], kc == 0, kc == 7, [w.buf, BMEMN], BPS[b])
            act(KT[:, ec, :], PS[b][:, 0:256], AF.Copy, [BPS[b]], [BKT])
            ws.done(w)
        for ec in range(8):
            w = ws.get(("w_xkv", l, 8 + ec, 0, 8))
            b = banks.next()
            for mb in range(2):
                for kc in range(8):
                    mm(PS[b][:, mb * 128:(mb + 1) * 128], MEMN[:, kc, mb * 128:(mb + 1) * 128], w.ap[:, kc, :], kc == 0,
                       kc == 7, [w.buf, BMEMN], BPS[b], inc=(kc == 7 and mb == 1))
            psv = PS[b][:, 0:256].rearrange("p (m c) -> p m c", m=2)
            act(VX[:, :, ec * 128:(ec + 1) * 128], psv, AF.Copy, [BPS[b]], [BVX])
            ws.done(w)
        norm_to_H(pb + 16, range(4))
        for dc in range(8):
            w = ws.get(("w_xq", l, dc, 0, 8))

            def ev(tt, ps, psb, dc=dc):
                act(QX[:, dc, ts(tt)], ps[:, :], AF.Copy, [psb], [BQX])
            proj_fm(w, 8, h_rhs, h_bufs, ev)
            ws.done(w)
        wo = [ws.get(("w_xo", l, dc, 0, 8)) for dc in range(8)]
        banks.set(range(6))

        def attn_tile(tt):
            OX, BOX = OXs[tt % 2], BOXs[tt % 2]

            def xs1(hd):
                pi = hd % 2
                bs_ = [banks.next(), banks.next()]
                for mb in range(2):
                    for ec in range(2):
                        mm(PS[bs_[mb]][:, :], KT[:, hd * 2 + ec, mb * 128:(mb + 1) * 128], QX[:, hd * 2 + ec, ts(tt)],
                           ec == 0, ec == 1, [BKT, BQX], BPS[bs_[mb]])
                    act(PTX[:, pi, mb, :], PS[bs_[mb]][:, :], AF.Exp, [BPS[bs_[mb]]], [BPTX[pi]], scale=1.0 / 16)

            def xs2(hd):
                pi = hd % 2
                bd = banks.next()
                for mb in range(2):
                    mm(PS[bd][:, :], ONESB[:, :], PTX[:, pi, mb, :], mb == 0, mb == 1, [BONES, BPTX[pi]], BPS[bd])
                recip(RD[:, pi, :], PS[bd][:, :], [BPS[bd]], [BRD[pi]])
                for ec in range(2):
                    bo = banks.next()
                    for mb in range(2):
                        mm(PS[bo][:, :], VX[:, mb, (hd * 2 + ec) * 128:(hd * 2 + ec + 1) * 128], PTX[:, pi, mb, :], mb == 0,
                           mb == 1, [BVX, BPTX[pi]], BPS[bo])
                    tt_(OX[:, hd * 2 + ec, :], PS[bo][:, :], RD[:, pi, :], ALU.mult, [BPS[bo], BRD[pi]], [BOX])

            for hd in range(4):
                xs1(hd)
                if hd > 0:
                    xs2(hd - 1)
            xs2(3)

        def xo_tile(tt):
            OX, BOX = OXs[tt % 2], BOXs[tt % 2]
            ysel = tt % 2
            sb_ = 6 + (tt % 2)
            for dc in range(8):
                b = banks.next()
                for kc in range(8):
                    mm(PS[b][:, :], wo[dc].ap[:, kc, :], OX[:, kc, :], kc == 0, kc == 7, [wo[dc].buf, BOX], BPS[b])
                flush_stats()
                y_chunk(PS[b][:, :], BPS[b], dc, ysel, sb_, dc == 0, dc == 7)
            post_update(pb + 24, tt, ysel, sb_)

        if XATTN_PIPE:
            for tt in range(4):
                attn_tile(tt)
                if tt > 0:
                    xo_tile(tt - 1)
            xo_tile(3)
        else:
            for tt in range(4):
                attn_tile(tt)
                xo_tile(tt)
        for t_ in wo:
            ws.done(t_)

    def sublayer_ffn(l, tile_done=None):
        pb = l * PPL
        A.reset()
        ACTB, BACTB = A.alloc("ACTB", [128, 22, 1024], BF16)
        GT, BGT = A.alloc("GT", [128, 2, 516], F32, nbufs=2)
        CV, BCV = A.alloc("CV", [128, 512], F32)
        GL, BGL = A.alloc("GL", [128, 2, 512], F32, nbufs=2)
        memset(HALO[:, :, :], 0.0, [BHALO])
        fw = pb + 216
        gi = 0
        for hf in range(2):
            banks.set(range(8))
            norm_to_H(pb + 40, (2 * hf, 2 * hf + 1))
            for j in range(22):
                wg = ws.get(("w_up", l, j, 0, 8))
                wv = ws.get(("w_up", l, 22 + j, 0, 8))
                for t2 in range(2):
                    tt = hf * 2 + t2
                    bg, bv = banks.next(), banks.next()
                    for kc in range(8):
                        mm(PS[bg][:, :], wg.ap[:, kc, :], H[:, kc, ts(tt)], kc == 0, kc == 7, [wg.buf, BH[tt]], BPS[bg])
                    for kc in range(8):
                        mm(PS[bv][:, :], wv.ap[:, kc, :], H[:, kc, ts(tt)], kc == 0, kc == 7, [wv.buf, BH[tt]], BPS[bv])
                    g2 = gi % 2
                    gi += 1
                    act(GT[:, g2, 2:514], PS[bg][:, :], AF.Copy, [BPS[bg]], [BGT[g2]])
                    cpy(GT[:, g2, 0:2], HALO[:, j, :], [BHALO], [BGT[g2]])
                    cpy(HALO[:, j, :], GT[:, g2, 512:514], [BGT[g2]], [BHALO])
                    act(CV[:, :], PS[bg][:, :], AF.Identity, [BPS[bg], BPP], [BCV], bias=PP[:, pb + 282 + j:pb + 283 + j],
                        scale=PP[:, fw + 44 + j:fw + 45 + j])
                    stt(CV[:, :], GT[:, g2, 0:512], PP[:, fw + j:fw + j + 1], CV[:, :], ALU.mult, ALU.add,
                        [BGT[g2], BPP, BCV], [BCV])
                    stt(CV[:, :], GT[:, g2, 1:513], PP[:, fw + 22 + j:fw + 23 + j], CV[:, :], ALU.mult, ALU.add,
                        [BGT[g2], BPP, BCV], [BCV])
                    act(GL[:, g2, :], CV[:, :], AF.Gelu_apprx_tanh, [BCV], [BGL[g2]])
                    tt_(ACTB[:, j, ts(t2)], PS[bv][:, :], GL[:, g2, :], ALU.mult, [BPS[bv], BGL[g2]], [BACTB])
                ws.done(wg)
                ws.done(wv)
            banks.set(range(6))
            ysels = (1, 0) if hf == 0 else (0, 1)
            for dc in range(8):
                wd = [ws.get(("w_down", l, dc, 0, 8)), ws.get(("w_down", l, dc, 8, 8)), ws.get(("w_down", l, dc, 16, 6))]
                for t2 in range(2):
                    b = banks.next()
                    for kc in range(22):
                        w = wd[kc // 8]
                        mm(PS[b][:, :], w.ap[:, kc % 8, :], ACTB[:, kc, ts(t2)], kc == 0, kc == 21, [w.buf, BACTB], BPS[b])
                    flush_stats()
                    y_chunk(PS[b][:, :], BPS[b], dc, ysels[t2], 6 + t2, dc == 0, dc == 7)
                for w in wd:
                    ws.done(w)
            for t2 in range(2):
                tt = hf * 2 + t2
                post_update(pb + 48, tt, ysels[t2], 6 + t2)
                if tile_done is not None:
                    tile_done(tt)

    stopped = False

    def load_x(s_, tt):
        if not dry:
            k.dma("sp", [(XRES[:, kc, ts(tt)], T["xT"][s_, kc, :, ts(tt)]) for kc in range(8)], writes=[BX[tt]])

    def store_x(s_, tt):
        if not dry:
            k.dma("sp", [(T["outT"][s_, kc, :, ts(tt)], XRES[:, kc, ts(tt)]) for kc in range(8)], reads=[BX[tt]],
                  sembuf=BX[tt])

    try:
        for tt in range(4):
            load_x(0, tt)
        for s in range(nseq):
            def tile_done(tt, s=s):
                store_x(s, tt)
                if s + 1 < nseq:
                    load_x(s + 1, tt)
            for l in range(nlayer):
                stopped = sublayer_mixer(l)
                if stopped:
                    break
                dump(f"x1_{l}", XRES[:, :, :], [128, 8, S], F32, BX)
                if stop_after == "mixer":
                    stopped = True
                    break
                sublayer_xattn(l, s)
                dump(f"x2_{l}", XRES[:, :, :], [128, 8, S], F32, BX)
                if stop_after == "xattn":
                    stopped = True
                    break
                sublayer_ffn(l, tile_done if l == nlayer - 1 else None)
            if stopped:
                for tt in range(4):
                    store_x(s, tt)
                break
    except _Stop:
        for tt in range(4):
            store_x(0, tt)
    k.wait_all("sp", BX + dbg_out)
    return ws


def make_program(nseq=SEQ_PER_CORE, nlayer=DEPTH, stop_after=None, dbg=None):
    kd = K(None, dry=True)
    wsd = build_program(None, kd, None, nseq, nlayer, stop_after, dbg)
    seq = wsd.rec
    nc = bass.Bass("TRN2", target_bir_lowering=False)
    k = K(nc)
    ws = build_program(nc, k, seq, nseq, nlayer, stop_after, dbg)
    assert ws.idx == len(seq) and ws.loaded == len(seq)
    return nc, k


def _t5_bucket(n):
    n = np.maximum(n, 0)
    nf = np.maximum(n, 1).astype(np.float32)
    large = 16 + (np.log(nf / np.float32(16)) / np.float32(np.log(2048 / 16)) * np.float32(16)).astype(np.int32)
    large = np.minimum(large, 31)
    return np.where(n < 16, n, large)


def _wtiles(W):
    L, Kd, N = W.shape
    return np.ascontiguousarray(W.reshape(L, Kd // 128, 128, N // 128, 128).transpose(0, 3, 2, 1, 4))


def _cols(v):
    v = np.asarray(v)
    C = v.shape[-1]
    lead = v.shape[:-1]
    a = v.reshape(*lead, C // 128, 128)
    a = np.moveaxis(a, -1, 0)
    return a.reshape(128, -1)


def prep_shared(inp):
    L = DEPTH
    pp = np.zeros((128, L * PPL), np.float32)
    for l in range(L):
        o = l * PPL
        pp[:, o + 0:o + 8] = _cols(inp["mix_pre_g"][l])
        pp[:, o + 8:o + 16] = _cols(inp["mix_post_g"][l])
        pp[:, o + 16:o + 24] = _cols(inp["x_pre_g"][l])
        pp[:, o + 24:o + 32] = _cols(inp["x_post_g"][l])
        pp[:, o + 32:o + 40] = _cols(inp["mem_g"][l])
        pp[:, o + 40:o + 48] = _cols(inp["ffn_pre_g"][l])
        pp[:, o + 48:o + 56] = _cols(inp["ffn_post_g"][l])
        pp[:, o + 56:o + 80] = _cols(inp["b_gate"][l])
        pp[:, o + 80:o + 204] = _cols(inp["conv_a_w"][l])
        pp[:, o + 204:o + 208] = _cols(inp["conv_a_b"][l])
        pp[:, o + 208:o + 212] = _cols(inp["ln_a_g"][l])
        pp[:, o + 212:o + 216] = _cols(inp["ln_a_b"][l])
        pp[:, o + 216:o + 282] = _cols(inp["conv_f_w"][l])
        pp[:, o + 282:o + 304] = _cols(inp["conv_f_b"][l])
        pp[:, o + 304:o + 308] = _cols(inp["ln_b_g"][l])
        pp[:, o + 308:o + 312] = _cols(inp["ln_b_b"][l])
    rb = np.asarray(inp["rel_bias"])
    kk = np.arange(128)[:, None]
    qq = np.arange(128)[None, :]
    abias = np.zeros((128, 3, 4, 2, 128), np.float32)
    amask = np.zeros((128, 3, 4, 2, 128), np.float32)
    for g, dil in enumerate(DILS):
        for pc in range(2):
            rel = qq + 128 - kk if pc == 0 else qq - kk
            valid = (rel >= 0) & (rel <= 128)
            bucket = _t5_bucket(rel * dil)
            for h in range(4):
                abias[:, g, h, pc, :] = rb[bucket, g * 4 + h]
                amask[:, g, h, pc, :] = valid
    pbc = np.zeros((L, 128, 1536), np.float32)
    for l in range(L):
        pbc[l, :, 0:512] = np.broadcast_to(inp["ln_b_g"][l][None, :], (128, 512))
        pbc[l, :, 512:1024] = np.broadcast_to(inp["ln_b_b"][l][None, :], (128, 512))
        pbc[l, :, 1024:1536] = np.broadcast_to(np.asarray(inp["b_s"][l]).reshape(1, 512), (128, 512))
    wsT = np.ascontiguousarray(np.asarray(inp["w_s"]).transpose(0, 3, 1, 2))
    tril = (np.arange(128)[None, :] >= np.arange(128)[:, None]).astype(np.float32)
    trilm = np.ascontiguousarray(np.broadcast_to(tril[:, None, :], (128, 4, 128)))
    w_c = np.asarray(inp["w_c_out"])
    shared = {
        "pp": pp,
        "abias": abias.reshape(128, 3072),
        "amask": amask.reshape(128, 3072),
        "pbc": pbc,
        "wsT": wsT,
        "trilm": trilm,
        "ident": np.eye(128, dtype=np.float32),
        "w_in": _wtiles(np.asarray(inp["w_in"])),
        "w_a": _wtiles(np.asarray(inp["w_a_out"])),
        "w_b": _wtiles(np.asarray(inp["w_b_out"])),
        "w_c": _wtiles(w_c),
        "w_mix": _wtiles(np.asarray(inp["w_mix_out"])),
        "w_xq": _wtiles(np.asarray(inp["w_xq"])),
        "w_xkv": _wtiles(np.asarray(inp["w_xkv"])),
        "w_xo": _wtiles(np.asarray(inp["w_xo"])),
        "w_up": _wtiles(np.asarray(inp["w_up"])),
        "w_down": _wtiles(np.asarray(inp["w_down"])),
    }
    return shared


def prep_core(inp, c, nseq=SEQ_PER_CORE):
    xs = np.asarray(inp["x"][c * SEQ_PER_CORE:c * SEQ_PER_CORE + nseq])
    ms = np.asarray(inp["mem"][c * SEQ_PER_CORE:c * SEQ_PER_CORE + nseq])
    xT = np.zeros((SEQ_PER_CORE, 8, 128, S), np.float32)
    mT = np.zeros((SEQ_PER_CORE, 8, 128, 256), np.float32)
    xT[:nseq] = xs.transpose(0, 2, 1).reshape(nseq, 8, 128, S)
    mT[:nseq] = ms.transpose(0, 2, 1).reshape(nseq, 8, 128, 256)
    return {"xT": xT, "memT": mT}


_PROG = {}


def kernel(**inputs):
    if "p" not in _PROG:
        _PROG["p"] = make_program()
    nc, _ = _PROG["p"]
    shared = prep_shared(inputs)
    in_maps = []
    for c in range(N_CORES):
        m = dict(shared)
        m.update(prep_core(inputs, c))
        in_maps.append(m)
    res = run_bass_kernel_spmd(nc, in_maps, core_ids=list(range(N_CORES)))
    out = np.empty((N_CORES * SEQ_PER_CORE, S, D), np.float32)
    for c in range(N_CORES):
        oT = res.results[c]["outT"]
        out[c * SEQ_PER_CORE:(c + 1) * SEQ_PER_CORE] = oT.reshape(SEQ_PER_CORE, D, S).transpose(0, 2, 1)
    return out
```

```python
import numpy as np
import concourse.bass as bass
import concourse.mybir as mybir
from concourse.bass_utils import run_bass_kernel_spmd

F32 = mybir.dt.float32
BF16 = mybir.dt.bfloat16
AF = mybir.ActivationFunctionType
ALU = mybir.AluOpType
AX = mybir.AxisListType

EPOCH = 12000
SAME_SYNC = True

N_CORES = 8
SEQ_PER_CORE = 4
DEPTH = 2
D = 1024
S = 2048
NSLOT = 10
XATTN_PIPE = False
XATTN_DEFER_POST = True
DEFER_STATS = True
PPL = 312
DILS = (1, 4, 16)


class Buf:
    __slots__ = ("name", "w", "r", "al", "dsem", "dcnt")

    def __init__(self, name):
        self.name = name
        self.w = None
        self.r = {}
        self.al = []
        self.dsem = None
        self.dcnt = 0


class Dummy:
    shape = ()

    def __getitem__(self, i):
        return self

    def rearrange(self, *a, **k):
        return self


DUMMY = Dummy()


class Eng:
    def __init__(self, K, name, obj):
        self.name = name
        self.obj = obj
        self.sem = None if K.dry else K.nc.alloc_semaphore(f"s_{name}_0")
        self.nsem = 1
        self.cnt = 0
        self.own = set() if K.dry else {id(self.sem)}
        self.waited = {}
        self.nwaits = 0
        self.nins = 0


class K:
    def __init__(self, nc, dry=False):
        self.nc = nc
        self.dry = dry
        if dry:
            self.engs = {n: Eng(self, n, None) for n in ("pe", "act", "dve", "pool", "sp")}
        else:
            self.engs = {
                "pe": Eng(self, "pe", nc.tensor),
                "act": Eng(self, "act", nc.scalar),
                "dve": Eng(self, "dve", nc.vector),
                "pool": Eng(self, "pool", nc.gpsimd),
                "sp": Eng(self, "sp", nc.sync),
            }
        self.semobj = {}
        if not dry:
            for e in self.engs.values():
                self.semobj[id(e.sem)] = e.sem
        self.ndsem = 0

    def _deps(self, reads, writes):
        deps = {}

        def add(d):
            if d is None:
                return
            s, v = d
            if deps.get(s, 0) < v:
                deps[s] = v

        for b in reads:
            add(b.w)
            for a in b.al:
                add(a.w)
        for b in writes:
            add(b.w)
            for d in b.r.values():
                add(d)
            for a in b.al:
                add(a.w)
                for d in a.r.values():
                    add(d)
        return deps

    def _wait(self, E, deps):
        for s, v in deps.items():
            if s in E.own:
                if E.name in ("pe", "sp") or not SAME_SYNC:
                    continue
                if s == id(E.sem) and v > E.cnt:
                    continue
                if E.name in ("act", "dve") and (s != id(E.sem) or v < E.cnt):
                    continue
            if E.waited.get(s, 0) >= v:
                continue
            E.obj.wait_ge(self.semobj[s], v)
            E.waited[s] = v
            E.nwaits += 1

    def _tag(self, E, ins, inc):
        E.nins += 1
        if inc:
            ins.then_inc(E.sem, 1)
            E.cnt += 1
            tag = (id(E.sem), E.cnt)
            if E.cnt >= EPOCH:
                E.sem = self.nc.alloc_semaphore(f"s_{E.name}_{E.nsem}")
                E.nsem += 1
                E.cnt = 0
                E.own.add(id(E.sem))
                self.semobj[id(E.sem)] = E.sem
        else:
            tag = (id(E.sem), E.cnt + 1)
        return tag

    def op(self, eng, fn, reads=(), writes=(), inc=True):
        if self.dry:
            return None
        E = self.engs[eng]
        self._wait(E, self._deps(reads, writes))
        ins = fn(E.obj)
        tag = self._tag(E, ins, inc)
        for b in writes:
            b.w = tag
            b.r = {}
        for b in reads:
            b.r[eng] = tag
        return ins

    def dma(self, eng, pairs, reads=(), writes=(), sembuf=None):
        if self.dry:
            return None
        E = self.engs[eng]
        sb = sembuf or (writes[0] if writes else reads[0])
        if sb.dsem is None:
            sb.dsem = self.nc.alloc_semaphore(f"d_{self.ndsem}")
            self.ndsem += 1
            self.semobj[id(sb.dsem)] = sb.dsem
        deps = self._deps(reads, writes)
        if sb.dcnt:
            s = id(sb.dsem)
            if deps.get(s, 0) < sb.dcnt:
                deps[s] = sb.dcnt
        self._wait(E, deps)
        for (o, i) in pairs:
            E.obj.dma_start(out=o, in_=i).then_inc(sb.dsem, 16)
            E.nins += 1
            sb.dcnt += 16
        tag = (id(sb.dsem), sb.dcnt)
        for b in writes:
            b.w = tag
            b.r = {}
        for b in reads:
            b.r["dma_" + sb.name] = tag
        return tag

    def wait_all(self, eng, bufs):
        if self.dry:
            return
        E = self.engs[eng]
        self._wait(E, self._deps(bufs, bufs))

    def stats(self):
        return {e.name: (e.nins, e.nwaits, e.nsem) for e in self.engs.values()}


class _Stop(Exception):
    pass


class Tile:
    __slots__ = ("i", "ap", "buf")

    def __init__(self, i, ap, buf):
        self.i = i
        self.ap = ap
        self.buf = buf


class WStream:
    def __init__(self, k, ring, rbufs, seq, ap_of):
        self.k = k
        self.ring = ring
        self.rbufs = rbufs
        self.seq = seq
        self.ap_of = ap_of
        self.rec = []
        self.idx = 0
        self.loaded = 0
        self.done_ = set()

    def _issue(self):
        while self.loaded < len(self.seq):
            i = self.loaded
            if i - NSLOT >= 0 and (i - NSLOT) not in self.done_:
                break
            slot = i % NSLOT
            src, kcn = self.ap_of(self.seq[i])
            self.k.dma("pool", [(self.ring[:, slot, 0:kcn, :], src)], writes=[self.rbufs[slot]])
            self.loaded += 1

    def get(self, desc):
        i = self.idx
        self.idx += 1
        if self.k.dry:
            self.rec.append(desc)
            return Tile(i, DUMMY, None)
        assert self.seq[i] == desc, (i, self.seq[i], desc)
        self._issue()
        assert self.loaded > i, f"weight ring too small at tile {i} {desc}"
        slot = i % NSLOT
        return Tile(i, self.ring[:, slot], self.rbufs[slot])

    def done(self, t):
        if self.k.dry:
            return
        self.done_.add(t.i)
        self._issue()


class Arena:
    def __init__(self, k, nc, base, size):
        self.k, self.nc, self.base, self.size = k, nc, base, size
        self.off = 0
        self.regs = []
        self.n = 0

    def reset(self, off=0):
        self.off = off

    def alloc(self, name, shape, dtype, nbufs=1):
        esz = 2 if dtype == BF16 else 4
        per = esz
        for s_ in shape[1:]:
            per *= s_
        per = (per + 63) // 64 * 64
        lo = self.off
        hi = lo + per
        assert hi <= self.size, f"arena overflow {name}: {hi} > {self.size}"
        self.off = hi
        bufs = [Buf(f"{name}{i}") for i in range(nbufs)]
        pbuf = per // nbufs
        new = []
        for bi, b in enumerate(bufs):
            blo, bhi = lo + bi * pbuf, (lo + (bi + 1) * pbuf if bi < nbufs - 1 else hi)
            for (l2, h2, b2) in self.regs:
                if l2 < bhi and blo < h2:
                    if b2 not in b.al:
                        b.al.append(b2)
                    if b not in b2.al:
                        b2.al.append(b)
            new.append((blo, bhi, b))
        self.regs.extend(new)
        if self.k.dry:
            t = DUMMY
        else:
            self.n += 1
            t = self.nc.alloc_sbuf_tensor_at(f"ar{self.n}_{name}", list(shape), dtype, offset=self.base + lo)
        return (t, bufs[0]) if nbufs == 1 else (t, bufs)


def ts(tt, n=512):
    return slice(tt * n, (tt + 1) * n)


def build_program(nc, k, wseq, nseq=SEQ_PER_CORE, nlayer=DEPTH, stop_after=None, dbg=None):
    dry = k.dry
    T = {}
    if not dry:
        def dram(name, shape, kind="ExternalInput"):
            T[name] = nc.dram_tensor(name, list(shape), F32, kind=kind).ap()
        dram("xT", [SEQ_PER_CORE, 8, 128, S])
        dram("memT", [SEQ_PER_CORE, 8, 128, 256])
        dram("pp", [128, DEPTH * PPL])
        dram("abias", [128, 3072])
        dram("amask", [128, 3072])
        dram("pbc", [DEPTH, 128, 1536])
        dram("wsT", [DEPTH, 128, 4, 128])
        dram("trilm", [128, 4, 128])
        dram("ident", [128, 128])
        dram("w_in", [DEPTH, 58, 128, 8, 128])
        dram("w_a", [DEPTH, 8, 128, 4, 128])
        dram("w_b", [DEPTH, 8, 128, 4, 128])
        dram("w_c", [DEPTH, 8, 128, 2, 128])
        dram("w_mix", [DEPTH, 8, 128, 8, 128])
        dram("w_xq", [DEPTH, 8, 128, 8, 128])
        dram("w_xkv", [DEPTH, 16, 128, 8, 128])
        dram("w_xo", [DEPTH, 8, 128, 8, 128])
        dram("w_up", [DEPTH, 44, 128, 8, 128])
        dram("w_down", [DEPTH, 8, 128, 22, 128])
        dram("outT", [SEQ_PER_CORE, 8, 128, S], kind="ExternalOutput")

    def ap_of(desc):
        name, l, j, k0, kcn = desc
        return T[name][l, j, :, k0:k0 + kcn, :], kcn

    BASE = 16512
    XRES_O = BASE
    H_O = XRES_O + 65536
    RING_O = H_O + 32768
    C_O = RING_O + NSLOT * 2048
    coff = [C_O]

    def calloc(name, shape, dtype):
        esz = 2 if dtype == BF16 else 4
        per = esz
        for s_ in shape[1:]:
            per *= s_
        per = (per + 63) // 64 * 64
        o = coff[0]
        coff[0] += per
        if dry:
            return DUMMY
        return nc.alloc_sbuf_tensor_at(name, list(shape), dtype, offset=o)

    if dry:
        XRES = H = Y = RING = DUMMY
    else:
        XRES = nc.alloc_sbuf_tensor_at("XRES", [128, 8, S], F32, offset=XRES_O)
        H = nc.alloc_sbuf_tensor_at("H", [128, 8, S], BF16, offset=H_O)
        Y = nc.alloc_sbuf_tensor_at("Y", [128, 8, 1024], F32, offset=H_O)
        RING = nc.alloc_sbuf_tensor_at("RING", [128, NSLOT, 8, 128], BF16, offset=RING_O)
    PP = calloc("PP", [128, DEPTH * PPL], F32)
    EXPBM = calloc("EXPBM", [128, 3, 4, 2, 128], BF16)
    PBC = calloc("PBC", [128, 1536], F32)
    WSRAW = calloc("WSRAW", [128, 4, 128], BF16)
    TRILM = calloc("TRILM", [128, 4, 128], BF16)
    WSM = calloc("WSM", [128, 4, 128], BF16)
    ONESB = calloc("ONESB", [128, 128], BF16)
    IDENT = calloc("IDENT", [128, 128], BF16)
    ONESF = calloc("ONESF", [128, 128], F32)
    RSTD = calloc("RSTD", [128, 2, 512], F32)
    SQ = calloc("SQ", [128, 2, 512], BF16)
    HALO = calloc("HALO", [128, 22, 2], F32)
    EPS = calloc("EPS", [128, 2], F32)
    SMALL = calloc("SMALL", [128, 2, 8], F32)
    ST6 = calloc("ST6", [128, 2, 6], F32)
    ARENA_O = (coff[0] + 63) // 64 * 64
    ARENA_SZ = 229344 - ARENA_O
    A = Arena(k, nc, ARENA_O, ARENA_SZ)

    BX = [Buf(f"X{t}") for t in range(4)]
    BH = [Buf(f"H{t}") for t in range(4)]
    BY = [Buf("Y0"), Buf("Y1")]
    for yi, hts in ((0, (0, 1)), (1, (2, 3))):
        for t in hts:
            BY[yi].al.append(BH[t])
            BH[t].al.append(BY[yi])
    RB = [Buf(f"ring{i}") for i in range(NSLOT)]
    BPP, BEXPBM, BPBC, BWSRAW, BTRILM, BWSM, BONES, BHALO, BEPS = [Buf(n) for n in (
        "PP", "EXPBM", "PBC", "WSRAW", "TRILM", "WSM", "ONES", "HALO", "EPS")]
    BRSTD = [Buf("RSTD0"), Buf("RSTD1")]
    BIDENT = Buf("IDENT")
    BSQ = [Buf("SQ0"), Buf("SQ1")]
    BSMALL = [Buf("SM0"), Buf("SM1")]
    if dry:
        PS = [DUMMY] * 8
    else:
        PS = [nc.alloc_psum_tensor(f"ps{i}", [128, 512], F32) for i in range(8)]
    BPS = [Buf(f"ps{i}") for i in range(8)]

    ws = WStream(k, RING, RB, wseq, ap_of)

    class Banks:
        def __init__(self):
            self.rot = list(range(8))
            self.i = 0

        def set(self, lst):
            self.rot = list(lst)
            self.i = 0

        def next(self):
            b = self.rot[self.i % len(self.rot)]
            self.i += 1
            return b

    banks = Banks()

    def mm(out, lhsT, rhs, first, last, reads, psb, inc=None):
        k.op("pe", lambda e: e.matmul(out, lhsT=lhsT, rhs=rhs, start=first, stop=last), reads=reads, writes=[psb],
             inc=last if inc is None else inc)

    def act(out, in_, func, reads, writes, bias=None, scale=None, accum=None):
        kw = {}
        if bias is not None:
            kw["bias"] = bias
        if scale is not None:
            kw["scale"] = scale
        if accum is not None:
            kw["accum_out"] = accum
        k.op("act", lambda e: e.activation(out=out, in_=in_, func=func, **kw), reads=reads, writes=writes)

    def tt_(out, in0, in1, op, reads, writes, eng="dve"):
        k.op(eng, lambda e: e.tensor_tensor(out=out, in0=in0, in1=in1, op=op), reads=reads, writes=writes)

    def tsc(out, in0, s1, s2, op0, op1, reads, writes, eng="dve"):
        if op1 is None:
            k.op(eng, lambda e: e.tensor_scalar(out=out, in0=in0, scalar1=s1, scalar2=None, op0=op0), reads=reads,
                 writes=writes)
        else:
            k.op(eng, lambda e: e.tensor_scalar(out=out, in0=in0, scalar1=s1, scalar2=s2, op0=op0, op1=op1),
                 reads=reads, writes=writes)

    def stt(out, in0, scalar, in1, op0, op1, reads, writes):
        k.op("dve", lambda e: e.scalar_tensor_tensor(out=out, in0=in0, scalar=scalar, in1=in1, op0=op0, op1=op1),
             reads=reads, writes=writes)

    def cpy(out, in_, reads, writes, eng="dve"):
        k.op(eng, lambda e: e.tensor_copy(out=out, in_=in_), reads=reads, writes=writes)

    def recip(out, in_, reads, writes):
        k.op("dve", lambda e: e.reciprocal(out=out, in_=in_), reads=reads, writes=writes)

    def memset(ap, val, writes, eng="dve"):
        k.op(eng, lambda e: e.memset(ap, val), writes=writes)

    dbg_out = []

    def chk(name):
        if stop_after == name:
            raise _Stop()

    def dump(name, ap, shape, dtype, bufs):
        if dry or dbg is None or name not in dbg:
            return
        t = nc.dram_tensor("dbg_" + name, list(shape), dtype, kind="ExternalOutput").ap()
        b = Buf("dbg_" + name)
        k.dma("sp", [(t, ap)], reads=bufs, writes=[b], sembuf=b)
        dbg_out.append(b)

    def proj_fm(tile, kcn, rhs_of, rbufs_of, evac, tiles=range(4)):
        for tt in tiles:
            b = banks.next()
            for kc in range(kcn):
                mm(PS[b][:, :], tile.ap[:, kc, :], rhs_of(kc, tt), kc == 0, kc == kcn - 1,
                   [tile.buf] + rbufs_of(tt), BPS[b])
            evac(tt, PS[b], BPS[b])

    def h_rhs(kc, tt):
        return H[:, kc, ts(tt)]

    def h_bufs(tt):
        return [BH[tt]]

    def rstd_from(psb_idx, ri, scale, epscol):
        act(RSTD[:, ri, :], PS[psb_idx][:, :], AF.Sqrt, [BPS[psb_idx], BEPS], [BRSTD[ri]], bias=EPS[:, epscol:epscol + 1],
            scale=scale)
        recip(RSTD[:, ri, :], RSTD[:, ri, :], [BRSTD[ri]], [BRSTD[ri]])

    nrm_ctr = [0]

    def norm_to_H(gcol, tiles):
        for tt in tiles:
            b = banks.next()
            for kc in range(8):
                si = kc % 2
                act(SQ[:, si, :], XRES[:, kc, ts(tt)], AF.Square, [BX[tt]], [BSQ[si]])
                mm(PS[b][:, :], ONESB[:, :], SQ[:, si, :], kc == 0, kc == 7, [BONES, BSQ[si]], BPS[b], inc=True)
            ri = nrm_ctr[0] % 2
            nrm_ctr[0] += 1
            rstd_from(b, ri, 1.0 / D, 0)
            for kc in range(8):
                stt(H[:, kc, ts(tt)], XRES[:, kc, ts(tt)], PP[:, gcol + kc:gcol + kc + 1], RSTD[:, ri, :], ALU.mult,
                    ALU.mult, [BX[tt], BPP, BRSTD[ri]], [BH[tt]])

    pend = []

    def y_chunk(ps_ap, psbuf, dc, ysel, sbank, first, last):
        ysl = slice(ysel * 512, ysel * 512 + 512)
        act(Y[:, dc, ysl], ps_ap, AF.Copy, [psbuf], [BY[ysel]])
        si = dc % 2
        act(SQ[:, si, :], ps_ap, AF.Square, [psbuf], [BSQ[si]])
        pend.append((sbank, si, first, last))
        if not DEFER_STATS:
            flush_stats()

    def flush_stats(keep=0):
        while len(pend) > keep:
            sbank, si, first, last = pend.pop(0)
            mm(PS[sbank][:, :], ONESB[:, :], SQ[:, si, :], first, last, [BONES, BSQ[si]], BPS[sbank], inc=True)

    def post_update(gcol, tt, ysel, sbank):
        flush_stats()
        ysl = slice(ysel * 512, ysel * 512 + 512)
        ri = nrm_ctr[0] % 2
        nrm_ctr[0] += 1
        rstd_from(sbank, ri, 1.0 / D, 0)
        for dc in range(8):
            tt_(Y[:, dc, ysl], Y[:, dc, ysl], RSTD[:, ri, :], ALU.mult, [BY[ysel], BRSTD[ri]], [BY[ysel]])
            stt(XRES[:, dc, ts(tt)], Y[:, dc, ysl], PP[:, gcol + dc:gcol + dc + 1], XRES[:, dc, ts(tt)], ALU.mult,
                ALU.add, [BY[ysel], BPP, BX[tt]], [BX[tt]])

    if not dry:
        k.dma("sp", [(PP[:, :], T["pp"][:, :])], writes=[BPP])
    memset(ONESB[:, :], 1.0, [BONES])
    memset(ONESF[:, :], 1.0, [BONES])
    memset(EPS[:, 0:1], 1e-6, [BEPS])
    memset(EPS[:, 1:2], 1e-5, [BEPS])
    if not dry:
        k.dma("pool", [(TRILM[:, :, :], T["trilm"][:, :, :])], writes=[BTRILM])
        k.dma("pool", [(IDENT[:, :], T["ident"][:, :])], writes=[BIDENT])
    A.reset()
    AB, BAB = A.alloc("abias", [128, 3072], F32)
    AM, BAM = A.alloc("amask", [128, 3072], F32)
    if not dry:
        k.dma("sp", [(AB[:, :], T["abias"][:, :])], writes=[BAB])
        k.dma("sp", [(AM[:, :], T["amask"][:, :])], writes=[BAM])
    if not dry:
        ebm_flat = EXPBM[:, :, :, :, :].rearrange("p g h c q -> p (g h c q)")
    else:
        ebm_flat = DUMMY
    act(AB[:, :], AB[:, :], AF.Exp, [BAB], [BAB])
    tt_(ebm_flat, AB[:, :], AM[:, :], ALU.mult, [BAB, BAM], [BEXPBM])

    def sublayer_mixer(l):
        pb = l * PPL
        banks.set(range(8))
        if not dry:
            k.dma("sp", [(PBC[:, :], T["pbc"][l, :, :])], writes=[BPBC])
            k.dma("pool", [(WSRAW[:, :, :], T["wsT"][l, :, :, :])], writes=[BWSRAW])
        tt_(WSM[:, :, :], WSRAW[:, :, :], TRILM[:, :, :], ALU.mult, [BWSRAW, BTRILM], [BWSM])
        norm_to_H(pb + 0, range(4))
        dump(f"h1_{l}", H[:, :, :], [128, 8, S], BF16, BH)
        if stop_after == "norm1":
            return True

        A.reset()
        OC, BOC = A.alloc("OC", [128, 2, S], BF16)
        keep_c = A.off
        Ut, BU = A.alloc("U", [128, 2, S], F32)
        qk_o = A.off
        Qb, BQ = A.alloc("Q", [128, 2, S], BF16, nbufs=2)
        Kb, BK = A.alloc("K", [128, 2, S], BF16, nbufs=2)
        end_qk = A.off
        A.reset(qk_o)
        R_, BR = A.alloc("R", [128, S], F32)
        A.reset(end_qk)
        VA, BVA = A.alloc("VA", [128, 2, 16, 2, 128], BF16, nbufs=2)
        NPT = 4
        PT, BPT = A.alloc("PT", [128, NPT, 2, 2, 128], BF16, nbufs=NPT)
        memset(VA[:, :, :, 0, 64:128], 1.0, BVA)
        memset(VA[:, :, :, 1, 0:64], 1.0, BVA)
        gctr = 0
        for hp in range(2):
            for g in range(3):
                d = DILS[g]
                nb = 16 // d
                pq = gctr % 2
                gctr += 1
                tq = ws.get(("w_in", l, 16 + g * 2 + hp, 0, 8))
                tk = ws.get(("w_in", l, 22 + g * 2 + hp, 0, 8))
                tv = ws.get(("w_in", l, 28 + g * 2 + hp, 0, 8))
                for (tile, dst, bdst) in ((tq, Qb, BQ[pq]), (tk, Kb, BK[pq])):
                    def ev(tt, ps, psb, dst=dst, bdst=bdst, pq=pq):
                        act(dst[:, pq, ts(tt)], ps[:, :], AF.Copy, [psb], [bdst])
                    proj_fm(tile, 8, h_rhs, h_bufs, ev)
                    ws.done(tile)
                    chk("Cq")
                chk("Cqk")

                def tokslice(j, d=d, nb=nb):
                    c, n = divmod(j, nb)
                    st = c + d * 128 * n
                    return slice(st, st + d * 127 + 1, d), n

                for j0 in range(0, 16, 4):
                    b = banks.next()
                    for jj in range(4):
                        tok, n = tokslice(j0 + jj)
                        for kc in range(8):
                            mm(PS[b][:, jj * 128:(jj + 1) * 128], H[:, kc, tok], tv.ap[:, kc, :], kc == 0, kc == 7,
                               [tv.buf] + BH, BPS[b], inc=(kc == 7 and jj == 3))
                    psv = PS[b][:, :].rearrange("p (j c) -> p j c", c=128)
                    act(VA[:, pq, j0:j0 + 4, 0, 0:64], psv[:, :, 0:64], AF.Copy, [BPS[b]], [BVA[pq]])
                    act(VA[:, pq, j0:j0 + 4, 1, 64:128], psv[:, :, 64:128], AF.Copy, [BPS[b]], [BVA[pq]])
                ws.done(tv)
                chk("Cv")

                def stage1(j):
                    tok, n = tokslice(j)
                    pcs = (0, 1) if n > 0 else (1,)
                    eb = j % NPT
                    for h in range(2):
                        b = banks.next()
                        pr = slice(64 * h, 64 * h + 64)
                        for pc in pcs:
                            ktok, _ = tokslice(j - 1 if pc == 0 else j)
                            mm(PS[b][:, pc * 128:(pc + 1) * 128], Kb[pr, pq, ktok], Qb[pr, pq, tok], True, True,
                               [BQ[pq], BK[pq]], BPS[b], inc=(pc == 1))
                        psv = PS[b][:, 0:256].rearrange("p (c q) -> p c q", c=2)
                        if n > 0:
                            p_ap, s_ap = PT[:, eb, h], psv
                        else:
                            p_ap, s_ap = PT[:, eb, h, 1, :], psv[:, 1, :]
                        act(p_ap, s_ap, AF.Exp, [BPS[b]], [BPT[eb]], scale=0.125)
                    if n > 0:
                        p_ap, m_ap = PT[:, eb], EXPBM[:, g, hp * 2:hp * 2 + 2]
                    else:
                        p_ap, m_ap = PT[:, eb, :, 1, :], EXPBM[:, g, hp * 2:hp * 2 + 2, 1, :]
                    tt_(p_ap, p_ap, m_ap, ALU.mult, [BPT[eb], BEXPBM], [BPT[eb]])

                def stage2(j):
                    tok, n = tokslice(j)
                    pcs = (0, 1) if n > 0 else (1,)
                    eb = j % NPT
                    b2 = banks.next()
                    for h in range(2):
                        for i, pc in enumerate(pcs):
                            vj = j - 1 if pc == 0 else j
                            mm(PS[b2][:, h * 128:(h + 1) * 128], VA[:, pq, vj, h, :], PT[:, eb, h, pc, :], i == 0,
                               i == len(pcs) - 1, [BVA[pq], BPT[eb]], BPS[b2], inc=(h == 1 and i == len(pcs) - 1))
                    psu = PS[b2][:, 0:256].rearrange("p (h q) -> p h q", h=2)
                    if g == 0:
                        act(Ut[:, :, tok], psu, AF.Copy, [BPS[b2]], [BU])
                    else:
                        tt_(Ut[:, :, tok], psu, Ut[:, :, tok], ALU.add, [BPS[b2], BU], [BU])

                DEPTH_C = 2
                for j in range(16):
                    stage1(j)
                    if j >= DEPTH_C:
                        stage2(j - DEPTH_C)
                for j in range(16 - DEPTH_C, 16):
                    stage2(j)
            recip(R_[0:64, :], Ut[64:128, 0, :], [BU], [BR])
            tt_(OC[0:64, hp, :], Ut[0:64, 0, :], R_[0:64, :], ALU.mult, [BU, BR], [BOC])
            recip(R_[64:128, :], Ut[0:64, 1, :], [BU], [BR])
            tt_(OC[64:128, hp, :], Ut[64:128, 1, :], R_[64:128, :], ALU.mult, [BU, BR], [BOC])
        dump(f"oc{l}", OC[:, :, :], [128, 2, S], BF16, [BOC])
        if stop_after == "C":
            return True

        A.reset(keep_c)
        AACT, BAACT = A.alloc("AACT", [128, 4, S], BF16)
        keep_a = A.off
        APAD, BAPAD = A.alloc("APAD", [128, 4, 32 + S], BF16)
        tmp_o = A.off
        SG, BSG = A.alloc("SG", [128, 2, 512], F32, nbufs=2)
        A.reset(tmp_o)
        CT, BCT = A.alloc("CT", [128, 4, 512], F32, nbufs=4)
        NDG = 16
        DG, BDG = A.alloc("DG", [128, NDG, 128], BF16, nbufs=NDG)
        SQF, BSQF = A.alloc("SQF", [128, 2, 512], F32, nbufs=2)
        MEAN, BMEAN = A.alloc("MEAN", [128, 512], F32)
        VAR, BVAR = A.alloc("VAR", [128, 512], F32)
        memset(APAD[:, :, 0:32], 0.0, [BAPAD])
        for c in range(4):
            tval = ws.get(("w_in", l, c, 0, 8))
            tgate = ws.get(("w_in", l, 4 + c, 0, 8))
            for tt in range(4):
                bv = banks.next()
                bg = banks.next()
                for kc in range(8):
                    mm(PS[bv][:, :], tval.ap[:, kc, :], H[:, kc, ts(tt)], kc == 0, kc == 7, [tval.buf, BH[tt]], BPS[bv])
                for kc in range(8):
                    mm(PS[bg][:, :], tgate.ap[:, kc, :], H[:, kc, ts(tt)], kc == 0, kc == 7, [tgate.buf, BH[tt]],
                       BPS[bg])
                si = tt % 2
                act(SG[:, si, :], PS[bg][:, :], AF.Sigmoid, [BPS[bg]], [BSG[si]])
                tt_(APAD[:, c, 32 + tt * 512:32 + (tt + 1) * 512], PS[bv][:, :], SG[:, si, :], ALU.mult,
                    [BPS[bv], BSG[si]], [BAPAD])
            ws.done(tval)
            ws.done(tgate)
        dump(f"glu{l}", APAD[:, :, :], [128, 4, 32 + S], BF16, [BAPAD])
        cw = pb + 80
        dctr = 0
        for tt in range(4):
            base = 2 + tt * 512
            for c in range(4):
                b = banks.next()
                for kk in range(31):
                    sl = dctr % NDG
                    dctr += 1
                    wcol = PP[:, cw + kk * 4 + c:cw + kk * 4 + c + 1]
                    if kk % 2 == 0:
                        act(DG[:, sl, :], IDENT[:, :], AF.Identity, [BIDENT, BPP], [BDG[sl]], scale=wcol)
                    else:
                        tsc(DG[:, sl, :], IDENT[:, :], wcol, None, ALU.mult, None, [BIDENT, BPP], [BDG[sl]])
                    mm(PS[b][:, :], DG[:, sl, :], APAD[:, c, base + kk:base + kk + 512], kk == 0, kk == 30,
                       [BDG[sl], BAPAD], BPS[b], inc=True)
                act(CT[:, c, :], PS[b][:, :], AF.Identity, [BPS[b], BPP], [BCT[c]], bias=PP[:, pb + 204 + c:pb + 205 + c])
            bsum = banks.next()
            bsq = banks.next()
            for c in range(4):
                o = CT[:, c, :]
                si = c % 2
                act(SQF[:, si, :], o, AF.Square, [BCT[c]], [BSQF[si]])
                mm(PS[bsum][:, :], ONESF[:, :], o, c == 0, c == 3, [BONES, BCT[c]], BPS[bsum], inc=True)
                mm(PS[bsq][:, :], ONESF[:, :], SQF[:, si, :], c == 0, c == 3, [BONES, BSQF[si]], BPS[bsq], inc=True)
            act(MEAN[:, :], PS[bsum][:, :], AF.Identity, [BPS[bsum]], [BMEAN], scale=1.0 / 512)
            tt_(VAR[:, :], MEAN[:, :], MEAN[:, :], ALU.mult, [BMEAN], [BVAR])
            stt(VAR[:, :], PS[bsq][:, :], 1.0 / 512, VAR[:, :], ALU.mult, ALU.subtract, [BPS[bsq], BVAR], [BVAR])
            act(VAR[:, :], VAR[:, :], AF.Sqrt, [BVAR, BEPS], [BVAR], bias=EPS[:, 1:2])
            recip(VAR[:, :], VAR[:, :], [BVAR], [BVAR])
            for c in range(4):
                tt_(CT[:, c, :], CT[:, c, :], MEAN[:, :], ALU.subtract, [BCT[c], BMEAN], [BCT[c]])
            for c in range(4):
                tt_(CT[:, c, :], CT[:, c, :], VAR[:, :], ALU.mult, [BCT[c], BVAR], [BCT[c]])
            for c in range(4):
                act(AACT[:, c, ts(tt)], CT[:, c, :], AF.Silu, [BCT[c], BPP], [BAACT], bias=PP[:, pb + 212 + c:pb + 213 + c],
                    scale=PP[:, pb + 208 + c:pb + 209 + c])
        dump(f"aact{l}", AACT[:, :, :], [128, 4, S], BF16, [BAACT])
        if stop_after == "A":
            return True

        A.reset(keep_a)
        UB, BUB = A.alloc("UB", [128, 4, S], BF16)
        keep_b = A.off
        UU, BUU = A.alloc("UU", [128, 4, S], BF16)
        VG, BVG = A.alloc("VG", [128, 2, 512], F32, nbufs=2)
        VL, BVL = A.alloc("VL", [128, 2, 512], BF16, nbufs=2)
        TB, BTB = A.alloc("TB", [128, 512], F32)
        CB, BCB = A.alloc("CB", [128, 512], F32)
        brw = banks.next()
        for g in range(4):
            mm(PS[brw][:, g * 128:(g + 1) * 128], ONESB[:, :], WSM[:, g, :], True, True, [BONES, BWSM], BPS[brw],
               inc=(g == 3))
        for g in range(4):
            stt(CB[:, g * 128:(g + 1) * 128], PS[brw][:, g * 128:(g + 1) * 128], PP[:, pb + 308 + g:pb + 309 + g],
                PBC[:, 1024 + g * 128:1024 + (g + 1) * 128], ALU.mult, ALU.add, [BPS[brw], BPP, BPBC], [BCB])
        for c in range(4):
            tu = ws.get(("w_in", l, 8 + c, 0, 8))

            def ev(tt, ps, psb, c=c):
                act(UU[:, c, ts(tt)], ps[:, :], AF.Gelu_apprx_tanh, [psb], [BUU])
            proj_fm(tu, 8, h_rhs, h_bufs, ev)
            ws.done(tu)
        tvs = [ws.get(("w_in", l, 12 + cc, 0, 8)) for cc in range(4)]
        for n in range(16):
            b = banks.next()
            tt = n // 4
            for cc in range(4):
                for kc in range(8):
                    mm(PS[b][:, cc * 128:(cc + 1) * 128], H[:, kc, n * 128:(n + 1) * 128], tvs[cc].ap[:, kc, :], kc == 0,
                       kc == 7, [tvs[cc].buf, BH[tt]], BPS[b], inc=(kc == 7 and cc == 3))
            vi = n % 2
            sm = SMALL[:, vi, :]
            act(VG[:, vi, :], PS[b][:, :], AF.Gelu_apprx_tanh, [BPS[b]], [BVG[vi]])
            k.op("dve", lambda e, o=ST6[:, vi, :], i=VG[:, vi, :]: e.bn_stats(out=o, in_=i), reads=[BVG[vi]],
                 writes=[BSMALL[vi]])
            k.op("dve", lambda e, o=SMALL[:, vi, 2:4], i=ST6[:, vi, :]: e.bn_aggr(out=o, in_=i), reads=[BSMALL[vi]],
                 writes=[BSMALL[vi]])
            act(SMALL[:, vi, 5:6], SMALL[:, vi, 3:4], AF.Sqrt, [BSMALL[vi], BEPS], [BSMALL[vi]], bias=EPS[:, 1:2])
            recip(SMALL[:, vi, 6:7], SMALL[:, vi, 5:6], [BSMALL[vi]], [BSMALL[vi]])
            tsc(VL[:, vi, :], VG[:, vi, :], SMALL[:, vi, 2:3], SMALL[:, vi, 6:7], ALU.subtract, ALU.mult,
                [BVG[vi], BSMALL[vi]], [BVL[vi]])
            b2 = banks.next()
            for g in range(4):
                mm(PS[b2][:, g * 128:(g + 1) * 128], VL[:, vi, g * 128:(g + 1) * 128], WSM[:, g, :], True, True,
                   [BVL[vi], BWSM], BPS[b2], inc=(g == 3))
            for g in range(4):
                stt(TB[:, g * 128:(g + 1) * 128], PS[b2][:, g * 128:(g + 1) * 128], PP[:, pb + 304 + g:pb + 305 + g],
                    CB[:, g * 128:(g + 1) * 128], ALU.mult, ALU.add, [BPS[b2], BPP, BCB], [BTB])
            tbv = TB[:, :].rearrange("p (g t) -> p g t", g=4)
            tt_(UB[:, :, n * 128:(n + 1) * 128], tbv, UU[:, :, n * 128:(n + 1) * 128], ALU.mult, [BTB, BUU], [BUB])
        for t_ in tvs:
            ws.done(t_)
        dump(f"ub{l}", UB[:, :, :], [128, 4, S], BF16, [BUB])
        if stop_after == "B":
            return True

        A.reset(keep_b)
        MG, BMG = A.alloc("MG", [128, 8, 1024], BF16)
        SGM, BSGM = A.alloc("SGM", [128, 3, 512], F32)
        M0, BM0 = A.alloc("M0", [128, 512], F32)
        M1, BM1 = A.alloc("M1", [128, 512], F32)
        for hf in range(2):
            banks.set(range(8))
            for dc in range(8):
                wa = ws.get(("w_a", l, dc, 0, 4))
                wb = ws.get(("w_b", l, dc, 0, 4))
                wc = ws.get(("w_c", l, dc, 0, 2))
                wg = [ws.get(("w_in", l, 34 + i * 8 + dc, 0, 8)) for i in range(3)]
                for t2 in range(2):
                    tt = hf * 2 + t2
                    pa, pb_, pc_ = banks.next(), banks.next(), banks.next()
                    pg = [banks.next() for _ in range(3)]
                    for kc in range(4):
                        mm(PS[pa][:, :], wa.ap[:, kc, :], AACT[:, kc, ts(tt)], kc == 0, kc == 3, [wa.buf, BAACT], BPS[pa])
                    for kc in range(4):
                        mm(PS[pb_][:, :], wb.ap[:, kc, :], UB[:, kc, ts(tt)], kc == 0, kc == 3, [wb.buf, BUB], BPS[pb_])
                    for kc in range(2):
                        mm(PS[pc_][:, :], wc.ap[:, kc, :], OC[:, kc, ts(tt)], kc == 0, kc == 1, [wc.buf, BOC], BPS[pc_])
                    for i in range(3):
                        for kc in range(8):
                            mm(PS[pg[i]][:, :], wg[i].ap[:, kc, :], H[:, kc, ts(tt)], kc == 0, kc == 7,
                               [wg[i].buf, BH[tt]], BPS[pg[i]])
                    for i in range(3):
                        col = pb + 56 + i * 8 + dc
                        act(SGM[:, i, :], PS[pg[i]][:, :], AF.Sigmoid, [BPS[pg[i]], BPP], [BSGM], bias=PP[:, col:col + 1])
                    tt_(M0[:, :], PS[pa][:, :], SGM[:, 0, :], ALU.mult, [BPS[pa], BSGM], [BM0])
                    tt_(M1[:, :], PS[pb_][:, :], SGM[:, 1, :], ALU.mult, [BPS[pb_], BSGM], [BM1])
                    tt_(M0[:, :], M0[:, :], M1[:, :], ALU.add, [BM0, BM1], [BM0])
                    tt_(M1[:, :], PS[pc_][:, :], SGM[:, 2, :], ALU.mult, [BPS[pc_], BSGM], [BM1])
                    tt_(MG[:, dc, ts(t2)], M0[:, :], M1[:, :], ALU.add, [BM0, BM1], [BMG])
                for t_ in [wa, wb, wc] + wg:
                    ws.done(t_)
            if dbg is not None:
                dump(f"merged{l}_{hf}", MG[:, :, :], [128, 8, 1024], BF16, [BMG])
            banks.set(range(6))
            wm = [ws.get(("w_mix", l, dc, 0, 8)) for dc in range(8)]
            for t2 in range(2):
                tt = hf * 2 + t2
                sb_ = 6 + t2
                for dc in range(8):
                    b = banks.next()
                    for kc in range(8):
                        mm(PS[b][:, :], wm[dc].ap[:, kc, :], MG[:, kc, ts(t2)], kc == 0, kc == 7, [wm[dc].buf, BMG], BPS[b])
                    flush_stats()
                    y_chunk(PS[b][:, :], BPS[b], dc, hf, sb_, dc == 0, dc == 7)
                post_update(pb + 8, tt, hf, sb_)
            for t_ in wm:
                ws.done(t_)
        return False

    def sublayer_xattn(l, s):
        pb = l * PPL
        banks.set(range(8))
        A.reset()
        MEMS, BMEMS = A.alloc("MEMS", [128, 8, 256], F32)
        KT, BKT = A.alloc("KT", [128, 8, 256], BF16)
        VX, BVX = A.alloc("VX", [128, 2, 1024], BF16)
        QX, BQX = A.alloc("QX", [128, 8, S], BF16)
        PTX, BPTX = A.alloc("PTX", [128, 2, 2, 512], BF16, nbufs=2)
        RD, BRD = A.alloc("RD", [128, 2, 512], F32, nbufs=2)
        memn_o = A.off
        MEMN, BMEMN = A.alloc("MEMN", [128, 8, 256], BF16)
        A.reset(0)
        OX0, BOX0 = A.alloc("OX0", [128, 8, 512], BF16)
        A.reset(memn_o)
        OX1, BOX1 = A.alloc("OX1", [128, 8, 512], BF16)
        OXs, BOXs = [OX0, OX1], [BOX0, BOX1]
        if not dry:
            k.dma("sp", [(MEMS[:, kc, :], T["memT"][s, kc, :, :]) for kc in range(8)], writes=[BMEMS])
        b = banks.next()
        for kc in range(8):
            si = kc % 2
            act(SQ[:, si, 0:256], MEMS[:, kc, :], AF.Square, [BMEMS], [BSQ[si]])
            mm(PS[b][:, 0:256], ONESB[:, :], SQ[:, si, 0:256], kc == 0, kc == 7, [BONES, BSQ[si]], BPS[b], inc=True)
        ri = nrm_ctr[0] % 2
        nrm_ctr[0] += 1
        act(RSTD[:, ri, 0:256], PS[b][:, 0:256], AF.Sqrt, [BPS[b], BEPS], [BRSTD[ri]], bias=EPS[:, 0:1], scale=1.0 / D)
        recip(RSTD[:, ri, 0:256], RSTD[:, ri, 0:256], [BRSTD[ri]], [BRSTD[ri]])
        for kc in range(8):
            stt(MEMN[:, kc, :], MEMS[:, kc, :], PP[:, pb + 32 + kc:pb + 33 + kc], RSTD[:, ri, 0:256], ALU.mult, ALU.mult,
                [BMEMS, BPP, BRSTD[ri]], [BMEMN])
        for ec in range(8):
            w = ws.get(("w_xkv", l, ec, 0, 8))
            b = banks.next()
            for kc in range(8):
                mm(PS[b][:, 0:256], w.ap[:, kc, :], MEMN[:, kc, :], kc == 0, kc == 7, [w.buf, BMEMN], BPS[b])
            act(KT[:, ec, :], PS[b][:, 0:256], AF.Copy, [BPS[b]], [BKT])
            ws.done(w)
        for ec in range(8):
            w = ws.get(("w_xkv", l, 8 + ec, 0, 8))
            b = banks.next()
            for mb in range(2):
                for kc in range(8):
                    mm(PS[b][:, mb * 128:(mb + 1) * 128], MEMN[:, kc, mb * 128:(mb + 1) * 128], w.ap[:, kc, :], kc == 0,
                       kc == 7, [w.buf, BMEMN], BPS[b], inc=(kc == 7 and mb == 1))
            psv = PS[b][:, 0:256].rearrange("p (m c) -> p m c", m=2)
            act(VX[:, :, ec * 128:(ec + 1) * 128], psv, AF.Copy, [BPS[b]], [BVX])
            ws.done(w)
        norm_to_H(pb + 16, range(4))
        for dc in range(8):
            w = ws.get(("w_xq", l, dc, 0, 8))

            def ev(tt, ps, psb, dc=dc):
                act(QX[:, dc, ts(tt)], ps[:, :], AF.Copy, [psb], [BQX])
            proj_fm(w, 8, h_rhs, h_bufs, ev)
            ws.done(w)
        wo = [ws.get(("w_xo", l, dc, 0, 8)) for dc in range(8)]
        banks.set(range(6))

        def attn_tile(tt):
            OX, BOX = OXs[tt % 2], BOXs[tt % 2]

            def xs1(hd):
                pi = hd % 2
                bs_ = [banks.next(), banks.next()]
                for mb in range(2):
                    for ec in range(2):
                        mm(PS[bs_[mb]][:, :], KT[:, hd * 2 + ec, mb * 128:(mb + 1) * 128], QX[:, hd * 2 + ec, ts(tt)],
                           ec == 0, ec == 1, [BKT, BQX], BPS[bs_[mb]])
                    act(PTX[:, pi, mb, :], PS[bs_[mb]][:, :], AF.Exp, [BPS[bs_[mb]]], [BPTX[pi]], scale=1.0 / 16)

            def xs2(hd):
                pi = hd % 2
                bd = banks.next()
                for mb in range(2):
                    mm(PS[bd][:, :], ONESB[:, :], PTX[:, pi, mb, :], mb == 0, mb == 1, [BONES, BPTX[pi]], BPS[bd])
                recip(RD[:, pi, :], PS[bd][:, :], [BPS[bd]], [BRD[pi]])
                for ec in range(2):
                    bo = banks.next()
                    for mb in range(2):
                        mm(PS[bo][:, :], VX[:, mb, (hd * 2 + ec) * 128:(hd * 2 + ec + 1) * 128], PTX[:, pi, mb, :], mb == 0,
                           mb == 1, [BVX, BPTX[pi]], BPS[bo])
                    tt_(OX[:, hd * 2 + ec, :], PS[bo][:, :], RD[:, pi, :], ALU.mult, [BPS[bo], BRD[pi]], [BOX])

            for hd in range(4):
                xs1(hd)
                if hd > 0:
                    xs2(hd - 1)
            xs2(3)

        def xo_tile(tt):
            OX, BOX = OXs[tt % 2], BOXs[tt % 2]
            ysel = tt % 2
            sb_ = 6 + (tt % 2)
            for dc in range(8):
                b = banks.next()
                for kc in range(8):
                    mm(PS[b][:, :], wo[dc].ap[:, kc, :], OX[:, kc, :], kc == 0, kc == 7, [wo[dc].buf, BOX], BPS[b])
                flush_stats()
                y_chunk(PS[b][:, :], BPS[b], dc, ysel, sb_, dc == 0, dc == 7)
            flush_stats()
            if not XATTN_DEFER_POST:
                post_update(pb + 24, tt, ysel, sb_)

        def xpost(tt):
            post_update(pb + 24, tt, tt % 2, 6 + (tt % 2))

        if XATTN_DEFER_POST:
            for tt in range(4):
                attn_tile(tt)
                if tt > 0:
                    xpost(tt - 1)
                xo_tile(tt)
            xpost(3)
        elif XATTN_PIPE:
            for tt in range(4):
                attn_tile(tt)
                if tt > 0:
                    xo_tile(tt - 1)
            xo_tile(3)
        else:
            for tt in range(4):
                attn_tile(tt)
                xo_tile(tt)
        for t_ in wo:
            ws.done(t_)

    def sublayer_ffn(l, tile_done=None):
        pb = l * PPL
        A.reset()
        ACTB, BACTB = A.alloc("ACTB", [128, 22, 1024], BF16)
        GT, BGT = A.alloc("GT", [128, 2, 516], F32, nbufs=2)
        CV, BCV = A.alloc("CV", [128, 512], F32)
        GL, BGL = A.alloc("GL", [128, 2, 512], F32, nbufs=2)
        memset(HALO[:, :, :], 0.0, [BHALO])
        fw = pb + 216
        gi = 0
        for hf in range(2):
            banks.set(range(8))
            norm_to_H(pb + 40, (2 * hf, 2 * hf + 1))
            for j in range(22):
                wg = ws.get(("w_up", l, j, 0, 8))
                wv = ws.get(("w_up", l, 22 + j, 0, 8))
                for t2 in range(2):
                    tt = hf * 2 + t2
                    bg, bv = banks.next(), banks.next()
                    for kc in range(8):
                        mm(PS[bg][:, :], wg.ap[:, kc, :], H[:, kc, ts(tt)], kc == 0, kc == 7, [wg.buf, BH[tt]], BPS[bg])
                    for kc in range(8):
                        mm(PS[bv][:, :], wv.ap[:, kc, :], H[:, kc, ts(tt)], kc == 0, kc == 7, [wv.buf, BH[tt]], BPS[bv])
                    g2 = gi % 2
                    gi += 1
                    act(GT[:, g2, 2:514], PS[bg][:, :], AF.Copy, [BPS[bg]], [BGT[g2]])
                    cpy(GT[:, g2, 0:2], HALO[:, j, :], [BHALO], [BGT[g2]])
                    cpy(HALO[:, j, :], GT[:, g2, 512:514], [BGT[g2]], [BHALO])
                    act(CV[:, :], PS[bg][:, :], AF.Identity, [BPS[bg], BPP], [BCV], bias=PP[:, pb + 282 + j:pb + 283 + j],
                        scale=PP[:, fw + 44 + j:fw + 45 + j])
                    stt(CV[:, :], GT[:, g2, 0:512], PP[:, fw + j:fw + j + 1], CV[:, :], ALU.mult, ALU.add,
                        [BGT[g2], BPP, BCV], [BCV])
                    stt(CV[:, :], GT[:, g2, 1:513], PP[:, fw + 22 + j:fw + 23 + j], CV[:, :], ALU.mult, ALU.add,
                        [BGT[g2], BPP, BCV], [BCV])
                    act(GL[:, g2, :], CV[:, :], AF.Gelu_apprx_tanh, [BCV], [BGL[g2]])
                    tt_(ACTB[:, j, ts(t2)], PS[bv][:, :], GL[:, g2, :], ALU.mult, [BPS[bv], BGL[g2]], [BACTB])
                ws.done(wg)
                ws.done(wv)
            banks.set(range(6))
            ysels = (1, 0) if hf == 0 else (0, 1)
            for dc in range(8):
                wd = [ws.get(("w_down", l, dc, 0, 8)), ws.get(("w_down", l, dc, 8, 8)), ws.get(("w_down", l, dc, 16, 6))]
                for t2 in range(2):
                    b = banks.next()
                    for kc in range(22):
                        w = wd[kc // 8]
                        mm(PS[b][:, :], w.ap[:, kc % 8, :], ACTB[:, kc, ts(t2)], kc == 0, kc == 21, [w.buf, BACTB], BPS[b])
                    flush_stats()
                    y_chunk(PS[b][:, :], BPS[b], dc, ysels[t2], 6 + t2, dc == 0, dc == 7)
                for w in wd:
                    ws.done(w)
            for t2 in range(2):
                tt = hf * 2 + t2
                post_update(pb + 48, tt, ysels[t2], 6 + t2)
                if tile_done is not None:
                    tile_done(tt)

    stopped = False

    def load_x(s_, tt):
        if not dry:
            k.dma("sp", [(XRES[:, kc, ts(tt)], T["xT"][s_, kc, :, ts(tt)]) for kc in range(8)], writes=[BX[tt]])

    def store_x(s_, tt):
        if not dry:
            k.dma("sp", [(T["outT"][s_, kc, :, ts(tt)], XRES[:, kc, ts(tt)]) for kc in range(8)], reads=[BX[tt]],
                  sembuf=BX[tt])

    try:
        for tt in range(4):
            load_x(0, tt)
        for s in range(nseq):
            def tile_done(tt, s=s):
                store_x(s, tt)
                if s + 1 < nseq:
                    load_x(s + 1, tt)
            for l in range(nlayer):
                stopped = sublayer_mixer(l)
                if stopped:
                    break
                dump(f"x1_{l}", XRES[:, :, :], [128, 8, S], F32, BX)
                if stop_after == "mixer":
                    stopped = True
                    break
                sublayer_xattn(l, s)
                dump(f"x2_{l}", XRES[:, :, :], [128, 8, S], F32, BX)
                if stop_after == "xattn":
                    stopped = True
                    break
                sublayer_ffn(l, tile_done if l == nlayer - 1 else None)
            if stopped:
                for tt in range(4):
                    store_x(s, tt)
                break
    except _Stop:
        for tt in range(4):
            store_x(0, tt)
    k.wait_all("sp", BX + dbg_out)
    return ws


def make_program(nseq=SEQ_PER_CORE, nlayer=DEPTH, stop_after=None, dbg=None):
    kd = K(None, dry=True)
    wsd = build_program(None, kd, None, nseq, nlayer, stop_after, dbg)
    seq = wsd.rec
    nc = bass.Bass("TRN2", target_bir_lowering=False)
    k = K(nc)
    ws = build_program(nc, k, seq, nseq, nlayer, stop_after, dbg)
    assert ws.idx == len(seq) and ws.loaded == len(seq)
    return nc, k


def _t5_bucket(n):
    n = np.maximum(n, 0)
    nf = np.maximum(n, 1).astype(np.float32)
    large = 16 + (np.log(nf / np.float32(16)) / np.float32(np.log(2048 / 16)) * np.float32(16)).astype(np.int32)
    large = np.minimum(large, 31)
    return np.where(n < 16, n, large)


def _wtiles(W):
    L, Kd, N = W.shape
    return np.ascontiguousarray(W.reshape(L, Kd // 128, 128, N // 128, 128).transpose(0, 3, 2, 1, 4))


def _cols(v):
    v = np.asarray(v)
    C = v.shape[-1]
    lead = v.shape[:-1]
    a = v.reshape(*lead, C // 128, 128)
    a = np.moveaxis(a, -1, 0)
    return a.reshape(128, -1)


def prep_shared(inp):
    L = DEPTH
    pp = np.zeros((128, L * PPL), np.float32)
    for l in range(L):
        o = l * PPL
        pp[:, o + 0:o + 8] = _cols(inp["mix_pre_g"][l])
        pp[:, o + 8:o + 16] = _cols(inp["mix_post_g"][l])
        pp[:, o + 16:o + 24] = _cols(inp["x_pre_g"][l])
        pp[:, o + 24:o + 32] = _cols(inp["x_post_g"][l])
        pp[:, o + 32:o + 40] = _cols(inp["mem_g"][l])
        pp[:, o + 40:o + 48] = _cols(inp["ffn_pre_g"][l])
        pp[:, o + 48:o + 56] = _cols(inp["ffn_post_g"][l])
        pp[:, o + 56:o + 80] = _cols(inp["b_gate"][l])
        pp[:, o + 80:o + 204] = _cols(inp["conv_a_w"][l])
        pp[:, o + 204:o + 208] = _cols(inp["conv_a_b"][l])
        pp[:, o + 208:o + 212] = _cols(inp["ln_a_g"][l])
        pp[:, o + 212:o + 216] = _cols(inp["ln_a_b"][l])
        pp[:, o + 216:o + 282] = _cols(inp["conv_f_w"][l])
        pp[:, o + 282:o + 304] = _cols(inp["conv_f_b"][l])
        pp[:, o + 304:o + 308] = _cols(inp["ln_b_g"][l])
        pp[:, o + 308:o + 312] = _cols(inp["ln_b_b"][l])
    rb = np.asarray(inp["rel_bias"])
    kk = np.arange(128)[:, None]
    qq = np.arange(128)[None, :]
    abias = np.zeros((128, 3, 4, 2, 128), np.float32)
    amask = np.zeros((128, 3, 4, 2, 128), np.float32)
    for g, dil in enumerate(DILS):
        for pc in range(2):
            rel = qq + 128 - kk if pc == 0 else qq - kk
            valid = (rel >= 0) & (rel <= 128)
            bucket = _t5_bucket(rel * dil)
            for h in range(4):
                abias[:, g, h, pc, :] = rb[bucket, g * 4 + h]
                amask[:, g, h, pc, :] = valid
    pbc = np.zeros((L, 128, 1536), np.float32)
    for l in range(L):
        pbc[l, :, 0:512] = np.broadcast_to(inp["ln_b_g"][l][None, :], (128, 512))
        pbc[l, :, 512:1024] = np.broadcast_to(inp["ln_b_b"][l][None, :], (128, 512))
        pbc[l, :, 1024:1536] = np.broadcast_to(np.asarray(inp["b_s"][l]).reshape(1, 512), (128, 512))
    wsT = np.ascontiguousarray(np.asarray(inp["w_s"]).transpose(0, 3, 1, 2))
    tril = (np.arange(128)[None, :] >= np.arange(128)[:, None]).astype(np.float32)
    trilm = np.ascontiguousarray(np.broadcast_to(tril[:, None, :], (128, 4, 128)))
    w_c = np.asarray(inp["w_c_out"])
    shared = {
        "pp": pp,
        "abias": abias.reshape(128, 3072),
        "amask": amask.reshape(128, 3072),
        "pbc": pbc,
        "wsT": wsT,
        "trilm": trilm,
        "ident": np.eye(128, dtype=np.float32),
        "w_in": _wtiles(np.asarray(inp["w_in"])),
        "w_a": _wtiles(np.asarray(inp["w_a_out"])),
        "w_b": _wtiles(np.asarray(inp["w_b_out"])),
        "w_c": _wtiles(w_c),
        "w_mix": _wtiles(np.asarray(inp["w_mix_out"])),
        "w_xq": _wtiles(np.asarray(inp["w_xq"])),
        "w_xkv": _wtiles(np.asarray(inp["w_xkv"])),
        "w_xo": _wtiles(np.asarray(inp["w_xo"])),
        "w_up": _wtiles(np.asarray(inp["w_up"])),
        "w_down": _wtiles(np.asarray(inp["w_down"])),
    }
    return shared


def prep_core(inp, c, nseq=SEQ_PER_CORE):
    xs = np.asarray(inp["x"][c * SEQ_PER_CORE:c * SEQ_PER_CORE + nseq])
    ms = np.asarray(inp["mem"][c * SEQ_PER_CORE:c * SEQ_PER_CORE + nseq])
    xT = np.zeros((SEQ_PER_CORE, 8, 128, S), np.float32)
    mT = np.zeros((SEQ_PER_CORE, 8, 128, 256), np.float32)
    xT[:nseq] = xs.transpose(0, 2, 1).reshape(nseq, 8, 128, S)
    mT[:nseq] = ms.transpose(0, 2, 1).reshape(nseq, 8, 128, 256)
    return {"xT": xT, "memT": mT}


_PROG = {}


def kernel(**inputs):
    if "p" not in _PROG:
        _PROG["p"] = make_program()
    nc, _ = _PROG["p"]
    shared = prep_shared(inputs)
    in_maps = []
    for c in range(N_CORES):
        m = dict(shared)
        m.update(prep_core(inputs, c))
        in_maps.append(m)
    res = run_bass_kernel_spmd(nc, in_maps, core_ids=list(range(N_CORES)))
    out = np.empty((N_CORES * SEQ_PER_CORE, S, D), np.float32)
    for c in range(N_CORES):
        oT = res.results[c]["outT"]
        out[c * SEQ_PER_CORE:(c + 1) * SEQ_PER_CORE] = oT.reshape(SEQ_PER_CORE, D, S).transpose(0, 2, 1)
    return out
```

```python
import numpy as np
import concourse.bass as bass
import concourse.mybir as mybir
from concourse.bass_utils import run_bass_kernel_spmd

F32 = mybir.dt.float32
BF16 = mybir.dt.bfloat16
AF = mybir.ActivationFunctionType
ALU = mybir.AluOpType
AX = mybir.AxisListType

EPOCH = 12000
SAME_SYNC = True

N_CORES = 8
SEQ_PER_CORE = 4
DEPTH = 2
D = 1024
S = 2048
NSLOT = 10
XATTN_PIPE = False
XATTN_DEFER_POST = False
DEFER_STATS = True
PPL = 312
DILS = (1, 4, 16)


class Buf:
    __slots__ = ("name", "w", "r", "al", "dsem", "dcnt")

    def __init__(self, name):
        self.name = name
        self.w = None
        self.r = {}
        self.al = []
        self.dsem = None
        self.dcnt = 0


class Dummy:
    shape = ()

    def __getitem__(self, i):
        return self

    def rearrange(self, *a, **k):
        return self


DUMMY = Dummy()


class Eng:
    def __init__(self, K, name, obj):
        self.name = name
        self.obj = obj
        self.sem = None if K.dry else K.nc.alloc_semaphore(f"s_{name}_0")
        self.nsem = 1
        self.cnt = 0
        self.own = set() if K.dry else {id(self.sem)}
        self.waited = {}
        self.nwaits = 0
        self.nins = 0


class K:
    def __init__(self, nc, dry=False):
        self.nc = nc
        self.dry = dry
        if dry:
            self.engs = {n: Eng(self, n, None) for n in ("pe", "act", "dve", "pool", "sp")}
        else:
            self.engs = {
                "pe": Eng(self, "pe", nc.tensor),
                "act": Eng(self, "act", nc.scalar),
                "dve": Eng(self, "dve", nc.vector),
                "pool": Eng(self, "pool", nc.gpsimd),
                "sp": Eng(self, "sp", nc.sync),
            }
        self.semobj = {}
        if not dry:
            for e in self.engs.values():
                self.semobj[id(e.sem)] = e.sem
        self.ndsem = 0

    def _deps(self, reads, writes):
        deps = {}

        def add(d):
            if d is None:
                return
            s, v = d
            if deps.get(s, 0) < v:
                deps[s] = v

        for b in reads:
            add(b.w)
            for a in b.al:
                add(a.w)
        for b in writes:
            add(b.w)
            for d in b.r.values():
                add(d)
            for a in b.al:
                add(a.w)
                for d in a.r.values():
                    add(d)
        return deps

    def _wait(self, E, deps):
        for s, v in deps.items():
            if s in E.own:
                if E.name in ("pe", "sp") or not SAME_SYNC:
                    continue
                if s == id(E.sem) and v > E.cnt:
                    continue
                if E.name in ("act", "dve") and (s != id(E.sem) or v < E.cnt):
                    continue
            if E.waited.get(s, 0) >= v:
                continue
            E.obj.wait_ge(self.semobj[s], v)
            E.waited[s] = v
            E.nwaits += 1

    def _tag(self, E, ins, inc):
        E.nins += 1
        if inc:
            ins.then_inc(E.sem, 1)
            E.cnt += 1
            tag = (id(E.sem), E.cnt)
            if E.cnt >= EPOCH:
                E.sem = self.nc.alloc_semaphore(f"s_{E.name}_{E.nsem}")
                E.nsem += 1
                E.cnt = 0
                E.own.add(id(E.sem))
                self.semobj[id(E.sem)] = E.sem
        else:
            tag = (id(E.sem), E.cnt + 1)
        return tag

    def op(self, eng, fn, reads=(), writes=(), inc=True):
        if self.dry:
            return None
        E = self.engs[eng]
        self._wait(E, self._deps(reads, writes))
        ins = fn(E.obj)
        tag = self._tag(E, ins, inc)
        for b in writes:
            b.w = tag
            b.r = {}
        for b in reads:
            b.r[eng] = tag
        return ins

    def dma(self, eng, pairs, reads=(), writes=(), sembuf=None):
        if self.dry:
            return None
        E = self.engs[eng]
        sb = sembuf or (writes[0] if writes else reads[0])
        if sb.dsem is None:
            sb.dsem = self.nc.alloc_semaphore(f"d_{self.ndsem}")
            self.ndsem += 1
            self.semobj[id(sb.dsem)] = sb.dsem
        deps = self._deps(reads, writes)
        if sb.dcnt:
            s = id(sb.dsem)
            if deps.get(s, 0) < sb.dcnt:
                deps[s] = sb.dcnt
        self._wait(E, deps)
        for (o, i) in pairs:
            E.obj.dma_start(out=o, in_=i).then_inc(sb.dsem, 16)
            E.nins += 1
            sb.dcnt += 16
        tag = (id(sb.dsem), sb.dcnt)
        for b in writes:
            b.w = tag
            b.r = {}
        for b in reads:
            b.r["dma_" + sb.name] = tag
        return tag

    def wait_all(self, eng, bufs):
        if self.dry:
            return
        E = self.engs[eng]
        self._wait(E, self._deps(bufs, bufs))

    def stats(self):
        return {e.name: (e.nins, e.nwaits, e.nsem) for e in self.engs.values()}


class _Stop(Exception):
    pass


class Tile:
    __slots__ = ("i", "ap", "buf")

    def __init__(self, i, ap, buf):
        self.i = i
        self.ap = ap
        self.buf = buf


class WStream:
    def __init__(self, k, ring, rbufs, seq, ap_of):
        self.k = k
        self.ring = ring
        self.rbufs = rbufs
        self.seq = seq
        self.ap_of = ap_of
        self.rec = []
        self.idx = 0
        self.loaded = 0
        self.done_ = set()

    def _issue(self):
        while self.loaded < len(self.seq):
            i = self.loaded
            if i - NSLOT >= 0 and (i - NSLOT) not in self.done_:
                break
            slot = i % NSLOT
            src, kcn = self.ap_of(self.seq[i])
            self.k.dma("pool", [(self.ring[:, slot, 0:kcn, :], src)], writes=[self.rbufs[slot]])
            self.loaded += 1

    def get(self, desc):
        i = self.idx
        self.idx += 1
        if self.k.dry:
            self.rec.append(desc)
            return Tile(i, DUMMY, None)
        assert self.seq[i] == desc, (i, self.seq[i], desc)
        self._issue()
        assert self.loaded > i, f"weight ring too small at tile {i} {desc}"
        slot = i % NSLOT
        return Tile(i, self.ring[:, slot], self.rbufs[slot])

    def done(self, t):
        if self.k.dry:
            return
        self.done_.add(t.i)
        self._issue()


class Arena:
    def __init__(self, k, nc, base, size):
        self.k, self.nc, self.base, self.size = k, nc, base, size
        self.off = 0
        self.regs = []
        self.n = 0

    def reset(self, off=0):
        self.off = off

    def alloc(self, name, shape, dtype, nbufs=1):
        esz = 2 if dtype == BF16 else 4
        per = esz
        for s_ in shape[1:]:
            per *= s_
        per = (per + 63) // 64 * 64
        lo = self.off
        hi = lo + per
        assert hi <= self.size, f"arena overflow {name}: {hi} > {self.size}"
        self.off = hi
        bufs = [Buf(f"{name}{i}") for i in range(nbufs)]
        pbuf = per // nbufs
        new = []
        for bi, b in enumerate(bufs):
            blo, bhi = lo + bi * pbuf, (lo + (bi + 1) * pbuf if bi < nbufs - 1 else hi)
            for (l2, h2, b2) in self.regs:
                if l2 < bhi and blo < h2:
                    if b2 not in b.al:
                        b.al.append(b2)
                    if b not in b2.al:
                        b2.al.append(b)
            new.append((blo, bhi, b))
        self.regs.extend(new)
        if self.k.dry:
            t = DUMMY
        else:
            self.n += 1
            t = self.nc.alloc_sbuf_tensor_at(f"ar{self.n}_{name}", list(shape), dtype, offset=self.base + lo)
        return (t, bufs[0]) if nbufs == 1 else (t, bufs)


def ts(tt, n=512):
    return slice(tt * n, (tt + 1) * n)


def build_program(nc, k, wseq, nseq=SEQ_PER_CORE, nlayer=DEPTH, stop_after=None, dbg=None):
    dry = k.dry
    T = {}
    if not dry:
        def dram(name, shape, kind="ExternalInput"):
            T[name] = nc.dram_tensor(name, list(shape), F32, kind=kind).ap()
        dram("xT", [SEQ_PER_CORE, 8, 128, S])
        dram("memT", [SEQ_PER_CORE, 8, 128, 256])
        dram("pp", [128, DEPTH * PPL])
        dram("abias", [128, 3072])
        dram("amask", [128, 3072])
        dram("pbc", [DEPTH, 128, 1536])
        dram("wsT", [DEPTH, 128, 4, 128])
        dram("trilm", [128, 4, 128])
        dram("ident", [128, 128])
        dram("w_in", [DEPTH, 58, 128, 8, 128])
        dram("w_a", [DEPTH, 8, 128, 4, 128])
        dram("w_b", [DEPTH, 8, 128, 4, 128])
        dram("w_c", [DEPTH, 8, 128, 2, 128])
        dram("w_mix", [DEPTH, 8, 128, 8, 128])
        dram("w_xq", [DEPTH, 8, 128, 8, 128])
        dram("w_xkv", [DEPTH, 16, 128, 8, 128])
        dram("w_xo", [DEPTH, 8, 128, 8, 128])
        dram("w_up", [DEPTH, 44, 128, 8, 128])
        dram("w_down", [DEPTH, 8, 128, 22, 128])
        dram("outT", [SEQ_PER_CORE, 8, 128, S], kind="ExternalOutput")

    def ap_of(desc):
        name, l, j, k0, kcn = desc
        return T[name][l, j, :, k0:k0 + kcn, :], kcn

    BASE = 16512
    XRES_O = BASE
    H_O = XRES_O + 65536
    RING_O = H_O + 32768
    C_O = RING_O + NSLOT * 2048
    coff = [C_O]

    def calloc(name, shape, dtype):
        esz = 2 if dtype == BF16 else 4
        per = esz
        for s_ in shape[1:]:
            per *= s_
        per = (per + 63) // 64 * 64
        o = coff[0]
        coff[0] += per
        if dry:
            return DUMMY
        return nc.alloc_sbuf_tensor_at(name, list(shape), dtype, offset=o)

    if dry:
        XRES = H = Y = RING = DUMMY
    else:
        XRES = nc.alloc_sbuf_tensor_at("XRES", [128, 8, S], F32, offset=XRES_O)
        H = nc.alloc_sbuf_tensor_at("H", [128, 8, S], BF16, offset=H_O)
        Y = nc.alloc_sbuf_tensor_at("Y", [128, 8, 1024], F32, offset=H_O)
        RING = nc.alloc_sbuf_tensor_at("RING", [128, NSLOT, 8, 128], BF16, offset=RING_O)
    PP = calloc("PP", [128, DEPTH * PPL], F32)
    EXPBM = calloc("EXPBM", [128, 3, 4, 2, 128], BF16)
    PBC = calloc("PBC", [128, 1536], F32)
    WSRAW = calloc("WSRAW", [128, 4, 128], BF16)
    TRILM = calloc("TRILM", [128, 4, 128], BF16)
    WSM = calloc("WSM", [128, 4, 128], BF16)
    ONESB = calloc("ONESB", [128, 128], BF16)
    IDENT = calloc("IDENT", [128, 128], BF16)
    ONESF = calloc("ONESF", [128, 128], F32)
    RSTD = calloc("RSTD", [128, 2, 512], F32)
    SQ = calloc("SQ", [128, 2, 512], BF16)
    HALO = calloc("HALO", [128, 22, 2], F32)
    EPS = calloc("EPS", [128, 2], F32)
    SMALL = calloc("SMALL", [128, 2, 8], F32)
    ST6 = calloc("ST6", [128, 2, 6], F32)
    ARENA_O = (coff[0] + 63) // 64 * 64
    ARENA_SZ = 229344 - ARENA_O
    A = Arena(k, nc, ARENA_O, ARENA_SZ)

    BX = [Buf(f"X{t}") for t in range(4)]
    BH = [Buf(f"H{t}") for t in range(4)]
    BY = [Buf("Y0"), Buf("Y1")]
    for yi, hts in ((0, (0, 1)), (1, (2, 3))):
        for t in hts:
            BY[yi].al.append(BH[t])
            BH[t].al.append(BY[yi])
    RB = [Buf(f"ring{i}") for i in range(NSLOT)]
    BPP, BEXPBM, BPBC, BWSRAW, BTRILM, BWSM, BONES, BHALO, BEPS = [Buf(n) for n in (
        "PP", "EXPBM", "PBC", "WSRAW", "TRILM", "WSM", "ONES", "HALO", "EPS")]
    BRSTD = [Buf("RSTD0"), Buf("RSTD1")]
    BIDENT = Buf("IDENT")
    BSQ = [Buf("SQ0"), Buf("SQ1")]
    BSMALL = [Buf("SM0"), Buf("SM1")]
    if dry:
        PS = [DUMMY] * 8
    else:
        PS = [nc.alloc_psum_tensor(f"ps{i}", [128, 512], F32) for i in range(8)]
    BPS = [Buf(f"ps{i}") for i in range(8)]

    ws = WStream(k, RING, RB, wseq, ap_of)

    class Banks:
        def __init__(self):
            self.rot = list(range(8))
            self.i = 0

        def set(self, lst):
            self.rot = list(lst)
            self.i = 0

        def next(self):
            b = self.rot[self.i % len(self.rot)]
            self.i += 1
            return b

    banks = Banks()

    def mm(out, lhsT, rhs, first, last, reads, psb, inc=None):
        k.op("pe", lambda e: e.matmul(out, lhsT=lhsT, rhs=rhs, start=first, stop=last), reads=reads, writes=[psb],
             inc=last if inc is None else inc)

    def act(out, in_, func, reads, writes, bias=None, scale=None, accum=None):
        kw = {}
        if bias is not None:
            kw["bias"] = bias
        if scale is not None:
            kw["scale"] = scale
        if accum is not None:
            kw["accum_out"] = accum
        k.op("act", lambda e: e.activation(out=out, in_=in_, func=func, **kw), reads=reads, writes=writes)

    def tt_(out, in0, in1, op, reads, writes, eng="dve"):
        k.op(eng, lambda e: e.tensor_tensor(out=out, in0=in0, in1=in1, op=op), reads=reads, writes=writes)

    def tsc(out, in0, s1, s2, op0, op1, reads, writes, eng="dve"):
        if op1 is None:
            k.op(eng, lambda e: e.tensor_scalar(out=out, in0=in0, scalar1=s1, scalar2=None, op0=op0), reads=reads,
                 writes=writes)
        else:
            k.op(eng, lambda e: e.tensor_scalar(out=out, in0=in0, scalar1=s1, scalar2=s2, op0=op0, op1=op1),
                 reads=reads, writes=writes)

    def stt(out, in0, scalar, in1, op0, op1, reads, writes):
        k.op("dve", lambda e: e.scalar_tensor_tensor(out=out, in0=in0, scalar=scalar, in1=in1, op0=op0, op1=op1),
             reads=reads, writes=writes)

    def cpy(out, in_, reads, writes, eng="dve"):
        k.op(eng, lambda e: e.tensor_copy(out=out, in_=in_), reads=reads, writes=writes)

    def recip(out, in_, reads, writes):
        k.op("dve", lambda e: e.reciprocal(out=out, in_=in_), reads=reads, writes=writes)

    def memset(ap, val, writes, eng="dve"):
        k.op(eng, lambda e: e.memset(ap, val), writes=writes)

    dbg_out = []

    def chk(name):
        if stop_after == name:
            raise _Stop()

    def dump(name, ap, shape, dtype, bufs):
        if dry or dbg is None or name not in dbg:
            return
        t = nc.dram_tensor("dbg_" + name, list(shape), dtype, kind="ExternalOutput").ap()
        b = Buf("dbg_" + name)
        k.dma("sp", [(t, ap)], reads=bufs, writes=[b], sembuf=b)
        dbg_out.append(b)

    def proj_fm(tile, kcn, rhs_of, rbufs_of, evac, tiles=range(4)):
        for tt in tiles:
            b = banks.next()
            for kc in range(kcn):
                mm(PS[b][:, :], tile.ap[:, kc, :], rhs_of(kc, tt), kc == 0, kc == kcn - 1,
                   [tile.buf] + rbufs_of(tt), BPS[b])
            evac(tt, PS[b], BPS[b])

    def h_rhs(kc, tt):
        return H[:, kc, ts(tt)]

    def h_bufs(tt):
        return [BH[tt]]

    def rstd_from(psb_idx, ri, scale, epscol):
        act(RSTD[:, ri, :], PS[psb_idx][:, :], AF.Sqrt, [BPS[psb_idx], BEPS], [BRSTD[ri]], bias=EPS[:, epscol:epscol + 1],
            scale=scale)
        recip(RSTD[:, ri, :], RSTD[:, ri, :], [BRSTD[ri]], [BRSTD[ri]])

    nrm_ctr = [0]

    def norm_to_H(gcol, tiles):
        for tt in tiles:
            b = banks.next()
            for kc in range(8):
                si = kc % 2
                act(SQ[:, si, :], XRES[:, kc, ts(tt)], AF.Square, [BX[tt]], [BSQ[si]])
                mm(PS[b][:, :], ONESB[:, :], SQ[:, si, :], kc == 0, kc == 7, [BONES, BSQ[si]], BPS[b], inc=True)
            ri = nrm_ctr[0] % 2
            nrm_ctr[0] += 1
            rstd_from(b, ri, 1.0 / D, 0)
            for kc in range(8):
                stt(H[:, kc, ts(tt)], XRES[:, kc, ts(tt)], PP[:, gcol + kc:gcol + kc + 1], RSTD[:, ri, :], ALU.mult,
                    ALU.mult, [BX[tt], BPP, BRSTD[ri]], [BH[tt]])

    pend = []

    def y_chunk(ps_ap, psbuf, dc, ysel, sbank, first, last):
        ysl = slice(ysel * 512, ysel * 512 + 512)
        act(Y[:, dc, ysl], ps_ap, AF.Copy, [psbuf], [BY[ysel]])
        si = dc % 2
        act(SQ[:, si, :], ps_ap, AF.Square, [psbuf], [BSQ[si]])
        pend.append((sbank, si, first, last))
        if not DEFER_STATS:
            flush_stats()

    def flush_stats(keep=0):
        while len(pend) > keep:
            sbank, si, first, last = pend.pop(0)
            mm(PS[sbank][:, :], ONESB[:, :], SQ[:, si, :], first, last, [BONES, BSQ[si]], BPS[sbank], inc=True)

    def post_update(gcol, tt, ysel, sbank):
        flush_stats()
        ysl = slice(ysel * 512, ysel * 512 + 512)
        ri = nrm_ctr[0] % 2
        nrm_ctr[0] += 1
        rstd_from(sbank, ri, 1.0 / D, 0)
        for dc in range(8):
            tt_(Y[:, dc, ysl], Y[:, dc, ysl], RSTD[:, ri, :], ALU.mult, [BY[ysel], BRSTD[ri]], [BY[ysel]])
            stt(XRES[:, dc, ts(tt)], Y[:, dc, ysl], PP[:, gcol + dc:gcol + dc + 1], XRES[:, dc, ts(tt)], ALU.mult,
                ALU.add, [BY[ysel], BPP, BX[tt]], [BX[tt]])

    if not dry:
        k.dma("sp", [(PP[:, :], T["pp"][:, :])], writes=[BPP])
    memset(ONESB[:, :], 1.0, [BONES])
    memset(ONESF[:, :], 1.0, [BONES])
    memset(EPS[:, 0:1], 1e-6, [BEPS])
    memset(EPS[:, 1:2], 1e-5, [BEPS])
    if not dry:
        k.dma("pool", [(TRILM[:, :, :], T["trilm"][:, :, :])], writes=[BTRILM])
        k.dma("pool", [(IDENT[:, :], T["ident"][:, :])], writes=[BIDENT])
    A.reset()
    AB, BAB = A.alloc("abias", [128, 3072], F32)
    AM, BAM = A.alloc("amask", [128, 3072], F32)
    if not dry:
        k.dma("sp", [(AB[:, :], T["abias"][:, :])], writes=[BAB])
        k.dma("sp", [(AM[:, :], T["amask"][:, :])], writes=[BAM])
    if not dry:
        ebm_flat = EXPBM[:, :, :, :, :].rearrange("p g h c q -> p (g h c q)")
    else:
        ebm_flat = DUMMY
    act(AB[:, :], AB[:, :], AF.Exp, [BAB], [BAB])
    tt_(ebm_flat, AB[:, :], AM[:, :], ALU.mult, [BAB, BAM], [BEXPBM])

    def sublayer_mixer(l):
        pb = l * PPL
        banks.set(range(8))
        if not dry:
            k.dma("sp", [(PBC[:, :], T["pbc"][l, :, :])], writes=[BPBC])
            k.dma("pool", [(WSRAW[:, :, :], T["wsT"][l, :, :, :])], writes=[BWSRAW])
        tt_(WSM[:, :, :], WSRAW[:, :, :], TRILM[:, :, :], ALU.mult, [BWSRAW, BTRILM], [BWSM])
        norm_to_H(pb + 0, range(4))
        dump(f"h1_{l}", H[:, :, :], [128, 8, S], BF16, BH)
        if stop_after == "norm1":
            return True

        A.reset()
        OC, BOC = A.alloc("OC", [128, 2, S], BF16)
        keep_c = A.off
        Ut, BU = A.alloc("U", [128, 2, S], F32)
        qk_o = A.off
        Qb, BQ = A.alloc("Q", [128, 2, S], BF16, nbufs=2)
        Kb, BK = A.alloc("K", [128, 2, S], BF16, nbufs=2)
        end_qk = A.off
        A.reset(qk_o)
        R_, BR = A.alloc("R", [128, S], F32)
        A.reset(end_qk)
        VA, BVA = A.alloc("VA", [128, 2, 16, 2, 128], BF16, nbufs=2)
        NPT = 4
        PT, BPT = A.alloc("PT", [128, NPT, 2, 2, 128], BF16, nbufs=NPT)
        memset(VA[:, :, :, 0, 64:128], 1.0, BVA)
        memset(VA[:, :, :, 1, 0:64], 1.0, BVA)
        gctr = 0
        for hp in range(2):
            for g in range(3):
                d = DILS[g]
                nb = 16 // d
                pq = gctr % 2
                gctr += 1
                tq = ws.get(("w_in", l, 16 + g * 2 + hp, 0, 8))
                tk = ws.get(("w_in", l, 22 + g * 2 + hp, 0, 8))
                tv = ws.get(("w_in", l, 28 + g * 2 + hp, 0, 8))
                for (tile, dst, bdst) in ((tq, Qb, BQ[pq]), (tk, Kb, BK[pq])):
                    def ev(tt, ps, psb, dst=dst, bdst=bdst, pq=pq):
                        act(dst[:, pq, ts(tt)], ps[:, :], AF.Copy, [psb], [bdst])
                    proj_fm(tile, 8, h_rhs, h_bufs, ev)
                    ws.done(tile)
                    chk("Cq")
                chk("Cqk")

                def tokslice(j, d=d, nb=nb):
                    c, n = divmod(j, nb)
                    st = c + d * 128 * n
                    return slice(st, st + d * 127 + 1, d), n

                for j0 in range(0, 16, 4):
                    b = banks.next()
                    for jj in range(4):
                        tok, n = tokslice(j0 + jj)
                        for kc in range(8):
                            mm(PS[b][:, jj * 128:(jj + 1) * 128], H[:, kc, tok], tv.ap[:, kc, :], kc == 0, kc == 7,
                               [tv.buf] + BH, BPS[b], inc=(kc == 7 and jj == 3))
                    psv = PS[b][:, :].rearrange("p (j c) -> p j c", c=128)
                    act(VA[:, pq, j0:j0 + 4, 0, 0:64], psv[:, :, 0:64], AF.Copy, [BPS[b]], [BVA[pq]])
                    act(VA[:, pq, j0:j0 + 4, 1, 64:128], psv[:, :, 64:128], AF.Copy, [BPS[b]], [BVA[pq]])
                ws.done(tv)
                chk("Cv")

                def stage1(j):
                    tok, n = tokslice(j)
                    pcs = (0, 1) if n > 0 else (1,)
                    eb = j % NPT
                    for h in range(2):
                        b = banks.next()
                        pr = slice(64 * h, 64 * h + 64)
                        for pc in pcs:
                            ktok, _ = tokslice(j - 1 if pc == 0 else j)
                            mm(PS[b][:, pc * 128:(pc + 1) * 128], Kb[pr, pq, ktok], Qb[pr, pq, tok], True, True,
                               [BQ[pq], BK[pq]], BPS[b], inc=(pc == 1))
                        psv = PS[b][:, 0:256].rearrange("p (c q) -> p c q", c=2)
                        if n > 0:
                            p_ap, s_ap = PT[:, eb, h], psv
                        else:
                            p_ap, s_ap = PT[:, eb, h, 1, :], psv[:, 1, :]
                        act(p_ap, s_ap, AF.Exp, [BPS[b]], [BPT[eb]], scale=0.125)
                    if n > 0:
                        p_ap, m_ap = PT[:, eb], EXPBM[:, g, hp * 2:hp * 2 + 2]
                    else:
                        p_ap, m_ap = PT[:, eb, :, 1, :], EXPBM[:, g, hp * 2:hp * 2 + 2, 1, :]
                    tt_(p_ap, p_ap, m_ap, ALU.mult, [BPT[eb], BEXPBM], [BPT[eb]])

                def stage2(j):
                    tok, n = tokslice(j)
                    pcs = (0, 1) if n > 0 else (1,)
                    eb = j % NPT
                    b2 = banks.next()
                    for h in range(2):
                        for i, pc in enumerate(pcs):
                            vj = j - 1 if pc == 0 else j
                            mm(PS[b2][:, h * 128:(h + 1) * 128], VA[:, pq, vj, h, :], PT[:, eb, h, pc, :], i == 0,
                               i == len(pcs) - 1, [BVA[pq], BPT[eb]], BPS[b2], inc=(h == 1 and i == len(pcs) - 1))
                    psu = PS[b2][:, 0:256].rearrange("p (h q) -> p h q", h=2)
                    if g == 0:
                        act(Ut[:, :, tok], psu, AF.Copy, [BPS[b2]], [BU])
                    else:
                        tt_(Ut[:, :, tok], psu, Ut[:, :, tok], ALU.add, [BPS[b2], BU], [BU])

                DEPTH_C = 2
                for j in range(16):
                    stage1(j)
                    if j >= DEPTH_C:
                        stage2(j - DEPTH_C)
                for j in range(16 - DEPTH_C, 16):
                    stage2(j)
            recip(R_[0:64, :], Ut[64:128, 0, :], [BU], [BR])
            tt_(OC[0:64, hp, :], Ut[0:64, 0, :], R_[0:64, :], ALU.mult, [BU, BR], [BOC])
            recip(R_[64:128, :], Ut[0:64, 1, :], [BU], [BR])
            tt_(OC[64:128, hp, :], Ut[64:128, 1, :], R_[64:128, :], ALU.mult, [BU, BR], [BOC])
        dump(f"oc{l}", OC[:, :, :], [128, 2, S], BF16, [BOC])
        if stop_after == "C":
            return True

        A.reset(keep_c)
        AACT, BAACT = A.alloc("AACT", [128, 4, S], BF16)
        keep_a = A.off
        APAD, BAPAD = A.alloc("APAD", [128, 4, 32 + S], BF16)
        tmp_o = A.off
        SG, BSG = A.alloc("SG", [128, 2, 512], F32, nbufs=2)
        A.reset(tmp_o)
        CT, BCT = A.alloc("CT", [128, 4, 1024], F32, nbufs=4)
        NDG = 8
        DG, BDG = A.alloc("DG", [128, NDG, 128], BF16, nbufs=NDG)
        SQF, BSQF = A.alloc("SQF", [128, 2, 512], F32, nbufs=2)
        MEAN, BMEAN = A.alloc("MEAN", [128, 512], F32)
        VAR, BVAR = A.alloc("VAR", [128, 512], F32)
        memset(APAD[:, :, 0:32], 0.0, [BAPAD])
        for c in range(4):
            tval = ws.get(("w_in", l, c, 0, 8))
            tgate = ws.get(("w_in", l, 4 + c, 0, 8))
            for tt in range(4):
                bv = banks.next()
                bg = banks.next()
                for kc in range(8):
                    mm(PS[bv][:, :], tval.ap[:, kc, :], H[:, kc, ts(tt)], kc == 0, kc == 7, [tval.buf, BH[tt]], BPS[bv])
                for kc in range(8):
                    mm(PS[bg][:, :], tgate.ap[:, kc, :], H[:, kc, ts(tt)], kc == 0, kc == 7, [tgate.buf, BH[tt]],
                       BPS[bg])
                si = tt % 2
                act(SG[:, si, :], PS[bg][:, :], AF.Sigmoid, [BPS[bg]], [BSG[si]])
                tt_(APAD[:, c, 32 + tt * 512:32 + (tt + 1) * 512], PS[bv][:, :], SG[:, si, :], ALU.mult,
                    [BPS[bv], BSG[si]], [BAPAD])
            ws.done(tval)
            ws.done(tgate)
        dump(f"glu{l}", APAD[:, :, :], [128, 4, 32 + S], BF16, [BAPAD])
        cw = pb + 80
        dctr = 0
        for hf in range(2):
            for c in range(4):
                b0 = banks.next()
                b1 = banks.next()
                for kk in range(31):
                    sl = dctr % NDG
                    dctr += 1
                    wcol = PP[:, cw + kk * 4 + c:cw + kk * 4 + c + 1]
                    if kk % 2 == 0:
                        act(DG[:, sl, :], IDENT[:, :], AF.Identity, [BIDENT, BPP], [BDG[sl]], scale=wcol)
                    else:
                        tsc(DG[:, sl, :], IDENT[:, :], wcol, None, ALU.mult, None, [BIDENT, BPP], [BDG[sl]])
                    for t2, bb in enumerate((b0, b1)):
                        base = 2 + (hf * 2 + t2) * 512
                        mm(PS[bb][:, :], DG[:, sl, :], APAD[:, c, base + kk:base + kk + 512], kk == 0, kk == 30,
                           [BDG[sl], BAPAD], BPS[bb], inc=True)
                for t2, bb in enumerate((b0, b1)):
                    act(CT[:, c, ts(t2)], PS[bb][:, :], AF.Identity, [BPS[bb], BPP], [BCT[c]],
                        bias=PP[:, pb + 204 + c:pb + 205 + c])
            for t2 in range(2):
                tt = hf * 2 + t2
                bsum = banks.next()
                bsq = banks.next()
                for c in range(4):
                    o = CT[:, c, ts(t2)]
                    si = c % 2
                    act(SQF[:, si, :], o, AF.Square, [BCT[c]], [BSQF[si]])
                    mm(PS[bsum][:, :], ONESF[:, :], o, c == 0, c == 3, [BONES, BCT[c]], BPS[bsum], inc=True)
                    mm(PS[bsq][:, :], ONESF[:, :], SQF[:, si, :], c == 0, c == 3, [BONES, BSQF[si]], BPS[bsq], inc=True)
                act(MEAN[:, :], PS[bsum][:, :], AF.Identity, [BPS[bsum]], [BMEAN], scale=1.0 / 512)
                tt_(VAR[:, :], MEAN[:, :], MEAN[:, :], ALU.mult, [BMEAN], [BVAR])
                stt(VAR[:, :], PS[bsq][:, :], 1.0 / 512, VAR[:, :], ALU.mult, ALU.subtract, [BPS[bsq], BVAR], [BVAR])
                act(VAR[:, :], VAR[:, :], AF.Sqrt, [BVAR, BEPS], [BVAR], bias=EPS[:, 1:2])
                recip(VAR[:, :], VAR[:, :], [BVAR], [BVAR])
                for c in range(4):
                    tt_(CT[:, c, ts(t2)], CT[:, c, ts(t2)], MEAN[:, :], ALU.subtract, [BCT[c], BMEAN], [BCT[c]])
                for c in range(4):
                    tt_(CT[:, c, ts(t2)], CT[:, c, ts(t2)], VAR[:, :], ALU.mult, [BCT[c], BVAR], [BCT[c]])
                for c in range(4):
                    act(AACT[:, c, ts(tt)], CT[:, c, ts(t2)], AF.Silu, [BCT[c], BPP], [BAACT],
                        bias=PP[:, pb + 212 + c:pb + 213 + c], scale=PP[:, pb + 208 + c:pb + 209 + c])
        dump(f"aact{l}", AACT[:, :, :], [128, 4, S], BF16, [BAACT])
        if stop_after == "A":
            return True

        A.reset(keep_a)
        UB, BUB = A.alloc("UB", [128, 4, S], BF16)
        keep_b = A.off
        UU, BUU = A.alloc("UU", [128, 4, S], BF16)
        VG, BVG = A.alloc("VG", [128, 2, 512], F32, nbufs=2)
        VL, BVL = A.alloc("VL", [128, 2, 512], BF16, nbufs=2)
        TB, BTB = A.alloc("TB", [128, 512], F32)
        CB, BCB = A.alloc("CB", [128, 512], F32)
        brw = banks.next()
        for g in range(4):
            mm(PS[brw][:, g * 128:(g + 1) * 128], ONESB[:, :], WSM[:, g, :], True, True, [BONES, BWSM], BPS[brw],
               inc=(g == 3))
        for g in range(4):
            stt(CB[:, g * 128:(g + 1) * 128], PS[brw][:, g * 128:(g + 1) * 128], PP[:, pb + 308 + g:pb + 309 + g],
                PBC[:, 1024 + g * 128:1024 + (g + 1) * 128], ALU.mult, ALU.add, [BPS[brw], BPP, BPBC], [BCB])
        for c in range(4):
            tu = ws.get(("w_in", l, 8 + c, 0, 8))

            def ev(tt, ps, psb, c=c):
                act(UU[:, c, ts(tt)], ps[:, :], AF.Gelu_apprx_tanh, [psb], [BUU])
            proj_fm(tu, 8, h_rhs, h_bufs, ev)
            ws.done(tu)
        tvs = [ws.get(("w_in", l, 12 + cc, 0, 8)) for cc in range(4)]
        for n in range(16):
            b = banks.next()
            tt = n // 4
            for cc in range(4):
                for kc in range(8):
                    mm(PS[b][:, cc * 128:(cc + 1) * 128], H[:, kc, n * 128:(n + 1) * 128], tvs[cc].ap[:, kc, :], kc == 0,
                       kc == 7, [tvs[cc].buf, BH[tt]], BPS[b], inc=(kc == 7 and cc == 3))
            vi = n % 2
            sm = SMALL[:, vi, :]
            act(VG[:, vi, :], PS[b][:, :], AF.Gelu_apprx_tanh, [BPS[b]], [BVG[vi]])
            k.op("dve", lambda e, o=ST6[:, vi, :], i=VG[:, vi, :]: e.bn_stats(out=o, in_=i), reads=[BVG[vi]],
                 writes=[BSMALL[vi]])
            k.op("dve", lambda e, o=SMALL[:, vi, 2:4], i=ST6[:, vi, :]: e.bn_aggr(out=o, in_=i), reads=[BSMALL[vi]],
                 writes=[BSMALL[vi]])
            act(SMALL[:, vi, 5:6], SMALL[:, vi, 3:4], AF.Sqrt, [BSMALL[vi], BEPS], [BSMALL[vi]], bias=EPS[:, 1:2])
            recip(SMALL[:, vi, 6:7], SMALL[:, vi, 5:6], [BSMALL[vi]], [BSMALL[vi]])
            tsc(VL[:, vi, :], VG[:, vi, :], SMALL[:, vi, 2:3], SMALL[:, vi, 6:7], ALU.subtract, ALU.mult,
                [BVG[vi], BSMALL[vi]], [BVL[vi]])
            b2 = banks.next()
            for g in range(4):
                mm(PS[b2][:, g * 128:(g + 1) * 128], VL[:, vi, g * 128:(g + 1) * 128], WSM[:, g, :], True, True,
                   [BVL[vi], BWSM], BPS[b2], inc=(g == 3))
            for g in range(4):
                stt(TB[:, g * 128:(g + 1) * 128], PS[b2][:, g * 128:(g + 1) * 128], PP[:, pb + 304 + g:pb + 305 + g],
                    CB[:, g * 128:(g + 1) * 128], ALU.mult, ALU.add, [BPS[b2], BPP, BCB], [BTB])
            tbv = TB[:, :].rearrange("p (g t) -> p g t", g=4)
            tt_(UB[:, :, n * 128:(n + 1) * 128], tbv, UU[:, :, n * 128:(n + 1) * 128], ALU.mult, [BTB, BUU], [BUB])
        for t_ in tvs:
            ws.done(t_)
        dump(f"ub{l}", UB[:, :, :], [128, 4, S], BF16, [BUB])
        if stop_after == "B":
            return True

        A.reset(keep_b)
        MG, BMG = A.alloc("MG", [128, 8, 1024], BF16)
        SGM, BSGM = A.alloc("SGM", [128, 3, 512], F32)
        M0, BM0 = A.alloc("M0", [128, 512], F32)
        M1, BM1 = A.alloc("M1", [128, 512], F32)
        for hf in range(2):
            banks.set(range(8))
            for dc in range(8):
                wa = ws.get(("w_a", l, dc, 0, 4))
                wb = ws.get(("w_b", l, dc, 0, 4))
                wc = ws.get(("w_c", l, dc, 0, 2))
                wg = [ws.get(("w_in", l, 34 + i * 8 + dc, 0, 8)) for i in range(3)]
                for t2 in range(2):
                    tt = hf * 2 + t2
                    pa, pb_, pc_ = banks.next(), banks.next(), banks.next()
                    pg = [banks.next() for _ in range(3)]
                    for kc in range(4):
                        mm(PS[pa][:, :], wa.ap[:, kc, :], AACT[:, kc, ts(tt)], kc == 0, kc == 3, [wa.buf, BAACT], BPS[pa])
                    for kc in range(4):
                        mm(PS[pb_][:, :], wb.ap[:, kc, :], UB[:, kc, ts(tt)], kc == 0, kc == 3, [wb.buf, BUB], BPS[pb_])
                    for kc in range(2):
                        mm(PS[pc_][:, :], wc.ap[:, kc, :], OC[:, kc, ts(tt)], kc == 0, kc == 1, [wc.buf, BOC], BPS[pc_])
                    for i in range(3):
                        for kc in range(8):
                            mm(PS[pg[i]][:, :], wg[i].ap[:, kc, :], H[:, kc, ts(tt)], kc == 0, kc == 7,
                               [wg[i].buf, BH[tt]], BPS[pg[i]])
                    for i in range(3):
                        col = pb + 56 + i * 8 + dc
                        act(SGM[:, i, :], PS[pg[i]][:, :], AF.Sigmoid, [BPS[pg[i]], BPP], [BSGM], bias=PP[:, col:col + 1])
                    tt_(M0[:, :], PS[pa][:, :], SGM[:, 0, :], ALU.mult, [BPS[pa], BSGM], [BM0])
                    tt_(M1[:, :], PS[pb_][:, :], SGM[:, 1, :], ALU.mult, [BPS[pb_], BSGM], [BM1])
                    tt_(M0[:, :], M0[:, :], M1[:, :], ALU.add, [BM0, BM1], [BM0])
                    tt_(M1[:, :], PS[pc_][:, :], SGM[:, 2, :], ALU.mult, [BPS[pc_], BSGM], [BM1])
                    tt_(MG[:, dc, ts(t2)], M0[:, :], M1[:, :], ALU.add, [BM0, BM1], [BMG])
                for t_ in [wa, wb, wc] + wg:
                    ws.done(t_)
            if dbg is not None:
                dump(f"merged{l}_{hf}", MG[:, :, :], [128, 8, 1024], BF16, [BMG])
            banks.set(range(6))
            wm = [ws.get(("w_mix", l, dc, 0, 8)) for dc in range(8)]
            for t2 in range(2):
                tt = hf * 2 + t2
                sb_ = 6 + t2
                for dc in range(8):
                    b = banks.next()
                    for kc in range(8):
                        mm(PS[b][:, :], wm[dc].ap[:, kc, :], MG[:, kc, ts(t2)], kc == 0, kc == 7, [wm[dc].buf, BMG], BPS[b])
                    flush_stats()
                    y_chunk(PS[b][:, :], BPS[b], dc, hf, sb_, dc == 0, dc == 7)
                post_update(pb + 8, tt, hf, sb_)
            for t_ in wm:
                ws.done(t_)
        return False

    def sublayer_xattn(l, s):
        pb = l * PPL
        banks.set(range(8))
        A.reset()
        MEMS, BMEMS = A.alloc("MEMS", [128, 8, 256], F32)
        KT, BKT = A.alloc("KT", [128, 8, 256], BF16)
        VX, BVX = A.alloc("VX", [128, 2, 1024], BF16)
        QX, BQX = A.alloc("QX", [128, 8, S], BF16)
        PTX, BPTX = A.alloc("PTX", [128, 2, 2, 512], BF16, nbufs=2)
        RD, BRD = A.alloc("RD", [128, 2, 512], F32, nbufs=2)
        memn_o = A.off
        MEMN, BMEMN = A.alloc("MEMN", [128, 8, 256], BF16)
        A.reset(0)
        OX0, BOX0 = A.alloc("OX0", [128, 8, 512], BF16)
        A.reset(memn_o)
        OX1, BOX1 = A.alloc("OX1", [128, 8, 512], BF16)
        OXs, BOXs = [OX0, OX1], [BOX0, BOX1]
        if not dry:
            k.dma("sp", [(MEMS[:, kc, :], T["memT"][s, kc, :, :]) for kc in range(8)], writes=[BMEMS])
        b = banks.next()
        for kc in range(8):
            si = kc % 2
            act(SQ[:, si, 0:256], MEMS[:, kc, :], AF.Square, [BMEMS], [BSQ[si]])
            mm(PS[b][:, 0:256], ONESB[:, :], SQ[:, si, 0:256], kc == 0, kc == 7, [BONES, BSQ[si]], BPS[b], inc=True)
        ri = nrm_ctr[0] % 2
        nrm_ctr[0] += 1
        act(RSTD[:, ri, 0:256], PS[b][:, 0:256], AF.Sqrt, [BPS[b], BEPS], [BRSTD[ri]], bias=EPS[:, 0:1], scale=1.0 / D)
        recip(RSTD[:, ri, 0:256], RSTD[:, ri, 0:256], [BRSTD[ri]], [BRSTD[ri]])
        for kc in range(8):
            stt(MEMN[:, kc, :], MEMS[:, kc, :], PP[:, pb + 32 + kc:pb + 33 + kc], RSTD[:, ri, 0:256], ALU.mult, ALU.mult,
                [BMEMS, BPP, BRSTD[ri]], [BMEMN])
        def kproj(ec):
            w = ws.get(("w_xkv", l, ec, 0, 8))
            b = banks.next()
            for kc in range(8):
                mm(PS[b][:, 0:256], w.ap[:, kc, :], MEMN[:, kc, :], kc == 0, kc == 7, [w.buf, BMEMN], BPS[b])
            act(KT[:, ec, :], PS[b][:, 0:256], AF.Copy, [BPS[b]], [BKT])
            ws.done(w)

        def vproj(ec):
            w = ws.get(("w_xkv", l, 8 + ec, 0, 8))
            b = banks.next()
            for mb in range(2):
                for kc in range(8):
                    mm(PS[b][:, mb * 128:(mb + 1) * 128], MEMN[:, kc, mb * 128:(mb + 1) * 128], w.ap[:, kc, :], kc == 0,
                       kc == 7, [w.buf, BMEMN], BPS[b], inc=(kc == 7 and mb == 1))
            psv = PS[b][:, 0:256].rearrange("p (m c) -> p m c", m=2)
            act(VX[:, :, ec * 128:(ec + 1) * 128], psv, AF.Copy, [BPS[b]], [BVX])
            ws.done(w)

        for tt in range(4):
            norm_to_H(pb + 16, (tt,))
            for ec in (2 * tt, 2 * tt + 1):
                kproj(ec)
        for ec in range(8):
            vproj(ec)
        for dc in range(8):
            w = ws.get(("w_xq", l, dc, 0, 8))

            def ev(tt, ps, psb, dc=dc):
                act(QX[:, dc, ts(tt)], ps[:, :], AF.Copy, [psb], [BQX])
            proj_fm(w, 8, h_rhs, h_bufs, ev)
            ws.done(w)
        wo = [ws.get(("w_xo", l, dc, 0, 8)) for dc in range(8)]
        banks.set(range(6))

        def attn_tile(tt):
            OX, BOX = OXs[tt % 2], BOXs[tt % 2]

            def xs1(hd):
                pi = hd % 2
                bs_ = [banks.next(), banks.next()]
                for mb in range(2):
                    for ec in range(2):
                        mm(PS[bs_[mb]][:, :], KT[:, hd * 2 + ec, mb * 128:(mb + 1) * 128], QX[:, hd * 2 + ec, ts(tt)],
                           ec == 0, ec == 1, [BKT, BQX], BPS[bs_[mb]])
                    act(PTX[:, pi, mb, :], PS[bs_[mb]][:, :], AF.Exp, [BPS[bs_[mb]]], [BPTX[pi]], scale=1.0 / 16)

            def xs2(hd):
                pi = hd % 2
                bd = banks.next()
                for mb in range(2):
                    mm(PS[bd][:, :], ONESB[:, :], PTX[:, pi, mb, :], mb == 0, mb == 1, [BONES, BPTX[pi]], BPS[bd])
                recip(RD[:, pi, :], PS[bd][:, :], [BPS[bd]], [BRD[pi]])
                for ec in range(2):
                    bo = banks.next()
                    for mb in range(2):
                        mm(PS[bo][:, :], VX[:, mb, (hd * 2 + ec) * 128:(hd * 2 + ec + 1) * 128], PTX[:, pi, mb, :], mb == 0,
                           mb == 1, [BVX, BPTX[pi]], BPS[bo])
                    tt_(OX[:, hd * 2 + ec, :], PS[bo][:, :], RD[:, pi, :], ALU.mult, [BPS[bo], BRD[pi]], [BOX])

            for hd in range(4):
                xs1(hd)
                if hd > 0:
                    xs2(hd - 1)
            xs2(3)

        def xo_tile(tt):
            OX, BOX = OXs[tt % 2], BOXs[tt % 2]
            ysel = tt % 2
            sb_ = 6 + (tt % 2)
            for dc in range(8):
                b = banks.next()
                for kc in range(8):
                    mm(PS[b][:, :], wo[dc].ap[:, kc, :], OX[:, kc, :], kc == 0, kc == 7, [wo[dc].buf, BOX], BPS[b])
                flush_stats()
                y_chunk(PS[b][:, :], BPS[b], dc, ysel, sb_, dc == 0, dc == 7)
            flush_stats()
            if not XATTN_DEFER_POST:
                post_update(pb + 24, tt, ysel, sb_)

        def xpost(tt):
            post_update(pb + 24, tt, tt % 2, 6 + (tt % 2))

        if XATTN_DEFER_POST:
            for tt in range(4):
                attn_tile(tt)
                if tt > 0:
                    xpost(tt - 1)
                xo_tile(tt)
            xpost(3)
        elif XATTN_PIPE:
            for tt in range(4):
                attn_tile(tt)
                if tt > 0:
                    xo_tile(tt - 1)
            xo_tile(3)
        else:
            for tt in range(4):
                attn_tile(tt)
                xo_tile(tt)
        for t_ in wo:
            ws.done(t_)

    def sublayer_ffn(l, tile_done=None):
        pb = l * PPL
        A.reset()
        ACTB, BACTB = A.alloc("ACTB", [128, 22, 1024], BF16)
        GT, BGT = A.alloc("GT", [128, 2, 516], F32, nbufs=2)
        CV, BCV = A.alloc("CV", [128, 512], F32)
        GL, BGL = A.alloc("GL", [128, 2, 512], F32, nbufs=2)
        memset(HALO[:, :, :], 0.0, [BHALO])
        fw = pb + 216
        gi = 0
        for hf in range(2):
            banks.set(range(8))
            norm_to_H(pb + 40, (2 * hf, 2 * hf + 1))
            for j in range(22):
                wg = ws.get(("w_up", l, j, 0, 8))
                wv = ws.get(("w_up", l, 22 + j, 0, 8))
                for t2 in range(2):
                    tt = hf * 2 + t2
                    bg, bv = banks.next(), banks.next()
                    for kc in range(8):
                        mm(PS[bg][:, :], wg.ap[:, kc, :], H[:, kc, ts(tt)], kc == 0, kc == 7, [wg.buf, BH[tt]], BPS[bg])
                    for kc in range(8):
                        mm(PS[bv][:, :], wv.ap[:, kc, :], H[:, kc, ts(tt)], kc == 0, kc == 7, [wv.buf, BH[tt]], BPS[bv])
                    g2 = gi % 2
                    gi += 1
                    act(GT[:, g2, 2:514], PS[bg][:, :], AF.Copy, [BPS[bg]], [BGT[g2]])
                    cpy(GT[:, g2, 0:2], HALO[:, j, :], [BHALO], [BGT[g2]])
                    cpy(HALO[:, j, :], GT[:, g2, 512:514], [BGT[g2]], [BHALO])
                    act(CV[:, :], PS[bg][:, :], AF.Identity, [BPS[bg], BPP], [BCV], bias=PP[:, pb + 282 + j:pb + 283 + j],
                        scale=PP[:, fw + 44 + j:fw + 45 + j])
                    stt(CV[:, :], GT[:, g2, 0:512], PP[:, fw + j:fw + j + 1], CV[:, :], ALU.mult, ALU.add,
                        [BGT[g2], BPP, BCV], [BCV])
                    stt(CV[:, :], GT[:, g2, 1:513], PP[:, fw + 22 + j:fw + 23 + j], CV[:, :], ALU.mult, ALU.add,
                        [BGT[g2], BPP, BCV], [BCV])
                    act(GL[:, g2, :], CV[:, :], AF.Gelu_apprx_tanh, [BCV], [BGL[g2]])
                    tt_(ACTB[:, j, ts(t2)], PS[bv][:, :], GL[:, g2, :], ALU.mult, [BPS[bv], BGL[g2]], [BACTB])
                ws.done(wg)
                ws.done(wv)
            banks.set(range(6))
            ysels = (1, 0) if hf == 0 else (0, 1)
            for dc in range(8):
                wd = [ws.get(("w_down", l, dc, 0, 8)), ws.get(("w_down", l, dc, 8, 8)), ws.get(("w_down", l, dc, 16, 6))]
                for t2 in range(2):
                    b = banks.next()
                    for kc in range(22):
                        w = wd[kc // 8]
                        mm(PS[b][:, :], w.ap[:, kc % 8, :], ACTB[:, kc, ts(t2)], kc == 0, kc == 21, [w.buf, BACTB], BPS[b])
                    flush_stats()
                    y_chunk(PS[b][:, :], BPS[b], dc, ysels[t2], 6 + t2, dc == 0, dc == 7)
                for w in wd:
                    ws.done(w)
            for t2 in range(2):
                tt = hf * 2 + t2
                post_update(pb + 48, tt, ysels[t2], 6 + t2)
                if tile_done is not None:
                    tile_done(tt)

    stopped = False

    def load_x(s_, tt):
        if not dry:
            k.dma("sp", [(XRES[:, kc, ts(tt)], T["xT"][s_, kc, :, ts(tt)]) for kc in range(8)], writes=[BX[tt]])

    def store_x(s_, tt):
        if not dry:
            k.dma("sp", [(T["outT"][s_, kc, :, ts(tt)], XRES[:, kc, ts(tt)]) for kc in range(8)], reads=[BX[tt]],
                  sembuf=BX[tt])

    try:
        for tt in range(4):
            load_x(0, tt)
        for s in range(nseq):
            def tile_done(tt, s=s):
                store_x(s, tt)
                if s + 1 < nseq:
                    load_x(s + 1, tt)
            for l in range(nlayer):
                stopped = sublayer_mixer(l)
                if stopped:
                    break
                dump(f"x1_{l}", XRES[:, :, :], [128, 8, S], F32, BX)
                if stop_after == "mixer":
                    stopped = True
                    break
                sublayer_xattn(l, s)
                dump(f"x2_{l}", XRES[:, :, :], [128, 8, S], F32, BX)
                if stop_after == "xattn":
                    stopped = True
                    break
                sublayer_ffn(l, tile_done if l == nlayer - 1 else None)
            if stopped:
                for tt in range(4):
                    store_x(s, tt)
                break
    except _Stop:
        for tt in range(4):
            store_x(0, tt)
    k.wait_all("sp", BX + dbg_out)
    return ws


def make_program(nseq=SEQ_PER_CORE, nlayer=DEPTH, stop_after=None, dbg=None):
    kd = K(None, dry=True)
    wsd = build_program(None, kd, None, nseq, nlayer, stop_after, dbg)
    seq = wsd.rec
    nc = bass.Bass("TRN2", target_bir_lowering=False)
    k = K(nc)
    ws = build_program(nc, k, seq, nseq, nlayer, stop_after, dbg)
    assert ws.idx == len(seq) and ws.loaded == len(seq)
    return nc, k


def _t5_bucket(n):
    n = np.maximum(n, 0)
    nf = np.maximum(n, 1).astype(np.float32)
    large = 16 + (np.log(nf / np.float32(16)) / np.float32(np.log(2048 / 16)) * np.float32(16)).astype(np.int32)
    large = np.minimum(large, 31)
    return np.where(n < 16, n, large)


def _wtiles(W):
    L, Kd, N = W.shape
    return np.ascontiguousarray(W.reshape(L, Kd // 128, 128, N // 128, 128).transpose(0, 3, 2, 1, 4))


def _cols(v):
    v = np.asarray(v)
    C = v.shape[-1]
    lead = v.shape[:-1]
    a = v.reshape(*lead, C // 128, 128)
    a = np.moveaxis(a, -1, 0)
    return a.reshape(128, -1)


def prep_shared(inp):
    L = DEPTH
    pp = np.zeros((128, L * PPL), np.float32)
    for l in range(L):
        o = l * PPL
        pp[:, o + 0:o + 8] = _cols(inp["mix_pre_g"][l])
        pp[:, o + 8:o + 16] = _cols(inp["mix_post_g"][l])
        pp[:, o + 16:o + 24] = _cols(inp["x_pre_g"][l])
        pp[:, o + 24:o + 32] = _cols(inp["x_post_g"][l])
        pp[:, o + 32:o + 40] = _cols(inp["mem_g"][l])
        pp[:, o + 40:o + 48] = _cols(inp["ffn_pre_g"][l])
        pp[:, o + 48:o + 56] = _cols(inp["ffn_post_g"][l])
        pp[:, o + 56:o + 80] = _cols(inp["b_gate"][l])
        pp[:, o + 80:o + 204] = _cols(inp["conv_a_w"][l])
        pp[:, o + 204:o + 208] = _cols(inp["conv_a_b"][l])
        pp[:, o + 208:o + 212] = _cols(inp["ln_a_g"][l])
        pp[:, o + 212:o + 216] = _cols(inp["ln_a_b"][l])
        pp[:, o + 216:o + 282] = _cols(inp["conv_f_w"][l])
        pp[:, o + 282:o + 304] = _cols(inp["conv_f_b"][l])
        pp[:, o + 304:o + 308] = _cols(inp["ln_b_g"][l])
        pp[:, o + 308:o + 312] = _cols(inp["ln_b_b"][l])
    rb = np.asarray(inp["rel_bias"])
    kk = np.arange(128)[:, None]
    qq = np.arange(128)[None, :]
    abias = np.zeros((128, 3, 4, 2, 128), np.float32)
    amask = np.zeros((128, 3, 4, 2, 128), np.float32)
    for g, dil in enumerate(DILS):
        for pc in range(2):
            rel = qq + 128 - kk if pc == 0 else qq - kk
            valid = (rel >= 0) & (rel <= 128)
            bucket = _t5_bucket(rel * dil)
            for h in range(4):
                abias[:, g, h, pc, :] = rb[bucket, g * 4 + h]
                amask[:, g, h, pc, :] = valid
    pbc = np.zeros((L, 128, 1536), np.float32)
    for l in range(L):
        pbc[l, :, 0:512] = np.broadcast_to(inp["ln_b_g"][l][None, :], (128, 512))
        pbc[l, :, 512:1024] = np.broadcast_to(inp["ln_b_b"][l][None, :], (128, 512))
        pbc[l, :, 1024:1536] = np.broadcast_to(np.asarray(inp["b_s"][l]).reshape(1, 512), (128, 512))
    wsT = np.ascontiguousarray(np.asarray(inp["w_s"]).transpose(0, 3, 1, 2))
    tril = (np.arange(128)[None, :] >= np.arange(128)[:, None]).astype(np.float32)
    trilm = np.ascontiguousarray(np.broadcast_to(tril[:, None, :], (128, 4, 128)))
    w_c = np.asarray(inp["w_c_out"])
    shared = {
        "pp": pp,
        "abias": abias.reshape(128, 3072),
        "amask": amask.reshape(128, 3072),
        "pbc": pbc,
        "wsT": wsT,
        "trilm": trilm,
        "ident": np.eye(128, dtype=np.float32),
        "w_in": _wtiles(np.asarray(inp["w_in"])),
        "w_a": _wtiles(np.asarray(inp["w_a_out"])),
        "w_b": _wtiles(np.asarray(inp["w_b_out"])),
        "w_c": _wtiles(w_c),
        "w_mix": _wtiles(np.asarray(inp["w_mix_out"])),
        "w_xq": _wtiles(np.asarray(inp["w_xq"])),
        "w_xkv": _wtiles(np.asarray(inp["w_xkv"])),
        "w_xo": _wtiles(np.asarray(inp["w_xo"])),
        "w_up": _wtiles(np.asarray(inp["w_up"])),
        "w_down": _wtiles(np.asarray(inp["w_down"])),
    }
    return shared


def prep_core(inp, c, nseq=SEQ_PER_CORE):
    xs = np.asarray(inp["x"][c * SEQ_PER_CORE:c * SEQ_PER_CORE + nseq])
    ms = np.asarray(inp["mem"][c * SEQ_PER_CORE:c * SEQ_PER_CORE + nseq])
    xT = np.zeros((SEQ_PER_CORE, 8, 128, S), np.float32)
    mT = np.zeros((SEQ_PER_CORE, 8, 128, 256), np.float32)
    xT[:nseq] = xs.transpose(0, 2, 1).reshape(nseq, 8, 128, S)
    mT[:nseq] = ms.transpose(0, 2, 1).reshape(nseq, 8, 128, 256)
    return {"xT": xT, "memT": mT}


_PROG = {}


def kernel(**inputs):
    if "p" not in _PROG:
        _PROG["p"] = make_program()
    nc, _ = _PROG["p"]
    shared = prep_shared(inputs)
    in_maps = []
    for c in range(N_CORES):
        m = dict(shared)
        m.update(prep_core(inputs, c))
        in_maps.append(m)
    res = run_bass_kernel_spmd(nc, in_maps, core_ids=list(range(N_CORES)))
    out = np.empty((N_CORES * SEQ_PER_CORE, S, D), np.float32)
    for c in range(N_CORES):
        oT = res.results[c]["outT"]
        out[c * SEQ_PER_CORE:(c + 1) * SEQ_PER_CORE] = oT.reshape(SEQ_PER_CORE, D, S).transpose(0, 2, 1)
    return out
```

```python
import numpy as np
import concourse.bass as bass
import concourse.mybir as mybir
from concourse.bass_utils import run_bass_kernel_spmd

F32 = mybir.dt.float32
BF16 = mybir.dt.bfloat16
AF = mybir.ActivationFunctionType
ALU = mybir.AluOpType
AX = mybir.AxisListType

EPOCH = 12000
SAME_SYNC = True

N_CORES = 8
SEQ_PER_CORE = 4
DEPTH = 2
D = 1024
S = 2048
NSLOT = 10
XATTN_PIPE = False
XATTN_DEFER_POST = False
DEFER_STATS = True
PPL = 312
DILS = (1, 4, 16)


class Buf:
    __slots__ = ("name", "w", "r", "al", "dsem", "dcnt")

    def __init__(self, name):
        self.name = name
        self.w = None
        self.r = {}
        self.al = []
        self.dsem = None
        self.dcnt = 0


class Dummy:
    shape = ()

    def __getitem__(self, i):
        return self

    def rearrange(self, *a, **k):
        return self


DUMMY = Dummy()


class Eng:
    def __init__(self, K, name, obj):
        self.name = name
        self.obj = obj
        self.sem = None if K.dry else K.nc.alloc_semaphore(f"s_{name}_0")
        self.nsem = 1
        self.cnt = 0
        self.own = set() if K.dry else {id(self.sem)}
        self.waited = {}
        self.nwaits = 0
        self.nins = 0


class K:
    def __init__(self, nc, dry=False):
        self.nc = nc
        self.dry = dry
        if dry:
            self.engs = {n: Eng(self, n, None) for n in ("pe", "act", "dve", "pool", "sp")}
        else:
            self.engs = {
                "pe": Eng(self, "pe", nc.tensor),
                "act": Eng(self, "act", nc.scalar),
                "dve": Eng(self, "dve", nc.vector),
                "pool": Eng(self, "pool", nc.gpsimd),
                "sp": Eng(self, "sp", nc.sync),
            }
        self.semobj = {}
        if not dry:
            for e in self.engs.values():
                self.semobj[id(e.sem)] = e.sem
        self.ndsem = 0

    def _deps(self, reads, writes):
        deps = {}

        def add(d):
            if d is None:
                return
            s, v = d
            if deps.get(s, 0) < v:
                deps[s] = v

        for b in reads:
            add(b.w)
            for a in b.al:
                add(a.w)
        for b in writes:
            add(b.w)
            for d in b.r.values():
                add(d)
            for a in b.al:
                add(a.w)
                for d in a.r.values():
                    add(d)
        return deps

    def _wait(self, E, deps):
        for s, v in deps.items():
            if s in E.own:
                if E.name in ("pe", "sp") or not SAME_SYNC:
                    continue
                if s == id(E.sem) and v > E.cnt:
                    continue
                if E.name in ("act", "dve") and (s != id(E.sem) or v < E.cnt):
                    continue
            if E.waited.get(s, 0) >= v:
                continue
            E.obj.wait_ge(self.semobj[s], v)
            E.waited[s] = v
            E.nwaits += 1

    def _tag(self, E, ins, inc):
        E.nins += 1
        if inc:
            ins.then_inc(E.sem, 1)
            E.cnt += 1
            tag = (id(E.sem), E.cnt)
            if E.cnt >= EPOCH:
                E.sem = self.nc.alloc_semaphore(f"s_{E.name}_{E.nsem}")
                E.nsem += 1
                E.cnt = 0
                E.own.add(id(E.sem))
                self.semobj[id(E.sem)] = E.sem
        else:
            tag = (id(E.sem), E.cnt + 1)
        return tag

    def op(self, eng, fn, reads=(), writes=(), inc=True):
        if self.dry:
            return None
        E = self.engs[eng]
        self._wait(E, self._deps(reads, writes))
        ins = fn(E.obj)
        tag = self._tag(E, ins, inc)
        for b in writes:
            b.w = tag
            b.r = {}
        for b in reads:
            b.r[eng] = tag
        return ins

    def dma(self, eng, pairs, reads=(), writes=(), sembuf=None):
        if self.dry:
            return None
        E = self.engs[eng]
        sb = sembuf or (writes[0] if writes else reads[0])
        if sb.dsem is None:
            sb.dsem = self.nc.alloc_semaphore(f"d_{self.ndsem}")
            self.ndsem += 1
            self.semobj[id(sb.dsem)] = sb.dsem
        deps = self._deps(reads, writes)
        if sb.dcnt:
            s = id(sb.dsem)
            if deps.get(s, 0) < sb.dcnt:
                deps[s] = sb.dcnt
        self._wait(E, deps)
        for (o, i) in pairs:
            E.obj.dma_start(out=o, in_=i).then_inc(sb.dsem, 16)
            E.nins += 1
            sb.dcnt += 16
        tag = (id(sb.dsem), sb.dcnt)
        for b in writes:
            b.w = tag
            b.r = {}
        for b in reads:
            b.r["dma_" + sb.name] = tag
        return tag

    def wait_all(self, eng, bufs):
        if self.dry:
            return
        E = self.engs[eng]
        self._wait(E, self._deps(bufs, bufs))

    def stats(self):
        return {e.name: (e.nins, e.nwaits, e.nsem) for e in self.engs.values()}


class _Stop(Exception):
    pass


class Tile:
    __slots__ = ("i", "ap", "buf")

    def __init__(self, i, ap, buf):
        self.i = i
        self.ap = ap
        self.buf = buf


class WStream:
    def __init__(self, k, ring, rbufs, seq, ap_of):
        self.k = k
        self.ring = ring
        self.rbufs = rbufs
        self.seq = seq
        self.ap_of = ap_of
        self.rec = []
        self.idx = 0
        self.loaded = 0
        self.done_ = set()

    def _issue(self):
        while self.loaded < len(self.seq):
            i = self.loaded
            if i - NSLOT >= 0 and (i - NSLOT) not in self.done_:
                break
            slot = i % NSLOT
            src, kcn = self.ap_of(self.seq[i])
            self.k.dma("pool", [(self.ring[:, slot, 0:kcn, :], src)], writes=[self.rbufs[slot]])
            self.loaded += 1

    def get(self, desc):
        i = self.idx
        self.idx += 1
        if self.k.dry:
            self.rec.append(desc)
            return Tile(i, DUMMY, None)
        assert self.seq[i] == desc, (i, self.seq[i], desc)
        self._issue()
        assert self.loaded > i, f"weight ring too small at tile {i} {desc}"
        slot = i % NSLOT
        return Tile(i, self.ring[:, slot], self.rbufs[slot])

    def done(self, t):
        if self.k.dry:
            return
        self.done_.add(t.i)
        self._issue()


class Arena:
    def __init__(self, k, nc, base, size):
        self.k, self.nc, self.base, self.size = k, nc, base, size
        self.off = 0
        self.regs = []
        self.n = 0

    def reset(self, off=0):
        self.off = off

    def alloc(self, name, shape, dtype, nbufs=1):
        esz = 2 if dtype == BF16 else 4
        per = esz
        for s_ in shape[1:]:
            per *= s_
        per = (per + 63) // 64 * 64
        lo = self.off
        hi = lo + per
        assert hi <= self.size, f"arena overflow {name}: {hi} > {self.size}"
        self.off = hi
        bufs = [Buf(f"{name}{i}") for i in range(nbufs)]
        pbuf = per // nbufs
        new = []
        for bi, b in enumerate(bufs):
            blo, bhi = lo + bi * pbuf, (lo + (bi + 1) * pbuf if bi < nbufs - 1 else hi)
            for (l2, h2, b2) in self.regs:
                if l2 < bhi and blo < h2:
                    if b2 not in b.al:
                        b.al.append(b2)
                    if b not in b2.al:
                        b2.al.append(b)
            new.append((blo, bhi, b))
        self.regs.extend(new)
        if self.k.dry:
            t = DUMMY
        else:
            self.n += 1
            t = self.nc.alloc_sbuf_tensor_at(f"ar{self.n}_{name}", list(shape), dtype, offset=self.base + lo)
        return (t, bufs[0]) if nbufs == 1 else (t, bufs)


def ts(tt, n=512):
    return slice(tt * n, (tt + 1) * n)


def build_program(nc, k, wseq, nseq=SEQ_PER_CORE, nlayer=DEPTH, stop_after=None, dbg=None):
    dry = k.dry
    T = {}
    if not dry:
        def dram(name, shape, kind="ExternalInput"):
            T[name] = nc.dram_tensor(name, list(shape), F32, kind=kind).ap()
        dram("xT", [SEQ_PER_CORE, 8, 128, S])
        dram("memT", [SEQ_PER_CORE, 8, 128, 256])
        dram("pp", [128, DEPTH * PPL])
        dram("abias", [128, 3072])
        dram("amask", [128, 3072])
        dram("pbc", [DEPTH, 128, 1536])
        dram("wsT", [DEPTH, 128, 4, 128])
        dram("trilm", [128, 4, 128])
        dram("ident", [128, 128])
        dram("w_in", [DEPTH, 58, 128, 8, 128])
        dram("w_a", [DEPTH, 8, 128, 4, 128])
        dram("w_b", [DEPTH, 8, 128, 4, 128])
        dram("w_c", [DEPTH, 8, 128, 2, 128])
        dram("w_mix", [DEPTH, 8, 128, 8, 128])
        dram("w_xq", [DEPTH, 8, 128, 8, 128])
        dram("w_xkv", [DEPTH, 16, 128, 8, 128])
        dram("w_xo", [DEPTH, 8, 128, 8, 128])
        dram("w_up", [DEPTH, 44, 128, 8, 128])
        dram("w_down", [DEPTH, 8, 128, 22, 128])
        dram("outT", [SEQ_PER_CORE, 8, 128, S], kind="ExternalOutput")

    def ap_of(desc):
        name, l, j, k0, kcn = desc
        return T[name][l, j, :, k0:k0 + kcn, :], kcn

    BASE = 16512
    XRES_O = BASE
    H_O = XRES_O + 65536
    RING_O = H_O + 32768
    C_O = RING_O + NSLOT * 2048
    coff = [C_O]

    def calloc(name, shape, dtype):
        esz = 2 if dtype == BF16 else 4
        per = esz
        for s_ in shape[1:]:
            per *= s_
        per = (per + 63) // 64 * 64
        o = coff[0]
        coff[0] += per
        if dry:
            return DUMMY
        return nc.alloc_sbuf_tensor_at(name, list(shape), dtype, offset=o)

    if dry:
        XRES = H = Y = RING = DUMMY
    else:
        XRES = nc.alloc_sbuf_tensor_at("XRES", [128, 8, S], F32, offset=XRES_O)
        H = nc.alloc_sbuf_tensor_at("H", [128, 8, S], BF16, offset=H_O)
        Y = nc.alloc_sbuf_tensor_at("Y", [128, 8, 1024], F32, offset=H_O)
        RING = nc.alloc_sbuf_tensor_at("RING", [128, NSLOT, 8, 128], BF16, offset=RING_O)
    PP = calloc("PP", [128, DEPTH * PPL], F32)
    EXPBM = calloc("EXPBM", [128, 3, 4, 2, 128], BF16)
    PBC = calloc("PBC", [128, 1536], F32)
    WSRAW = calloc("WSRAW", [128, 4, 128], BF16)
    TRILM = calloc("TRILM", [128, 4, 128], BF16)
    WSM = calloc("WSM", [128, 4, 128], BF16)
    ONESB = calloc("ONESB", [128, 128], BF16)
    IDENT = calloc("IDENT", [128, 128], BF16)
    ONESF = calloc("ONESF", [128, 128], F32)
    RSTD = calloc("RSTD", [128, 2, 512], F32)
    SQ = calloc("SQ", [128, 2, 512], BF16)
    HALO = calloc("HALO", [128, 22, 2], F32)
    EPS = calloc("EPS", [128, 2], F32)
    SMALL = calloc("SMALL", [128, 2, 8], F32)
    ST6 = calloc("ST6", [128, 2, 6], F32)
    ARENA_O = (coff[0] + 63) // 64 * 64
    ARENA_SZ = 229344 - ARENA_O
    A = Arena(k, nc, ARENA_O, ARENA_SZ)

    BX = [Buf(f"X{t}") for t in range(4)]
    BH = [Buf(f"H{t}") for t in range(4)]
    BY = [Buf("Y0"), Buf("Y1")]
    for yi, hts in ((0, (0, 1)), (1, (2, 3))):
        for t in hts:
            BY[yi].al.append(BH[t])
            BH[t].al.append(BY[yi])
    RB = [Buf(f"ring{i}") for i in range(NSLOT)]
    BPP, BEXPBM, BPBC, BWSRAW, BTRILM, BWSM, BONES, BHALO, BEPS = [Buf(n) for n in (
        "PP", "EXPBM", "PBC", "WSRAW", "TRILM", "WSM", "ONES", "HALO", "EPS")]
    BRSTD = [Buf("RSTD0"), Buf("RSTD1")]
    BIDENT = Buf("IDENT")
    BSQ = [Buf("SQ0"), Buf("SQ1")]
    BSMALL = [Buf("SM0"), Buf("SM1")]
    if dry:
        PS = [DUMMY] * 8
    else:
        PS = [nc.alloc_psum_tensor(f"ps{i}", [128, 512], F32) for i in range(8)]
    BPS = [Buf(f"ps{i}") for i in range(8)]

    ws = WStream(k, RING, RB, wseq, ap_of)

    class Banks:
        def __init__(self):
            self.rot = list(range(8))
            self.i = 0

        def set(self, lst):
            self.rot = list(lst)
            self.i = 0

        def next(self):
            b = self.rot[self.i % len(self.rot)]
            self.i += 1
            return b

    banks = Banks()

    def mm(out, lhsT, rhs, first, last, reads, psb, inc=None):
        k.op("pe", lambda e: e.matmul(out, lhsT=lhsT, rhs=rhs, start=first, stop=last), reads=reads, writes=[psb],
             inc=last if inc is None else inc)

    def act(out, in_, func, reads, writes, bias=None, scale=None, accum=None):
        kw = {}
        if bias is not None:
            kw["bias"] = bias
        if scale is not None:
            kw["scale"] = scale
        if accum is not None:
            kw["accum_out"] = accum
        k.op("act", lambda e: e.activation(out=out, in_=in_, func=func, **kw), reads=reads, writes=writes)

    def tt_(out, in0, in1, op, reads, writes, eng="dve"):
        k.op(eng, lambda e: e.tensor_tensor(out=out, in0=in0, in1=in1, op=op), reads=reads, writes=writes)

    def tsc(out, in0, s1, s2, op0, op1, reads, writes, eng="dve"):
        if op1 is None:
            k.op(eng, lambda e: e.tensor_scalar(out=out, in0=in0, scalar1=s1, scalar2=None, op0=op0), reads=reads,
                 writes=writes)
        else:
            k.op(eng, lambda e: e.tensor_scalar(out=out, in0=in0, scalar1=s1, scalar2=s2, op0=op0, op1=op1),
                 reads=reads, writes=writes)

    def stt(out, in0, scalar, in1, op0, op1, reads, writes):
        k.op("dve", lambda e: e.scalar_tensor_tensor(out=out, in0=in0, scalar=scalar, in1=in1, op0=op0, op1=op1),
             reads=reads, writes=writes)

    def cpy(out, in_, reads, writes, eng="dve"):
        k.op(eng, lambda e: e.tensor_copy(out=out, in_=in_), reads=reads, writes=writes)

    def recip(out, in_, reads, writes):
        k.op("dve", lambda e: e.reciprocal(out=out, in_=in_), reads=reads, writes=writes)

    def memset(ap, val, writes, eng="dve"):
        k.op(eng, lambda e: e.memset(ap, val), writes=writes)

    dbg_out = []

    def chk(name):
        if stop_after == name:
            raise _Stop()

    def dump(name, ap, shape, dtype, bufs):
        if dry or dbg is None or name not in dbg:
            return
        t = nc.dram_tensor("dbg_" + name, list(shape), dtype, kind="ExternalOutput").ap()
        b = Buf("dbg_" + name)
        k.dma("sp", [(t, ap)], reads=bufs, writes=[b], sembuf=b)
        dbg_out.append(b)

    def proj_fm(tile, kcn, rhs_of, rbufs_of, evac, tiles=range(4)):
        for tt in tiles:
            b = banks.next()
            for kc in range(kcn):
                mm(PS[b][:, :], tile.ap[:, kc, :], rhs_of(kc, tt), kc == 0, kc == kcn - 1,
                   [tile.buf] + rbufs_of(tt), BPS[b])
            evac(tt, PS[b], BPS[b])

    def h_rhs(kc, tt):
        return H[:, kc, ts(tt)]

    def h_bufs(tt):
        return [BH[tt]]

    def rstd_from(psb_idx, ri, scale, epscol):
        act(RSTD[:, ri, :], PS[psb_idx][:, :], AF.Sqrt, [BPS[psb_idx], BEPS], [BRSTD[ri]], bias=EPS[:, epscol:epscol + 1],
            scale=scale)
        recip(RSTD[:, ri, :], RSTD[:, ri, :], [BRSTD[ri]], [BRSTD[ri]])

    nrm_ctr = [0]

    def norm_to_H(gcol, tiles):
        for tt in tiles:
            b = banks.next()
            for kc in range(8):
                si = kc % 2
                act(SQ[:, si, :], XRES[:, kc, ts(tt)], AF.Square, [BX[tt]], [BSQ[si]])
                mm(PS[b][:, :], ONESB[:, :], SQ[:, si, :], kc == 0, kc == 7, [BONES, BSQ[si]], BPS[b], inc=True)
            ri = nrm_ctr[0] % 2
            nrm_ctr[0] += 1
            rstd_from(b, ri, 1.0 / D, 0)
            for kc in range(8):
                stt(H[:, kc, ts(tt)], XRES[:, kc, ts(tt)], PP[:, gcol + kc:gcol + kc + 1], RSTD[:, ri, :], ALU.mult,
                    ALU.mult, [BX[tt], BPP, BRSTD[ri]], [BH[tt]])

    pend = []

    def y_chunk(ps_ap, psbuf, dc, ysel, sbank, first, last):
        ysl = slice(ysel * 512, ysel * 512 + 512)
        act(Y[:, dc, ysl], ps_ap, AF.Copy, [psbuf], [BY[ysel]])
        si = dc % 2
        act(SQ[:, si, :], ps_ap, AF.Square, [psbuf], [BSQ[si]])
        pend.append((sbank, si, first, last))
        if not DEFER_STATS:
            flush_stats()

    def flush_stats(keep=0):
        while len(pend) > keep:
            sbank, si, first, last = pend.pop(0)
            mm(PS[sbank][:, :], ONESB[:, :], SQ[:, si, :], first, last, [BONES, BSQ[si]], BPS[sbank], inc=True)

    def post_update(gcol, tt, ysel, sbank):
        flush_stats()
        ysl = slice(ysel * 512, ysel * 512 + 512)
        ri = nrm_ctr[0] % 2
        nrm_ctr[0] += 1
        rstd_from(sbank, ri, 1.0 / D, 0)
        for dc in range(8):
            tt_(Y[:, dc, ysl], Y[:, dc, ysl], RSTD[:, ri, :], ALU.mult, [BY[ysel], BRSTD[ri]], [BY[ysel]])
            stt(XRES[:, dc, ts(tt)], Y[:, dc, ysl], PP[:, gcol + dc:gcol + dc + 1], XRES[:, dc, ts(tt)], ALU.mult,
                ALU.add, [BY[ysel], BPP, BX[tt]], [BX[tt]])

    if not dry:
        k.dma("sp", [(PP[:, :], T["pp"][:, :])], writes=[BPP])
    memset(ONESB[:, :], 1.0, [BONES])
    memset(ONESF[:, :], 1.0, [BONES])
    memset(EPS[:, 0:1], 1e-6, [BEPS])
    memset(EPS[:, 1:2], 1e-5, [BEPS])
    if not dry:
        k.dma("pool", [(TRILM[:, :, :], T["trilm"][:, :, :])], writes=[BTRILM])
        k.dma("pool", [(IDENT[:, :], T["ident"][:, :])], writes=[BIDENT])
    A.reset()
    AB, BAB = A.alloc("abias", [128, 3072], F32)
    AM, BAM = A.alloc("amask", [128, 3072], F32)
    if not dry:
        k.dma("sp", [(AB[:, :], T["abias"][:, :])], writes=[BAB])
        k.dma("sp", [(AM[:, :], T["amask"][:, :])], writes=[BAM])
    if not dry:
        ebm_flat = EXPBM[:, :, :, :, :].rearrange("p g h c q -> p (g h c q)")
    else:
        ebm_flat = DUMMY
    act(AB[:, :], AB[:, :], AF.Exp, [BAB], [BAB])
    tt_(ebm_flat, AB[:, :], AM[:, :], ALU.mult, [BAB, BAM], [BEXPBM])

    def sublayer_mixer(l):
        pb = l * PPL
        banks.set(range(8))
        if not dry:
            k.dma("sp", [(PBC[:, :], T["pbc"][l, :, :])], writes=[BPBC])
            k.dma("pool", [(WSRAW[:, :, :], T["wsT"][l, :, :, :])], writes=[BWSRAW])
        tt_(WSM[:, :, :], WSRAW[:, :, :], TRILM[:, :, :], ALU.mult, [BWSRAW, BTRILM], [BWSM])
        norm_to_H(pb + 0, range(4))
        dump(f"h1_{l}", H[:, :, :], [128, 8, S], BF16, BH)
        if stop_after == "norm1":
            return True

        A.reset()
        OC, BOC = A.alloc("OC", [128, 2, S], BF16)
        keep_c = A.off
        Ut, BU = A.alloc("U", [128, 2, S], F32)
        qk_o = A.off
        Qb, BQ = A.alloc("Q", [128, 2, S], BF16, nbufs=2)
        Kb, BK = A.alloc("K", [128, 2, S], BF16, nbufs=2)
        end_qk = A.off
        A.reset(qk_o)
        R_, BR = A.alloc("R", [128, S], F32)
        A.reset(end_qk)
        VA, BVA = A.alloc("VA", [128, 2, 16, 2, 128], BF16, nbufs=2)
        NPT = 4
        PT, BPT = A.alloc("PT", [128, NPT, 2, 2, 128], BF16, nbufs=NPT)
        memset(VA[:, :, :, 0, 64:128], 1.0, BVA)
        memset(VA[:, :, :, 1, 0:64], 1.0, BVA)
        gctr = 0
        for hp in range(2):
            for g in range(3):
                d = DILS[g]
                nb = 16 // d
                pq = gctr % 2
                gctr += 1
                tq = ws.get(("w_in", l, 16 + g * 2 + hp, 0, 8))
                tk = ws.get(("w_in", l, 22 + g * 2 + hp, 0, 8))
                tv = ws.get(("w_in", l, 28 + g * 2 + hp, 0, 8))
                for (tile, dst, bdst) in ((tq, Qb, BQ[pq]), (tk, Kb, BK[pq])):
                    def ev(tt, ps, psb, dst=dst, bdst=bdst, pq=pq):
                        act(dst[:, pq, ts(tt)], ps[:, :], AF.Copy, [psb], [bdst])
                    proj_fm(tile, 8, h_rhs, h_bufs, ev)
                    ws.done(tile)
                    chk("Cq")
                chk("Cqk")

                def tokslice(j, d=d, nb=nb):
                    c, n = divmod(j, nb)
                    st = c + d * 128 * n
                    return slice(st, st + d * 127 + 1, d), n

                for j0 in range(0, 16, 4):
                    b = banks.next()
                    for jj in range(4):
                        tok, n = tokslice(j0 + jj)
                        for kc in range(8):
                            mm(PS[b][:, jj * 128:(jj + 1) * 128], H[:, kc, tok], tv.ap[:, kc, :], kc == 0, kc == 7,
                               [tv.buf] + BH, BPS[b], inc=(kc == 7 and jj == 3))
                    psv = PS[b][:, :].rearrange("p (j c) -> p j c", c=128)
                    act(VA[:, pq, j0:j0 + 4, 0, 0:64], psv[:, :, 0:64], AF.Copy, [BPS[b]], [BVA[pq]])
                    act(VA[:, pq, j0:j0 + 4, 1, 64:128], psv[:, :, 64:128], AF.Copy, [BPS[b]], [BVA[pq]])
                ws.done(tv)
                chk("Cv")

                def stage1(j):
                    tok, n = tokslice(j)
                    pcs = (0, 1) if n > 0 else (1,)
                    eb = j % NPT
                    for h in range(2):
                        b = banks.next()
                        pr = slice(64 * h, 64 * h + 64)
                        for pc in pcs:
                            ktok, _ = tokslice(j - 1 if pc == 0 else j)
                            mm(PS[b][:, pc * 128:(pc + 1) * 128], Kb[pr, pq, ktok], Qb[pr, pq, tok], True, True,
                               [BQ[pq], BK[pq]], BPS[b], inc=(pc == 1))
                        psv = PS[b][:, 0:256].rearrange("p (c q) -> p c q", c=2)
                        if n > 0:
                            p_ap, s_ap = PT[:, eb, h], psv
                        else:
                            p_ap, s_ap = PT[:, eb, h, 1, :], psv[:, 1, :]
                        act(p_ap, s_ap, AF.Exp, [BPS[b]], [BPT[eb]], scale=0.125)
                    if n > 0:
                        p_ap, m_ap = PT[:, eb], EXPBM[:, g, hp * 2:hp * 2 + 2]
                    else:
                        p_ap, m_ap = PT[:, eb, :, 1, :], EXPBM[:, g, hp * 2:hp * 2 + 2, 1, :]
                    tt_(p_ap, p_ap, m_ap, ALU.mult, [BPT[eb], BEXPBM], [BPT[eb]])

                def stage2(j):
                    tok, n = tokslice(j)
                    pcs = (0, 1) if n > 0 else (1,)
                    eb = j % NPT
                    b2 = banks.next()
                    for h in range(2):
                        for i, pc in enumerate(pcs):
                            vj = j - 1 if pc == 0 else j
                            mm(PS[b2][:, h * 128:(h + 1) * 128], VA[:, pq, vj, h, :], PT[:, eb, h, pc, :], i == 0,
                               i == len(pcs) - 1, [BVA[pq], BPT[eb]], BPS[b2], inc=(h == 1 and i == len(pcs) - 1))
                    psu = PS[b2][:, 0:256].rearrange("p (h q) -> p h q", h=2)
                    if g == 0:
                        act(Ut[:, :, tok], psu, AF.Copy, [BPS[b2]], [BU])
                    else:
                        tt_(Ut[:, :, tok], psu, Ut[:, :, tok], ALU.add, [BPS[b2], BU], [BU])

                DEPTH_C = 2
                for j in range(16):
                    stage1(j)
                    if j >= DEPTH_C:
                        stage2(j - DEPTH_C)
                for j in range(16 - DEPTH_C, 16):
                    stage2(j)
            recip(R_[0:64, :], Ut[64:128, 0, :], [BU], [BR])
            tt_(OC[0:64, hp, :], Ut[0:64, 0, :], R_[0:64, :], ALU.mult, [BU, BR], [BOC])
            recip(R_[64:128, :], Ut[0:64, 1, :], [BU], [BR])
            tt_(OC[64:128, hp, :], Ut[64:128, 1, :], R_[64:128, :], ALU.mult, [BU, BR], [BOC])
        dump(f"oc{l}", OC[:, :, :], [128, 2, S], BF16, [BOC])
        if stop_after == "C":
            return True

        A.reset(keep_c)
        AACT, BAACT = A.alloc("AACT", [128, 4, S], BF16)
        keep_a = A.off
        APAD, BAPAD = A.alloc("APAD", [128, 4, 32 + S], BF16)
        tmp_o = A.off
        SG, BSG = A.alloc("SG", [128, 2, 512], F32, nbufs=2)
        A.reset(tmp_o)
        CT, BCT = A.alloc("CT", [128, 2, 4, 512], F32, nbufs=8)
        NDG = 8
        DG, BDG = A.alloc("DG", [128, NDG, 128], BF16, nbufs=NDG)
        SQF, BSQF = A.alloc("SQF", [128, 2, 512], F32, nbufs=2)
        MEAN, BMEAN = A.alloc("MEAN", [128, 512], F32)
        VAR, BVAR = A.alloc("VAR", [128, 512], F32)
        memset(APAD[:, :, 0:32], 0.0, [BAPAD])
        for c in range(4):
            tval = ws.get(("w_in", l, c, 0, 8))
            tgate = ws.get(("w_in", l, 4 + c, 0, 8))
            for tt in range(4):
                bv = banks.next()
                bg = banks.next()
                for kc in range(8):
                    mm(PS[bv][:, :], tval.ap[:, kc, :], H[:, kc, ts(tt)], kc == 0, kc == 7, [tval.buf, BH[tt]], BPS[bv])
                for kc in range(8):
                    mm(PS[bg][:, :], tgate.ap[:, kc, :], H[:, kc, ts(tt)], kc == 0, kc == 7, [tgate.buf, BH[tt]],
                       BPS[bg])
                si = tt % 2
                act(SG[:, si, :], PS[bg][:, :], AF.Sigmoid, [BPS[bg]], [BSG[si]])
                tt_(APAD[:, c, 32 + tt * 512:32 + (tt + 1) * 512], PS[bv][:, :], SG[:, si, :], ALU.mult,
                    [BPS[bv], BSG[si]], [BAPAD])
            ws.done(tval)
            ws.done(tgate)
        dump(f"glu{l}", APAD[:, :, :], [128, 4, 32 + S], BF16, [BAPAD])
        cw = pb + 80
        dctr = [0]

        def conv_tile(tt):
            ci = tt % 2
            base = 2 + tt * 512
            for c in range(4):
                b = banks.next()
                for kk in range(31):
                    sl = dctr[0] % NDG
                    dctr[0] += 1
                    wcol = PP[:, cw + kk * 4 + c:cw + kk * 4 + c + 1]
                    if kk % 2 == 0:
                        act(DG[:, sl, :], IDENT[:, :], AF.Identity, [BIDENT, BPP], [BDG[sl]], scale=wcol)
                    else:
                        tsc(DG[:, sl, :], IDENT[:, :], wcol, None, ALU.mult, None, [BIDENT, BPP], [BDG[sl]])
                    mm(PS[b][:, :], DG[:, sl, :], APAD[:, c, base + kk:base + kk + 512], kk == 0, kk == 30,
                       [BDG[sl], BAPAD], BPS[b], inc=True)
                act(CT[:, ci, c, :], PS[b][:, :], AF.Identity, [BPS[b], BPP], [BCT[ci * 4 + c]],
                    bias=PP[:, pb + 204 + c:pb + 205 + c])

        def ln_tile(tt):
            ci = tt % 2
            bsum = banks.next()
            bsq = banks.next()
            for c in range(4):
                o = CT[:, ci, c, :]
                bc = BCT[ci * 4 + c]
                si = c % 2
                act(SQF[:, si, :], o, AF.Square, [bc], [BSQF[si]])
                mm(PS[bsum][:, :], ONESF[:, :], o, c == 0, c == 3, [BONES, bc], BPS[bsum], inc=True)
                mm(PS[bsq][:, :], ONESF[:, :], SQF[:, si, :], c == 0, c == 3, [BONES, BSQF[si]], BPS[bsq], inc=True)
            act(MEAN[:, :], PS[bsum][:, :], AF.Identity, [BPS[bsum]], [BMEAN], scale=1.0 / 512)
            tt_(VAR[:, :], MEAN[:, :], MEAN[:, :], ALU.mult, [BMEAN], [BVAR])
            stt(VAR[:, :], PS[bsq][:, :], 1.0 / 512, VAR[:, :], ALU.mult, ALU.subtract, [BPS[bsq], BVAR], [BVAR])
            act(VAR[:, :], VAR[:, :], AF.Sqrt, [BVAR, BEPS], [BVAR], bias=EPS[:, 1:2])
            recip(VAR[:, :], VAR[:, :], [BVAR], [BVAR])
            for c in range(4):
                tt_(CT[:, ci, c, :], CT[:, ci, c, :], MEAN[:, :], ALU.subtract, [BCT[ci * 4 + c], BMEAN], [BCT[ci * 4 + c]])
            for c in range(4):
                tt_(CT[:, ci, c, :], CT[:, ci, c, :], VAR[:, :], ALU.mult, [BCT[ci * 4 + c], BVAR], [BCT[ci * 4 + c]])
            for c in range(4):
                act(AACT[:, c, ts(tt)], CT[:, ci, c, :], AF.Silu, [BCT[ci * 4 + c], BPP], [BAACT],
                    bias=PP[:, pb + 212 + c:pb + 213 + c], scale=PP[:, pb + 208 + c:pb + 209 + c])

        conv_tile(0)
        for tt in range(4):
            if tt + 1 < 4:
                conv_tile(tt + 1)
            ln_tile(tt)
        dump(f"aact{l}", AACT[:, :, :], [128, 4, S], BF16, [BAACT])
        if stop_after == "A":
            return True

        A.reset(keep_a)
        UB, BUB = A.alloc("UB", [128, 4, S], BF16)
        keep_b = A.off
        UU, BUU = A.alloc("UU", [128, 4, S], BF16)
        VG, BVG = A.alloc("VG", [128, 2, 512], F32, nbufs=2)
        VL, BVL = A.alloc("VL", [128, 2, 512], BF16, nbufs=2)
        TB, BTB = A.alloc("TB", [128, 512], F32)
        CB, BCB = A.alloc("CB", [128, 512], F32)
        brw = banks.next()
        for g in range(4):
            mm(PS[brw][:, g * 128:(g + 1) * 128], ONESB[:, :], WSM[:, g, :], True, True, [BONES, BWSM], BPS[brw],
               inc=(g == 3))
        for g in range(4):
            stt(CB[:, g * 128:(g + 1) * 128], PS[brw][:, g * 128:(g + 1) * 128], PP[:, pb + 308 + g:pb + 309 + g],
                PBC[:, 1024 + g * 128:1024 + (g + 1) * 128], ALU.mult, ALU.add, [BPS[brw], BPP, BPBC], [BCB])
        for c in range(4):
            tu = ws.get(("w_in", l, 8 + c, 0, 8))

            def ev(tt, ps, psb, c=c):
                act(UU[:, c, ts(tt)], ps[:, :], AF.Gelu_apprx_tanh, [psb], [BUU])
            proj_fm(tu, 8, h_rhs, h_bufs, ev)
            ws.done(tu)
        tvs = [ws.get(("w_in", l, 12 + cc, 0, 8)) for cc in range(4)]
        def b_stage_a(n):
            b = banks.next()
            tt = n // 4
            for cc in range(4):
                for kc in range(8):
                    mm(PS[b][:, cc * 128:(cc + 1) * 128], H[:, kc, n * 128:(n + 1) * 128], tvs[cc].ap[:, kc, :], kc == 0,
                       kc == 7, [tvs[cc].buf, BH[tt]], BPS[b], inc=(kc == 7 and cc == 3))
            vi = n % 2
            act(VG[:, vi, :], PS[b][:, :], AF.Gelu_apprx_tanh, [BPS[b]], [BVG[vi]])
            k.op("dve", lambda e, o=ST6[:, vi, :], i=VG[:, vi, :]: e.bn_stats(out=o, in_=i), reads=[BVG[vi]],
                 writes=[BSMALL[vi]])
            k.op("dve", lambda e, o=SMALL[:, vi, 2:4], i=ST6[:, vi, :]: e.bn_aggr(out=o, in_=i), reads=[BSMALL[vi]],
                 writes=[BSMALL[vi]])
            act(SMALL[:, vi, 5:6], SMALL[:, vi, 3:4], AF.Sqrt, [BSMALL[vi], BEPS], [BSMALL[vi]], bias=EPS[:, 1:2])

        def b_stage_b(n):
            vi = n % 2
            recip(SMALL[:, vi, 6:7], SMALL[:, vi, 5:6], [BSMALL[vi]], [BSMALL[vi]])
            tsc(VL[:, vi, :], VG[:, vi, :], SMALL[:, vi, 2:3], SMALL[:, vi, 6:7], ALU.subtract, ALU.mult,
                [BVG[vi], BSMALL[vi]], [BVL[vi]])
            b2 = banks.next()
            for g in range(4):
                mm(PS[b2][:, g * 128:(g + 1) * 128], VL[:, vi, g * 128:(g + 1) * 128], WSM[:, g, :], True, True,
                   [BVL[vi], BWSM], BPS[b2], inc=(g == 3))
            for g in range(4):
                stt(TB[:, g * 128:(g + 1) * 128], PS[b2][:, g * 128:(g + 1) * 128], PP[:, pb + 304 + g:pb + 305 + g],
                    CB[:, g * 128:(g + 1) * 128], ALU.mult, ALU.add, [BPS[b2], BPP, BCB], [BTB])
            tbv = TB[:, :].rearrange("p (g t) -> p g t", g=4)
            tt_(UB[:, :, n * 128:(n + 1) * 128], tbv, UU[:, :, n * 128:(n + 1) * 128], ALU.mult, [BTB, BUU], [BUB])

        b_stage_a(0)
        for n in range(16):
            if n + 1 < 16:
                b_stage_a(n + 1)
            b_stage_b(n)
        for t_ in tvs:
            ws.done(t_)
        dump(f"ub{l}", UB[:, :, :], [128, 4, S], BF16, [BUB])
        if stop_after == "B":
            return True

        A.reset(keep_b)
        MG, BMG = A.alloc("MG", [128, 8, 1024], BF16)
        SGM, BSGM = A.alloc("SGM", [128, 3, 512], F32)
        M0, BM0 = A.alloc("M0", [128, 512], F32)
        M1, BM1 = A.alloc("M1", [128, 512], F32)
        for hf in range(2):
            banks.set(range(8))
            for dc in range(8):
                wa = ws.get(("w_a", l, dc, 0, 4))
                wb = ws.get(("w_b", l, dc, 0, 4))
                wc = ws.get(("w_c", l, dc, 0, 2))
                wg = [ws.get(("w_in", l, 34 + i * 8 + dc, 0, 8)) for i in range(3)]
                for t2 in range(2):
                    tt = hf * 2 + t2
                    pa, pb_, pc_ = banks.next(), banks.next(), banks.next()
                    pg = [banks.next() for _ in range(3)]
                    for kc in range(4):
                        mm(PS[pa][:, :], wa.ap[:, kc, :], AACT[:, kc, ts(tt)], kc == 0, kc == 3, [wa.buf, BAACT], BPS[pa])
                    for kc in range(4):
                        mm(PS[pb_][:, :], wb.ap[:, kc, :], UB[:, kc, ts(tt)], kc == 0, kc == 3, [wb.buf, BUB], BPS[pb_])
                    for kc in range(2):
                        mm(PS[pc_][:, :], wc.ap[:, kc, :], OC[:, kc, ts(tt)], kc == 0, kc == 1, [wc.buf, BOC], BPS[pc_])
                    for i in range(3):
                        for kc in range(8):
                            mm(PS[pg[i]][:, :], wg[i].ap[:, kc, :], H[:, kc, ts(tt)], kc == 0, kc == 7,
                               [wg[i].buf, BH[tt]], BPS[pg[i]])
                    for i in range(3):
                        col = pb + 56 + i * 8 + dc
                        act(SGM[:, i, :], PS[pg[i]][:, :], AF.Sigmoid, [BPS[pg[i]], BPP], [BSGM], bias=PP[:, col:col + 1])
                    tt_(M0[:, :], PS[pa][:, :], SGM[:, 0, :], ALU.mult, [BPS[pa], BSGM], [BM0])
                    tt_(M1[:, :], PS[pb_][:, :], SGM[:, 1, :], ALU.mult, [BPS[pb_], BSGM], [BM1])
                    tt_(M0[:, :], M0[:, :], M1[:, :], ALU.add, [BM0, BM1], [BM0])
                    tt_(M1[:, :], PS[pc_][:, :], SGM[:, 2, :], ALU.mult, [BPS[pc_], BSGM], [BM1])
                    tt_(MG[:, dc, ts(t2)], M0[:, :], M1[:, :], ALU.add, [BM0, BM1], [BMG])
                for t_ in [wa, wb, wc] + wg:
                    ws.done(t_)
            if dbg is not None:
                dump(f"merged{l}_{hf}", MG[:, :, :], [128, 8, 1024], BF16, [BMG])
            banks.set(range(6))
            wm = [ws.get(("w_mix", l, dc, 0, 8)) for dc in range(8)]
            for t2 in range(2):
                tt = hf * 2 + t2
                sb_ = 6 + t2
                for dc in range(8):
                    b = banks.next()
                    for kc in range(8):
                        mm(PS[b][:, :], wm[dc].ap[:, kc, :], MG[:, kc, ts(t2)], kc == 0, kc == 7, [wm[dc].buf, BMG], BPS[b])
                    flush_stats()
                    y_chunk(PS[b][:, :], BPS[b], dc, hf, sb_, dc == 0, dc == 7)
                post_update(pb + 8, tt, hf, sb_)
            for t_ in wm:
                ws.done(t_)
        return False

    def sublayer_xattn(l, s):
        pb = l * PPL
        banks.set(range(8))
        A.reset()
        MEMS, BMEMS = A.alloc("MEMS", [128, 8, 256], F32)
        KT, BKT = A.alloc("KT", [128, 8, 256], BF16)
        VX, BVX = A.alloc("VX", [128, 2, 1024], BF16)
        QX, BQX = A.alloc("QX", [128, 8, S], BF16)
        PTX, BPTX = A.alloc("PTX", [128, 2, 2, 512], BF16, nbufs=2)
        RD, BRD = A.alloc("RD", [128, 2, 512], F32, nbufs=2)
        memn_o = A.off
        MEMN, BMEMN = A.alloc("MEMN", [128, 8, 256], BF16)
        A.reset(0)
        OX0, BOX0 = A.alloc("OX0", [128, 8, 512], BF16)
        A.reset(memn_o)
        OX1, BOX1 = A.alloc("OX1", [128, 8, 512], BF16)
        OXs, BOXs = [OX0, OX1], [BOX0, BOX1]
        if not dry:
            k.dma("sp", [(MEMS[:, kc, :], T["memT"][s, kc, :, :]) for kc in range(8)], writes=[BMEMS])
        b = banks.next()
        for kc in range(8):
            si = kc % 2
            act(SQ[:, si, 0:256], MEMS[:, kc, :], AF.Square, [BMEMS], [BSQ[si]])
            mm(PS[b][:, 0:256], ONESB[:, :], SQ[:, si, 0:256], kc == 0, kc == 7, [BONES, BSQ[si]], BPS[b], inc=True)
        ri = nrm_ctr[0] % 2
        nrm_ctr[0] += 1
        act(RSTD[:, ri, 0:256], PS[b][:, 0:256], AF.Sqrt, [BPS[b], BEPS], [BRSTD[ri]], bias=EPS[:, 0:1], scale=1.0 / D)
        recip(RSTD[:, ri, 0:256], RSTD[:, ri, 0:256], [BRSTD[ri]], [BRSTD[ri]])
        for kc in range(8):
            stt(MEMN[:, kc, :], MEMS[:, kc, :], PP[:, pb + 32 + kc:pb + 33 + kc], RSTD[:, ri, 0:256], ALU.mult, ALU.mult,
                [BMEMS, BPP, BRSTD[ri]], [BMEMN])
        def kproj(ec):
            w = ws.get(("w_xkv", l, ec, 0, 8))
            b = banks.next()
            for kc in range(8):
                mm(PS[b][:, 0:256], w.ap[:, kc, :], MEMN[:, kc, :], kc == 0, kc == 7, [w.buf, BMEMN], BPS[b])
            act(KT[:, ec, :], PS[b][:, 0:256], AF.Copy, [BPS[b]], [BKT])
            ws.done(w)

        def vproj(ec):
            w = ws.get(("w_xkv", l, 8 + ec, 0, 8))
            b = banks.next()
            for mb in range(2):
                for kc in range(8):
                    mm(PS[b][:, mb * 128:(mb + 1) * 128], MEMN[:, kc, mb * 128:(mb + 1) * 128], w.ap[:, kc, :], kc == 0,
                       kc == 7, [w.buf, BMEMN], BPS[b], inc=(kc == 7 and mb == 1))
            psv = PS[b][:, 0:256].rearrange("p (m c) -> p m c", m=2)
            act(VX[:, :, ec * 128:(ec + 1) * 128], psv, AF.Copy, [BPS[b]], [BVX])
            ws.done(w)

        for tt in range(4):
            norm_to_H(pb + 16, (tt,))
            for ec in (2 * tt, 2 * tt + 1):
                kproj(ec)
        for ec in range(8):
            vproj(ec)
        for dc in range(8):
            w = ws.get(("w_xq", l, dc, 0, 8))

            def ev(tt, ps, psb, dc=dc):
                act(QX[:, dc, ts(tt)], ps[:, :], AF.Copy, [psb], [BQX])
            proj_fm(w, 8, h_rhs, h_bufs, ev)
            ws.done(w)
        wo = [ws.get(("w_xo", l, dc, 0, 8)) for dc in range(8)]
        banks.set(range(6))

        def attn_tile(tt):
            OX, BOX = OXs[tt % 2], BOXs[tt % 2]

            def xs1(hd):
                pi = hd % 2
                bs_ = [banks.next(), banks.next()]
                for mb in range(2):
                    for ec in range(2):
                        mm(PS[bs_[mb]][:, :], KT[:, hd * 2 + ec, mb * 128:(mb + 1) * 128], QX[:, hd * 2 + ec, ts(tt)],
                           ec == 0, ec == 1, [BKT, BQX], BPS[bs_[mb]])
                    act(PTX[:, pi, mb, :], PS[bs_[mb]][:, :], AF.Exp, [BPS[bs_[mb]]], [BPTX[pi]], scale=1.0 / 16)

            def xs2(hd):
                pi = hd % 2
                bd = banks.next()
                for mb in range(2):
                    mm(PS[bd][:, :], ONESB[:, :], PTX[:, pi, mb, :], mb == 0, mb == 1, [BONES, BPTX[pi]], BPS[bd])
                recip(RD[:, pi, :], PS[bd][:, :], [BPS[bd]], [BRD[pi]])
                for ec in range(2):
                    bo = banks.next()
                    for mb in range(2):
                        mm(PS[bo][:, :], VX[:, mb, (hd * 2 + ec) * 128:(hd * 2 + ec + 1) * 128], PTX[:, pi, mb, :], mb == 0,
                           mb == 1, [BVX, BPTX[pi]], BPS[bo])
                    tt_(OX[:, hd * 2 + ec, :], PS[bo][:, :], RD[:, pi, :], ALU.mult, [BPS[bo], BRD[pi]], [BOX])

            for hd in range(4):
                xs1(hd)
                if hd > 0:
                    xs2(hd - 1)
            xs2(3)

        def xo_tile(tt):
            OX, BOX = OXs[tt % 2], BOXs[tt % 2]
            ysel = tt % 2
            sb_ = 6 + (tt % 2)
            for dc in range(8):
                b = banks.next()
                for kc in range(8):
                    mm(PS[b][:, :], wo[dc].ap[:, kc, :], OX[:, kc, :], kc == 0, kc == 7, [wo[dc].buf, BOX], BPS[b])
                flush_stats()
                y_chunk(PS[b][:, :], BPS[b], dc, ysel, sb_, dc == 0, dc == 7)
            flush_stats()
            if not XATTN_DEFER_POST:
                post_update(pb + 24, tt, ysel, sb_)

        def xpost(tt):
            post_update(pb + 24, tt, tt % 2, 6 + (tt % 2))

        if XATTN_DEFER_POST:
            for tt in range(4):
                attn_tile(tt)
                if tt > 0:
                    xpost(tt - 1)
                xo_tile(tt)
            xpost(3)
        elif XATTN_PIPE:
            for tt in range(4):
                attn_tile(tt)
                if tt > 0:
                    xo_tile(tt - 1)
            xo_tile(3)
        else:
            for tt in range(4):
                attn_tile(tt)
                xo_tile(tt)
        for t_ in wo:
            ws.done(t_)

    def sublayer_ffn(l, tile_done=None):
        pb = l * PPL
        A.reset()
        ACTB, BACTB = A.alloc("ACTB", [128, 22, 1024], BF16)
        GT, BGT = A.alloc("GT", [128, 2, 516], F32, nbufs=2)
        CV, BCV = A.alloc("CV", [128, 512], F32)
        GL, BGL = A.alloc("GL", [128, 2, 512], F32, nbufs=2)
        memset(HALO[:, :, :], 0.0, [BHALO])
        fw = pb + 216
        gi = 0
        for hf in range(2):
            banks.set(range(8))
            norm_to_H(pb + 40, (2 * hf, 2 * hf + 1))
            for j in range(22):
                wg = ws.get(("w_up", l, j, 0, 8))
                wv = ws.get(("w_up", l, 22 + j, 0, 8))
                for t2 in range(2):
                    tt = hf * 2 + t2
                    bg, bv = banks.next(), banks.next()
                    for kc in range(8):
                        mm(PS[bg][:, :], wg.ap[:, kc, :], H[:, kc, ts(tt)], kc == 0, kc == 7, [wg.buf, BH[tt]], BPS[bg])
                    for kc in range(8):
                        mm(PS[bv][:, :], wv.ap[:, kc, :], H[:, kc, ts(tt)], kc == 0, kc == 7, [wv.buf, BH[tt]], BPS[bv])
                    g2 = gi % 2
                    gi += 1
                    act(GT[:, g2, 2:514], PS[bg][:, :], AF.Copy, [BPS[bg]], [BGT[g2]])
                    cpy(GT[:, g2, 0:2], HALO[:, j, :], [BHALO], [BGT[g2]])
                    cpy(HALO[:, j, :], GT[:, g2, 512:514], [BGT[g2]], [BHALO])
                    act(CV[:, :], PS[bg][:, :], AF.Identity, [BPS[bg], BPP], [BCV], bias=PP[:, pb + 282 + j:pb + 283 + j],
                        scale=PP[:, fw + 44 + j:fw + 45 + j])
                    stt(CV[:, :], GT[:, g2, 0:512], PP[:, fw + j:fw + j + 1], CV[:, :], ALU.mult, ALU.add,
                        [BGT[g2], BPP, BCV], [BCV])
                    stt(CV[:, :], GT[:, g2, 1:513], PP[:, fw + 22 + j:fw + 23 + j], CV[:, :], ALU.mult, ALU.add,
                        [BGT[g2], BPP, BCV], [BCV])
                    act(GL[:, g2, :], CV[:, :], AF.Gelu_apprx_tanh, [BCV], [BGL[g2]])
                    tt_(ACTB[:, j, ts(t2)], PS[bv][:, :], GL[:, g2, :], ALU.mult, [BPS[bv], BGL[g2]], [BACTB])
                ws.done(wg)
                ws.done(wv)
            banks.set(range(6))
            ysels = (1, 0) if hf == 0 else (0, 1)
            for dc in range(8):
                wd = [ws.get(("w_down", l, dc, 0, 8)), ws.get(("w_down", l, dc, 8, 8)), ws.get(("w_down", l, dc, 16, 6))]
                for t2 in range(2):
                    b = banks.next()
                    for kc in range(22):
                        w = wd[kc // 8]
                        mm(PS[b][:, :], w.ap[:, kc % 8, :], ACTB[:, kc, ts(t2)], kc == 0, kc == 21, [w.buf, BACTB], BPS[b])
                    flush_stats()
                    y_chunk(PS[b][:, :], BPS[b], dc, ysels[t2], 6 + t2, dc == 0, dc == 7)
                for w in wd:
                    ws.done(w)
            for t2 in range(2):
                tt = hf * 2 + t2
                post_update(pb + 48, tt, ysels[t2], 6 + t2)
                if tile_done is not None:
                    tile_done(tt)

    stopped = False

    def load_x(s_, tt):
        if not dry:
            k.dma("sp", [(XRES[:, kc, ts(tt)], T["xT"][s_, kc, :, ts(tt)]) for kc in range(8)], writes=[BX[tt]])

    def store_x(s_, tt):
        if not dry:
            k.dma("sp", [(T["outT"][s_, kc, :, ts(tt)], XRES[:, kc, ts(tt)]) for kc in range(8)], reads=[BX[tt]],
                  sembuf=BX[tt])

    try:
        for tt in range(4):
            load_x(0, tt)
        for s in range(nseq):
            def tile_done(tt, s=s):
                store_x(s, tt)
                if s + 1 < nseq:
                    load_x(s + 1, tt)
            for l in range(nlayer):
                stopped = sublayer_mixer(l)
                if stopped:
                    break
                dump(f"x1_{l}", XRES[:, :, :], [128, 8, S], F32, BX)
                if stop_after == "mixer":
                    stopped = True
                    break
                sublayer_xattn(l, s)
                dump(f"x2_{l}", XRES[:, :, :], [128, 8, S], F32, BX)
                if stop_after == "xattn":
                    stopped = True
                    break
                sublayer_ffn(l, tile_done if l == nlayer - 1 else None)
            if stopped:
                for tt in range(4):
                    store_x(s, tt)
                break
    except _Stop:
        for tt in range(4):
            store_x(0, tt)
    k.wait_all("sp", BX + dbg_out)
    return ws


def make_program(nseq=SEQ_PER_CORE, nlayer=DEPTH, stop_after=None, dbg=None):
    kd = K(None, dry=True)
    wsd = build_program(None, kd, None, nseq, nlayer, stop_after, dbg)
    seq = wsd.rec
    nc = bass.Bass("TRN2", target_bir_lowering=False)
    k = K(nc)
    ws = build_program(nc, k, seq, nseq, nlayer, stop_after, dbg)
    assert ws.idx == len(seq) and ws.loaded == len(seq)
    return nc, k


def _t5_bucket(n):
    n = np.maximum(n, 0)
    nf = np.maximum(n, 1).astype(np.float32)
    large = 16 + (np.log(nf / np.float32(16)) / np.float32(np.log(2048 / 16)) * np.float32(16)).astype(np.int32)
    large = np.minimum(large, 31)
    return np.where(n < 16, n, large)


def _wtiles(W):
    L, Kd, N = W.shape
    return np.ascontiguousarray(W.reshape(L, Kd // 128, 128, N // 128, 128).transpose(0, 3, 2, 1, 4))


def _cols(v):
    v = np.asarray(v)
    C = v.shape[-1]
    lead = v.shape[:-1]
    a = v.reshape(*lead, C // 128, 128)
    a = np.moveaxis(a, -1, 0)
    return a.reshape(128, -1)


def prep_shared(inp):
    L = DEPTH
    pp = np.zeros((128, L * PPL), np.float32)
    for l in range(L):
        o = l * PPL
        pp[:, o + 0:o + 8] = _cols(inp["mix_pre_g"][l])
        pp[:, o + 8:o + 16] = _cols(inp["mix_post_g"][l])
        pp[:, o + 16:o + 24] = _cols(inp["x_pre_g"][l])
        pp[:, o + 24:o + 32] = _cols(inp["x_post_g"][l])
        pp[:, o + 32:o + 40] = _cols(inp["mem_g"][l])
        pp[:, o + 40:o + 48] = _cols(inp["ffn_pre_g"][l])
        pp[:, o + 48:o + 56] = _cols(inp["ffn_post_g"][l])
        pp[:, o + 56:o + 80] = _cols(inp["b_gate"][l])
        pp[:, o + 80:o + 204] = _cols(inp["conv_a_w"][l])
        pp[:, o + 204:o + 208] = _cols(inp["conv_a_b"][l])
        pp[:, o + 208:o + 212] = _cols(inp["ln_a_g"][l])
        pp[:, o + 212:o + 216] = _cols(inp["ln_a_b"][l])
        pp[:, o + 216:o + 282] = _cols(inp["conv_f_w"][l])
        pp[:, o + 282:o + 304] = _cols(inp["conv_f_b"][l])
        pp[:, o + 304:o + 308] = _cols(inp["ln_b_g"][l])
        pp[:, o + 308:o + 312] = _cols(inp["ln_b_b"][l])
    rb = np.asarray(inp["rel_bias"])
    kk = np.arange(128)[:, None]
    qq = np.arange(128)[None, :]
    abias = np.zeros((128, 3, 4, 2, 128), np.float32)
    amask = np.zeros((128, 3, 4, 2, 128), np.float32)
    for g, dil in enumerate(DILS):
        for pc in range(2):
            rel = qq + 128 - kk if pc == 0 else qq - kk
            valid = (rel >= 0) & (rel <= 128)
            bucket = _t5_bucket(rel * dil)
            for h in range(4):
                abias[:, g, h, pc, :] = rb[bucket, g * 4 + h]
                amask[:, g, h, pc, :] = valid
    pbc = np.zeros((L, 128, 1536), np.float32)
    for l in range(L):
        pbc[l, :, 0:512] = np.broadcast_to(inp["ln_b_g"][l][None, :], (128, 512))
        pbc[l, :, 512:1024] = np.broadcast_to(inp["ln_b_b"][l][None, :], (128, 512))
        pbc[l, :, 1024:1536] = np.broadcast_to(np.asarray(inp["b_s"][l]).reshape(1, 512), (128, 512))
    wsT = np.ascontiguousarray(np.asarray(inp["w_s"]).transpose(0, 3, 1, 2))
    tril = (np.arange(128)[None, :] >= np.arange(128)[:, None]).astype(np.float32)
    trilm = np.ascontiguousarray(np.broadcast_to(tril[:, None, :], (128, 4, 128)))
    w_c = np.asarray(inp["w_c_out"])
    shared = {
        "pp": pp,
        "abias": abias.reshape(128, 3072),
        "amask": amask.reshape(128, 3072),
        "pbc": pbc,
        "wsT": wsT,
        "trilm": trilm,
        "ident": np.eye(128, dtype=np.float32),
        "w_in": _wtiles(np.asarray(inp["w_in"])),
        "w_a": _wtiles(np.asarray(inp["w_a_out"])),
        "w_b": _wtiles(np.asarray(inp["w_b_out"])),
        "w_c": _wtiles(w_c),
        "w_mix": _wtiles(np.asarray(inp["w_mix_out"])),
        "w_xq": _wtiles(np.asarray(inp["w_xq"])),
        "w_xkv": _wtiles(np.asarray(inp["w_xkv"])),
        "w_xo": _wtiles(np.asarray(inp["w_xo"])),
        "w_up": _wtiles(np.asarray(inp["w_up"])),
        "w_down": _wtiles(np.asarray(inp["w_down"])),
    }
    return shared


def prep_core(inp, c, nseq=SEQ_PER_CORE):
    xs = np.asarray(inp["x"][c * SEQ_PER_CORE:c * SEQ_PER_CORE + nseq])
    ms = np.asarray(inp["mem"][c * SEQ_PER_CORE:c * SEQ_PER_CORE + nseq])
    xT = np.zeros((SEQ_PER_CORE, 8, 128, S), np.float32)
    mT = np.zeros((SEQ_PER_CORE, 8, 128, 256), np.float32)
    xT[:nseq] = xs.transpose(0, 2, 1).reshape(nseq, 8, 128, S)
    mT[:nseq] = ms.transpose(0, 2, 1).reshape(nseq, 8, 128, 256)
    return {"xT": xT, "memT": mT}


_PROG = {}


def kernel(**inputs):
    if "p" not in _PROG:
        _PROG["p"] = make_program()
    nc, _ = _PROG["p"]
    shared = prep_shared(inputs)
    in_maps = []
    for c in range(N_CORES):
        m = dict(shared)
        m.update(prep_core(inputs, c))
        in_maps.append(m)
    res = run_bass_kernel_spmd(nc, in_maps, core_ids=list(range(N_CORES)))
    out = np.empty((N_CORES * SEQ_PER_CORE, S, D), np.float32)
    for c in range(N_CORES):
        oT = res.results[c]["outT"]
        out[c * SEQ_PER_CORE:(c + 1) * SEQ_PER_CORE] = oT.reshape(SEQ_PER_CORE, D, S).transpose(0, 2, 1)
    return out
```

```python
import numpy as np
import concourse.bass as bass
import concourse.mybir as mybir
from concourse.bass_utils import run_bass_kernel_spmd

F32 = mybir.dt.float32
BF16 = mybir.dt.bfloat16
AF = mybir.ActivationFunctionType
ALU = mybir.AluOpType
AX = mybir.AxisListType

EPOCH = 12000
SAME_SYNC = True

N_CORES = 8
SEQ_PER_CORE = 4
DEPTH = 2
D = 1024
S = 2048
NSLOT = 10
XATTN_PIPE = False
XATTN_DEFER_POST = False
DEFER_STATS = True
PPL = 312
DILS = (1, 4, 16)


class Buf:
    __slots__ = ("name", "w", "r", "al", "dsem", "dcnt")

    def __init__(self, name):
        self.name = name
        self.w = None
        self.r = {}
        self.al = []
        self.dsem = None
        self.dcnt = 0


class Dummy:
    shape = ()

    def __getitem__(self, i):
        return self

    def rearrange(self, *a, **k):
        return self


DUMMY = Dummy()


class Eng:
    def __init__(self, K, name, obj):
        self.name = name
        self.obj = obj
        self.sem = None if K.dry else K.nc.alloc_semaphore(f"s_{name}_0")
        self.nsem = 1
        self.cnt = 0
        self.own = set() if K.dry else {id(self.sem)}
        self.waited = {}
        self.nwaits = 0
        self.nins = 0


class K:
    def __init__(self, nc, dry=False):
        self.nc = nc
        self.dry = dry
        if dry:
            self.engs = {n: Eng(self, n, None) for n in ("pe", "act", "dve", "pool", "sp")}
        else:
            self.engs = {
                "pe": Eng(self, "pe", nc.tensor),
                "act": Eng(self, "act", nc.scalar),
                "dve": Eng(self, "dve", nc.vector),
                "pool": Eng(self, "pool", nc.gpsimd),
                "sp": Eng(self, "sp", nc.sync),
            }
        self.semobj = {}
        if not dry:
            for e in self.engs.values():
                self.semobj[id(e.sem)] = e.sem
        self.ndsem = 0

    def _deps(self, reads, writes):
        deps = {}

        def add(d):
            if d is None:
                return
            s, v = d
            if deps.get(s, 0) < v:
                deps[s] = v

        for b in reads:
            add(b.w)
            for a in b.al:
                add(a.w)
        for b in writes:
            add(b.w)
            for d in b.r.values():
                add(d)
            for a in b.al:
                add(a.w)
                for d in a.r.values():
                    add(d)
        return deps

    def _wait(self, E, deps):
        for s, v in deps.items():
            if s in E.own:
                if E.name in ("pe", "sp") or not SAME_SYNC:
                    continue
                if s == id(E.sem) and v > E.cnt:
                    continue
                if E.name in ("act", "dve") and (s != id(E.sem) or v < E.cnt):
                    continue
            if E.waited.get(s, 0) >= v:
                continue
            E.obj.wait_ge(self.semobj[s], v)
            E.waited[s] = v
            E.nwaits += 1

    def _tag(self, E, ins, inc):
        E.nins += 1
        if inc:
            ins.then_inc(E.sem, 1)
            E.cnt += 1
            tag = (id(E.sem), E.cnt)
            if E.cnt >= EPOCH:
                E.sem = self.nc.alloc_semaphore(f"s_{E.name}_{E.nsem}")
                E.nsem += 1
                E.cnt = 0
                E.own.add(id(E.sem))
                self.semobj[id(E.sem)] = E.sem
        else:
            tag = (id(E.sem), E.cnt + 1)
        return tag

    def op(self, eng, fn, reads=(), writes=(), inc=True):
        if self.dry:
            return None
        E = self.engs[eng]
        self._wait(E, self._deps(reads, writes))
        ins = fn(E.obj)
        tag = self._tag(E, ins, inc)
        for b in writes:
            b.w = tag
            b.r = {}
        for b in reads:
            b.r[eng] = tag
        return ins

    def dma(self, eng, pairs, reads=(), writes=(), sembuf=None):
        if self.dry:
            return None
        E = self.engs[eng]
        sb = sembuf or (writes[0] if writes else reads[0])
        if sb.dsem is None:
            sb.dsem = self.nc.alloc_semaphore(f"d_{self.ndsem}")
            self.ndsem += 1
            self.semobj[id(sb.dsem)] = sb.dsem
        deps = self._deps(reads, writes)
        if sb.dcnt:
            s = id(sb.dsem)
            if deps.get(s, 0) < sb.dcnt:
                deps[s] = sb.dcnt
        self._wait(E, deps)
        for (o, i) in pairs:
            E.obj.dma_start(out=o, in_=i).then_inc(sb.dsem, 16)
            E.nins += 1
            sb.dcnt += 16
        tag = (id(sb.dsem), sb.dcnt)
        for b in writes:
            b.w = tag
            b.r = {}
        for b in reads:
            b.r["dma_" + sb.name] = tag
        return tag

    def wait_all(self, eng, bufs):
        if self.dry:
            return
        E = self.engs[eng]
        self._wait(E, self._deps(bufs, bufs))

    def stats(self):
        return {e.name: (e.nins, e.nwaits, e.nsem) for e in self.engs.values()}


class _Stop(Exception):
    pass


class Tile:
    __slots__ = ("i", "ap", "buf")

    def __init__(self, i, ap, buf):
        self.i = i
        self.ap = ap
        self.buf = buf


class WStream:
    def __init__(self, k, ring, rbufs, seq, ap_of):
        self.k = k
        self.ring = ring
        self.rbufs = rbufs
        self.seq = seq
        self.ap_of = ap_of
        self.rec = []
        self.idx = 0
        self.loaded = 0
        self.done_ = set()

    def _issue(self):
        while self.loaded < len(self.seq):
            i = self.loaded
            if i - NSLOT >= 0 and (i - NSLOT) not in self.done_:
                break
            slot = i % NSLOT
            src, kcn = self.ap_of(self.seq[i])
            self.k.dma("pool", [(self.ring[:, slot, 0:kcn, :], src)], writes=[self.rbufs[slot]])
            self.loaded += 1

    def get(self, desc):
        i = self.idx
        self.idx += 1
        if self.k.dry:
            self.rec.append(desc)
            return Tile(i, DUMMY, None)
        assert self.seq[i] == desc, (i, self.seq[i], desc)
        self._issue()
        assert self.loaded > i, f"weight ring too small at tile {i} {desc}"
        slot = i % NSLOT
        return Tile(i, self.ring[:, slot], self.rbufs[slot])

    def done(self, t):
        if self.k.dry:
            return
        self.done_.add(t.i)
        self._issue()


class Arena:
    def __init__(self, k, nc, base, size):
        self.k, self.nc, self.base, self.size = k, nc, base, size
        self.off = 0
        self.regs = []
        self.n = 0

    def reset(self, off=0):
        self.off = off

    def alloc(self, name, shape, dtype, nbufs=1):
        esz = 2 if dtype == BF16 else 4
        per = esz
        for s_ in shape[1:]:
            per *= s_
        per = (per + 63) // 64 * 64
        lo = self.off
        hi = lo + per
        assert hi <= self.size, f"arena overflow {name}: {hi} > {self.size}"
        self.off = hi
        bufs = [Buf(f"{name}{i}") for i in range(nbufs)]
        pbuf = per // nbufs
        new = []
        for bi, b in enumerate(bufs):
            blo, bhi = lo + bi * pbuf, (lo + (bi + 1) * pbuf if bi < nbufs - 1 else hi)
            for (l2, h2, b2) in self.regs:
                if l2 < bhi and blo < h2:
                    if b2 not in b.al:
                        b.al.append(b2)
                    if b not in b2.al:
                        b2.al.append(b)
            new.append((blo, bhi, b))
        self.regs.extend(new)
        if self.k.dry:
            t = DUMMY
        else:
            self.n += 1
            t = self.nc.alloc_sbuf_tensor_at(f"ar{self.n}_{name}", list(shape), dtype, offset=self.base + lo)
        return (t, bufs[0]) if nbufs == 1 else (t, bufs)


def ts(tt, n=512):
    return slice(tt * n, (tt + 1) * n)


def build_program(nc, k, wseq, nseq=SEQ_PER_CORE, nlayer=DEPTH, stop_after=None, dbg=None):
    dry = k.dry
    T = {}
    if not dry:
        def dram(name, shape, kind="ExternalInput"):
            T[name] = nc.dram_tensor(name, list(shape), F32, kind=kind).ap()
        dram("xT", [SEQ_PER_CORE, 8, 128, S])
        dram("memT", [SEQ_PER_CORE, 8, 128, 256])
        dram("pp", [128, DEPTH * PPL])
        dram("abias", [128, 3072])
        dram("amask", [128, 3072])
        dram("pbc", [DEPTH, 128, 1536])
        dram("wsT", [DEPTH, 128, 4, 128])
        dram("trilm", [128, 4, 128])
        dram("ident", [128, 128])
        dram("w_in", [DEPTH, 58, 128, 8, 128])
        dram("w_a", [DEPTH, 8, 128, 4, 128])
        dram("w_b", [DEPTH, 8, 128, 4, 128])
        dram("w_c", [DEPTH, 8, 128, 2, 128])
        dram("w_mix", [DEPTH, 8, 128, 8, 128])
        dram("w_xq", [DEPTH, 8, 128, 8, 128])
        dram("w_xkv", [DEPTH, 16, 128, 8, 128])
        dram("w_xo", [DEPTH, 8, 128, 8, 128])
        dram("w_up", [DEPTH, 44, 128, 8, 128])
        dram("w_down", [DEPTH, 8, 128, 22, 128])
        dram("outT", [SEQ_PER_CORE, 8, 128, S], kind="ExternalOutput")

    def ap_of(desc):
        name, l, j, k0, kcn = desc
        return T[name][l, j, :, k0:k0 + kcn, :], kcn

    BASE = 16512
    XRES_O = BASE
    H_O = XRES_O + 65536
    RING_O = H_O + 32768
    C_O = RING_O + NSLOT * 2048
    coff = [C_O]

    def calloc(name, shape, dtype):
        esz = 2 if dtype == BF16 else 4
        per = esz
        for s_ in shape[1:]:
            per *= s_
        per = (per + 63) // 64 * 64
        o = coff[0]
        coff[0] += per
        if dry:
            return DUMMY
        return nc.alloc_sbuf_tensor_at(name, list(shape), dtype, offset=o)

    if dry:
        XRES = H = Y = RING = DUMMY
    else:
        XRES = nc.alloc_sbuf_tensor_at("XRES", [128, 8, S], F32, offset=XRES_O)
        H = nc.alloc_sbuf_tensor_at("H", [128, 8, S], BF16, offset=H_O)
        Y = nc.alloc_sbuf_tensor_at("Y", [128, 8, 1024], F32, offset=H_O)
        RING = nc.alloc_sbuf_tensor_at("RING", [128, NSLOT, 8, 128], BF16, offset=RING_O)
    PP = calloc("PP", [128, DEPTH * PPL], F32)
    EXPBM = calloc("EXPBM", [128, 3, 4, 2, 128], BF16)
    PBC = calloc("PBC", [128, 1536], F32)
    WSRAW = calloc("WSRAW", [128, 4, 128], BF16)
    TRILM = calloc("TRILM", [128, 4, 128], BF16)
    WSM = calloc("WSM", [128, 4, 128], BF16)
    ONESB = calloc("ONESB", [128, 128], BF16)
    IDENT = calloc("IDENT", [128, 128], BF16)
    ONESF = calloc("ONESF", [128, 128], F32)
    RSTD = calloc("RSTD", [128, 2, 512], F32)
    SQ = calloc("SQ", [128, 2, 512], BF16)
    HALO = calloc("HALO", [128, 22, 2], F32)
    EPS = calloc("EPS", [128, 2], F32)
    SMALL = calloc("SMALL", [128, 2, 8], F32)
    ST6 = calloc("ST6", [128, 2, 6], F32)
    ARENA_O = (coff[0] + 63) // 64 * 64
    ARENA_SZ = 229344 - ARENA_O
    A = Arena(k, nc, ARENA_O, ARENA_SZ)

    BX = [Buf(f"X{t}") for t in range(4)]
    BH = [Buf(f"H{t}") for t in range(4)]
    BY = [Buf("Y0"), Buf("Y1")]
    for yi, hts in ((0, (0, 1)), (1, (2, 3))):
        for t in hts:
            BY[yi].al.append(BH[t])
            BH[t].al.append(BY[yi])
    RB = [Buf(f"ring{i}") for i in range(NSLOT)]
    BPP, BEXPBM, BPBC, BWSRAW, BTRILM, BWSM, BONES, BHALO, BEPS = [Buf(n) for n in (
        "PP", "EXPBM", "PBC", "WSRAW", "TRILM", "WSM", "ONES", "HALO", "EPS")]
    BRSTD = [Buf("RSTD0"), Buf("RSTD1")]
    BIDENT = Buf("IDENT")
    BSQ = [Buf("SQ0"), Buf("SQ1")]
    BSMALL = [Buf("SM0"), Buf("SM1")]
    if dry:
        PS = [DUMMY] * 8
    else:
        PS = [nc.alloc_psum_tensor(f"ps{i}", [128, 512], F32) for i in range(8)]
    BPS = [Buf(f"ps{i}") for i in range(8)]

    ws = WStream(k, RING, RB, wseq, ap_of)

    class Banks:
        def __init__(self):
            self.rot = list(range(8))
            self.i = 0

        def set(self, lst):
            self.rot = list(lst)
            self.i = 0

        def next(self):
            b = self.rot[self.i % len(self.rot)]
            self.i += 1
            return b

    banks = Banks()

    def mm(out, lhsT, rhs, first, last, reads, psb, inc=None):
        k.op("pe", lambda e: e.matmul(out, lhsT=lhsT, rhs=rhs, start=first, stop=last), reads=reads, writes=[psb],
             inc=last if inc is None else inc)

    def act(out, in_, func, reads, writes, bias=None, scale=None, accum=None):
        kw = {}
        if bias is not None:
            kw["bias"] = bias
        if scale is not None:
            kw["scale"] = scale
        if accum is not None:
            kw["accum_out"] = accum
        k.op("act", lambda e: e.activation(out=out, in_=in_, func=func, **kw), reads=reads, writes=writes)

    def tt_(out, in0, in1, op, reads, writes, eng="dve"):
        k.op(eng, lambda e: e.tensor_tensor(out=out, in0=in0, in1=in1, op=op), reads=reads, writes=writes)

    def tsc(out, in0, s1, s2, op0, op1, reads, writes, eng="dve"):
        if op1 is None:
            k.op(eng, lambda e: e.tensor_scalar(out=out, in0=in0, scalar1=s1, scalar2=None, op0=op0), reads=reads,
                 writes=writes)
        else:
            k.op(eng, lambda e: e.tensor_scalar(out=out, in0=in0, scalar1=s1, scalar2=s2, op0=op0, op1=op1),
                 reads=reads, writes=writes)

    def stt(out, in0, scalar, in1, op0, op1, reads, writes):
        k.op("dve", lambda e: e.scalar_tensor_tensor(out=out, in0=in0, scalar=scalar, in1=in1, op0=op0, op1=op1),
             reads=reads, writes=writes)

    def cpy(out, in_, reads, writes, eng="dve"):
        k.op(eng, lambda e: e.tensor_copy(out=out, in_=in_), reads=reads, writes=writes)

    def recip(out, in_, reads, writes):
        k.op("dve", lambda e: e.reciprocal(out=out, in_=in_), reads=reads, writes=writes)

    def memset(ap, val, writes, eng="dve"):
        k.op(eng, lambda e: e.memset(ap, val), writes=writes)

    dbg_out = []

    def chk(name):
        if stop_after == name:
            raise _Stop()

    def dump(name, ap, shape, dtype, bufs):
        if dry or dbg is None or name not in dbg:
            return
        t = nc.dram_tensor("dbg_" + name, list(shape), dtype, kind="ExternalOutput").ap()
        b = Buf("dbg_" + name)
        k.dma("sp", [(t, ap)], reads=bufs, writes=[b], sembuf=b)
        dbg_out.append(b)

    def proj_fm(tile, kcn, rhs_of, rbufs_of, evac, tiles=range(4)):
        for tt in tiles:
            b = banks.next()
            for kc in range(kcn):
                mm(PS[b][:, :], tile.ap[:, kc, :], rhs_of(kc, tt), kc == 0, kc == kcn - 1,
                   [tile.buf] + rbufs_of(tt), BPS[b])
            evac(tt, PS[b], BPS[b])

    def h_rhs(kc, tt):
        return H[:, kc, ts(tt)]

    def h_bufs(tt):
        return [BH[tt]]

    def rstd_from(psb_idx, ri, scale, epscol):
        act(RSTD[:, ri, :], PS[psb_idx][:, :], AF.Sqrt, [BPS[psb_idx], BEPS], [BRSTD[ri]], bias=EPS[:, epscol:epscol + 1],
            scale=scale)
        recip(RSTD[:, ri, :], RSTD[:, ri, :], [BRSTD[ri]], [BRSTD[ri]])

    nrm_ctr = [0]

    def norm_to_H(gcol, tiles):
        for tt in tiles:
            b = banks.next()
            for kc in range(8):
                si = kc % 2
                act(SQ[:, si, :], XRES[:, kc, ts(tt)], AF.Square, [BX[tt]], [BSQ[si]])
                mm(PS[b][:, :], ONESB[:, :], SQ[:, si, :], kc == 0, kc == 7, [BONES, BSQ[si]], BPS[b], inc=True)
            ri = nrm_ctr[0] % 2
            nrm_ctr[0] += 1
            rstd_from(b, ri, 1.0 / D, 0)
            for kc in range(8):
                stt(H[:, kc, ts(tt)], XRES[:, kc, ts(tt)], PP[:, gcol + kc:gcol + kc + 1], RSTD[:, ri, :], ALU.mult,
                    ALU.mult, [BX[tt], BPP, BRSTD[ri]], [BH[tt]])

    pend = []

    def y_chunk(ps_ap, psbuf, dc, ysel, sbank, first, last):
        ysl = slice(ysel * 512, ysel * 512 + 512)
        act(Y[:, dc, ysl], ps_ap, AF.Copy, [psbuf], [BY[ysel]])
        si = dc % 2
        act(SQ[:, si, :], ps_ap, AF.Square, [psbuf], [BSQ[si]])
        pend.append((sbank, si, first, last))
        if not DEFER_STATS:
            flush_stats()

    def flush_stats(keep=0):
        while len(pend) > keep:
            sbank, si, first, last = pend.pop(0)
            mm(PS[sbank][:, :], ONESB[:, :], SQ[:, si, :], first, last, [BONES, BSQ[si]], BPS[sbank], inc=True)

    def post_update(gcol, tt, ysel, sbank):
        flush_stats()
        ysl = slice(ysel * 512, ysel * 512 + 512)
        ri = nrm_ctr[0] % 2
        nrm_ctr[0] += 1
        rstd_from(sbank, ri, 1.0 / D, 0)
        for dc in range(8):
            tt_(Y[:, dc, ysl], Y[:, dc, ysl], RSTD[:, ri, :], ALU.mult, [BY[ysel], BRSTD[ri]], [BY[ysel]])
            stt(XRES[:, dc, ts(tt)], Y[:, dc, ysl], PP[:, gcol + dc:gcol + dc + 1], XRES[:, dc, ts(tt)], ALU.mult,
                ALU.add, [BY[ysel], BPP, BX[tt]], [BX[tt]])

    if not dry:
        k.dma("sp", [(PP[:, :], T["pp"][:, :])], writes=[BPP])
    memset(ONESB[:, :], 1.0, [BONES])
    memset(ONESF[:, :], 1.0, [BONES])
    memset(EPS[:, 0:1], 1e-6, [BEPS])
    memset(EPS[:, 1:2], 1e-5, [BEPS])
    if not dry:
        k.dma("pool", [(TRILM[:, :, :], T["trilm"][:, :, :])], writes=[BTRILM])
        k.dma("pool", [(IDENT[:, :], T["ident"][:, :])], writes=[BIDENT])
    A.reset()
    AB, BAB = A.alloc("abias", [128, 3072], F32)
    AM, BAM = A.alloc("amask", [128, 3072], F32)
    if not dry:
        k.dma("sp", [(AB[:, :], T["abias"][:, :])], writes=[BAB])
        k.dma("sp", [(AM[:, :], T["amask"][:, :])], writes=[BAM])
    if not dry:
        ebm_flat = EXPBM[:, :, :, :, :].rearrange("p g h c q -> p (g h c q)")
    else:
        ebm_flat = DUMMY
    act(AB[:, :], AB[:, :], AF.Exp, [BAB], [BAB])
    tt_(ebm_flat, AB[:, :], AM[:, :], ALU.mult, [BAB, BAM], [BEXPBM])

    def sublayer_mixer(l):
        pb = l * PPL
        banks.set(range(8))
        if not dry:
            k.dma("sp", [(PBC[:, :], T["pbc"][l, :, :])], writes=[BPBC])
            k.dma("pool", [(WSRAW[:, :, :], T["wsT"][l, :, :, :])], writes=[BWSRAW])
        tt_(WSM[:, :, :], WSRAW[:, :, :], TRILM[:, :, :], ALU.mult, [BWSRAW, BTRILM], [BWSM])
        norm_to_H(pb + 0, range(4))
        dump(f"h1_{l}", H[:, :, :], [128, 8, S], BF16, BH)
        if stop_after == "norm1":
            return True

        A.reset()
        OC, BOC = A.alloc("OC", [128, 2, S], BF16)
        keep_c = A.off
        Ut, BU = A.alloc("U", [128, 2, S], F32)
        Qb, BQ = A.alloc("Q", [128, 2, S], BF16, nbufs=2)
        Kb, BK = A.alloc("K", [128, 2, S], BF16, nbufs=2)
        VA, BVA = A.alloc("VA", [128, 2, 16, 2, 128], BF16, nbufs=2)
        NPT = 4
        PT, BPT = A.alloc("PT", [128, NPT, 2, 2, 128], BF16, nbufs=NPT)
        R_, BR = A.alloc("R", [128, 2, 512], F32, nbufs=2)
        memset(VA[:, :, :, 0, 64:128], 1.0, BVA)
        memset(VA[:, :, :, 1, 0:64], 1.0, BVA)
        gctr = 0
        for hp in range(2):
            for g in range(3):
                d = DILS[g]
                nb = 16 // d
                pq = gctr % 2
                gctr += 1
                tq = ws.get(("w_in", l, 16 + g * 2 + hp, 0, 8))
                tk = ws.get(("w_in", l, 22 + g * 2 + hp, 0, 8))
                tv = ws.get(("w_in", l, 28 + g * 2 + hp, 0, 8))
                for (tile, dst, bdst) in ((tq, Qb, BQ[pq]), (tk, Kb, BK[pq])):
                    def ev(tt, ps, psb, dst=dst, bdst=bdst, pq=pq):
                        act(dst[:, pq, ts(tt)], ps[:, :], AF.Copy, [psb], [bdst])
                    proj_fm(tile, 8, h_rhs, h_bufs, ev)
                    ws.done(tile)
                    chk("Cq")
                chk("Cqk")

                def tokslice(j, d=d, nb=nb):
                    c, n = divmod(j, nb)
                    st = c + d * 128 * n
                    return slice(st, st + d * 127 + 1, d), n

                for j0 in range(0, 16, 4):
                    b = banks.next()
                    for jj in range(4):
                        tok, n = tokslice(j0 + jj)
                        for kc in range(8):
                            mm(PS[b][:, jj * 128:(jj + 1) * 128], H[:, kc, tok], tv.ap[:, kc, :], kc == 0, kc == 7,
                               [tv.buf] + BH, BPS[b], inc=(kc == 7 and jj == 3))
                    psv = PS[b][:, :].rearrange("p (j c) -> p j c", c=128)
                    act(VA[:, pq, j0:j0 + 4, 0, 0:64], psv[:, :, 0:64], AF.Copy, [BPS[b]], [BVA[pq]])
                    act(VA[:, pq, j0:j0 + 4, 1, 64:128], psv[:, :, 64:128], AF.Copy, [BPS[b]], [BVA[pq]])
                ws.done(tv)
                chk("Cv")

                def stage1(j):
                    tok, n = tokslice(j)
                    pcs = (0, 1) if n > 0 else (1,)
                    eb = j % NPT
                    for h in range(2):
                        b = banks.next()
                        pr = slice(64 * h, 64 * h + 64)
                        for pc in pcs:
                            ktok, _ = tokslice(j - 1 if pc == 0 else j)
                            mm(PS[b][:, pc * 128:(pc + 1) * 128], Kb[pr, pq, ktok], Qb[pr, pq, tok], True, True,
                               [BQ[pq], BK[pq]], BPS[b], inc=(pc == 1))
                        psv = PS[b][:, 0:256].rearrange("p (c q) -> p c q", c=2)
                        if n > 0:
                            p_ap, s_ap = PT[:, eb, h], psv
                        else:
                            p_ap, s_ap = PT[:, eb, h, 1, :], psv[:, 1, :]
                        act(p_ap, s_ap, AF.Exp, [BPS[b]], [BPT[eb]], scale=0.125)
                    if n > 0:
                        p_ap, m_ap = PT[:, eb], EXPBM[:, g, hp * 2:hp * 2 + 2]
                    else:
                        p_ap, m_ap = PT[:, eb, :, 1, :], EXPBM[:, g, hp * 2:hp * 2 + 2, 1, :]
                    tt_(p_ap, p_ap, m_ap, ALU.mult, [BPT[eb], BEXPBM], [BPT[eb]])

                def stage2(j):
                    tok, n = tokslice(j)
                    pcs = (0, 1) if n > 0 else (1,)
                    eb = j % NPT
                    b2 = banks.next()
                    for h in range(2):
                        for i, pc in enumerate(pcs):
                            vj = j - 1 if pc == 0 else j
                            mm(PS[b2][:, h * 128:(h + 1) * 128], VA[:, pq, vj, h, :], PT[:, eb, h, pc, :], i == 0,
                               i == len(pcs) - 1, [BVA[pq], BPT[eb]], BPS[b2], inc=(h == 1 and i == len(pcs) - 1))
                    psu = PS[b2][:, 0:256].rearrange("p (h q) -> p h q", h=2)
                    if g == 0:
                        act(Ut[:, :, tok], psu, AF.Copy, [BPS[b2]], [BU])
                    else:
                        tt_(Ut[:, :, tok], psu, Ut[:, :, tok], ALU.add, [BPS[b2], BU], [BU])

                DEPTH_C = 2
                for j in range(16):
                    stage1(j)
                    if j >= DEPTH_C:
                        stage2(j - DEPTH_C)
                for j in range(16 - DEPTH_C, 16):
                    stage2(j)
            for tq in range(4):
                ri = tq % 2
                recip(R_[0:64, ri, :], Ut[64:128, 0, ts(tq)], [BU], [BR[ri]])
                recip(R_[64:128, ri, :], Ut[0:64, 1, ts(tq)], [BU], [BR[ri]])
                tt_(OC[0:64, hp, ts(tq)], Ut[0:64, 0, ts(tq)], R_[0:64, ri, :], ALU.mult, [BU, BR[ri]], [BOC])
                tt_(OC[64:128, hp, ts(tq)], Ut[64:128, 1, ts(tq)], R_[64:128, ri, :], ALU.mult, [BU, BR[ri]], [BOC])
        dump(f"oc{l}", OC[:, :, :], [128, 2, S], BF16, [BOC])
        if stop_after == "C":
            return True

        A.reset(keep_c)
        AACT, BAACT = A.alloc("AACT", [128, 4, S], BF16)
        keep_a = A.off
        APAD, BAPAD = A.alloc("APAD", [128, 4, 32 + S], BF16)
        tmp_o = A.off
        SG, BSG = A.alloc("SG", [128, 2, 512], F32, nbufs=2)
        A.reset(tmp_o)
        CT, BCT = A.alloc("CT", [128, 2, 4, 512], F32, nbufs=8)
        NDG = 8
        DG, BDG = A.alloc("DG", [128, NDG, 128], BF16, nbufs=NDG)
        SQF, BSQF = A.alloc("SQF", [128, 2, 512], F32, nbufs=2)
        MEAN, BMEAN = A.alloc("MEAN", [128, 512], F32)
        VAR, BVAR = A.alloc("VAR", [128, 512], F32)
        memset(APAD[:, :, 0:32], 0.0, [BAPAD])
        for c in range(4):
            tval = ws.get(("w_in", l, c, 0, 8))
            tgate = ws.get(("w_in", l, 4 + c, 0, 8))
            for tt in range(4):
                bv = banks.next()
                bg = banks.next()
                for kc in range(8):
                    mm(PS[bv][:, :], tval.ap[:, kc, :], H[:, kc, ts(tt)], kc == 0, kc == 7, [tval.buf, BH[tt]], BPS[bv])
                for kc in range(8):
                    mm(PS[bg][:, :], tgate.ap[:, kc, :], H[:, kc, ts(tt)], kc == 0, kc == 7, [tgate.buf, BH[tt]],
                       BPS[bg])
                si = tt % 2
                act(SG[:, si, :], PS[bg][:, :], AF.Sigmoid, [BPS[bg]], [BSG[si]])
                tt_(APAD[:, c, 32 + tt * 512:32 + (tt + 1) * 512], PS[bv][:, :], SG[:, si, :], ALU.mult,
                    [BPS[bv], BSG[si]], [BAPAD])
            ws.done(tval)
            ws.done(tgate)
        dump(f"glu{l}", APAD[:, :, :], [128, 4, 32 + S], BF16, [BAPAD])
        cw = pb + 80
        dctr = [0]

        def conv_tile(tt):
            ci = tt % 2
            base = 2 + tt * 512
            for c in range(4):
                b = banks.next()
                for kk in range(31):
                    sl = dctr[0] % NDG
                    dctr[0] += 1
                    wcol = PP[:, cw + kk * 4 + c:cw + kk * 4 + c + 1]
                    if kk % 2 == 0:
                        act(DG[:, sl, :], IDENT[:, :], AF.Identity, [BIDENT, BPP], [BDG[sl]], scale=wcol)
                    else:
                        tsc(DG[:, sl, :], IDENT[:, :], wcol, None, ALU.mult, None, [BIDENT, BPP], [BDG[sl]])
                    mm(PS[b][:, :], DG[:, sl, :], APAD[:, c, base + kk:base + kk + 512], kk == 0, kk == 30,
                       [BDG[sl], BAPAD], BPS[b], inc=True)
                act(CT[:, ci, c, :], PS[b][:, :], AF.Identity, [BPS[b], BPP], [BCT[ci * 4 + c]],
                    bias=PP[:, pb + 204 + c:pb + 205 + c])

        def ln_tile(tt):
            ci = tt % 2
            bsum = banks.next()
            bsq = banks.next()
            for c in range(4):
                o = CT[:, ci, c, :]
                bc = BCT[ci * 4 + c]
                si = c % 2
                act(SQF[:, si, :], o, AF.Square, [bc], [BSQF[si]])
                mm(PS[bsum][:, :], ONESF[:, :], o, c == 0, c == 3, [BONES, bc], BPS[bsum], inc=True)
                mm(PS[bsq][:, :], ONESF[:, :], SQF[:, si, :], c == 0, c == 3, [BONES, BSQF[si]], BPS[bsq], inc=True)
            act(MEAN[:, :], PS[bsum][:, :], AF.Identity, [BPS[bsum]], [BMEAN], scale=1.0 / 512)
            tt_(VAR[:, :], MEAN[:, :], MEAN[:, :], ALU.mult, [BMEAN], [BVAR])
            stt(VAR[:, :], PS[bsq][:, :], 1.0 / 512, VAR[:, :], ALU.mult, ALU.subtract, [BPS[bsq], BVAR], [BVAR])
            act(VAR[:, :], VAR[:, :], AF.Sqrt, [BVAR, BEPS], [BVAR], bias=EPS[:, 1:2])
            recip(VAR[:, :], VAR[:, :], [BVAR], [BVAR])
            for c in range(4):
                tt_(CT[:, ci, c, :], CT[:, ci, c, :], MEAN[:, :], ALU.subtract, [BCT[ci * 4 + c], BMEAN], [BCT[ci * 4 + c]])
            for c in range(4):
                tt_(CT[:, ci, c, :], CT[:, ci, c, :], VAR[:, :], ALU.mult, [BCT[ci * 4 + c], BVAR], [BCT[ci * 4 + c]])
            for c in range(4):
                act(AACT[:, c, ts(tt)], CT[:, ci, c, :], AF.Silu, [BCT[ci * 4 + c], BPP], [BAACT],
                    bias=PP[:, pb + 212 + c:pb + 213 + c], scale=PP[:, pb + 208 + c:pb + 209 + c])

        conv_tile(0)
        for tt in range(4):
            if tt + 1 < 4:
                conv_tile(tt + 1)
            ln_tile(tt)
        dump(f"aact{l}", AACT[:, :, :], [128, 4, S], BF16, [BAACT])
        if stop_after == "A":
            return True

        A.reset(keep_a)
        UB, BUB = A.alloc("UB", [128, 4, S], BF16)
        keep_b = A.off
        UU, BUU = A.alloc("UU", [128, 4, S], BF16)
        VG, BVG = A.alloc("VG", [128, 2, 512], F32, nbufs=2)
        VL, BVL = A.alloc("VL", [128, 2, 512], BF16, nbufs=2)
        TB, BTB = A.alloc("TB", [128, 512], F32)
        CB, BCB = A.alloc("CB", [128, 512], F32)
        brw = banks.next()
        for g in range(4):
            mm(PS[brw][:, g * 128:(g + 1) * 128], ONESB[:, :], WSM[:, g, :], True, True, [BONES, BWSM], BPS[brw],
               inc=(g == 3))
        for g in range(4):
            stt(CB[:, g * 128:(g + 1) * 128], PS[brw][:, g * 128:(g + 1) * 128], PP[:, pb + 308 + g:pb + 309 + g],
                PBC[:, 1024 + g * 128:1024 + (g + 1) * 128], ALU.mult, ALU.add, [BPS[brw], BPP, BPBC], [BCB])
        for c in range(4):
            tu = ws.get(("w_in", l, 8 + c, 0, 8))

            def ev(tt, ps, psb, c=c):
                act(UU[:, c, ts(tt)], ps[:, :], AF.Gelu_apprx_tanh, [psb], [BUU])
            proj_fm(tu, 8, h_rhs, h_bufs, ev)
            ws.done(tu)
        tvs = [ws.get(("w_in", l, 12 + cc, 0, 8)) for cc in range(4)]
        def b_stage_a(n):
            b = banks.next()
            tt = n // 4
            for cc in range(4):
                for kc in range(8):
                    mm(PS[b][:, cc * 128:(cc + 1) * 128], H[:, kc, n * 128:(n + 1) * 128], tvs[cc].ap[:, kc, :], kc == 0,
                       kc == 7, [tvs[cc].buf, BH[tt]], BPS[b], inc=(kc == 7 and cc == 3))
            vi = n % 2
            act(VG[:, vi, :], PS[b][:, :], AF.Gelu_apprx_tanh, [BPS[b]], [BVG[vi]])
            k.op("dve", lambda e, o=ST6[:, vi, :], i=VG[:, vi, :]: e.bn_stats(out=o, in_=i), reads=[BVG[vi]],
                 writes=[BSMALL[vi]])
            k.op("dve", lambda e, o=SMALL[:, vi, 2:4], i=ST6[:, vi, :]: e.bn_aggr(out=o, in_=i), reads=[BSMALL[vi]],
                 writes=[BSMALL[vi]])
            act(SMALL[:, vi, 5:6], SMALL[:, vi, 3:4], AF.Sqrt, [BSMALL[vi], BEPS], [BSMALL[vi]], bias=EPS[:, 1:2])

        def b_stage_b(n):
            vi = n % 2
            recip(SMALL[:, vi, 6:7], SMALL[:, vi, 5:6], [BSMALL[vi]], [BSMALL[vi]])
            tsc(VL[:, vi, :], VG[:, vi, :], SMALL[:, vi, 2:3], SMALL[:, vi, 6:7], ALU.subtract, ALU.mult,
                [BVG[vi], BSMALL[vi]], [BVL[vi]])
            b2 = banks.next()
            for g in range(4):
                mm(PS[b2][:, g * 128:(g + 1) * 128], VL[:, vi, g * 128:(g + 1) * 128], WSM[:, g, :], True, True,
                   [BVL[vi], BWSM], BPS[b2], inc=(g == 3))
            for g in range(4):
                stt(TB[:, g * 128:(g + 1) * 128], PS[b2][:, g * 128:(g + 1) * 128], PP[:, pb + 304 + g:pb + 305 + g],
                    CB[:, g * 128:(g + 1) * 128], ALU.mult, ALU.add, [BPS[b2], BPP, BCB], [BTB])
            tbv = TB[:, :].rearrange("p (g t) -> p g t", g=4)
            tt_(UB[:, :, n * 128:(n + 1) * 128], tbv, UU[:, :, n * 128:(n + 1) * 128], ALU.mult, [BTB, BUU], [BUB])

        b_stage_a(0)
        for n in range(16):
            if n + 1 < 16:
                b_stage_a(n + 1)
            b_stage_b(n)
        for t_ in tvs:
            ws.done(t_)
        dump(f"ub{l}", UB[:, :, :], [128, 4, S], BF16, [BUB])
        if stop_after == "B":
            return True

        A.reset(keep_b)
        MG, BMG = A.alloc("MG", [128, 8, 1024], BF16)
        SGM, BSGM = A.alloc("SGM", [128, 3, 512], F32)
        M0, BM0 = A.alloc("M0", [128, 512], F32)
        M1, BM1 = A.alloc("M1", [128, 512], F32)
        for hf in range(2):
            banks.set(range(8))
            for dc in range(8):
                wa = ws.get(("w_a", l, dc, 0, 4))
                wb = ws.get(("w_b", l, dc, 0, 4))
                wc = ws.get(("w_c", l, dc, 0, 2))
                wg = [ws.get(("w_in", l, 34 + i * 8 + dc, 0, 8)) for i in range(3)]
                for t2 in range(2):
                    tt = hf * 2 + t2
                    pa, pb_, pc_ = banks.next(), banks.next(), banks.next()
                    pg = [banks.next() for _ in range(3)]
                    for kc in range(4):
                        mm(PS[pa][:, :], wa.ap[:, kc, :], AACT[:, kc, ts(tt)], kc == 0, kc == 3, [wa.buf, BAACT], BPS[pa])
                    for kc in range(4):
                        mm(PS[pb_][:, :], wb.ap[:, kc, :], UB[:, kc, ts(tt)], kc == 0, kc == 3, [wb.buf, BUB], BPS[pb_])
                    for kc in range(2):
                        mm(PS[pc_][:, :], wc.ap[:, kc, :], OC[:, kc, ts(tt)], kc == 0, kc == 1, [wc.buf, BOC], BPS[pc_])
                    for i in range(3):
                        for kc in range(8):
                            mm(PS[pg[i]][:, :], wg[i].ap[:, kc, :], H[:, kc, ts(tt)], kc == 0, kc == 7,
                               [wg[i].buf, BH[tt]], BPS[pg[i]])
                    for i in range(3):
                        col = pb + 56 + i * 8 + dc
                        act(SGM[:, i, :], PS[pg[i]][:, :], AF.Sigmoid, [BPS[pg[i]], BPP], [BSGM], bias=PP[:, col:col + 1])
                    tt_(M0[:, :], PS[pa][:, :], SGM[:, 0, :], ALU.mult, [BPS[pa], BSGM], [BM0])
                    tt_(M1[:, :], PS[pb_][:, :], SGM[:, 1, :], ALU.mult, [BPS[pb_], BSGM], [BM1])
                    tt_(M0[:, :], M0[:, :], M1[:, :], ALU.add, [BM0, BM1], [BM0])
                    tt_(M1[:, :], PS[pc_][:, :], SGM[:, 2, :], ALU.mult, [BPS[pc_], BSGM], [BM1])
                    tt_(MG[:, dc, ts(t2)], M0[:, :], M1[:, :], ALU.add, [BM0, BM1], [BMG])
                for t_ in [wa, wb, wc] + wg:
                    ws.done(t_)
            if dbg is not None:
                dump(f"merged{l}_{hf}", MG[:, :, :], [128, 8, 1024], BF16, [BMG])
            banks.set(range(6))
            wm = [ws.get(("w_mix", l, dc, 0, 8)) for dc in range(8)]
            for t2 in range(2):
                tt = hf * 2 + t2
                sb_ = 6 + t2
                for dc in range(8):
                    b = banks.next()
                    for kc in range(8):
                        mm(PS[b][:, :], wm[dc].ap[:, kc, :], MG[:, kc, ts(t2)], kc == 0, kc == 7, [wm[dc].buf, BMG], BPS[b])
                    flush_stats()
                    y_chunk(PS[b][:, :], BPS[b], dc, hf, sb_, dc == 0, dc == 7)
                post_update(pb + 8, tt, hf, sb_)
            for t_ in wm:
                ws.done(t_)
        return False

    def sublayer_xattn(l, s):
        pb = l * PPL
        banks.set(range(8))
        A.reset()
        MEMS, BMEMS = A.alloc("MEMS", [128, 8, 256], F32)
        KT, BKT = A.alloc("KT", [128, 8, 256], BF16)
        VX, BVX = A.alloc("VX", [128, 2, 1024], BF16)
        QX, BQX = A.alloc("QX", [128, 8, S], BF16)
        PTX, BPTX = A.alloc("PTX", [128, 2, 2, 512], BF16, nbufs=2)
        RD, BRD = A.alloc("RD", [128, 2, 512], F32, nbufs=2)
        memn_o = A.off
        MEMN, BMEMN = A.alloc("MEMN", [128, 8, 256], BF16)
        A.reset(0)
        OX0, BOX0 = A.alloc("OX0", [128, 8, 512], BF16)
        A.reset(memn_o)
        OX1, BOX1 = A.alloc("OX1", [128, 8, 512], BF16)
        OXs, BOXs = [OX0, OX1], [BOX0, BOX1]
        if not dry:
            k.dma("sp", [(MEMS[:, kc, :], T["memT"][s, kc, :, :]) for kc in range(8)], writes=[BMEMS])
        b = banks.next()
        for kc in range(8):
            si = kc % 2
            act(SQ[:, si, 0:256], MEMS[:, kc, :], AF.Square, [BMEMS], [BSQ[si]])
            mm(PS[b][:, 0:256], ONESB[:, :], SQ[:, si, 0:256], kc == 0, kc == 7, [BONES, BSQ[si]], BPS[b], inc=True)
        ri = nrm_ctr[0] % 2
        nrm_ctr[0] += 1
        act(RSTD[:, ri, 0:256], PS[b][:, 0:256], AF.Sqrt, [BPS[b], BEPS], [BRSTD[ri]], bias=EPS[:, 0:1], scale=1.0 / D)
        recip(RSTD[:, ri, 0:256], RSTD[:, ri, 0:256], [BRSTD[ri]], [BRSTD[ri]])
        for kc in range(8):
            stt(MEMN[:, kc, :], MEMS[:, kc, :], PP[:, pb + 32 + kc:pb + 33 + kc], RSTD[:, ri, 0:256], ALU.mult, ALU.mult,
                [BMEMS, BPP, BRSTD[ri]], [BMEMN])
        def kproj(ec):
            w = ws.get(("w_xkv", l, ec, 0, 8))
            b = banks.next()
            for kc in range(8):
                mm(PS[b][:, 0:256], w.ap[:, kc, :], MEMN[:, kc, :], kc == 0, kc == 7, [w.buf, BMEMN], BPS[b])
            act(KT[:, ec, :], PS[b][:, 0:256], AF.Copy, [BPS[b]], [BKT])
            ws.done(w)

        def vproj(ec):
            w = ws.get(("w_xkv", l, 8 + ec, 0, 8))
            b = banks.next()
            for mb in range(2):
                for kc in range(8):
                    mm(PS[b][:, mb * 128:(mb + 1) * 128], MEMN[:, kc, mb * 128:(mb + 1) * 128], w.ap[:, kc, :], kc == 0,
                       kc == 7, [w.buf, BMEMN], BPS[b], inc=(kc == 7 and mb == 1))
            psv = PS[b][:, 0:256].rearrange("p (m c) -> p m c", m=2)
            act(VX[:, :, ec * 128:(ec + 1) * 128], psv, AF.Copy, [BPS[b]], [BVX])
            ws.done(w)

        for tt in range(4):
            norm_to_H(pb + 16, (tt,))
            for ec in (2 * tt, 2 * tt + 1):
                kproj(ec)
        for ec in range(8):
            vproj(ec)
        for dc in range(8):
            w = ws.get(("w_xq", l, dc, 0, 8))

            def ev(tt, ps, psb, dc=dc):
                act(QX[:, dc, ts(tt)], ps[:, :], AF.Copy, [psb], [BQX])
            proj_fm(w, 8, h_rhs, h_bufs, ev)
            ws.done(w)
        wo = [ws.get(("w_xo", l, dc, 0, 8)) for dc in range(8)]
        banks.set(range(6))

        def attn_tile(tt):
            OX, BOX = OXs[tt % 2], BOXs[tt % 2]

            def xs1(hd):
                pi = hd % 2
                bs_ = [banks.next(), banks.next()]
                for mb in range(2):
                    for ec in range(2):
                        mm(PS[bs_[mb]][:, :], KT[:, hd * 2 + ec, mb * 128:(mb + 1) * 128], QX[:, hd * 2 + ec, ts(tt)],
                           ec == 0, ec == 1, [BKT, BQX], BPS[bs_[mb]])
                    act(PTX[:, pi, mb, :], PS[bs_[mb]][:, :], AF.Exp, [BPS[bs_[mb]]], [BPTX[pi]], scale=1.0 / 16)

            def xs2(hd):
                pi = hd % 2
                bd = banks.next()
                for mb in range(2):
                    mm(PS[bd][:, :], ONESB[:, :], PTX[:, pi, mb, :], mb == 0, mb == 1, [BONES, BPTX[pi]], BPS[bd])
                recip(RD[:, pi, :], PS[bd][:, :], [BPS[bd]], [BRD[pi]])
                for ec in range(2):
                    bo = banks.next()
                    for mb in range(2):
                        mm(PS[bo][:, :], VX[:, mb, (hd * 2 + ec) * 128:(hd * 2 + ec + 1) * 128], PTX[:, pi, mb, :], mb == 0,
                           mb == 1, [BVX, BPTX[pi]], BPS[bo])
                    tt_(OX[:, hd * 2 + ec, :], PS[bo][:, :], RD[:, pi, :], ALU.mult, [BPS[bo], BRD[pi]], [BOX])

            for hd in range(4):
                xs1(hd)
                if hd > 0:
                    xs2(hd - 1)
            xs2(3)

        def xo_tile(tt):
            OX, BOX = OXs[tt % 2], BOXs[tt % 2]
            ysel = tt % 2
            sb_ = 6 + (tt % 2)
            for dc in range(8):
                b = banks.next()
                for kc in range(8):
                    mm(PS[b][:, :], wo[dc].ap[:, kc, :], OX[:, kc, :], kc == 0, kc == 7, [wo[dc].buf, BOX], BPS[b])
                flush_stats()
                y_chunk(PS[b][:, :], BPS[b], dc, ysel, sb_, dc == 0, dc == 7)
            flush_stats()
            if not XATTN_DEFER_POST:
                post_update(pb + 24, tt, ysel, sb_)

        def xpost(tt):
            post_update(pb + 24, tt, tt % 2, 6 + (tt % 2))

        if XATTN_DEFER_POST:
            for tt in range(4):
                attn_tile(tt)
                if tt > 0:
                    xpost(tt - 1)
                xo_tile(tt)
            xpost(3)
        elif XATTN_PIPE:
            for tt in range(4):
                attn_tile(tt)
                if tt > 0:
                    xo_tile(tt - 1)
            xo_tile(3)
        else:
            for tt in range(4):
                attn_tile(tt)
                xo_tile(tt)
        for t_ in wo:
            ws.done(t_)

    def sublayer_ffn(l, tile_done=None):
        pb = l * PPL
        A.reset()
        ACTB, BACTB = A.alloc("ACTB", [128, 22, 1024], BF16)
        GT, BGT = A.alloc("GT", [128, 2, 516], F32, nbufs=2)
        CV, BCV = A.alloc("CV", [128, 512], F32)
        GL, BGL = A.alloc("GL", [128, 2, 512], F32, nbufs=2)
        memset(HALO[:, :, :], 0.0, [BHALO])
        fw = pb + 216
        gi = 0
        for hf in range(2):
            banks.set(range(8))
            norm_to_H(pb + 40, (2 * hf, 2 * hf + 1))
            for j in range(22):
                wg = ws.get(("w_up", l, j, 0, 8))
                wv = ws.get(("w_up", l, 22 + j, 0, 8))
                for t2 in range(2):
                    tt = hf * 2 + t2
                    bg, bv = banks.next(), banks.next()
                    for kc in range(8):
                        mm(PS[bg][:, :], wg.ap[:, kc, :], H[:, kc, ts(tt)], kc == 0, kc == 7, [wg.buf, BH[tt]], BPS[bg])
                    for kc in range(8):
                        mm(PS[bv][:, :], wv.ap[:, kc, :], H[:, kc, ts(tt)], kc == 0, kc == 7, [wv.buf, BH[tt]], BPS[bv])
                    g2 = gi % 2
                    gi += 1
                    act(GT[:, g2, 2:514], PS[bg][:, :], AF.Copy, [BPS[bg]], [BGT[g2]])
                    cpy(GT[:, g2, 0:2], HALO[:, j, :], [BHALO], [BGT[g2]])
                    cpy(HALO[:, j, :], GT[:, g2, 512:514], [BGT[g2]], [BHALO])
                    act(CV[:, :], PS[bg][:, :], AF.Identity, [BPS[bg], BPP], [BCV], bias=PP[:, pb + 282 + j:pb + 283 + j],
                        scale=PP[:, fw + 44 + j:fw + 45 + j])
                    stt(CV[:, :], GT[:, g2, 0:512], PP[:, fw + j:fw + j + 1], CV[:, :], ALU.mult, ALU.add,
                        [BGT[g2], BPP, BCV], [BCV])
                    stt(CV[:, :], GT[:, g2, 1:513], PP[:, fw + 22 + j:fw + 23 + j], CV[:, :], ALU.mult, ALU.add,
                        [BGT[g2], BPP, BCV], [BCV])
                    act(GL[:, g2, :], CV[:, :], AF.Gelu_apprx_tanh, [BCV], [BGL[g2]])
                    tt_(ACTB[:, j, ts(t2)], PS[bv][:, :], GL[:, g2, :], ALU.mult, [BPS[bv], BGL[g2]], [BACTB])
                ws.done(wg)
                ws.done(wv)
            banks.set(range(6))
            ysels = (1, 0) if hf == 0 else (0, 1)
            for dc in range(8):
                wd = [ws.get(("w_down", l, dc, 0, 8)), ws.get(("w_down", l, dc, 8, 8)), ws.get(("w_down", l, dc, 16, 6))]
                for t2 in range(2):
                    b = banks.next()
                    for kc in range(22):
                        w = wd[kc // 8]
                        mm(PS[b][:, :], w.ap[:, kc % 8, :], ACTB[:, kc, ts(t2)], kc == 0, kc == 21, [w.buf, BACTB], BPS[b])
                    flush_stats()
                    y_chunk(PS[b][:, :], BPS[b], dc, ysels[t2], 6 + t2, dc == 0, dc == 7)
                for w in wd:
                    ws.done(w)
            for t2 in range(2):
                tt = hf * 2 + t2
                post_update(pb + 48, tt, ysels[t2], 6 + t2)
                if tile_done is not None:
                    tile_done(tt)

    stopped = False

    def load_x(s_, tt):
        if not dry:
            k.dma("sp", [(XRES[:, kc, ts(tt)], T["xT"][s_, kc, :, ts(tt)]) for kc in range(8)], writes=[BX[tt]])

    def store_x(s_, tt):
        if not dry:
            k.dma("sp", [(T["outT"][s_, kc, :, ts(tt)], XRES[:, kc, ts(tt)]) for kc in range(8)], reads=[BX[tt]],
                  sembuf=BX[tt])

    try:
        for tt in range(4):
            load_x(0, tt)
        for s in range(nseq):
            def tile_done(tt, s=s):
                store_x(s, tt)
                if s + 1 < nseq:
                    load_x(s + 1, tt)
            for l in range(nlayer):
                stopped = sublayer_mixer(l)
                if stopped:
                    break
                dump(f"x1_{l}", XRES[:, :, :], [128, 8, S], F32, BX)
                if stop_after == "mixer":
                    stopped = True
                    break
                sublayer_xattn(l, s)
                dump(f"x2_{l}", XRES[:, :, :], [128, 8, S], F32, BX)
                if stop_after == "xattn":
                    stopped = True
                    break
                sublayer_ffn(l, tile_done if l == nlayer - 1 else None)
            if stopped:
                for tt in range(4):
                    store_x(s, tt)
                break
    except _Stop:
        for tt in range(4):
            store_x(0, tt)
    k.wait_all("sp", BX + dbg_out)
    return ws


def make_program(nseq=SEQ_PER_CORE, nlayer=DEPTH, stop_after=None, dbg=None):
    kd = K(None, dry=True)
    wsd = build_program(None, kd, None, nseq, nlayer, stop_after, dbg)
    seq = wsd.rec
    nc = bass.Bass("TRN2", target_bir_lowering=False)
    k = K(nc)
    ws = build_program(nc, k, seq, nseq, nlayer, stop_after, dbg)
    assert ws.idx == len(seq) and ws.loaded == len(seq)
    return nc, k


def _t5_bucket(n):
    n = np.maximum(n, 0)
    nf = np.maximum(n, 1).astype(np.float32)
    large = 16 + (np.log(nf / np.float32(16)) / np.float32(np.log(2048 / 16)) * np.float32(16)).astype(np.int32)
    large = np.minimum(large, 31)
    return np.where(n < 16, n, large)


def _wtiles(W):
    L, Kd, N = W.shape
    return np.ascontiguousarray(W.reshape(L, Kd // 128, 128, N // 128, 128).transpose(0, 3, 2, 1, 4))


def _cols(v):
    v = np.asarray(v)
    C = v.shape[-1]
    lead = v.shape[:-1]
    a = v.reshape(*lead, C // 128, 128)
    a = np.moveaxis(a, -1, 0)
    return a.reshape(128, -1)


def prep_shared(inp):
    L = DEPTH
    pp = np.zeros((128, L * PPL), np.float32)
    for l in range(L):
        o = l * PPL
        pp[:, o + 0:o + 8] = _cols(inp["mix_pre_g"][l])
        pp[:, o + 8:o + 16] = _cols(inp["mix_post_g"][l])
        pp[:, o + 16:o + 24] = _cols(inp["x_pre_g"][l])
        pp[:, o + 24:o + 32] = _cols(inp["x_post_g"][l])
        pp[:, o + 32:o + 40] = _cols(inp["mem_g"][l])
        pp[:, o + 40:o + 48] = _cols(inp["ffn_pre_g"][l])
        pp[:, o + 48:o + 56] = _cols(inp["ffn_post_g"][l])
        pp[:, o + 56:o + 80] = _cols(inp["b_gate"][l])
        pp[:, o + 80:o + 204] = _cols(inp["conv_a_w"][l])
        pp[:, o + 204:o + 208] = _cols(inp["conv_a_b"][l])
        pp[:, o + 208:o + 212] = _cols(inp["ln_a_g"][l])
        pp[:, o + 212:o + 216] = _cols(inp["ln_a_b"][l])
        pp[:, o + 216:o + 282] = _cols(inp["conv_f_w"][l])
        pp[:, o + 282:o + 304] = _cols(inp["conv_f_b"][l])
        pp[:, o + 304:o + 308] = _cols(inp["ln_b_g"][l])
        pp[:, o + 308:o + 312] = _cols(inp["ln_b_b"][l])
    rb = np.asarray(inp["rel_bias"])
    kk = np.arange(128)[:, None]
    qq = np.arange(128)[None, :]
    abias = np.zeros((128, 3, 4, 2, 128), np.float32)
    amask = np.zeros((128, 3, 4, 2, 128), np.float32)
    for g, dil in enumerate(DILS):
        for pc in range(2):
            rel = qq + 128 - kk if pc == 0 else qq - kk
            valid = (rel >= 0) & (rel <= 128)
            bucket = _t5_bucket(rel * dil)
            for h in range(4):
                abias[:, g, h, pc, :] = rb[bucket, g * 4 + h]
                amask[:, g, h, pc, :] = valid
    pbc = np.zeros((L, 128, 1536), np.float32)
    for l in range(L):
        pbc[l, :, 0:512] = np.broadcast_to(inp["ln_b_g"][l][None, :], (128, 512))
        pbc[l, :, 512:1024] = np.broadcast_to(inp["ln_b_b"][l][None, :], (128, 512))
        pbc[l, :, 1024:1536] = np.broadcast_to(np.asarray(inp["b_s"][l]).reshape(1, 512), (128, 512))
    wsT = np.ascontiguousarray(np.asarray(inp["w_s"]).transpose(0, 3, 1, 2))
    tril = (np.arange(128)[None, :] >= np.arange(128)[:, None]).astype(np.float32)
    trilm = np.ascontiguousarray(np.broadcast_to(tril[:, None, :], (128, 4, 128)))
    w_c = np.asarray(inp["w_c_out"])
    shared = {
        "pp": pp,
        "abias": abias.reshape(128, 3072),
        "amask": amask.reshape(128, 3072),
        "pbc": pbc,
        "wsT": wsT,
        "trilm": trilm,
        "ident": np.eye(128, dtype=np.float32),
        "w_in": _wtiles(np.asarray(inp["w_in"])),
        "w_a": _wtiles(np.asarray(inp["w_a_out"])),
        "w_b": _wtiles(np.asarray(inp["w_b_out"])),
        "w_c": _wtiles(w_c),
        "w_mix": _wtiles(np.asarray(inp["w_mix_out"])),
        "w_xq": _wtiles(np.asarray(inp["w_xq"])),
        "w_xkv": _wtiles(np.asarray(inp["w_xkv"])),
        "w_xo": _wtiles(np.asarray(inp["w_xo"])),
        "w_up": _wtiles(np.asarray(inp["w_up"])),
        "w_down": _wtiles(np.asarray(inp["w_down"])),
    }
    return shared


def prep_core(inp, c, nseq=SEQ_PER_CORE):
    xs = np.asarray(inp["x"][c * SEQ_PER_CORE:c * SEQ_PER_CORE + nseq])
    ms = np.asarray(inp["mem"][c * SEQ_PER_CORE:c * SEQ_PER_CORE + nseq])
    xT = np.zeros((SEQ_PER_CORE, 8, 128, S), np.float32)
    mT = np.zeros((SEQ_PER_CORE, 8, 128, 256), np.float32)
    xT[:nseq] = xs.transpose(0, 2, 1).reshape(nseq, 8, 128, S)
    mT[:nseq] = ms.transpose(0, 2, 1).reshape(nseq, 8, 128, 256)
    return {"xT": xT, "memT": mT}


_PROG = {}


def kernel(**inputs):
    if "p" not in _PROG:
        _PROG["p"] = make_program()
    nc, _ = _PROG["p"]
    shared = prep_shared(inputs)
    in_maps = []
    for c in range(N_CORES):
        m = dict(shared)
        m.update(prep_core(inputs, c))
        in_maps.append(m)
    res = run_bass_kernel_spmd(nc, in_maps, core_ids=list(range(N_CORES)))
    out = np.empty((N_CORES * SEQ_PER_CORE, S, D), np.float32)
    for c in range(N_CORES):
        oT = res.results[c]["outT"]
        out[c * SEQ_PER_CORE:(c + 1) * SEQ_PER_CORE] = oT.reshape(SEQ_PER_CORE, D, S).transpose(0, 2, 1)
    return out
```

```python
import numpy as np
import concourse.bass as bass
import concourse.mybir as mybir
from concourse.bass_utils import run_bass_kernel_spmd

F32 = mybir.dt.float32
BF16 = mybir.dt.bfloat16
AF = mybir.ActivationFunctionType
ALU = mybir.AluOpType
AX = mybir.AxisListType

EPOCH = 12000
SAME_SYNC = True

N_CORES = 8
SEQ_PER_CORE = 4
DEPTH = 2
D = 1024
S = 2048
NSLOT = 10
XATTN_PIPE = False
XATTN_DEFER_POST = False
DEFER_STATS = True
PPL = 312
DILS = (1, 4, 16)


class Buf:
    __slots__ = ("name", "w", "r", "al", "dsem", "dcnt")

    def __init__(self, name):
        self.name = name
        self.w = None
        self.r = {}
        self.al = []
        self.dsem = None
        self.dcnt = 0


class Dummy:
    shape = ()

    def __getitem__(self, i):
        return self

    def rearrange(self, *a, **k):
        return self


DUMMY = Dummy()


class Eng:
    def __init__(self, K, name, obj):
        self.name = name
        self.obj = obj
        self.sem = None if K.dry else K.nc.alloc_semaphore(f"s_{name}_0")
        self.nsem = 1
        self.cnt = 0
        self.own = set() if K.dry else {id(self.sem)}
        self.waited = {}
        self.nwaits = 0
        self.nins = 0


class K:
    def __init__(self, nc, dry=False):
        self.nc = nc
        self.dry = dry
        if dry:
            self.engs = {n: Eng(self, n, None) for n in ("pe", "act", "dve", "pool", "sp")}
        else:
            self.engs = {
                "pe": Eng(self, "pe", nc.tensor),
                "act": Eng(self, "act", nc.scalar),
                "dve": Eng(self, "dve", nc.vector),
                "pool": Eng(self, "pool", nc.gpsimd),
                "sp": Eng(self, "sp", nc.sync),
            }
        self.semobj = {}
        if not dry:
            for e in self.engs.values():
                self.semobj[id(e.sem)] = e.sem
        self.ndsem = 0

    def _deps(self, reads, writes):
        deps = {}
        raw = {}

        def add(d, israw):
            if d is None:
                return
            s, v = d
            if deps.get(s, 0) < v:
                deps[s] = v
            if israw and raw.get(s, 0) < v:
                raw[s] = v

        for b in reads:
            add(b.w, True)
            for a in b.al:
                add(a.w, True)
        for b in writes:
            add(b.w, False)
            for d in b.r.values():
                add(d, False)
            for a in b.al:
                add(a.w, False)
                for d in a.r.values():
                    add(d, False)
        self._raw = raw
        return deps

    def _wait(self, E, deps):
        for s, v in deps.items():
            if s in E.own:
                if E.name in ("pe", "sp") or not SAME_SYNC:
                    continue
                if s == id(E.sem) and v > E.cnt:
                    continue
                if E.name in ("act", "dve") and (s != id(E.sem) or v < E.cnt):
                    continue
                if E.name in ("act", "dve") and self._raw.get(s, 0) < v:
                    continue
            if E.waited.get(s, 0) >= v:
                continue
            E.obj.wait_ge(self.semobj[s], v)
            E.waited[s] = v
            E.nwaits += 1

    def _tag(self, E, ins, inc):
        E.nins += 1
        if inc:
            ins.then_inc(E.sem, 1)
            E.cnt += 1
            tag = (id(E.sem), E.cnt)
            if E.cnt >= EPOCH:
                E.sem = self.nc.alloc_semaphore(f"s_{E.name}_{E.nsem}")
                E.nsem += 1
                E.cnt = 0
                E.own.add(id(E.sem))
                self.semobj[id(E.sem)] = E.sem
        else:
            tag = (id(E.sem), E.cnt + 1)
        return tag

    def op(self, eng, fn, reads=(), writes=(), inc=True):
        if self.dry:
            return None
        E = self.engs[eng]
        self._wait(E, self._deps(reads, writes))
        ins = fn(E.obj)
        tag = self._tag(E, ins, inc)
        for b in writes:
            b.w = tag
            b.r = {}
        for b in reads:
            b.r[eng] = tag
        return ins

    def dma(self, eng, pairs, reads=(), writes=(), sembuf=None):
        if self.dry:
            return None
        E = self.engs[eng]
        sb = sembuf or (writes[0] if writes else reads[0])
        if sb.dsem is None:
            sb.dsem = self.nc.alloc_semaphore(f"d_{self.ndsem}")
            self.ndsem += 1
            self.semobj[id(sb.dsem)] = sb.dsem
        deps = self._deps(reads, writes)
        if sb.dcnt:
            s = id(sb.dsem)
            if deps.get(s, 0) < sb.dcnt:
                deps[s] = sb.dcnt
        self._wait(E, deps)
        for (o, i) in pairs:
            E.obj.dma_start(out=o, in_=i).then_inc(sb.dsem, 16)
            E.nins += 1
            sb.dcnt += 16
        tag = (id(sb.dsem), sb.dcnt)
        for b in writes:
            b.w = tag
            b.r = {}
        for b in reads:
            b.r["dma_" + sb.name] = tag
        return tag

    def wait_all(self, eng, bufs):
        if self.dry:
            return
        E = self.engs[eng]
        self._wait(E, self._deps(bufs, bufs))

    def stats(self):
        return {e.name: (e.nins, e.nwaits, e.nsem) for e in self.engs.values()}


class _Stop(Exception):
    pass


class Tile:
    __slots__ = ("i", "ap", "buf")

    def __init__(self, i, ap, buf):
        self.i = i
        self.ap = ap
        self.buf = buf


class WStream:
    def __init__(self, k, ring, rbufs, seq, ap_of):
        self.k = k
        self.ring = ring
        self.rbufs = rbufs
        self.seq = seq
        self.ap_of = ap_of
        self.rec = []
        self.idx = 0
        self.loaded = 0
        self.done_ = set()

    def _issue(self):
        while self.loaded < len(self.seq):
            i = self.loaded
            if i - NSLOT >= 0 and (i - NSLOT) not in self.done_:
                break
            slot = i % NSLOT
            src, kcn = self.ap_of(self.seq[i])
            self.k.dma("pool", [(self.ring[:, slot, 0:kcn, :], src)], writes=[self.rbufs[slot]])
            self.loaded += 1

    def get(self, desc):
        i = self.idx
        self.idx += 1
        if self.k.dry:
            self.rec.append(desc)
            return Tile(i, DUMMY, None)
        assert self.seq[i] == desc, (i, self.seq[i], desc)
        self._issue()
        assert self.loaded > i, f"weight ring too small at tile {i} {desc}"
        slot = i % NSLOT
        return Tile(i, self.ring[:, slot], self.rbufs[slot])

    def done(self, t):
        if self.k.dry:
            return
        self.done_.add(t.i)
        self._issue()


class Arena:
    def __init__(self, k, nc, base, size):
        self.k, self.nc, self.base, self.size = k, nc, base, size
        self.off = 0
        self.regs = []
        self.n = 0

    def reset(self, off=0):
        self.off = off

    def alloc(self, name, shape, dtype, nbufs=1):
        esz = 2 if dtype == BF16 else 4
        per = esz
        for s_ in shape[1:]:
            per *= s_
        per = (per + 63) // 64 * 64
        lo = self.off
        hi = lo + per
        assert hi <= self.size, f"arena overflow {name}: {hi} > {self.size}"
        self.off = hi
        bufs = [Buf(f"{name}{i}") for i in range(nbufs)]
        pbuf = per // nbufs
        new = []
        for bi, b in enumerate(bufs):
            blo, bhi = lo + bi * pbuf, (lo + (bi + 1) * pbuf if bi < nbufs - 1 else hi)
            for (l2, h2, b2) in self.regs:
                if l2 < bhi and blo < h2:
                    if b2 not in b.al:
                        b.al.append(b2)
                    if b not in b2.al:
                        b2.al.append(b)
            new.append((blo, bhi, b))
        self.regs.extend(new)
        if self.k.dry:
            t = DUMMY
        else:
            self.n += 1
            t = self.nc.alloc_sbuf_tensor_at(f"ar{self.n}_{name}", list(shape), dtype, offset=self.base + lo)
        return (t, bufs[0]) if nbufs == 1 else (t, bufs)


def ts(tt, n=512):
    return slice(tt * n, (tt + 1) * n)


def build_program(nc, k, wseq, nseq=SEQ_PER_CORE, nlayer=DEPTH, stop_after=None, dbg=None):
    dry = k.dry
    T = {}
    if not dry:
        def dram(name, shape, kind="ExternalInput"):
            T[name] = nc.dram_tensor(name, list(shape), F32, kind=kind).ap()
        dram("xT", [SEQ_PER_CORE, 8, 128, S])
        dram("memT", [SEQ_PER_CORE, 8, 128, 256])
        dram("pp", [128, DEPTH * PPL])
        dram("abias", [128, 3072])
        dram("amask", [128, 3072])
        dram("pbc", [DEPTH, 128, 1536])
        dram("wsT", [DEPTH, 128, 4, 128])
        dram("trilm", [128, 4, 128])
        dram("ident", [128, 128])
        dram("w_in", [DEPTH, 58, 128, 8, 128])
        dram("w_a", [DEPTH, 8, 128, 4, 128])
        dram("w_b", [DEPTH, 8, 128, 4, 128])
        dram("w_c", [DEPTH, 8, 128, 2, 128])
        dram("w_mix", [DEPTH, 8, 128, 8, 128])
        dram("w_xq", [DEPTH, 8, 128, 8, 128])
        dram("w_xkv", [DEPTH, 16, 128, 8, 128])
        dram("w_xo", [DEPTH, 8, 128, 8, 128])
        dram("w_up", [DEPTH, 44, 128, 8, 128])
        dram("w_down", [DEPTH, 8, 128, 22, 128])
        dram("outT", [SEQ_PER_CORE, 8, 128, S], kind="ExternalOutput")

    def ap_of(desc):
        name, l, j, k0, kcn = desc
        return T[name][l, j, :, k0:k0 + kcn, :], kcn

    BASE = 16512
    XRES_O = BASE
    H_O = XRES_O + 65536
    RING_O = H_O + 32768
    C_O = RING_O + NSLOT * 2048
    coff = [C_O]

    def calloc(name, shape, dtype):
        esz = 2 if dtype == BF16 else 4
        per = esz
        for s_ in shape[1:]:
            per *= s_
        per = (per + 63) // 64 * 64
        o = coff[0]
        coff[0] += per
        if dry:
            return DUMMY
        return nc.alloc_sbuf_tensor_at(name, list(shape), dtype, offset=o)

    if dry:
        XRES = H = Y = RING = DUMMY
    else:
        XRES = nc.alloc_sbuf_tensor_at("XRES", [128, 8, S], F32, offset=XRES_O)
        H = nc.alloc_sbuf_tensor_at("H", [128, 8, S], BF16, offset=H_O)
        Y = nc.alloc_sbuf_tensor_at("Y", [128, 8, 1024], F32, offset=H_O)
        RING = nc.alloc_sbuf_tensor_at("RING", [128, NSLOT, 8, 128], BF16, offset=RING_O)
    PP = calloc("PP", [128, DEPTH * PPL], F32)
    EXPBM = calloc("EXPBM", [128, 3, 4, 2, 128], BF16)
    PBC = calloc("PBC", [128, 1536], F32)
    WSRAW = calloc("WSRAW", [128, 4, 128], BF16)
    TRILM = calloc("TRILM", [128, 4, 128], BF16)
    WSM = calloc("WSM", [128, 4, 128], BF16)
    ONESB = calloc("ONESB", [128, 128], BF16)
    IDENT = calloc("IDENT", [128, 128], BF16)
    ONESF = calloc("ONESF", [128, 128], F32)
    RSTD = calloc("RSTD", [128, 2, 512], F32)
    SQ = calloc("SQ", [128, 2, 512], BF16)
    HALO = calloc("HALO", [128, 22, 2], F32)
    EPS = calloc("EPS", [128, 2], F32)
    SMALL = calloc("SMALL", [128, 2, 8], F32)
    ST6 = calloc("ST6", [128, 2, 6], F32)
    ARENA_O = (coff[0] + 63) // 64 * 64
    ARENA_SZ = 229344 - ARENA_O
    A = Arena(k, nc, ARENA_O, ARENA_SZ)

    BX = [Buf(f"X{t}") for t in range(4)]
    BH = [Buf(f"H{t}") for t in range(4)]
    BY = [Buf("Y0"), Buf("Y1")]
    for yi, hts in ((0, (0, 1)), (1, (2, 3))):
        for t in hts:
            BY[yi].al.append(BH[t])
            BH[t].al.append(BY[yi])
    RB = [Buf(f"ring{i}") for i in range(NSLOT)]
    BPP, BEXPBM, BPBC, BWSRAW, BTRILM, BWSM, BONES, BHALO, BEPS = [Buf(n) for n in (
        "PP", "EXPBM", "PBC", "WSRAW", "TRILM", "WSM", "ONES", "HALO", "EPS")]
    BRSTD = [Buf("RSTD0"), Buf("RSTD1")]
    BIDENT = Buf("IDENT")
    BSQ = [Buf("SQ0"), Buf("SQ1")]
    BSMALL = [Buf("SM0"), Buf("SM1")]
    if dry:
        PS = [DUMMY] * 8
    else:
        PS = [nc.alloc_psum_tensor(f"ps{i}", [128, 512], F32) for i in range(8)]
    BPS = [Buf(f"ps{i}") for i in range(8)]

    ws = WStream(k, RING, RB, wseq, ap_of)

    class Banks:
        def __init__(self):
            self.rot = list(range(8))
            self.i = 0

        def set(self, lst):
            self.rot = list(lst)
            self.i = 0

        def next(self):
            b = self.rot[self.i % len(self.rot)]
            self.i += 1
            return b

    banks = Banks()

    def mm(out, lhsT, rhs, first, last, reads, psb, inc=None):
        k.op("pe", lambda e: e.matmul(out, lhsT=lhsT, rhs=rhs, start=first, stop=last), reads=reads, writes=[psb],
             inc=last if inc is None else inc)

    def act(out, in_, func, reads, writes, bias=None, scale=None, accum=None):
        kw = {}
        if bias is not None:
            kw["bias"] = bias
        if scale is not None:
            kw["scale"] = scale
        if accum is not None:
            kw["accum_out"] = accum
        k.op("act", lambda e: e.activation(out=out, in_=in_, func=func, **kw), reads=reads, writes=writes)

    def tt_(out, in0, in1, op, reads, writes, eng="dve"):
        k.op(eng, lambda e: e.tensor_tensor(out=out, in0=in0, in1=in1, op=op), reads=reads, writes=writes)

    def tsc(out, in0, s1, s2, op0, op1, reads, writes, eng="dve"):
        if op1 is None:
            k.op(eng, lambda e: e.tensor_scalar(out=out, in0=in0, scalar1=s1, scalar2=None, op0=op0), reads=reads,
                 writes=writes)
        else:
            k.op(eng, lambda e: e.tensor_scalar(out=out, in0=in0, scalar1=s1, scalar2=s2, op0=op0, op1=op1),
                 reads=reads, writes=writes)

    def stt(out, in0, scalar, in1, op0, op1, reads, writes):
        k.op("dve", lambda e: e.scalar_tensor_tensor(out=out, in0=in0, scalar=scalar, in1=in1, op0=op0, op1=op1),
             reads=reads, writes=writes)

    def cpy(out, in_, reads, writes, eng="dve"):
        k.op(eng, lambda e: e.tensor_copy(out=out, in_=in_), reads=reads, writes=writes)

    def recip(out, in_, reads, writes):
        k.op("dve", lambda e: e.reciprocal(out=out, in_=in_), reads=reads, writes=writes)

    def memset(ap, val, writes, eng="dve"):
        k.op(eng, lambda e: e.memset(ap, val), writes=writes)

    dbg_out = []

    def chk(name):
        if stop_after == name:
            raise _Stop()

    def dump(name, ap, shape, dtype, bufs):
        if dry or dbg is None or name not in dbg:
            return
        t = nc.dram_tensor("dbg_" + name, list(shape), dtype, kind="ExternalOutput").ap()
        b = Buf("dbg_" + name)
        k.dma("sp", [(t, ap)], reads=bufs, writes=[b], sembuf=b)
        dbg_out.append(b)

    def proj_fm(tile, kcn, rhs_of, rbufs_of, evac, tiles=range(4)):
        for tt in tiles:
            b = banks.next()
            for kc in range(kcn):
                mm(PS[b][:, :], tile.ap[:, kc, :], rhs_of(kc, tt), kc == 0, kc == kcn - 1,
                   [tile.buf] + rbufs_of(tt), BPS[b])
            evac(tt, PS[b], BPS[b])

    def h_rhs(kc, tt):
        return H[:, kc, ts(tt)]

    def h_bufs(tt):
        return [BH[tt]]

    def rstd_from(psb_idx, ri, scale, epscol):
        act(RSTD[:, ri, :], PS[psb_idx][:, :], AF.Sqrt, [BPS[psb_idx], BEPS], [BRSTD[ri]], bias=EPS[:, epscol:epscol + 1],
            scale=scale)
        recip(RSTD[:, ri, :], RSTD[:, ri, :], [BRSTD[ri]], [BRSTD[ri]])

    nrm_ctr = [0]

    def norm_to_H(gcol, tiles):
        for tt in tiles:
            b = banks.next()
            for kc in range(8):
                si = kc % 2
                act(SQ[:, si, :], XRES[:, kc, ts(tt)], AF.Square, [BX[tt]], [BSQ[si]])
                mm(PS[b][:, :], ONESB[:, :], SQ[:, si, :], kc == 0, kc == 7, [BONES, BSQ[si]], BPS[b], inc=True)
            ri = nrm_ctr[0] % 2
            nrm_ctr[0] += 1
            rstd_from(b, ri, 1.0 / D, 0)
            for kc in range(8):
                stt(H[:, kc, ts(tt)], XRES[:, kc, ts(tt)], PP[:, gcol + kc:gcol + kc + 1], RSTD[:, ri, :], ALU.mult,
                    ALU.mult, [BX[tt], BPP, BRSTD[ri]], [BH[tt]])

    pend = []

    def y_chunk(ps_ap, psbuf, dc, ysel, sbank, first, last):
        ysl = slice(ysel * 512, ysel * 512 + 512)
        act(Y[:, dc, ysl], ps_ap, AF.Copy, [psbuf], [BY[ysel]])
        si = dc % 2
        act(SQ[:, si, :], ps_ap, AF.Square, [psbuf], [BSQ[si]])
        pend.append((sbank, si, first, last))
        if not DEFER_STATS:
            flush_stats()

    def flush_stats(keep=0):
        while len(pend) > keep:
            sbank, si, first, last = pend.pop(0)
            mm(PS[sbank][:, :], ONESB[:, :], SQ[:, si, :], first, last, [BONES, BSQ[si]], BPS[sbank], inc=True)

    def post_update(gcol, tt, ysel, sbank):
        flush_stats()
        ysl = slice(ysel * 512, ysel * 512 + 512)
        ri = nrm_ctr[0] % 2
        nrm_ctr[0] += 1
        rstd_from(sbank, ri, 1.0 / D, 0)
        for dc in range(8):
            tt_(Y[:, dc, ysl], Y[:, dc, ysl], RSTD[:, ri, :], ALU.mult, [BY[ysel], BRSTD[ri]], [BY[ysel]])
        for dc in range(8):
            stt(XRES[:, dc, ts(tt)], Y[:, dc, ysl], PP[:, gcol + dc:gcol + dc + 1], XRES[:, dc, ts(tt)], ALU.mult,
                ALU.add, [BY[ysel], BPP, BX[tt]], [BX[tt]])

    if not dry:
        k.dma("sp", [(PP[:, :], T["pp"][:, :])], writes=[BPP])
    memset(ONESB[:, :], 1.0, [BONES])
    memset(ONESF[:, :], 1.0, [BONES])
    memset(EPS[:, 0:1], 1e-6, [BEPS])
    memset(EPS[:, 1:2], 1e-5, [BEPS])
    if not dry:
        k.dma("pool", [(TRILM[:, :, :], T["trilm"][:, :, :])], writes=[BTRILM])
        k.dma("pool", [(IDENT[:, :], T["ident"][:, :])], writes=[BIDENT])
    A.reset()
    AB, BAB = A.alloc("abias", [128, 3072], F32)
    AM, BAM = A.alloc("amask", [128, 3072], F32)
    if not dry:
        k.dma("sp", [(AB[:, :], T["abias"][:, :])], writes=[BAB])
        k.dma("sp", [(AM[:, :], T["amask"][:, :])], writes=[BAM])
    if not dry:
        ebm_flat = EXPBM[:, :, :, :, :].rearrange("p g h c q -> p (g h c q)")
    else:
        ebm_flat = DUMMY
    act(AB[:, :], AB[:, :], AF.Exp, [BAB], [BAB])
    tt_(ebm_flat, AB[:, :], AM[:, :], ALU.mult, [BAB, BAM], [BEXPBM])

    def sublayer_mixer(l):
        pb = l * PPL
        banks.set(range(8))
        if not dry:
            k.dma("sp", [(PBC[:, :], T["pbc"][l, :, :])], writes=[BPBC])
            k.dma("pool", [(WSRAW[:, :, :], T["wsT"][l, :, :, :])], writes=[BWSRAW])
        tt_(WSM[:, :, :], WSRAW[:, :, :], TRILM[:, :, :], ALU.mult, [BWSRAW, BTRILM], [BWSM])
        norm_to_H(pb + 0, range(4))
        dump(f"h1_{l}", H[:, :, :], [128, 8, S], BF16, BH)
        if stop_after == "norm1":
            return True

        A.reset()
        OC, BOC = A.alloc("OC", [128, 2, S], BF16)
        keep_c = A.off
        Ut, BU = A.alloc("U", [128, 2, S], F32)
        Qb, BQ = A.alloc("Q", [128, 2, S], BF16, nbufs=2)
        Kb, BK = A.alloc("K", [128, 2, S], BF16, nbufs=2)
        VA, BVA = A.alloc("VA", [128, 2, 16, 2, 128], BF16, nbufs=2)
        NPT = 4
        PT, BPT = A.alloc("PT", [128, NPT, 2, 2, 128], BF16, nbufs=NPT)
        R_, BR = A.alloc("R", [128, 2, 512], F32, nbufs=2)
        memset(VA[:, :, :, 0, 64:128], 1.0, BVA)
        memset(VA[:, :, :, 1, 0:64], 1.0, BVA)
        gctr = 0
        for hp in range(2):
            for g in range(3):
                d = DILS[g]
                nb = 16 // d
                pq = gctr % 2
                gctr += 1
                tq = ws.get(("w_in", l, 16 + g * 2 + hp, 0, 8))
                tk = ws.get(("w_in", l, 22 + g * 2 + hp, 0, 8))
                tv = ws.get(("w_in", l, 28 + g * 2 + hp, 0, 8))
                for (tile, dst, bdst) in ((tq, Qb, BQ[pq]), (tk, Kb, BK[pq])):
                    def ev(tt, ps, psb, dst=dst, bdst=bdst, pq=pq):
                        act(dst[:, pq, ts(tt)], ps[:, :], AF.Copy, [psb], [bdst])
                    proj_fm(tile, 8, h_rhs, h_bufs, ev)
                    ws.done(tile)
                    chk("Cq")
                chk("Cqk")

                def tokslice(j, d=d, nb=nb):
                    c, n = divmod(j, nb)
                    st = c + d * 128 * n
                    return slice(st, st + d * 127 + 1, d), n

                for j0 in range(0, 16, 4):
                    b = banks.next()
                    for jj in range(4):
                        tok, n = tokslice(j0 + jj)
                        for kc in range(8):
                            mm(PS[b][:, jj * 128:(jj + 1) * 128], H[:, kc, tok], tv.ap[:, kc, :], kc == 0, kc == 7,
                               [tv.buf] + BH, BPS[b], inc=(kc == 7 and jj == 3))
                    psv = PS[b][:, :].rearrange("p (j c) -> p j c", c=128)
                    act(VA[:, pq, j0:j0 + 4, 0, 0:64], psv[:, :, 0:64], AF.Copy, [BPS[b]], [BVA[pq]])
                    act(VA[:, pq, j0:j0 + 4, 1, 64:128], psv[:, :, 64:128], AF.Copy, [BPS[b]], [BVA[pq]])
                ws.done(tv)
                chk("Cv")

                def stage1(j):
                    tok, n = tokslice(j)
                    pcs = (0, 1) if n > 0 else (1,)
                    eb = j % NPT
                    for h in range(2):
                        b = banks.next()
                        pr = slice(64 * h, 64 * h + 64)
                        for pc in pcs:
                            ktok, _ = tokslice(j - 1 if pc == 0 else j)
                            mm(PS[b][:, pc * 128:(pc + 1) * 128], Kb[pr, pq, ktok], Qb[pr, pq, tok], True, True,
                               [BQ[pq], BK[pq]], BPS[b], inc=(pc == 1))
                        psv = PS[b][:, 0:256].rearrange("p (c q) -> p c q", c=2)
                        if n > 0:
                            p_ap, s_ap = PT[:, eb, h], psv
                        else:
                            p_ap, s_ap = PT[:, eb, h, 1, :], psv[:, 1, :]
                        act(p_ap, s_ap, AF.Exp, [BPS[b]], [BPT[eb]], scale=0.125)
                    if n > 0:
                        p_ap, m_ap = PT[:, eb], EXPBM[:, g, hp * 2:hp * 2 + 2]
                    else:
                        p_ap, m_ap = PT[:, eb, :, 1, :], EXPBM[:, g, hp * 2:hp * 2 + 2, 1, :]
                    tt_(p_ap, p_ap, m_ap, ALU.mult, [BPT[eb], BEXPBM], [BPT[eb]])

                def stage2(j):
                    tok, n = tokslice(j)
                    pcs = (0, 1) if n > 0 else (1,)
                    eb = j % NPT
                    b2 = banks.next()
                    for h in range(2):
                        for i, pc in enumerate(pcs):
                            vj = j - 1 if pc == 0 else j
                            mm(PS[b2][:, h * 128:(h + 1) * 128], VA[:, pq, vj, h, :], PT[:, eb, h, pc, :], i == 0,
                               i == len(pcs) - 1, [BVA[pq], BPT[eb]], BPS[b2], inc=(h == 1 and i == len(pcs) - 1))
                    psu = PS[b2][:, 0:256].rearrange("p (h q) -> p h q", h=2)
                    if g == 0:
                        act(Ut[:, :, tok], psu, AF.Copy, [BPS[b2]], [BU])
                    else:
                        tt_(Ut[:, :, tok], psu, Ut[:, :, tok], ALU.add, [BPS[b2], BU], [BU])

                DEPTH_C = 2
                for j in range(16):
                    stage1(j)
                    if j >= DEPTH_C:
                        stage2(j - DEPTH_C)
                for j in range(16 - DEPTH_C, 16):
                    stage2(j)
            for tq in range(4):
                ri = tq % 2
                recip(R_[0:64, ri, :], Ut[64:128, 0, ts(tq)], [BU], [BR[ri]])
                recip(R_[64:128, ri, :], Ut[0:64, 1, ts(tq)], [BU], [BR[ri]])
                tt_(OC[0:64, hp, ts(tq)], Ut[0:64, 0, ts(tq)], R_[0:64, ri, :], ALU.mult, [BU, BR[ri]], [BOC])
                tt_(OC[64:128, hp, ts(tq)], Ut[64:128, 1, ts(tq)], R_[64:128, ri, :], ALU.mult, [BU, BR[ri]], [BOC])
        dump(f"oc{l}", OC[:, :, :], [128, 2, S], BF16, [BOC])
        if stop_after == "C":
            return True

        A.reset(keep_c)
        AACT, BAACT = A.alloc("AACT", [128, 4, S], BF16)
        keep_a = A.off
        APAD, BAPAD = A.alloc("APAD", [128, 4, 32 + S], BF16)
        tmp_o = A.off
        SG, BSG = A.alloc("SG", [128, 2, 512], F32, nbufs=2)
        A.reset(tmp_o)
        CT, BCT = A.alloc("CT", [128, 2, 4, 512], F32, nbufs=8)
        NDG = 8
        DG, BDG = A.alloc("DG", [128, NDG, 128], BF16, nbufs=NDG)
        SQF, BSQF = A.alloc("SQF", [128, 2, 512], F32, nbufs=2)
        MEAN, BMEAN = A.alloc("MEAN", [128, 512], F32)
        VAR, BVAR = A.alloc("VAR", [128, 512], F32)
        memset(APAD[:, :, 0:32], 0.0, [BAPAD])
        for c in range(4):
            tval = ws.get(("w_in", l, c, 0, 8))
            tgate = ws.get(("w_in", l, 4 + c, 0, 8))
            for tt in range(4):
                bv = banks.next()
                bg = banks.next()
                for kc in range(8):
                    mm(PS[bv][:, :], tval.ap[:, kc, :], H[:, kc, ts(tt)], kc == 0, kc == 7, [tval.buf, BH[tt]], BPS[bv])
                for kc in range(8):
                    mm(PS[bg][:, :], tgate.ap[:, kc, :], H[:, kc, ts(tt)], kc == 0, kc == 7, [tgate.buf, BH[tt]],
                       BPS[bg])
                si = tt % 2
                act(SG[:, si, :], PS[bg][:, :], AF.Sigmoid, [BPS[bg]], [BSG[si]])
                tt_(APAD[:, c, 32 + tt * 512:32 + (tt + 1) * 512], PS[bv][:, :], SG[:, si, :], ALU.mult,
                    [BPS[bv], BSG[si]], [BAPAD])
            ws.done(tval)
            ws.done(tgate)
        dump(f"glu{l}", APAD[:, :, :], [128, 4, 32 + S], BF16, [BAPAD])
        cw = pb + 80
        dctr = [0]

        def conv_tile(tt):
            ci = tt % 2
            base = 2 + tt * 512
            for c in range(4):
                b = banks.next()
                for kk in range(31):
                    sl = dctr[0] % NDG
                    dctr[0] += 1
                    wcol = PP[:, cw + kk * 4 + c:cw + kk * 4 + c + 1]
                    if kk % 2 == 0:
                        act(DG[:, sl, :], IDENT[:, :], AF.Identity, [BIDENT, BPP], [BDG[sl]], scale=wcol)
                    else:
                        tsc(DG[:, sl, :], IDENT[:, :], wcol, None, ALU.mult, None, [BIDENT, BPP], [BDG[sl]])
                    mm(PS[b][:, :], DG[:, sl, :], APAD[:, c, base + kk:base + kk + 512], kk == 0, kk == 30,
                       [BDG[sl], BAPAD], BPS[b], inc=True)
                act(CT[:, ci, c, :], PS[b][:, :], AF.Identity, [BPS[b], BPP], [BCT[ci * 4 + c]],
                    bias=PP[:, pb + 204 + c:pb + 205 + c])

        def ln_tile(tt):
            ci = tt % 2
            bsum = banks.next()
            bsq = banks.next()
            for c in range(4):
                o = CT[:, ci, c, :]
                bc = BCT[ci * 4 + c]
                si = c % 2
                act(SQF[:, si, :], o, AF.Square, [bc], [BSQF[si]])
                mm(PS[bsum][:, :], ONESF[:, :], o, c == 0, c == 3, [BONES, bc], BPS[bsum], inc=True)
                mm(PS[bsq][:, :], ONESF[:, :], SQF[:, si, :], c == 0, c == 3, [BONES, BSQF[si]], BPS[bsq], inc=True)
            act(MEAN[:, :], PS[bsum][:, :], AF.Identity, [BPS[bsum]], [BMEAN], scale=1.0 / 512)
            tt_(VAR[:, :], MEAN[:, :], MEAN[:, :], ALU.mult, [BMEAN], [BVAR])
            stt(VAR[:, :], PS[bsq][:, :], 1.0 / 512, VAR[:, :], ALU.mult, ALU.subtract, [BPS[bsq], BVAR], [BVAR])
            act(VAR[:, :], VAR[:, :], AF.Sqrt, [BVAR, BEPS], [BVAR], bias=EPS[:, 1:2])
            recip(VAR[:, :], VAR[:, :], [BVAR], [BVAR])
            for c in range(4):
                tt_(CT[:, ci, c, :], CT[:, ci, c, :], MEAN[:, :], ALU.subtract, [BCT[ci * 4 + c], BMEAN], [BCT[ci * 4 + c]])
            for c in range(4):
                tt_(CT[:, ci, c, :], CT[:, ci, c, :], VAR[:, :], ALU.mult, [BCT[ci * 4 + c], BVAR], [BCT[ci * 4 + c]])
            for c in range(4):
                act(AACT[:, c, ts(tt)], CT[:, ci, c, :], AF.Silu, [BCT[ci * 4 + c], BPP], [BAACT],
                    bias=PP[:, pb + 212 + c:pb + 213 + c], scale=PP[:, pb + 208 + c:pb + 209 + c])

        conv_tile(0)
        for tt in range(4):
            if tt + 1 < 4:
                conv_tile(tt + 1)
            ln_tile(tt)
        dump(f"aact{l}", AACT[:, :, :], [128, 4, S], BF16, [BAACT])
        if stop_after == "A":
            return True

        A.reset(keep_a)
        UB, BUB = A.alloc("UB", [128, 4, S], BF16)
        keep_b = A.off
        UU, BUU = A.alloc("UU", [128, 4, S], BF16)
        VG, BVG = A.alloc("VG", [128, 2, 512], F32, nbufs=2)
        VL, BVL = A.alloc("VL", [128, 2, 512], BF16, nbufs=2)
        TB, BTB = A.alloc("TB", [128, 512], F32)
        CB, BCB = A.alloc("CB", [128, 512], F32)
        brw = banks.next()
        for g in range(4):
            mm(PS[brw][:, g * 128:(g + 1) * 128], ONESB[:, :], WSM[:, g, :], True, True, [BONES, BWSM], BPS[brw],
               inc=(g == 3))
        for g in range(4):
            stt(CB[:, g * 128:(g + 1) * 128], PS[brw][:, g * 128:(g + 1) * 128], PP[:, pb + 308 + g:pb + 309 + g],
                PBC[:, 1024 + g * 128:1024 + (g + 1) * 128], ALU.mult, ALU.add, [BPS[brw], BPP, BPBC], [BCB])
        for c in range(4):
            tu = ws.get(("w_in", l, 8 + c, 0, 8))

            def ev(tt, ps, psb, c=c):
                act(UU[:, c, ts(tt)], ps[:, :], AF.Gelu_apprx_tanh, [psb], [BUU])
            proj_fm(tu, 8, h_rhs, h_bufs, ev)
            ws.done(tu)
        tvs = [ws.get(("w_in", l, 12 + cc, 0, 8)) for cc in range(4)]
        def b_stage_a(n):
            b = banks.next()
            tt = n // 4
            for cc in range(4):
                for kc in range(8):
                    mm(PS[b][:, cc * 128:(cc + 1) * 128], H[:, kc, n * 128:(n + 1) * 128], tvs[cc].ap[:, kc, :], kc == 0,
                       kc == 7, [tvs[cc].buf, BH[tt]], BPS[b], inc=(kc == 7 and cc == 3))
            vi = n % 2
            act(VG[:, vi, :], PS[b][:, :], AF.Gelu_apprx_tanh, [BPS[b]], [BVG[vi]])
            k.op("dve", lambda e, o=ST6[:, vi, :], i=VG[:, vi, :]: e.bn_stats(out=o, in_=i), reads=[BVG[vi]],
                 writes=[BSMALL[vi]])
            k.op("dve", lambda e, o=SMALL[:, vi, 2:4], i=ST6[:, vi, :]: e.bn_aggr(out=o, in_=i), reads=[BSMALL[vi]],
                 writes=[BSMALL[vi]])
            act(SMALL[:, vi, 5:6], SMALL[:, vi, 3:4], AF.Sqrt, [BSMALL[vi], BEPS], [BSMALL[vi]], bias=EPS[:, 1:2])

        def b_stage_b(n):
            vi = n % 2
            recip(SMALL[:, vi, 6:7], SMALL[:, vi, 5:6], [BSMALL[vi]], [BSMALL[vi]])
            tsc(VL[:, vi, :], VG[:, vi, :], SMALL[:, vi, 2:3], SMALL[:, vi, 6:7], ALU.subtract, ALU.mult,
                [BVG[vi], BSMALL[vi]], [BVL[vi]])
            b2 = banks.next()
            for g in range(4):
                mm(PS[b2][:, g * 128:(g + 1) * 128], VL[:, vi, g * 128:(g + 1) * 128], WSM[:, g, :], True, True,
                   [BVL[vi], BWSM], BPS[b2], inc=(g == 3))
            for g in range(4):
                stt(TB[:, g * 128:(g + 1) * 128], PS[b2][:, g * 128:(g + 1) * 128], PP[:, pb + 304 + g:pb + 305 + g],
                    CB[:, g * 128:(g + 1) * 128], ALU.mult, ALU.add, [BPS[b2], BPP, BCB], [BTB])
            tbv = TB[:, :].rearrange("p (g t) -> p g t", g=4)
            tt_(UB[:, :, n * 128:(n + 1) * 128], tbv, UU[:, :, n * 128:(n + 1) * 128], ALU.mult, [BTB, BUU], [BUB])

        b_stage_a(0)
        for n in range(16):
            if n + 1 < 16:
                b_stage_a(n + 1)
            b_stage_b(n)
        for t_ in tvs:
            ws.done(t_)
        dump(f"ub{l}", UB[:, :, :], [128, 4, S], BF16, [BUB])
        if stop_after == "B":
            return True

        A.reset(keep_b)
        MG, BMG = A.alloc("MG", [128, 8, 1024], BF16)
        SGM, BSGM = A.alloc("SGM", [128, 3, 512], F32)
        M0, BM0 = A.alloc("M0", [128, 512], F32)
        M1, BM1 = A.alloc("M1", [128, 512], F32)
        for hf in range(2):
            banks.set(range(8))
            for dc in range(8):
                wa = ws.get(("w_a", l, dc, 0, 4))
                wb = ws.get(("w_b", l, dc, 0, 4))
                wc = ws.get(("w_c", l, dc, 0, 2))
                wg = [ws.get(("w_in", l, 34 + i * 8 + dc, 0, 8)) for i in range(3)]
                for t2 in range(2):
                    tt = hf * 2 + t2
                    pa, pb_, pc_ = banks.next(), banks.next(), banks.next()
                    pg = [banks.next() for _ in range(3)]
                    for kc in range(4):
                        mm(PS[pa][:, :], wa.ap[:, kc, :], AACT[:, kc, ts(tt)], kc == 0, kc == 3, [wa.buf, BAACT], BPS[pa])
                    for kc in range(4):
                        mm(PS[pb_][:, :], wb.ap[:, kc, :], UB[:, kc, ts(tt)], kc == 0, kc == 3, [wb.buf, BUB], BPS[pb_])
                    for kc in range(2):
                        mm(PS[pc_][:, :], wc.ap[:, kc, :], OC[:, kc, ts(tt)], kc == 0, kc == 1, [wc.buf, BOC], BPS[pc_])
                    for i in range(3):
                        for kc in range(8):
                            mm(PS[pg[i]][:, :], wg[i].ap[:, kc, :], H[:, kc, ts(tt)], kc == 0, kc == 7,
                               [wg[i].buf, BH[tt]], BPS[pg[i]])
                    for i in range(3):
                        col = pb + 56 + i * 8 + dc
                        act(SGM[:, i, :], PS[pg[i]][:, :], AF.Sigmoid, [BPS[pg[i]], BPP], [BSGM], bias=PP[:, col:col + 1])
                    tt_(M0[:, :], PS[pa][:, :], SGM[:, 0, :], ALU.mult, [BPS[pa], BSGM], [BM0])
                    tt_(M1[:, :], PS[pb_][:, :], SGM[:, 1, :], ALU.mult, [BPS[pb_], BSGM], [BM1])
                    tt_(M0[:, :], M0[:, :], M1[:, :], ALU.add, [BM0, BM1], [BM0])
                    tt_(M1[:, :], PS[pc_][:, :], SGM[:, 2, :], ALU.mult, [BPS[pc_], BSGM], [BM1])
                    tt_(MG[:, dc, ts(t2)], M0[:, :], M1[:, :], ALU.add, [BM0, BM1], [BMG])
                for t_ in [wa, wb, wc] + wg:
                    ws.done(t_)
            if dbg is not None:
                dump(f"merged{l}_{hf}", MG[:, :, :], [128, 8, 1024], BF16, [BMG])
            banks.set(range(6))
            wm = [ws.get(("w_mix", l, dc, 0, 8)) for dc in range(8)]
            for t2 in range(2):
                tt = hf * 2 + t2
                sb_ = 6 + t2
                for dc in range(8):
                    b = banks.next()
                    for kc in range(8):
                        mm(PS[b][:, :], wm[dc].ap[:, kc, :], MG[:, kc, ts(t2)], kc == 0, kc == 7, [wm[dc].buf, BMG], BPS[b])
                    flush_stats()
                    y_chunk(PS[b][:, :], BPS[b], dc, hf, sb_, dc == 0, dc == 7)
                post_update(pb + 8, tt, hf, sb_)
            for t_ in wm:
                ws.done(t_)
        return False

    def sublayer_xattn(l, s):
        pb = l * PPL
        banks.set(range(8))
        A.reset()
        MEMS, BMEMS = A.alloc("MEMS", [128, 8, 256], F32)
        KT, BKT = A.alloc("KT", [128, 8, 256], BF16)
        VX, BVX = A.alloc("VX", [128, 2, 1024], BF16)
        QX, BQX = A.alloc("QX", [128, 8, S], BF16)
        PTX, BPTX = A.alloc("PTX", [128, 2, 2, 512], BF16, nbufs=2)
        RD, BRD = A.alloc("RD", [128, 2, 512], F32, nbufs=2)
        memn_o = A.off
        MEMN, BMEMN = A.alloc("MEMN", [128, 8, 256], BF16)
        A.reset(0)
        OX0, BOX0 = A.alloc("OX0", [128, 8, 512], BF16)
        A.reset(memn_o)
        OX1, BOX1 = A.alloc("OX1", [128, 8, 512], BF16)
        OXs, BOXs = [OX0, OX1], [BOX0, BOX1]
        if not dry:
            k.dma("sp", [(MEMS[:, kc, :], T["memT"][s, kc, :, :]) for kc in range(8)], writes=[BMEMS])
        b = banks.next()
        for kc in range(8):
            si = kc % 2
            act(SQ[:, si, 0:256], MEMS[:, kc, :], AF.Square, [BMEMS], [BSQ[si]])
            mm(PS[b][:, 0:256], ONESB[:, :], SQ[:, si, 0:256], kc == 0, kc == 7, [BONES, BSQ[si]], BPS[b], inc=True)
        ri = nrm_ctr[0] % 2
        nrm_ctr[0] += 1
        act(RSTD[:, ri, 0:256], PS[b][:, 0:256], AF.Sqrt, [BPS[b], BEPS], [BRSTD[ri]], bias=EPS[:, 0:1], scale=1.0 / D)
        recip(RSTD[:, ri, 0:256], RSTD[:, ri, 0:256], [BRSTD[ri]], [BRSTD[ri]])
        for kc in range(8):
            stt(MEMN[:, kc, :], MEMS[:, kc, :], PP[:, pb + 32 + kc:pb + 33 + kc], RSTD[:, ri, 0:256], ALU.mult, ALU.mult,
                [BMEMS, BPP, BRSTD[ri]], [BMEMN])
        def kproj(ec):
            w = ws.get(("w_xkv", l, ec, 0, 8))
            b = banks.next()
            for kc in range(8):
                mm(PS[b][:, 0:256], w.ap[:, kc, :], MEMN[:, kc, :], kc == 0, kc == 7, [w.buf, BMEMN], BPS[b])
            act(KT[:, ec, :], PS[b][:, 0:256], AF.Copy, [BPS[b]], [BKT])
            ws.done(w)

        def vproj(ec):
            w = ws.get(("w_xkv", l, 8 + ec, 0, 8))
            b = banks.next()
            for mb in range(2):
                for kc in range(8):
                    mm(PS[b][:, mb * 128:(mb + 1) * 128], MEMN[:, kc, mb * 128:(mb + 1) * 128], w.ap[:, kc, :], kc == 0,
                       kc == 7, [w.buf, BMEMN], BPS[b], inc=(kc == 7 and mb == 1))
            psv = PS[b][:, 0:256].rearrange("p (m c) -> p m c", m=2)
            act(VX[:, :, ec * 128:(ec + 1) * 128], psv, AF.Copy, [BPS[b]], [BVX])
            ws.done(w)

        for tt in range(4):
            norm_to_H(pb + 16, (tt,))
            for ec in (2 * tt, 2 * tt + 1):
                kproj(ec)
        for ec in range(8):
            vproj(ec)
        for dc in range(8):
            w = ws.get(("w_xq", l, dc, 0, 8))

            def ev(tt, ps, psb, dc=dc):
                act(QX[:, dc, ts(tt)], ps[:, :], AF.Copy, [psb], [BQX])
            proj_fm(w, 8, h_rhs, h_bufs, ev)
            ws.done(w)
        wo = [ws.get(("w_xo", l, dc, 0, 8)) for dc in range(8)]
        banks.set(range(6))

        def attn_tile(tt):
            OX, BOX = OXs[tt % 2], BOXs[tt % 2]

            def xs1(hd):
                pi = hd % 2
                bs_ = [banks.next(), banks.next()]
                for mb in range(2):
                    for ec in range(2):
                        mm(PS[bs_[mb]][:, :], KT[:, hd * 2 + ec, mb * 128:(mb + 1) * 128], QX[:, hd * 2 + ec, ts(tt)],
                           ec == 0, ec == 1, [BKT, BQX], BPS[bs_[mb]])
                    act(PTX[:, pi, mb, :], PS[bs_[mb]][:, :], AF.Exp, [BPS[bs_[mb]]], [BPTX[pi]], scale=1.0 / 16)

            def xs2(hd):
                pi = hd % 2
                bd = banks.next()
                for mb in range(2):
                    mm(PS[bd][:, :], ONESB[:, :], PTX[:, pi, mb, :], mb == 0, mb == 1, [BONES, BPTX[pi]], BPS[bd])
                recip(RD[:, pi, :], PS[bd][:, :], [BPS[bd]], [BRD[pi]])
                for ec in range(2):
                    bo = banks.next()
                    for mb in range(2):
                        mm(PS[bo][:, :], VX[:, mb, (hd * 2 + ec) * 128:(hd * 2 + ec + 1) * 128], PTX[:, pi, mb, :], mb == 0,
                           mb == 1, [BVX, BPTX[pi]], BPS[bo])
                    tt_(OX[:, hd * 2 + ec, :], PS[bo][:, :], RD[:, pi, :], ALU.mult, [BPS[bo], BRD[pi]], [BOX])

            for hd in range(4):
                xs1(hd)
                if hd > 0:
                    xs2(hd - 1)
            xs2(3)

        def xo_tile(tt):
            OX, BOX = OXs[tt % 2], BOXs[tt % 2]
            ysel = tt % 2
            sb_ = 6 + (tt % 2)
            for dc in range(8):
                b = banks.next()
                for kc in range(8):
                    mm(PS[b][:, :], wo[dc].ap[:, kc, :], OX[:, kc, :], kc == 0, kc == 7, [wo[dc].buf, BOX], BPS[b])
                flush_stats()
                y_chunk(PS[b][:, :], BPS[b], dc, ysel, sb_, dc == 0, dc == 7)
            flush_stats()
            if not XATTN_DEFER_POST:
                post_update(pb + 24, tt, ysel, sb_)

        def xpost(tt):
            post_update(pb + 24, tt, tt % 2, 6 + (tt % 2))

        if XATTN_DEFER_POST:
            for tt in range(4):
                attn_tile(tt)
                if tt > 0:
                    xpost(tt - 1)
                xo_tile(tt)
            xpost(3)
        elif XATTN_PIPE:
            for tt in range(4):
                attn_tile(tt)
                if tt > 0:
                    xo_tile(tt - 1)
            xo_tile(3)
        else:
            for tt in range(4):
                attn_tile(tt)
                xo_tile(tt)
        for t_ in wo:
            ws.done(t_)

    def sublayer_ffn(l, tile_done=None):
        pb = l * PPL
        A.reset()
        ACTB, BACTB = A.alloc("ACTB", [128, 22, 1024], BF16)
        GT, BGT = A.alloc("GT", [128, 2, 516], F32, nbufs=2)
        CV, BCV = A.alloc("CV", [128, 512], F32)
        GL, BGL = A.alloc("GL", [128, 2, 512], F32, nbufs=2)
        memset(HALO[:, :, :], 0.0, [BHALO])
        fw = pb + 216
        gi = 0
        for hf in range(2):
            banks.set(range(8))
            norm_to_H(pb + 40, (2 * hf, 2 * hf + 1))
            for j in range(22):
                wg = ws.get(("w_up", l, j, 0, 8))
                wv = ws.get(("w_up", l, 22 + j, 0, 8))
                for t2 in range(2):
                    tt = hf * 2 + t2
                    bg, bv = banks.next(), banks.next()
                    for kc in range(8):
                        mm(PS[bg][:, :], wg.ap[:, kc, :], H[:, kc, ts(tt)], kc == 0, kc == 7, [wg.buf, BH[tt]], BPS[bg])
                    for kc in range(8):
                        mm(PS[bv][:, :], wv.ap[:, kc, :], H[:, kc, ts(tt)], kc == 0, kc == 7, [wv.buf, BH[tt]], BPS[bv])
                    g2 = gi % 2
                    gi += 1
                    act(GT[:, g2, 2:514], PS[bg][:, :], AF.Copy, [BPS[bg]], [BGT[g2]])
                    cpy(GT[:, g2, 0:2], HALO[:, j, :], [BHALO], [BGT[g2]])
                    cpy(HALO[:, j, :], GT[:, g2, 512:514], [BGT[g2]], [BHALO])
                    act(CV[:, :], PS[bg][:, :], AF.Identity, [BPS[bg], BPP], [BCV], bias=PP[:, pb + 282 + j:pb + 283 + j],
                        scale=PP[:, fw + 44 + j:fw + 45 + j])
                    stt(CV[:, :], GT[:, g2, 0:512], PP[:, fw + j:fw + j + 1], CV[:, :], ALU.mult, ALU.add,
                        [BGT[g2], BPP, BCV], [BCV])
                    stt(CV[:, :], GT[:, g2, 1:513], PP[:, fw + 22 + j:fw + 23 + j], CV[:, :], ALU.mult, ALU.add,
                        [BGT[g2], BPP, BCV], [BCV])
                    act(GL[:, g2, :], CV[:, :], AF.Gelu_apprx_tanh, [BCV], [BGL[g2]])
                    tt_(ACTB[:, j, ts(t2)], PS[bv][:, :], GL[:, g2, :], ALU.mult, [BPS[bv], BGL[g2]], [BACTB])
                ws.done(wg)
                ws.done(wv)
            banks.set(range(6))
            ysels = (1, 0) if hf == 0 else (0, 1)
            for dc in range(8):
                wd = [ws.get(("w_down", l, dc, 0, 8)), ws.get(("w_down", l, dc, 8, 8)), ws.get(("w_down", l, dc, 16, 6))]
                for t2 in range(2):
                    b = banks.next()
                    for kc in range(22):
                        w = wd[kc // 8]
                        mm(PS[b][:, :], w.ap[:, kc % 8, :], ACTB[:, kc, ts(t2)], kc == 0, kc == 21, [w.buf, BACTB], BPS[b])
                    flush_stats()
                    y_chunk(PS[b][:, :], BPS[b], dc, ysels[t2], 6 + t2, dc == 0, dc == 7)
                for w in wd:
                    ws.done(w)
            for t2 in range(2):
                tt = hf * 2 + t2
                post_update(pb + 48, tt, ysels[t2], 6 + t2)
                if tile_done is not None:
                    tile_done(tt)

    stopped = False

    def load_x(s_, tt):
        if not dry:
            k.dma("sp", [(XRES[:, kc, ts(tt)], T["xT"][s_, kc, :, ts(tt)]) for kc in range(8)], writes=[BX[tt]])

    def store_x(s_, tt):
        if not dry:
            k.dma("sp", [(T["outT"][s_, kc, :, ts(tt)], XRES[:, kc, ts(tt)]) for kc in range(8)], reads=[BX[tt]],
                  sembuf=BX[tt])

    try:
        for tt in range(4):
            load_x(0, tt)
        for s in range(nseq):
            def tile_done(tt, s=s):
                store_x(s, tt)
                if s + 1 < nseq:
                    load_x(s + 1, tt)
            for l in range(nlayer):
                stopped = sublayer_mixer(l)
                if stopped:
                    break
                dump(f"x1_{l}", XRES[:, :, :], [128, 8, S], F32, BX)
                if stop_after == "mixer":
                    stopped = True
                    break
                sublayer_xattn(l, s)
                dump(f"x2_{l}", XRES[:, :, :], [128, 8, S], F32, BX)
                if stop_after == "xattn":
                    stopped = True
                    break
                sublayer_ffn(l, tile_done if l == nlayer - 1 else None)
            if stopped:
                for tt in range(4):
                    store_x(s, tt)
                break
    except _Stop:
        for tt in range(4):
            store_x(0, tt)
    k.wait_all("sp", BX + dbg_out)
    return ws


def make_program(nseq=SEQ_PER_CORE, nlayer=DEPTH, stop_after=None, dbg=None):
    kd = K(None, dry=True)
    wsd = build_program(None, kd, None, nseq, nlayer, stop_after, dbg)
    seq = wsd.rec
    nc = bass.Bass("TRN2", target_bir_lowering=False)
    k = K(nc)
    ws = build_program(nc, k, seq, nseq, nlayer, stop_after, dbg)
    assert ws.idx == len(seq) and ws.loaded == len(seq)
    return nc, k


def _t5_bucket(n):
    n = np.maximum(n, 0)
    nf = np.maximum(n, 1).astype(np.float32)
    large = 16 + (np.log(nf / np.float32(16)) / np.float32(np.log(2048 / 16)) * np.float32(16)).astype(np.int32)
    large = np.minimum(large, 31)
    return np.where(n < 16, n, large)


def _wtiles(W):
    L, Kd, N = W.shape
    return np.ascontiguousarray(W.reshape(L, Kd // 128, 128, N // 128, 128).transpose(0, 3, 2, 1, 4))


def _cols(v):
    v = np.asarray(v)
    C = v.shape[-1]
    lead = v.shape[:-1]
    a = v.reshape(*lead, C // 128, 128)
    a = np.moveaxis(a, -1, 0)
    return a.reshape(128, -1)


def prep_shared(inp):
    L = DEPTH
    pp = np.zeros((128, L * PPL), np.float32)
    for l in range(L):
        o = l * PPL
        pp[:, o + 0:o + 8] = _cols(inp["mix_pre_g"][l])
        pp[:, o + 8:o + 16] = _cols(inp["mix_post_g"][l])
        pp[:, o + 16:o + 24] = _cols(inp["x_pre_g"][l])
        pp[:, o + 24:o + 32] = _cols(inp["x_post_g"][l])
        pp[:, o + 32:o + 40] = _cols(inp["mem_g"][l])
        pp[:, o + 40:o + 48] = _cols(inp["ffn_pre_g"][l])
        pp[:, o + 48:o + 56] = _cols(inp["ffn_post_g"][l])
        pp[:, o + 56:o + 80] = _cols(inp["b_gate"][l])
        pp[:, o + 80:o + 204] = _cols(inp["conv_a_w"][l])
        pp[:, o + 204:o + 208] = _cols(inp["conv_a_b"][l])
        pp[:, o + 208:o + 212] = _cols(inp["ln_a_g"][l])
        pp[:, o + 212:o + 216] = _cols(inp["ln_a_b"][l])
        pp[:, o + 216:o + 282] = _cols(inp["conv_f_w"][l])
        pp[:, o + 282:o + 304] = _cols(inp["conv_f_b"][l])
        pp[:, o + 304:o + 308] = _cols(inp["ln_b_g"][l])
        pp[:, o + 308:o + 312] = _cols(inp["ln_b_b"][l])
    rb = np.asarray(inp["rel_bias"])
    kk = np.arange(128)[:, None]
    qq = np.arange(128)[None, :]
    abias = np.zeros((128, 3, 4, 2, 128), np.float32)
    amask = np.zeros((128, 3, 4, 2, 128), np.float32)
    for g, dil in enumerate(DILS):
        for pc in range(2):
            rel = qq + 128 - kk if pc == 0 else qq - kk
            valid = (rel >= 0) & (rel <= 128)
            bucket = _t5_bucket(rel * dil)
            for h in range(4):
                abias[:, g, h, pc, :] = rb[bucket, g * 4 + h]
                amask[:, g, h, pc, :] = valid
    pbc = np.zeros((L, 128, 1536), np.float32)
    for l in range(L):
        pbc[l, :, 0:512] = np.broadcast_to(inp["ln_b_g"][l][None, :], (128, 512))
        pbc[l, :, 512:1024] = np.broadcast_to(inp["ln_b_b"][l][None, :], (128, 512))
        pbc[l, :, 1024:1536] = np.broadcast_to(np.asarray(inp["b_s"][l]).reshape(1, 512), (128, 512))
    wsT = np.ascontiguousarray(np.asarray(inp["w_s"]).transpose(0, 3, 1, 2))
    tril = (np.arange(128)[None, :] >= np.arange(128)[:, None]).astype(np.float32)
    trilm = np.ascontiguousarray(np.broadcast_to(tril[:, None, :], (128, 4, 128)))
    w_c = np.asarray(inp["w_c_out"])
    shared = {
        "pp": pp,
        "abias": abias.reshape(128, 3072),
        "amask": amask.reshape(128, 3072),
        "pbc": pbc,
        "wsT": wsT,
        "trilm": trilm,
        "ident": np.eye(128, dtype=np.float32),
        "w_in": _wtiles(np.asarray(inp["w_in"])),
        "w_a": _wtiles(np.asarray(inp["w_a_out"])),
        "w_b": _wtiles(np.asarray(inp["w_b_out"])),
        "w_c": _wtiles(w_c),
        "w_mix": _wtiles(np.asarray(inp["w_mix_out"])),
        "w_xq": _wtiles(np.asarray(inp["w_xq"])),
        "w_xkv": _wtiles(np.asarray(inp["w_xkv"])),
        "w_xo": _wtiles(np.asarray(inp["w_xo"])),
        "w_up": _wtiles(np.asarray(inp["w_up"])),
        "w_down": _wtiles(np.asarray(inp["w_down"])),
    }
    return shared


def prep_core(inp, c, nseq=SEQ_PER_CORE):
    xs = np.asarray(inp["x"][c * SEQ_PER_CORE:c * SEQ_PER_CORE + nseq])
    ms = np.asarray(inp["mem"][c * SEQ_PER_CORE:c * SEQ_PER_CORE + nseq])
    xT = np.zeros((SEQ_PER_CORE, 8, 128, S), np.float32)
    mT = np.zeros((SEQ_PER_CORE, 8, 128, 256), np.float32)
    xT[:nseq] = xs.transpose(0, 2, 1).reshape(nseq, 8, 128, S)
    mT[:nseq] = ms.transpose(0, 2, 1).reshape(nseq, 8, 128, 256)
    return {"xT": xT, "memT": mT}


_PROG = {}


def kernel(**inputs):
    if "p" not in _PROG:
        _PROG["p"] = make_program()
    nc, _ = _PROG["p"]
    shared = prep_shared(inputs)
    in_maps = []
    for c in range(N_CORES):
        m = dict(shared)
        m.update(prep_core(inputs, c))
        in_maps.append(m)
    res = run_bass_kernel_spmd(nc, in_maps, core_ids=list(range(N_CORES)))
    out = np.empty((N_CORES * SEQ_PER_CORE, S, D), np.float32)
    for c in range(N_CORES):
        oT = res.results[c]["outT"]
        out[c * SEQ_PER_CORE:(c + 1) * SEQ_PER_CORE] = oT.reshape(SEQ_PER_CORE, D, S).transpose(0, 2, 1)
    return out
```
